# Optimizing a Trainium2 kernel written in Bass

```python
import math
import jax, jax.numpy as jnp
from jax import lax
import numpy as np

D_MODEL = 2048
BATCH = 4
SEQ = 2048
DEPTH = 1
DEC_BATCH = 128
DEC_SEQ = 1
PAST_LEN = 16384
PAGE_SIZE = 128

N_MEM = 256
S5_WIDTH = D_MODEL // 2
S5_GROUP = 16
S5_GROUPS = S5_WIDTH // S5_GROUP
S5_STATE = 64
DT_MIN = 1e-3
DT_MAX = 1e-1
ML_WIDTH = D_MODEL // 2
ML_HEADS = 4
ML_HEAD_DIM = ML_WIDTH // ML_HEADS
ML_CHUNK = 64
XA_WIDTH = D_MODEL // 2
XA_HEADS = 4
XA_HEAD_DIM = XA_WIDTH // XA_HEADS
N_BRANCH = 3
FF_DIM = 5504
EPS = 1e-6

IN_SIZES = (S5_WIDTH, ML_WIDTH, ML_WIDTH, ML_WIDTH, ML_WIDTH, ML_HEADS, ML_HEADS, XA_WIDTH, N_BRANCH * D_MODEL)
D_IN = sum(IN_SIZES)
IN_SPLIT = tuple(int(c) for c in np.cumsum(IN_SIZES)[:-1])

kernel_name = "hybrid_s5_mlstm_memxattn_decode_step"


def rmsnorm(x, g):
    xf = x.astype(jnp.float32)
    r = lax.rsqrt(jnp.mean(xf * xf, axis=-1, keepdims=True) + EPS)
    return (xf * r).astype(x.dtype) * g


def swiglu(x, w_gate, w_up, w_down):
    return (jax.nn.silu(x @ w_gate) * (x @ w_up)) @ w_down


def _complex_affine_combine(e1, e2):
    a1r, a1i, b1r, b1i = e1
    a2r, a2i, b2r, b2i = e2
    return (a2r * a1r - a2i * a1i, a2r * a1i + a2i * a1r,
            a2r * b1r - a2i * b1i + b2r, a2r * b1i + a2i * b1r + b2i)


def s5_branch(u, s_re, s_im, lam_re, lam_im, log_step, b_re, b_im, c_re, c_im, d, w_glu):
    bn, L, _ = u.shape
    f32 = jnp.float32
    lam_re = lam_re.astype(f32)
    lam_im = lam_im.astype(f32)
    dt = jnp.exp(log_step.astype(f32))[:, None]
    mag = jnp.exp(lam_re * dt)
    ab_re = mag * jnp.cos(lam_im * dt)
    ab_im = mag * jnp.sin(lam_im * dt)
    den = lam_re * lam_re + lam_im * lam_im
    nr = ab_re - 1.0
    z_re = (nr * lam_re + ab_im * lam_im) / den
    z_im = (ab_im * lam_re - nr * lam_im) / den
    b_re = b_re.astype(f32)
    b_im = b_im.astype(f32)
    bb_re = z_re[..., None] * b_re - z_im[..., None] * b_im
    bb_im = z_re[..., None] * b_im + z_im[..., None] * b_re
    ug = u.astype(f32).reshape(bn, L, S5_GROUPS, S5_GROUP)
    bu_re = jnp.einsum('blgh,gph->blgp', ug, bb_re)
    bu_im = jnp.einsum('blgh,gph->blgp', ug, bb_im)
    s_re = s_re.astype(f32)
    s_im = s_im.astype(f32)
    bu_re = bu_re.at[:, 0].add(ab_re * s_re - ab_im * s_im)
    bu_im = bu_im.at[:, 0].add(ab_re * s_im + ab_im * s_re)
    a_re = jnp.broadcast_to(ab_re, bu_re.shape)
    a_im = jnp.broadcast_to(ab_im, bu_im.shape)
    _, _, x_re, x_im = lax.associative_scan(_complex_affine_combine, (a_re, a_im, bu_re, bu_im), axis=1)
    y = (jnp.einsum('blgp,ghp->blgh', x_re, c_re.astype(f32))
         - jnp.einsum('blgp,ghp->blgh', x_im, c_im.astype(f32))
         + d.astype(f32) * ug)
    y = jax.nn.gelu(y.reshape(bn, L, S5_WIDTH)).astype(u.dtype)
    y = y * jax.nn.sigmoid(y @ w_glu)
    return y, x_re[:, -1], x_im[:, -1]


def mlstm_branch(q, k, v, i_pre, f_pre, C0, n0, m0):
    bn, L, _ = q.shape
    f32 = jnp.float32
    cl = math.gcd(ML_CHUNK, L)
    nc = L // cl

    def blocks(t):
        return t.astype(f32).reshape(bn, nc, cl, ML_HEADS, ML_HEAD_DIM).transpose(1, 0, 3, 2, 4)

    def gblocks(t):
        return t.astype(f32).reshape(bn, nc, cl, ML_HEADS).transpose(1, 0, 3, 2)

    qb = blocks(q)
    kb = blocks(k) * (ML_HEAD_DIM ** -0.5)
    vb = blocks(v)
    ib = gblocks(i_pre)
    lfb = jax.nn.log_sigmoid(gblocks(f_pre))
    causal = jnp.tril(jnp.ones((cl, cl), dtype=bool))

    def chunk_step(carry, xs):
        C, n, m = carry
        qc, kc, vc, ic, lfc = xs
        bcum = jnp.cumsum(lfc, axis=-1)
        g_inter = bcum + m[..., None]
        dlog = bcum[..., :, None] - bcum[..., None, :] + ic[..., None, :]
        dlog = jnp.where(causal, dlog, -jnp.inf)
        m_t = jnp.maximum(g_inter, jnp.max(dlog, axis=-1))
        w_inter = jnp.exp(g_inter - m_t)
        w_intra = jnp.exp(dlog - m_t[..., None])
        s = jnp.einsum('bhtd,bhsd->bhts', qc, kc) * w_intra
        num = jnp.einsum('bhts,bhsv->bhtv', s, vc) + w_inter[..., None] * jnp.einsum('bhtd,bhdv->bhtv', qc, C)
        nq = jnp.sum(s, axis=-1) + w_inter * jnp.einsum('bhtd,bhd->bht', qc, n)
        h = num / jnp.maximum(jnp.abs(nq), jnp.exp(-m_t))[..., None]
        w_last = w_intra[..., -1, :]
        C_new = w_inter[..., -1, None, None] * C + jnp.einsum('bhs,bhsd,bhsv->bhdv', w_last, kc, vc)
        n_new = w_inter[..., -1, None] * n + jnp.einsum('bhs,bhsd->bhd', w_last, kc)
        return (C_new, n_new, m_t[..., -1]), h

    (C, n, m), hb = lax.scan(chunk_step, (C0.astype(f32), n0.astype(f32), m0.astype(f32)),
                             (qb, kb, vb, ib, lfb))
    h = hb.transpose(1, 0, 3, 2, 4).reshape(bn, L, ML_HEADS, ML_HEAD_DIM)
    return h, C, n, m


def memory_kv(mem, g_mem, w_mem_k, w_mem_v):
    bn, nm, _ = mem.shape
    mn = rmsnorm(mem, g_mem)
    k = (mn @ w_mem_k).reshape(bn, nm, XA_HEADS, XA_HEAD_DIM)
    v = (mn @ w_mem_v).reshape(bn, nm, XA_HEADS, XA_HEAD_DIM)
    return k, v


def cross_attention(q, mem_k, mem_v):
    s = jnp.einsum('blhd,bmhd->bhlm', q, mem_k).astype(jnp.float32) * (XA_HEAD_DIM ** -0.5)
    p = jax.nn.softmax(s, axis=-1).astype(mem_v.dtype)
    return jnp.einsum('bhlm,bmhd->blhd', p, mem_v)


def hybrid_layer(x, mem_k, mem_v, s5_re, s5_im, C, n, m, lw):
    bn, L, _ = x.shape
    x = x + 0.5 * swiglu(rmsnorm(x, lw['g_ffn1']), lw['w1_gate'], lw['w1_up'], lw['w1_down'])
    h = rmsnorm(x, lw['g_mix'])
    z = h @ lw['w_in']
    u, q, k, v, o, ig, fg, qx, gates = jnp.split(z, IN_SPLIT, axis=-1)
    s5_out, s5_re, s5_im = s5_branch(u, s5_re, s5_im, lw['s5_lambda_re'], lw['s5_lambda_im'], lw['s5_log_step'],
                                     lw['s5_b_re'], lw['s5_b_im'], lw['s5_c_re'], lw['s5_c_im'], lw['s5_d'],
                                     lw['w_s5_glu'])
    hm, C, n, m = mlstm_branch(q, k, v, ig + lw['b_igate'], fg + lw['b_fgate'], C, n, m)
    hm = rmsnorm(hm.astype(x.dtype), lw['g_mlstm_head'].reshape(ML_HEADS, ML_HEAD_DIM)).reshape(bn, L, ML_WIDTH)
    ml_out = hm * jax.nn.sigmoid(o)
    xa_out = cross_attention(qx.reshape(bn, L, XA_HEADS, XA_HEAD_DIM), mem_k, mem_v).reshape(bn, L, XA_WIDTH)
    gts = jax.nn.sigmoid(gates).reshape(bn, L, N_BRANCH, D_MODEL)
    merged = (gts[:, :, 0] * (s5_out @ lw['w_br_s5'])
              + gts[:, :, 1] * (ml_out @ lw['w_br_ml'])
              + gts[:, :, 2] * (xa_out @ lw['w_br_xa']))
    x = x + merged @ lw['w_out']
    x = x + 0.5 * swiglu(rmsnorm(x, lw['g_ffn2']), lw['w2_gate'], lw['w2_up'], lw['w2_down'])
    return x, (s5_re, s5_im, C, n, m)


def setup_inputs(seed: int = 0) -> dict:
    key = jax.random.key(seed)
    ks = iter(jax.random.split(key, 64))
    f32 = jnp.float32

    def normal(shape, scale):
        return jax.random.normal(next(ks), shape, f32) * scale

    def dense(shape, fan_in):
        return normal(shape, fan_in ** -0.5)

    def gain(shape):
        return 1.0 + normal(shape, 0.02)

    Ld = DEPTH
    G, P, Hg = S5_GROUPS, S5_STATE, S5_GROUP
    inp = {}
    inp['x_prompt'] = normal((BATCH, SEQ, D_MODEL), 1.0)
    inp['x_sample'] = normal((DEC_BATCH, DEC_SEQ, D_MODEL), 1.0)
    inp['mem_prompt'] = normal((BATCH, N_MEM, D_MODEL), 1.0)
    inp['cache_mem_k'] = normal((Ld, DEC_BATCH, N_MEM, XA_HEADS, XA_HEAD_DIM), 1.0)
    inp['cache_mem_v'] = normal((Ld, DEC_BATCH, N_MEM, XA_HEADS, XA_HEAD_DIM), 1.0)
    inp['state_s5_re'] = normal((Ld, DEC_BATCH, G, P), 0.5)
    inp['state_s5_im'] = normal((Ld, DEC_BATCH, G, P), 0.5)
    inp['state_mlstm_C'] = normal((Ld, DEC_BATCH, ML_HEADS, ML_HEAD_DIM, ML_HEAD_DIM), 0.05)
    inp['state_mlstm_n'] = normal((Ld, DEC_BATCH, ML_HEADS, ML_HEAD_DIM), 0.1)
    inp['state_mlstm_m'] = jax.random.uniform(next(ks), (Ld, DEC_BATCH, ML_HEADS), f32, 0.0, 2.0)
    inp['g_ffn1'] = gain((Ld, D_MODEL))
    inp['w1_gate'] = dense((Ld, D_MODEL, FF_DIM), D_MODEL)
    inp['w1_up'] = dense((Ld, D_MODEL, FF_DIM), D_MODEL)
    inp['w1_down'] = dense((Ld, FF_DIM, D_MODEL), FF_DIM)
    inp['g_mix'] = gain((Ld, D_MODEL))
    inp['w_in'] = dense((Ld, D_MODEL, D_IN), D_MODEL)
    inp['s5_lambda_re'] = -0.5 + normal((Ld, G, P), 0.01)
    inp['s5_lambda_im'] = math.pi * jnp.broadcast_to(jnp.arange(P, dtype=f32), (Ld, G, P)) + normal((Ld, G, P), 0.01)
    inp['s5_log_step'] = jax.random.uniform(next(ks), (Ld, G), f32, math.log(DT_MIN), math.log(DT_MAX))
    inp['s5_b_re'] = dense((Ld, G, P, Hg), 2 * Hg)
    inp['s5_b_im'] = dense((Ld, G, P, Hg), 2 * Hg)
    inp['s5_c_re'] = dense((Ld, G, Hg, P), 2 * P)
    inp['s5_c_im'] = dense((Ld, G, Hg, P), 2 * P)
    inp['s5_d'] = normal((Ld, G, Hg), 0.5)
    inp['w_s5_glu'] = dense((Ld, S5_WIDTH, S5_WIDTH), S5_WIDTH)
    inp['b_igate'] = normal((Ld, ML_HEADS), 0.1)
    inp['b_fgate'] = 3.0 + normal((Ld, ML_HEADS), 0.1)
    inp['g_mlstm_head'] = gain((Ld, ML_WIDTH))
    inp['g_mem'] = gain((Ld, D_MODEL))
    inp['w_mem_k'] = dense((Ld, D_MODEL, XA_WIDTH), D_MODEL)
    inp['w_mem_v'] = dense((Ld, D_MODEL, XA_WIDTH), D_MODEL)
    inp['w_br_s5'] = dense((Ld, S5_WIDTH, D_MODEL), S5_WIDTH)
    inp['w_br_ml'] = dense((Ld, ML_WIDTH, D_MODEL), ML_WIDTH)
    inp['w_br_xa'] = dense((Ld, XA_WIDTH, D_MODEL), XA_WIDTH)
    inp['w_out'] = dense((Ld, D_MODEL, D_MODEL), D_MODEL)
    inp['g_ffn2'] = gain((Ld, D_MODEL))
    inp['w2_gate'] = dense((Ld, D_MODEL, FF_DIM), D_MODEL)
    inp['w2_up'] = dense((Ld, D_MODEL, FF_DIM), D_MODEL)
    inp['w2_down'] = dense((Ld, FF_DIM, D_MODEL), FF_DIM)
    inp['g_final'] = gain((D_MODEL,))
    return inp


def reference(x_prompt, x_sample, mem_prompt, cache_mem_k, cache_mem_v,
              state_s5_re, state_s5_im, state_mlstm_C, state_mlstm_n, state_mlstm_m,
              g_ffn1, w1_gate, w1_up, w1_down, g_mix, w_in,
              s5_lambda_re, s5_lambda_im, s5_log_step, s5_b_re, s5_b_im, s5_c_re, s5_c_im, s5_d, w_s5_glu,
              b_igate, b_fgate, g_mlstm_head, g_mem, w_mem_k, w_mem_v,
              w_br_s5, w_br_ml, w_br_xa, w_out, g_ffn2, w2_gate, w2_up, w2_down, g_final):
    f32 = jnp.float32
    bp = x_prompt.shape[0]
    xp, xs = x_prompt, x_sample
    prompt_rows, sample_rows = [], []
    for l in range(DEPTH):
        lw = dict(g_ffn1=g_ffn1[l], w1_gate=w1_gate[l], w1_up=w1_up[l], w1_down=w1_down[l],
                  g_mix=g_mix[l], w_in=w_in[l],
                  s5_lambda_re=s5_lambda_re[l], s5_lambda_im=s5_lambda_im[l], s5_log_step=s5_log_step[l],
                  s5_b_re=s5_b_re[l], s5_b_im=s5_b_im[l], s5_c_re=s5_c_re[l], s5_c_im=s5_c_im[l],
                  s5_d=s5_d[l], w_s5_glu=w_s5_glu[l],
                  b_igate=b_igate[l], b_fgate=b_fgate[l], g_mlstm_head=g_mlstm_head[l],
                  w_br_s5=w_br_s5[l], w_br_ml=w_br_ml[l], w_br_xa=w_br_xa[l], w_out=w_out[l],
                  g_ffn2=g_ffn2[l], w2_gate=w2_gate[l], w2_up=w2_up[l], w2_down=w2_down[l])
        mk_p, mv_p = memory_kv(mem_prompt, g_mem[l], w_mem_k[l], w_mem_v[l])
        s0 = jnp.zeros((bp, S5_GROUPS, S5_STATE), f32)
        C0 = jnp.zeros((bp, ML_HEADS, ML_HEAD_DIM, ML_HEAD_DIM), f32)
        n0 = jnp.zeros((bp, ML_HEADS, ML_HEAD_DIM), f32)
        m0 = jnp.zeros((bp, ML_HEADS), f32)
        xp, st_p = hybrid_layer(xp, mk_p, mv_p, s0, s0, C0, n0, m0, lw)
        prompt_rows.append((mk_p, mv_p) + st_p)
        xs, st_s = hybrid_layer(xs, cache_mem_k[l], cache_mem_v[l], state_s5_re[l], state_s5_im[l],
                                state_mlstm_C[l], state_mlstm_n[l], state_mlstm_m[l], lw)
        sample_rows.append(st_s)
    mk_p, mv_p, s5r_p, s5i_p, C_p, n_p, m_p = [jnp.stack(a) for a in zip(*prompt_rows)]
    s5r_s, s5i_s, C_s, n_s, m_s = [jnp.stack(a) for a in zip(*sample_rows)]
    y_prompt = rmsnorm(xp, g_final)
    y_sample = rmsnorm(xs, g_final)
    return (y_prompt, y_sample, mk_p, mv_p, s5r_p, s5i_p, C_p, n_p, m_p, s5r_s, s5i_s, C_s, n_s, m_s)
```

```python
import math
import numpy as np
import concourse.bass as bass
import concourse.mybir as mybir
from concourse.bass_utils import run_bass_kernel_spmd
from contextlib import ExitStack

F32 = mybir.dt.float32
BF16 = mybir.dt.bfloat16
I32 = mybir.dt.int32
AF = mybir.ActivationFunctionType
ALU = mybir.AluOpType
AX = mybir.AxisListType

D = 2048
FF = 5504
NPR = 2048
NSM = 16
NTOK = NPR + NSM
BLOCKS = [(0, 1024), (1024, NTOK)]
HPR = NPR // 2
HT = HPR + NSM
EPS = 1e-6
NDS = 24
POOL_RING_LIMIT = 700
DIN = 12296


class Sched:
    ENG = ['pe', 'dve', 'act', 'pool', 'sp']

    def __init__(self, nc, es):
        self.nc = nc
        self.prog = {e: [] for e in self.ENG}
        self.sem = {e: es.enter_context(nc.semaphore("s_" + e)) for e in self.ENG}
        self.cnt = {e: 0 for e in self.ENG}
        self.seen = {e: {} for e in self.ENG}
        self.dsem = [es.enter_context(nc.semaphore("d%d" % i)) for i in range(NDS)]
        self.dcount = [0] * NDS
        self.dnext = 0
        self.last_w = {}
        self.readers = {}
        self.pool_pending = []

    def _need(self, eng, tok):
        kind, ident, v = tok
        if v <= 0:
            return False
        if kind == 'e' and ident == eng and eng in ('pe', 'sp'):
            return False
        if self.seen[eng].get((kind, ident), 0) >= v:
            return False
        return True

    def _wait(self, eng, kind, ident, v):
        if self._need(eng, (kind, ident, v)):
            self.prog[eng].append(('wait', (kind, ident, v)))
            self.seen[eng][(kind, ident)] = v

    def _deps(self, eng, reads, writes, extra=()):
        deps = {}

        def add(tok):
            key = (tok[0], tok[1])
            if deps.get(key, 0) < tok[2]:
                deps[key] = tok[2]
        for k in reads:
            if k in self.last_w:
                add(self.last_w[k])
        for k in writes:
            if k in self.last_w:
                add(self.last_w[k])
            for t in self.readers.get(k, ()):
                add(t)
        for t in extra:
            add(t)
        for (kind, ident), v in deps.items():
            self._wait(eng, kind, ident, v)

    def _commit(self, tok, reads, writes):
        for k in writes:
            self.last_w[k] = tok
            self.readers[k] = []
        for k in reads:
            self.readers.setdefault(k, []).append(tok)

    def op(self, eng, fn, reads=(), writes=(), inc=True):
        self._deps(eng, reads, writes)
        if inc:
            self.cnt[eng] += 1
            tok = ('e', eng, self.cnt[eng])
            self.prog[eng].append(('op', fn))
        else:
            assert eng == 'pe'
            tok = ('e', eng, self.cnt[eng] + 1)
            self.prog[eng].append(('op_noinc', fn))
        self._commit(tok, reads, writes)
        return tok

    def dma(self, q, fn, reads=(), writes=(), ndesc=0):
        i = self.dnext
        self.dnext = (self.dnext + 1) % NDS
        prev = ('d', i, self.dcount[i])
        extra = [prev]
        if q == 'pool':
            pend = self.pool_pending
            while pend and sum(n for _, n in pend) + ndesc > POOL_RING_LIMIT:
                extra.append(pend.pop(0)[0])
        self._deps(q, reads, writes, extra=tuple(extra))
        self.dcount[i] += 16
        tok = ('d', i, self.dcount[i])
        self.prog[q].append(('dma', fn, i))
        self._commit(tok, reads, writes)
        if q == 'pool':
            self.pool_pending.append((tok, ndesc))
        return tok

    def barrier(self):
        for e in self.ENG:
            for e2 in self.ENG:
                if e2 != e:
                    self._wait(e, 'e', e2, self.cnt[e2])
            for i in range(NDS):
                self._wait(e, 'd', i, self.dcount[i])
        self.last_w = {}
        self.readers = {}

    def flush(self):
        block = self.block
        handles = {'pe': block.tensor, 'dve': block.vector, 'act': block.scalar,
                   'pool': block.gpsimd, 'sp': block.sync}
        for e in self.ENG:
            prog = self.prog[e]
            self.prog[e] = []
            if not prog:
                continue

            def body(eng, prog=prog, e=e):
                for item in prog:
                    if item[0] == 'wait':
                        kind, ident, v = item[1]
                        s = self.sem[ident] if kind == 'e' else self.dsem[ident]
                        eng.wait_ge(s, v)
                    elif item[0] == 'op':
                        item[1](eng).then_inc(self.sem[e], 1)
                    elif item[0] == 'op_noinc':
                        item[1](eng)
                    else:
                        item[1](eng).then_inc(self.dsem[item[2]], 16)
            handles[e](body)


def token_tiles(t0, t1):
    out = []
    r = t0
    while r < t1:
        nr = min(128, t1 - r)
        out.append((r, nr))
        r += nr
    return out


def nchunks(n, step=512):
    out = []
    c = 0
    while c < n:
        out.append((c, min(step, n - c)))
        c += step
    return out


class Builder:
    def __init__(self, stage=99, dbg=False, which=('xp', 'xs', 's5', 'mp', 'ms'), mixtest=False):
        self.which = which
        self.mixtest = mixtest
        self.stage = stage
        self.dbg = dbg
        self.nc = bass.Bass("TRN2", target_bir_lowering=False)
        self.uid = 0

    def dram_in(self, name, shape, dt=F32):
        return self.nc.dram_tensor(name, list(shape), dt, kind="ExternalInput").ap()

    def dram_out(self, name, shape, dt=F32):
        return self.nc.dram_tensor(name, list(shape), dt, kind="ExternalOutput").ap()

    def dram_scr(self, name, shape, dt):
        return self.nc.dram_tensor(name, list(shape), dt, kind="Internal").ap()

    def sb(self, es, shape, dt, name=None):
        self.uid += 1
        return es.enter_context(self.nc.sbuf_tensor("%s_%d" % (name or "t", self.uid), list(shape), dt))

    def psum_next(self):
        i = self.pa_i
        self.pa_i = (self.pa_i + 1) % len(self.PA)
        return self.PA[i], ('pa', i)

    def psumb_next(self):
        i = self.pb_i
        self.pb_i = (self.pb_i + 1) % len(self.PB)
        return self.PB[i], ('pb', i)

    def load_gain(self, gb, gvec, key):
        S = self.S
        S.dma('sp', lambda e: e.dma_start(out=gb[:], in_=gvec.partition_broadcast(128)), writes=[key])

    def norm_T(self, src_rows, nr, gb, gkey, dstT, dkey, c0, bufs):
        S = self.S
        xts, xn, st = bufs
        if not isinstance(xts, (list, tuple)):
            xts = [xts]
        self.nt_i = getattr(self, 'nt_i', 0) + 1
        xt = xts[self.nt_i % len(xts)]
        xtk = ('xt', self.nt_i % len(xts))
        S.dma('sp', lambda e: e.dma_start(out=xt[:nr, :], in_=src_rows), writes=[xtk])
        S.op('act', lambda e: e.activation(out=xn[:nr, :], in_=xt[:nr, :], func=AF.Square, accum_out=st[:nr, 0:1]),
             reads=[xtk], writes=['xn', 'st'])
        S.op('dve', lambda e: e.tensor_scalar(out=st[:nr, 1:2], in0=st[:nr, 0:1], scalar1=1.0 / D, scalar2=EPS,
                                              op0=ALU.mult, op1=ALU.add), reads=['st'], writes=['st'])
        S.op('act', lambda e: e.activation(out=st[:nr, 2:3], in_=st[:nr, 1:2], func=AF.Sqrt), reads=['st'], writes=['st'])
        S.op('dve', lambda e: e.reciprocal(out=st[:nr, 3:4], in_=st[:nr, 2:3]), reads=['st'], writes=['st'])
        S.op('dve', lambda e: e.scalar_tensor_tensor(out=xn[:nr, :], in0=xt[:nr, :], scalar=st[:nr, 3:4], in1=gb[:nr, :],
                                                     op0=ALU.mult, op1=ALU.mult),
             reads=[xtk, 'st', gkey], writes=['xn'])
        self.transpose_into(xn, 'xn', nr, 16, dstT, dkey, c0)

    def transpose_into(self, src, skey, nr, nk, dstT, dkey, c0):
        S = self.S
        for kq in range(0, nk, 4):
            pb, pk = self.psumb_next()
            n4 = min(4, nk - kq)
            for j in range(n4):
                k = kq + j
                S.op('pe', lambda e, j=j, k=k, pb=pb: e.transpose(out=pb[:, j * 128:j * 128 + nr], in_=src[:nr, k * 128:(k + 1) * 128],
                                                                  identity=self.identb[:nr, :nr]),
                     reads=[skey, 'identb'], writes=[pk])
            eng = 'act' if (kq // 4) % 2 == 0 else 'dve'
            pv = pb[:, 0:n4 * 128].rearrange("p (j t) -> p j t", j=n4)[:, :, 0:nr]
            if eng == 'act':
                S.op('act', lambda e, pv=pv, kq=kq, n4=n4: e.copy(out=dstT[:, kq:kq + n4, c0:c0 + nr], in_=pv),
                     reads=[pk], writes=[dkey])
            else:
                S.op('dve', lambda e, pv=pv, kq=kq, n4=n4: e.tensor_copy(out=dstT[:, kq:kq + n4, c0:c0 + nr], in_=pv),
                     reads=[pk], writes=[dkey])

    def wload(self, wslice, kt, cw, fmt=None):
        S = self.S
        i = self.wb_i
        self.wb_i = (self.wb_i + 1) % len(self.WB)
        wb = self.WB[i]
        view = wb[:, 0:kt * cw].rearrange("p (k n) -> p k n", k=kt)
        S.dma('pool', lambda e: e.dma_start(out=view, in_=wslice.rearrange("(k p) n -> p k n", p=128)),
              writes=[('wb', i)], ndesc=8 * kt)
        return view, ('wb', i)

    def ffn(self, es0, xsrc, xdst, gvec, wg, wu, wd, blk):
        S = self.S
        t0, t1 = blk
        NT = t1 - t0
        with ExitStack() as es:
            xT = self.sb(es, [128, 16, NT], BF16, "xT")
            hT = self.sb(es, [128, 43, NT], BF16, "hT")
            self.WB = [self.sb(es, [128, 11008], BF16, "wb") for _ in range(2)]
            self.wb_i = 0
            xt = [self.sb(es, [128, D], F32, "xt") for _ in range(2)]
            xn = self.sb(es, [128, D], BF16, "xn")
            st = self.sb(es, [128, 8], F32, "st")
            gb = self.sb(es, [128, D], F32, "gb")
            sg = [self.sb(es, [128, 512], F32, "sg") for _ in range(2)]
            xres = [self.sb(es, [128, 256], F32, "xres") for _ in range(2)]
            ot = [self.sb(es, [128, 256], F32, "ot") for _ in range(2)]
            self.load_gain(gb, gvec, 'gb')
            for (r0, nr) in token_tiles(t0, t1):
                self.norm_T(xsrc[r0:r0 + nr, :], nr, gb, 'gb', xT, 'xT', r0 - t0, (xt, xn, st))
            ncs = nchunks(NT)
            cnt = 0
            c0 = 0
            while c0 < FF:
                cw = min(256, FF - c0)
                i = self.wb_i
                self.wb_i = (self.wb_i + 1) % 2
                wb = self.WB[i]
                wkey = ('wb', i)
                gv = wb[:, 0:16 * cw].rearrange("p (k n) -> p k n", k=16)
                uv = wb[:, 16 * cw:32 * cw].rearrange("p (k n) -> p k n", k=16)
                S.dma('pool', lambda e, gv=gv, c0=c0, cw=cw: e.dma_start(
                    out=gv, in_=wg[:, c0:c0 + cw].rearrange("(k p) n -> p k n", p=128)), writes=[wkey], ndesc=128)
                S.dma('pool', lambda e, uv=uv, c0=c0, cw=cw: e.dma_start(
                    out=uv, in_=wu[:, c0:c0 + cw].rearrange("(k p) n -> p k n", p=128)), writes=[wkey], ndesc=128)
                for m in range(cw // 128):
                    f = c0 // 128 + m
                    for (n0, nn) in ncs:
                        pg, pgk = self.psum_next()
                        pu, puk = self.psum_next()
                        for k in range(16):
                            S.op('pe', lambda e, pg=pg, gv=gv, k=k, m=m, n0=n0, nn=nn: e.matmul(
                                pg[:, 0:nn], lhsT=gv[:, k, m * 128:(m + 1) * 128], rhs=xT[:, k, n0:n0 + nn],
                                start=(k == 0), stop=(k == 15)), reads=[wkey, 'xT'], writes=[pgk], inc=(k == 15))
                        for k in range(16):
                            S.op('pe', lambda e, pu=pu, uv=uv, k=k, m=m, n0=n0, nn=nn: e.matmul(
                                pu[:, 0:nn], lhsT=uv[:, k, m * 128:(m + 1) * 128], rhs=xT[:, k, n0:n0 + nn],
                                start=(k == 0), stop=(k == 15)), reads=[wkey, 'xT'], writes=[puk], inc=(k == 15))
                        sgt = sg[cnt % 2]
                        sk = ('sg', cnt % 2)
                        cnt += 1
                        S.op('act', lambda e, sgt=sgt, pg=pg, nn=nn: e.activation(out=sgt[:, 0:nn], in_=pg[:, 0:nn], func=AF.Silu),
                             reads=[pgk], writes=[sk])
                        S.op('dve', lambda e, sgt=sgt, pu=pu, f=f, n0=n0, nn=nn: e.tensor_tensor(
                            out=hT[:, f, n0:n0 + nn], in0=sgt[:, 0:nn], in1=pu[:, 0:nn], op=ALU.mult),
                            reads=[sk, puk], writes=['hT'])
                c0 += cw
            self.proj_tm_res(hT, 'hT', 43, wd, xsrc, xdst, 0.5, blk, xres, ot)
            S.barrier()
            S.flush()

    def proj_tm_res(self, actT, akey, nk, w, xsrc, xdst, scale, blk, xres, ot, cw=256):
        S = self.S
        t0, t1 = blk
        cnt = 0
        for c0 in range(0, D, cw):
            wv, wkey = self.wload(w[:, c0:c0 + cw], nk, cw)
            for (r0, nr) in token_tiles(t0, t1):
                rl = r0 - t0
                po, pok = self.psum_next()
                for f in range(nk):
                    S.op('pe', lambda e, po=po, wv=wv, f=f, rl=rl, nr=nr: e.matmul(
                        po[:nr, 0:cw], lhsT=actT[:, f, rl:rl + nr], rhs=wv[:, f, :], start=(f == 0), stop=(f == nk - 1)),
                        reads=[wkey, akey], writes=[pok], inc=(f == nk - 1))
                xr = xres[cnt % 2]
                o = ot[cnt % 2]
                xk = ('xres', cnt % 2)
                ok = ('ot', cnt % 2)
                cnt += 1
                S.dma('sp', lambda e, xr=xr, r0=r0, nr=nr, c0=c0: e.dma_start(out=xr[:nr, 0:cw], in_=xsrc[r0:r0 + nr, c0:c0 + cw]),
                      writes=[xk])
                S.op('dve', lambda e, o=o, po=po, xr=xr, nr=nr: e.scalar_tensor_tensor(
                    out=o[:nr, 0:cw], in0=po[:nr, 0:cw], scalar=scale, in1=xr[:nr, 0:cw], op0=ALU.mult, op1=ALU.add),
                    reads=[pok, xk], writes=[ok])
                S.dma('sp', lambda e, o=o, r0=r0, nr=nr, c0=c0: e.dma_start(out=xdst[r0:r0 + nr, c0:c0 + cw], in_=o[:nr, 0:cw]),
                      reads=[ok], writes=[('dram', 'xdst', r0)])

    def evac(self, i, out_ap, in_ap, reads, writes):
        S = self.S
        if i % 2 == 0:
            S.op('act', lambda e: e.copy(out=out_ap, in_=in_ap), reads=reads, writes=writes)
        else:
            S.op('dve', lambda e: e.tensor_copy(out=out_ap, in_=in_ap), reads=reads, writes=writes)

    def proj_fm_store(self, actT, akey, nk, w, col0, ncols, dst, dcol0, NT, ebufs, ekey, cw=256):
        S = self.S
        ncs = nchunks(NT)
        for c0 in range(0, ncols, cw):
            wv, wkey = self.wload(w[:, col0 + c0:col0 + c0 + cw], nk, cw)
            for m in range(cw // 128):
                for (n0, nn) in ncs:
                    ps, pk = self.psum_next()
                    for k in range(nk):
                        S.op('pe', lambda e, ps=ps, wv=wv, k=k, m=m, n0=n0, nn=nn: e.matmul(
                            ps[:, 0:nn], lhsT=wv[:, k, m * 128:(m + 1) * 128], rhs=actT[:, k, n0:n0 + nn],
                            start=(k == 0), stop=(k == nk - 1)), reads=[wkey, akey], writes=[pk], inc=(k == nk - 1))
                    i = self.ecnt
                    self.ecnt += 1
                    eb = ebufs[i % len(ebufs)]
                    ek = (ekey, i % len(ebufs))
                    self.evac(i, eb[:, 0:nn], ps[:, 0:nn], [pk], [ek])
                    r = c0 + m * 128
                    S.dma('sp', lambda e, eb=eb, r=r, n0=n0, nn=nn: e.dma_start(
                        out=dst[r:r + 128, dcol0 + n0:dcol0 + n0 + nn], in_=eb[:, 0:nn]), reads=[ek],
                        writes=[('dram', id(dst), r, n0)])

    def proj_tm_store(self, actT, akey, nk, w, col0, ncols, dst, blk, ebufs, ekey, cw=512, dcol0=0):
        S = self.S
        t0, t1 = blk
        for c0 in range(0, ncols, cw):
            cww = min(cw, ncols - c0)
            wv, wkey = self.wload(w[:, col0 + c0:col0 + c0 + cww], nk, cww)
            for (r0, nr) in token_tiles(t0, t1):
                rl = r0 - t0
                ps, pk = self.psum_next()
                for k in range(nk):
                    S.op('pe', lambda e, ps=ps, wv=wv, k=k, rl=rl, nr=nr, cww=cww: e.matmul(
                        ps[:nr, 0:cww], lhsT=actT[:, k, rl:rl + nr], rhs=wv[:, k, :],
                        start=(k == 0), stop=(k == nk - 1)), reads=[wkey, akey], writes=[pk], inc=(k == nk - 1))
                i = self.ecnt
                self.ecnt += 1
                eb = ebufs[i % len(ebufs)]
                ek = (ekey, i % len(ebufs))
                self.evac(i, eb[:nr, 0:cww], ps[:nr, 0:cww], [pk], [ek])
                S.dma('sp', lambda e, eb=eb, r0=r0, nr=nr, c0=c0, cww=cww: e.dma_start(
                    out=dst[r0:r0 + nr, dcol0 + c0:dcol0 + c0 + cww], in_=eb[:nr, 0:cww]), reads=[ek],
                    writes=[('dram', id(dst), r0, c0)])

    def inproj(self):
        S = self.S
        I = self.I
        R = self.R
        w_in = I['w_in']
        with ExitStack() as es:
            hTs = [self.sb(es, [128, 16, b1 - b0], BF16, "hT%d" % i) for i, (b0, b1) in enumerate(BLOCKS)]
            self.WB = [self.sb(es, [128, 8192], BF16, "wb") for _ in range(3)]
            self.wb_i = 0
            xt = [self.sb(es, [128, D], F32, "xt") for _ in range(2)]
            xn = self.sb(es, [128, D], BF16, "xn")
            st = self.sb(es, [128, 8], F32, "st")
            gb = self.sb(es, [128, D], F32, "gb")
            fli = self.sb(es, [128, 2], F32, "fli")
            eb16 = [self.sb(es, [128, 512], BF16, "eb16") for _ in range(3)]
            eb32 = [self.sb(es, [128, 512], F32, "eb32") for _ in range(3)]
            self.load_gain(gb, I['g_mix'], 'gb')
            S.dma('sp', lambda e: e.dma_start(out=fli[:], in_=I['flag'][:, :]), writes=['fli'])
            for bi, (t0, t1) in enumerate(BLOCKS):
                for (r0, nr) in token_tiles(t0, t1):
                    self.norm_T(R['X1'][r0:r0 + nr, :], nr, gb, 'gb', hTs[bi], 'hT%d' % bi, r0 - t0, (xt, xn, st))
            for bi, blk in enumerate(BLOCKS):
                t0, t1 = blk
                NT = t1 - t0
                hT = hTs[bi]
                hk = 'hT%d' % bi
                for (col0, dst) in [(0, R['UT']), (2048, R['KT'])]:
                    self.proj_fm_store(hT, hk, 16, w_in, col0, 1024, dst, t0, NT, eb16, 'eb16')
                for (col0, dst) in [(2048, R['KTM']), (3072, R['VTM'])]:
                    self.proj_tm_store(hT, hk, 16, w_in, col0, 1024, dst, blk, eb16, 'eb16')
                self.proj_tm_store(hT, hk, 16, w_in, 5120, 8, R['GTM'], blk, eb32, 'eb32', cw=8)
                wv, wkey = self.wload(w_in[:, 5120:5128], 16, 8)
                for gi, dst in [(0, R['IGR']), (1, R['FGR'])]:
                    for (n0, nn) in nchunks(NT):
                        ps, pk = self.psum_next()
                        for k in range(16):
                            S.op('pe', lambda e, ps=ps, wv=wv, k=k, gi=gi, n0=n0, nn=nn, hT=hT: e.matmul(
                                ps[0:4, 0:nn], lhsT=wv[:, k, gi * 4:gi * 4 + 4], rhs=hT[:, k, n0:n0 + nn],
                                start=(k == 0), stop=(k == 15)), reads=[wkey, hk], writes=[pk], inc=(k == 15))
                        i = self.ecnt
                        self.ecnt += 1
                        eb = eb32[i % 3]
                        ek = ('eb32', i % 3)
                        self.evac(i, eb[0:4, 0:nn], ps[0:4, 0:nn], [pk], [ek])
                        S.dma('sp', lambda e, eb=eb, dst=dst, n0=n0, nn=nn, t0=t0: e.dma_start(
                            out=dst[0:4, t0 + n0:t0 + n0 + nn], in_=eb[0:4, 0:nn]), reads=[ek], writes=[('dram', id(dst), t0, n0)])
            h0, h1 = hTs
            S.op('dve', lambda e: e.tensor_scalar(out=h0[:], in0=h0[:], scalar1=fli[:, 0:1], scalar2=None, op0=ALU.mult), reads=['hT0', 'fli'], writes=['hT0'])
            S.op('dve', lambda e: e.scalar_tensor_tensor(out=h0[:], in0=h1[:, :, 0:HPR], scalar=fli[:, 1:2], in1=h0[:], op0=ALU.mult, op1=ALU.add),
                 reads=['hT0', 'hT1', 'fli'], writes=['hT0'])
            segs = [(h0, 'hT0', n0, nn, n0) for (n0, nn) in nchunks(HPR)] + [(h1, 'hT1', HPR, NSM, HPR)]
            for (col0, dst) in [(1024, R['QTo']), (5128, R['QXTo'])]:
                for c0 in range(0, 1024, 256):
                    wv, wkey = self.wload(w_in[:, col0 + c0:col0 + c0 + 256], 16, 256)
                    for m in range(2):
                        for (act_, akey, a0, nn, dcol) in segs:
                            ps, pk = self.psum_next()
                            for k in range(16):
                                S.op('pe', lambda e, ps=ps, wv=wv, k=k, m=m, act_=act_, a0=a0, nn=nn: e.matmul(
                                    ps[:, 0:nn], lhsT=wv[:, k, m * 128:(m + 1) * 128], rhs=act_[:, k, a0:a0 + nn],
                                    start=(k == 0), stop=(k == 15)), reads=[wkey, akey], writes=[pk], inc=(k == 15))
                            i = self.ecnt
                            self.ecnt += 1
                            eb = eb16[i % 3]
                            ek = ('eb16', i % 3)
                            self.evac(i, eb[:, 0:nn], ps[:, 0:nn], [pk], [ek])
                            r = c0 + m * 128
                            S.dma('sp', lambda e, eb=eb, dst=dst, r=r, dcol=dcol, nn=nn: e.dma_start(
                                out=dst[r:r + 128, dcol:dcol + nn], in_=eb[:, 0:nn]), reads=[ek], writes=[('dram', id(dst), r, dcol)])
            tls = [(h0, 'hT0', i * 128, 128, i * 128) for i in range(HPR // 128)] + [(h1, 'hT1', HPR, NSM, HPR)]
            for c0 in range(0, 1024, 512):
                wv, wkey = self.wload(w_in[:, 4096 + c0:4096 + c0 + 512], 16, 512)
                for (act_, akey, rl, nr, row0) in tls:
                    ps, pk = self.psum_next()
                    for k in range(16):
                        S.op('pe', lambda e, ps=ps, wv=wv, k=k, act_=act_, rl=rl, nr=nr: e.matmul(
                            ps[:nr, 0:512], lhsT=act_[:, k, rl:rl + nr], rhs=wv[:, k, :], start=(k == 0), stop=(k == 15)),
                            reads=[wkey, akey], writes=[pk], inc=(k == 15))
                    i = self.ecnt
                    self.ecnt += 1
                    eb = eb32[i % 3]
                    ek = ('eb32', i % 3)
                    self.evac(i, eb[:nr, 0:512], ps[:nr, 0:512], [pk], [ek])
                    S.dma('sp', lambda e, eb=eb, row0=row0, nr=nr, c0=c0: e.dma_start(
                        out=R['OTMo'][row0:row0 + nr, c0:c0 + 512], in_=eb[:nr, 0:512]), reads=[ek], writes=[('dram', 'otmo', row0, c0)])
            S.barrier()
            S.flush()

    def memkv(self):
        S = self.S
        I = self.I
        R = self.R
        O = self.O
        with ExitStack() as es:
            mT = self.sb(es, [128, 16, 256], BF16, "mT")
            self.WB = [self.sb(es, [128, 8192], BF16, "wb") for _ in range(3)]
            self.wb_i = 0
            xt = self.sb(es, [128, D], F32, "xt")
            xn = self.sb(es, [128, D], BF16, "xn")
            st = self.sb(es, [128, 8], F32, "st")
            gb = self.sb(es, [128, D], F32, "gb")
            eb16 = [self.sb(es, [128, 512], BF16, "eb16") for _ in range(3)]
            eb32 = [self.sb(es, [128, 512], F32, "eb32") for _ in range(3)]
            self.load_gain(gb, I['g_mem'], 'gb')
            for (r0, nr) in token_tiles(0, 256):
                self.norm_T(I['mem'][r0:r0 + nr, :], nr, gb, 'gb', mT, 'mT', r0, (xt, xn, st))
            self.proj_tm_store(mT, 'mT', 16, I['w_mem_k'], 0, 1024, O['mk'], (0, 256), eb32, 'eb32')
            self.proj_tm_store(mT, 'mT', 16, I['w_mem_v'], 0, 1024, O['mv'], (0, 256), eb32, 'eb32')
            self.proj_fm_store(mT, 'mT', 16, I['w_mem_k'], 0, 1024, R['MKT'], 0, 256, eb16, 'eb16')
            S.barrier()
            S.flush()

    def merge(self):
        S = self.S
        I = self.I
        R = self.R
        blk = (0, HT)
        t0, t1 = blk
        NT = t1 - t0
        w_in = I['w_in']
        with ExitStack() as es:
            hT = self.sb(es, [128, 16, NT], BF16, "hT")
            mT = self.sb(es, [128, 16, NT], BF16, "mT")
            brT = [self.sb(es, [128, 8, NT], BF16, "brT") for _ in range(3)]
            self.WB = [self.sb(es, [128, 2048], BF16, "wb") for _ in range(9)]
            self.wb_i = 0
            xt = self.sb(es, [128, D], F32, "xt")
            xn = self.sb(es, [128, D], BF16, "xn")
            st = self.sb(es, [128, 8], F32, "st")
            gb = self.sb(es, [128, D], F32, "gb")
            gs4 = self.sb(es, [128, D], F32, "gs4")
            gs = [gs4[:, i * 512:(i + 1) * 512] for i in range(2)]
            tmp = [gs4[:, (2 + i) * 512:(3 + i) * 512] for i in range(2)]
            acc = [self.sb(es, [128, 512], F32, "acc") for _ in range(2)]
            xres = [self.sb(es, [128, 256], F32, "xres") for _ in range(2)]
            ot = [self.sb(es, [128, 256], F32, "ot") for _ in range(2)]
            self.load_gain(gb, I['g_mix'], 'gb')
            fl = self.sb(es, [128, 2], F32, "fl")
            S.dma('sp', lambda e: e.dma_start(out=fl[:], in_=I['flag'][:, :]), writes=['fl'])
            for (r0, nr) in token_tiles(0, HPR):
                S.dma('sp', lambda e, r0=r0, nr=nr: e.dma_start(out=xt[:nr, :], in_=R['X1'][r0:r0 + nr, :]), writes=['xt'])
                S.dma('sp', lambda e, r0=r0, nr=nr: e.dma_start(out=gs4[:nr, :], in_=R['X1'][HPR + r0:HPR + r0 + nr, :]), writes=['gs4'])
                S.op('dve', lambda e, nr=nr: e.tensor_scalar(out=xt[:nr, :], in0=xt[:nr, :], scalar1=fl[:nr, 0:1], scalar2=None, op0=ALU.mult),
                     reads=['xt', 'fl'], writes=['xt'])
                S.op('dve', lambda e, nr=nr: e.scalar_tensor_tensor(out=xt[:nr, :], in0=gs4[:nr, :], scalar=fl[:nr, 1:2], in1=xt[:nr, :],
                                                                   op0=ALU.mult, op1=ALU.add), reads=['xt', 'gs4', 'fl'], writes=['xt'])
                S.dma('sp', lambda e, r0=r0, nr=nr: e.dma_start(out=R['X1h'][r0:r0 + nr, :], in_=xt[:nr, :]), reads=['xt'],
                      writes=[('dram', 'x1h', r0)])
            S.dma('sp', lambda e: e.dma_start(out=xt[:NSM, :], in_=R['X1'][NPR:NTOK, :]), writes=['xt'])
            S.dma('sp', lambda e: e.dma_start(out=R['X1h'][HPR:HT, :], in_=xt[:NSM, :]), reads=['xt'], writes=[('dram', 'x1h', HPR)])
            S.barrier()
            for (r0, nr) in token_tiles(t0, t1):
                self.norm_T(R['X1h'][r0:r0 + nr, :], nr, gb, 'gb', hT, 'hT', r0 - t0, (xt, xn, st))
            for b, nm in enumerate(['BR0', 'BR1', 'BR2']):
                S.dma('sp', lambda e, b=b, nm=nm: e.dma_start(out=brT[b][:], in_=R[nm].rearrange("(k p) t -> p k t", p=128)), writes=[('brT', b)])
            wbr = [I['w_br_s5'], I['w_br_ml'], I['w_br_xa']]
            ncs = nchunks(NT)
            cnt = 0
            for j in range(16):
                wg = []
                for b in range(3):
                    gv, gk = self.wload(w_in[:, 6152 + b * 2048 + j * 128:6152 + b * 2048 + (j + 1) * 128], 16, 128)
                    wg.append((gv, gk))
                wb_ = []
                for b in range(3):
                    bv, bk = self.wload(wbr[b][:, j * 128:(j + 1) * 128], 8, 128)
                    wb_.append((bv, bk))
                for (n0, nn) in ncs:
                    a = acc[cnt % 2]
                    ak = ('acc', cnt % 2)
                    cnt += 1
                    for b in range(3):
                        gv, gk = wg[b]
                        bv, bk = wb_[b]
                        pg, pgk = self.psum_next()
                        pb, pbk = self.psum_next()
                        for k in range(16):
                            S.op('pe', lambda e, pg=pg, gv=gv, k=k, n0=n0, nn=nn: e.matmul(
                                pg[:, 0:nn], lhsT=gv[:, k, :], rhs=hT[:, k, n0:n0 + nn], start=(k == 0), stop=(k == 15)),
                                reads=[gk, 'hT'], writes=[pgk], inc=(k == 15))
                        for k in range(8):
                            S.op('pe', lambda e, pb=pb, bv=bv, k=k, b=b, n0=n0, nn=nn: e.matmul(
                                pb[:, 0:nn], lhsT=bv[:, k, :], rhs=brT[b][:, k, n0:n0 + nn], start=(k == 0), stop=(k == 7)),
                                reads=[bk, ('brT', b)], writes=[pbk], inc=(k == 7))
                        g = gs[b % 2]
                        gsk = ('gs', b % 2)
                        S.op('act', lambda e, g=g, pg=pg, nn=nn: e.activation(out=g[:, 0:nn], in_=pg[:, 0:nn], func=AF.Sigmoid),
                             reads=[pgk], writes=[gsk])
                        if b == 0:
                            S.op('dve', lambda e, a=a, g=g, pb=pb, nn=nn: e.tensor_tensor(out=a[:, 0:nn], in0=g[:, 0:nn], in1=pb[:, 0:nn],
                                                                                         op=ALU.mult), reads=[gsk, pbk], writes=[ak])
                        else:
                            t = tmp[b % 2]
                            tk = ('tmp', b % 2)
                            S.op('dve', lambda e, t=t, g=g, pb=pb, nn=nn: e.tensor_tensor(out=t[:, 0:nn], in0=g[:, 0:nn], in1=pb[:, 0:nn],
                                                                                         op=ALU.mult), reads=[gsk, pbk], writes=[tk])
                            if b == 1:
                                S.op('dve', lambda e, a=a, t=t, nn=nn: e.tensor_tensor(out=a[:, 0:nn], in0=a[:, 0:nn], in1=t[:, 0:nn],
                                                                                       op=ALU.add), reads=[ak, tk], writes=[ak])
                            else:
                                S.op('dve', lambda e, a=a, t=t, j=j, n0=n0, nn=nn: e.tensor_tensor(
                                    out=mT[:, j, n0:n0 + nn], in0=a[:, 0:nn], in1=t[:, 0:nn], op=ALU.add),
                                    reads=[ak, tk], writes=['mT'])
            self.proj_tm_res(mT, 'mT', 16, I['w_out'], R['X1h'], R['X2'], 1.0, blk, xres, ot, cw=128)
            S.barrier()
            S.flush()

    def mixers(self):
        which = self.which
        if 'xp' in which:
            self.xatt_prompt()
        if 'xs' in which:
            self.xatt_sample()
        if 's5' in which:
            self.s5()
        if 'mp' in which:
            self.mlstm_prompt()
        if 'ms' in which:
            self.mlstm_sample()

    def xatt_prompt(self):
        S = self.S
        R = self.R
        O = self.O
        with ExitStack() as es:
            qxT = self.sb(es, [128, 8, HPR], BF16, "qxT")
            mkT = self.sb(es, [128, 8, 256], BF16, "mkT")
            Vb = self.sb(es, [128, 2, 1024], BF16, "Vb")
            pT = [self.sb(es, [128, 2, HPR], BF16, "pT") for _ in range(2)]
            pe_ = [self.sb(es, [128, 256], F32, "pe") for _ in range(2)]
            pn = [self.sb(es, [128, 256], BF16, "pn") for _ in range(2)]
            sts = [self.sb(es, [128, 8], F32, "sts") for _ in range(2)]
            ob = [self.sb(es, [128, 512], BF16, "ob") for _ in range(2)]
            S.dma('sp', lambda e: e.dma_start(out=qxT[:], in_=R['QXTo'][:, 0:HPR].rearrange("(k p) t -> p k t", p=128)), writes=['qxT'])
            S.dma('sp', lambda e: e.dma_start(out=mkT[:], in_=R['MKT'].rearrange("(k p) t -> p k t", p=128)), writes=['mkT'])
            S.dma('pool', lambda e: e.dma_start(out=Vb[:], in_=R['MVB'].rearrange("(mt p) c -> p mt c", p=128)), writes=['Vb'], ndesc=16)
            oc = 0
            NTT = HPR // 128
            items = [(h, tt) for h in range(4) for tt in range(NTT)]
            pss = {}

            def xp_scores(n_):
                h, tt = items[n_]
                ps, pk = self.psum_next()
                for dh in range(2):
                    S.op('pe', lambda e, dh=dh: e.matmul(
                        ps[:, 0:256], lhsT=qxT[:, h * 2 + dh, tt * 128:(tt + 1) * 128], rhs=mkT[:, h * 2 + dh, :],
                        start=(dh == 0), stop=(dh == 1)), reads=['qxT', 'mkT'], writes=[pk])
                pss[n_] = (ps, pk)

            def xp_softmax(n_):
                h, tt = items[n_]
                ps, pk = pss.pop(n_)
                pTh = pT[h % 2]
                pTk = ('pT', h % 2)
                st = sts[n_ % 2]
                sk = ('sts', n_ % 2)
                pe = pe_[n_ % 2]
                pek = ('pe', n_ % 2)
                pnn = pn[n_ % 2]
                pnk = ('pn', n_ % 2)
                S.op('dve', lambda e: e.reduce_max(out=st[:, 0:1], in_=ps[:, 0:256], axis=AX.X), reads=[pk], writes=[sk])
                S.op('dve', lambda e: e.tensor_scalar(out=st[:, 1:2], in0=st[:, 0:1], scalar1=-1.0 / 16, scalar2=None, op0=ALU.mult),
                     reads=[sk], writes=[sk])
                S.op('act', lambda e: e.activation(out=pe[:], in_=ps[:, 0:256], func=AF.Exp, bias=st[:, 1:2],
                                                   scale=1.0 / 16, accum_out=st[:, 2:3]), reads=[pk, sk], writes=[pek, sk])
                S.op('dve', lambda e: e.reciprocal(out=st[:, 3:4], in_=st[:, 2:3]), reads=[sk], writes=[sk])
                S.op('dve', lambda e: e.tensor_scalar(out=pnn[:], in0=pe[:], scalar1=st[:, 3:4], scalar2=None, op0=ALU.mult),
                     reads=[sk, pek], writes=[pnk])
                pb, pbk = self.psumb_next()
                for mt in range(2):
                    S.op('pe', lambda e, mt=mt: e.transpose(out=pb[:, mt * 128:(mt + 1) * 128], in_=pnn[:, mt * 128:(mt + 1) * 128],
                                                            identity=self.identb[:]), reads=[pnk, 'identb'], writes=[pbk])
                S.op('act', lambda e: e.copy(out=pTh[:, :, tt * 128:(tt + 1) * 128], in_=pb[:, 0:256].rearrange("p (m t) -> p m t", m=2)),
                     reads=[pbk], writes=[pTk])

            def xp_pv(h):
                nonlocal oc
                pTh = pT[h % 2]
                pTk = ('pT', h % 2)
                for dh in range(2):
                    for (n0, nn) in nchunks(HPR):
                        ps, pk = self.psum_next()
                        for mt in range(2):
                            S.op('pe', lambda e, ps=ps, dh=dh, mt=mt, n0=n0, nn=nn: e.matmul(
                                ps[:, 0:nn], lhsT=Vb[:, mt, h * 256 + dh * 128:h * 256 + (dh + 1) * 128], rhs=pTh[:, mt, n0:n0 + nn],
                                start=(mt == 0), stop=(mt == 1)), reads=['Vb', pTk], writes=[pk])
                        o = ob[oc % 2]
                        ok = ('ob', oc % 2)
                        oc += 1
                        self.evac(oc, o[:, 0:nn], ps[:, 0:nn], [pk], [ok])
                        r = h * 256 + dh * 128
                        S.dma('sp', lambda e, o=o, r=r, n0=n0, nn=nn: e.dma_start(out=R['BR2'][r:r + 128, n0:n0 + nn], in_=o[:, 0:nn]),
                              reads=[ok], writes=[('dram', 'brt2', r, n0)])

            xp_scores(0)
            for n_ in range(len(items)):
                if n_ + 1 < len(items):
                    xp_scores(n_ + 1)
                xp_softmax(n_)
                if items[n_][1] == NTT - 1:
                    xp_pv(items[n_][0])
            S.barrier()
            S.flush()

    def xatt_sample(self):
        S = self.S
        R = self.R
        I = self.I
        with ExitStack() as es:
            qS = self.sb(es, [128, 8, NSM], BF16, "qS")
            qtm = self.sb(es, [NSM, 1024], BF16, "qtm")
            OH = self.sb(es, [NSM, NSM, 128], BF16, "OH")
            Kt = [self.sb(es, [128, 2, 1024], F32, "Kt") for _ in range(2)]
            Vt = [self.sb(es, [128, 2, 1024], BF16, "Vt") for _ in range(2)]
            prod = [self.sb(es, [128, 1024], F32, "prod") for _ in range(2)]
            SC = self.sb(es, [128, 2, 128], F32, "SC")
            pe = self.sb(es, [128, 256], F32, "pes")
            pn = self.sb(es, [128, 256], F32, "pns")
            st = self.sb(es, [128, 8], F32, "stx")
            P = self.sb(es, [128, 2, 128], BF16, "Pm")
            xrow = [self.sb(es, [1, 1024], F32, "xrow") for _ in range(2)]
            xas = self.sb(es, [NSM, 1024], F32, "xas")
            xasb = self.sb(es, [NSM, 1024], BF16, "xasb")
            xaT = self.sb(es, [128, 8, NSM], BF16, "xaT")
            S.dma('sp', lambda e: e.dma_start(out=qS[:], in_=R['QXTo'][:, HPR:HT].rearrange("(k p) t -> p k t", p=128)), writes=['qS'])
            for kq in range(0, 8, 4):
                pb, pbk = self.psumb_next()
                for j in range(4):
                    S.op('pe', lambda e, pb=pb, j=j, kq=kq: e.transpose(out=pb[:NSM, j * 128:(j + 1) * 128], in_=qS[:, kq + j, :],
                                                                       identity=self.identb[:]), reads=['qS', 'identb'], writes=[pbk])
                S.op('act', lambda e, pb=pb, kq=kq: e.copy(out=qtm[:, kq * 128:(kq + 4) * 128], in_=pb[:NSM, 0:512]), reads=[pbk], writes=['qtm'])
            S.op('dve', lambda e: e.tensor_copy(out=OH[:], in_=self.identb[:NSM, :NSM].unsqueeze(2).to_broadcast([NSM, NSM, 128])),
                 reads=['identb'], writes=['OH'])
            S.op('pool', lambda e: e.memset(SC[:], 0.0), writes=['SC'])
            for b in range(NSM):
                kt = Kt[b % 2]
                kk = ('Kt', b % 2)
                S.dma('sp', lambda e, kt=kt, b=b: e.dma_start(out=kt[:], in_=I['cache_k'][b].rearrange("(mt p) c -> p mt c", p=128)), writes=[kk])
                qb = []
                for hf in range(2):
                    ps, pk = self.psum_next()
                    S.op('pe', lambda e, ps=ps, b=b, hf=hf: e.matmul(ps[:, 0:512], lhsT=OH[:, b, :], rhs=qtm[:, hf * 512:(hf + 1) * 512],
                                                                    start=True, stop=True), reads=['OH', 'qtm'], writes=[pk])
                    qb.append((ps, pk))
                for mt in range(2):
                    pr = prod[mt]
                    prk = ('prod', mt)
                    for hf in range(2):
                        ps, pk = qb[hf]
                        S.op('dve', lambda e, pr=pr, kt=kt, ps=ps, mt=mt, hf=hf: e.tensor_tensor(
                            out=pr[:, hf * 512:(hf + 1) * 512], in0=kt[:, mt, hf * 512:(hf + 1) * 512], in1=ps[:, 0:512], op=ALU.mult),
                            reads=[kk, pk], writes=[prk])
                    S.op('dve', lambda e, pr=pr, mt=mt, b=b: e.tensor_reduce(out=SC[:, mt, b * 4:(b + 1) * 4],
                                                                            in_=pr[:].rearrange("p (h d) -> p h d", h=4), axis=AX.X, op=ALU.add),
                         reads=[prk], writes=['SC'])
            BH = 128
            if self.dbg:
                dSC = self.dram_out('dbg_SC', [128, 256])
                dq = self.dram_out('dbg_qtm', [NSM, 1024], BF16)
                S.dma('sp', lambda e: e.dma_start(out=dSC[:, :], in_=SC[:].rearrange("p a b -> p (a b)")), reads=['SC'], writes=[('dram', 'dsc')])
                S.dma('sp', lambda e: e.dma_start(out=dq[:, :], in_=qtm[:]), reads=['qtm'], writes=[('dram', 'dq')])
            ps, pk = self.psum_next()
            for mt in range(2):
                S.op('pe', lambda e, ps=ps, mt=mt: e.transpose(out=ps[:BH, mt * 128:(mt + 1) * 128], in_=SC[:, mt, :], identity=self.identf[:]),
                     reads=['SC', 'identf'], writes=[pk])
            S.op('dve', lambda e, ps=ps: e.reduce_max(out=st[:BH, 0:1], in_=ps[:BH, 0:256], axis=AX.X), reads=[pk], writes=['stx'])
            S.op('dve', lambda e: e.tensor_scalar(out=st[:BH, 1:2], in0=st[:BH, 0:1], scalar1=-1.0 / 16, scalar2=None, op0=ALU.mult),
                 reads=['stx'], writes=['stx'])
            S.op('act', lambda e, ps=ps: e.activation(out=pe[:BH, :], in_=ps[:BH, 0:256], func=AF.Exp, bias=st[:BH, 1:2], scale=1.0 / 16,
                                               accum_out=st[:BH, 2:3]), reads=[pk, 'stx'], writes=['pes', 'stx'])
            S.op('dve', lambda e: e.reciprocal(out=st[:BH, 3:4], in_=st[:BH, 2:3]), reads=['stx'], writes=['stx'])
            S.op('dve', lambda e: e.tensor_scalar(out=pn[:BH, :], in0=pe[:BH, :], scalar1=st[:BH, 3:4], scalar2=None, op0=ALU.mult),
                 reads=['stx', 'pes'], writes=['pns'])
            if self.dbg:
                dpn = self.dram_out('dbg_pn', [128, 256])
                dst_ = self.dram_out('dbg_st', [128, 8])
                S.dma('sp', lambda e: e.dma_start(out=dpn[:, :], in_=pn[:]), reads=['pns'], writes=[('dram', 'dpn')])
                S.dma('sp', lambda e: e.dma_start(out=dst_[:, :], in_=st[:]), reads=['stx'], writes=[('dram', 'dst')])
            for mt in range(2):
                ps2, pk2 = self.psum_next()
                S.op('pe', lambda e, ps2=ps2, mt=mt: e.transpose(out=ps2[:, 0:BH], in_=pn[:BH, mt * 128:(mt + 1) * 128], identity=self.identf[:BH, :BH]),
                     reads=['pns', 'identf'], writes=[pk2])
                S.op('act', lambda e, ps2=ps2, mt=mt: e.copy(out=P[:, mt, 0:BH], in_=ps2[:, 0:BH]), reads=[pk2], writes=['Pm'])
            for b in range(NSM):
                vt = Vt[b % 2]
                vk = ('Vt', b % 2)
                S.dma('pool', lambda e, vt=vt, b=b: e.dma_start(out=vt[:], in_=I['cache_v'][b].rearrange("(mt p) c -> p mt c", p=128)), writes=[vk], ndesc=16)
                xr = xrow[b % 2]
                xk = ('xrow', b % 2)
                for hp in range(2):
                    ps, pk = self.psum_next()
                    for hh in range(2):
                        h = hp * 2 + hh
                        for mt in range(2):
                            S.op('pe', lambda e, ps=ps, vt=vt, b=b, h=h, hh=hh, mt=mt: e.matmul(
                                ps[0:1, hh * 256:(hh + 1) * 256], lhsT=P[:, mt, b * 4 + h:b * 4 + h + 1], rhs=vt[:, mt, h * 256:(h + 1) * 256],
                                start=(mt == 0), stop=(mt == 1)), reads=['Pm', vk], writes=[pk])
                    S.op('act', lambda e, ps=ps, xr=xr, hp=hp: e.copy(out=xr[0:1, hp * 512:(hp + 1) * 512], in_=ps[0:1, 0:512]),
                         reads=[pk], writes=[xk])
                S.dma('sp', lambda e, xr=xr, b=b: e.dma_start(out=R['XAS'][b:b + 1, :], in_=xr[0:1, :]), reads=[xk], writes=[('dram', 'xas', b)])
            S.barrier()
            S.dma('sp', lambda e: e.dma_start(out=xas[:], in_=R['XAS'][:, :]), writes=['xas'])
            S.op('dve', lambda e: e.tensor_copy(out=xasb[:], in_=xas[:]), reads=['xas'], writes=['xasb'])
            self.transpose_into(xasb, 'xasb', NSM, 8, xaT, 'xaT', 0)
            S.dma('sp', lambda e: e.dma_start(out=R['BR2'][:, HPR:HT].rearrange("(k p) t -> p k t", p=128), in_=xaT[:]),
                  reads=['xaT'], writes=[('dram', 'brt2s')])
            S.barrier()
            S.flush()

    def s5(self):
        S = self.S
        I = self.I
        R = self.R
        O = self.O
        TWO_PI = 6.283185307179586
        dve = lambda fn, reads, writes: S.op('dve', fn, reads=reads, writes=writes)
        with ExitStack() as es:
            T = {}

            def t32(name):
                T[name] = self.sb(es, [128, 32], F32, name)
                return T[name]
            for nm in ['LR', 'LI', 'LS', 'DT', 'X', 'P', 'EM1', 'MAG', 'PHI', 'U', 'NF', 'RR', 'MSK', 'SIN', 'COS', 'AR', 'AI',
                       'CM1', 'NR', 'DEN', 'ZR', 'ZI', 'TA', 'TB', 'SPr', 'SPi', 'TC', 'TD']:
                t32(nm)
            NI = self.sb(es, [128, 32], I32, "NI")
            nat = [self.sb(es, [32, 128], F32, "nat") for _ in range(3)]
            LS2 = self.sb(es, [32, 2], F32, "LS2")
            LC = 64
            WvT = [self.sb(es, [32, 32, 128], BF16, "WvT%d" % i) for i in range(2)]
            CT = [self.sb(es, [128, 32, 32], BF16, "CT%d" % i) for i in range(2)]
            Dd = self.sb(es, [32, 32, 32], BF16, "Dd")
            Dcol = self.sb(es, [32, 32], F32, "Dcol")
            Gr = self.sb(es, [128, 32, LC], F32, "Gr")
            Gi = self.sb(es, [128, 32, LC], F32, "Gi")
            RHO = self.sb(es, [128, 32, LC], F32, "RHO")
            gx = [self.sb(es, [32, 512], F32, "gx") for _ in range(2)]
            g2t = [self.sb(es, [32, 512], F32, "g2t") for _ in range(2)]
            S.dma('sp', lambda e: e.dma_start(out=nat[0][:], in_=I['s5_lambda_re'].rearrange("(gp g2) p -> gp (g2 p)", g2=2)), writes=['nat0'])
            S.dma('sp', lambda e: e.dma_start(out=nat[1][:], in_=I['s5_lambda_im'].rearrange("(gp g2) p -> gp (g2 p)", g2=2)), writes=['nat1'])
            S.dma('sp', lambda e: e.dma_start(out=LS2[:], in_=I['s5_log_step'].rearrange("(gp g2) -> gp g2", g2=2)), writes=['LS2'])
            dve(lambda e: e.tensor_copy(out=nat[2][:].rearrange("g (a p) -> g a p", a=2), in_=LS2[:].unsqueeze(2).to_broadcast([32, 2, 64])),
                ['LS2'], ['nat2'])
            ps, pk = self.psum_next()
            for j in range(3):
                S.op('pe', lambda e, j=j, ps=ps: e.transpose(out=ps[:, j * 32:(j + 1) * 32], in_=nat[j][:, :], identity=self.identf[:32, :32]),
                     reads=['nat%d' % j, 'identf'], writes=[pk])
            for j, nm in enumerate(['LR', 'LI', 'LS']):
                dve(lambda e, j=j, nm=nm, ps=ps: e.tensor_copy(out=T[nm][:], in_=ps[:, j * 32:(j + 1) * 32]), [pk], [nm])
            S.op('act', lambda e: e.activation(out=T['DT'][:], in_=T['LS'][:], func=AF.Exp), reads=['LS'], writes=['DT'])
            tt = lambda o, a, b, op: dve(lambda e: e.tensor_tensor(out=T[o][:], in0=T[a][:], in1=T[b][:], op=op), [a, b], [o])
            ts = lambda o, a, s1, s2, op0, op1: dve(lambda e: e.tensor_scalar(out=T[o][:], in0=T[a][:], scalar1=s1, scalar2=s2, op0=op0, op1=op1), [a], [o])
            tt('X', 'LR', 'DT', ALU.mult)
            ts('P', 'X', 1.0 / 6, 1.0, ALU.mult, ALU.add)
            for dv in [5.0, 4.0, 3.0, 2.0]:
                tt('P', 'P', 'X', ALU.mult)
                ts('P', 'P', 1.0 / dv, 1.0, ALU.mult, ALU.add)
            tt('EM1', 'P', 'X', ALU.mult)
            ts('MAG', 'EM1', 1.0, None, ALU.add, ALU.bypass)
            tt('PHI', 'LI', 'DT', ALU.mult)

            def sin_of(dst, src, shift):
                ts('U', src, shift, 1.0 / TWO_PI, ALU.add, ALU.mult)
                dve(lambda e: e.tensor_copy(out=NI[:], in_=T['U'][:]), ['U'], ['NI'])
                dve(lambda e: e.tensor_copy(out=T['NF'][:], in_=NI[:]), ['NI'], ['NF'])
                ts('TA', src, shift, None, ALU.add, ALU.bypass)
                dve(lambda e: e.scalar_tensor_tensor(out=T['RR'][:], in0=T['NF'][:], scalar=-TWO_PI, in1=T['TA'][:], op0=ALU.mult, op1=ALU.add),
                    ['NF', 'TA'], ['RR'])
                ts('MSK', 'RR', math.pi, -TWO_PI, ALU.is_gt, ALU.mult)
                tt('RR', 'RR', 'MSK', ALU.add)
                ts('MSK', 'RR', -math.pi, TWO_PI, ALU.is_lt, ALU.mult)
                tt('RR', 'RR', 'MSK', ALU.add)
                ts('RR', 'RR', -3.1415925, 3.1415925, ALU.max, ALU.min)
                S.op('act', lambda e: e.activation(out=T[dst][:], in_=T['RR'][:], func=AF.Sin), reads=['RR'], writes=[dst])
            sin_of('SIN', 'PHI', 0.0)
            sin_of('COS', 'PHI', math.pi / 2)
            tt('AR', 'MAG', 'COS', ALU.mult)
            tt('AI', 'MAG', 'SIN', ALU.mult)
            ts('CM1', 'COS', -1.0, None, ALU.add, ALU.bypass)
            tt('NR', 'EM1', 'COS', ALU.mult)
            tt('NR', 'NR', 'CM1', ALU.add)
            tt('TA', 'LR', 'LR', ALU.mult)
            tt('TB', 'LI', 'LI', ALU.mult)
            tt('DEN', 'TA', 'TB', ALU.add)
            dve(lambda e: e.reciprocal(out=T['DEN'][:], in_=T['DEN'][:]), ['DEN'], ['DEN'])
            tt('TA', 'NR', 'LR', ALU.mult)
            tt('TB', 'AI', 'LI', ALU.mult)
            tt('TA', 'TA', 'TB', ALU.add)
            tt('ZR', 'TA', 'DEN', ALU.mult)
            tt('TA', 'AI', 'LR', ALU.mult)
            tt('TB', 'NR', 'LI', ALU.mult)
            tt('TA', 'TA', 'TB', ALU.subtract)
            tt('ZI', 'TA', 'DEN', ALU.mult)
            es1 = es.enter_context(ExitStack())
            BDr = self.sb(es1, [128, 32, 32], F32, "BDr")
            BDi = self.sb(es1, [128, 32, 32], F32, "BDi")
            BBr = self.sb(es1, [128, 32, 32], F32, "BBr")
            BBi = self.sb(es1, [128, 32, 32], F32, "BBi")
            BT1 = self.sb(es1, [128, 32, 32], F32, "BT1")
            BT2 = self.sb(es1, [128, 32, 32], F32, "BT2")
            CBD = [self.sb(es1, [32, 32, 128], F32, "CBD%d" % i) for i in range(2)]
            GT = [self.sb(es1, [128, 32, 32], F32, "GT%d" % i) for i in range(2)]
            S.op('pool', lambda e: e.memset(BDr[:], 0.0), writes=['BDr'])
            S.op('pool', lambda e: e.memset(BDi[:], 0.0), writes=['BDi'])
            for g2 in range(2):
                for (bd, bk, src) in [(BDr, 'BDr', I['s5_b_re']), (BDi, 'BDi', I['s5_b_im'])]:
                    S.dma('sp', lambda e, bd=bd, src=src, g2=g2: e.dma_start(
                        out=bd[g2 * 64:(g2 + 1) * 64, :, g2 * 16:(g2 + 1) * 16],
                        in_=src.rearrange("(gp g2) p h -> g2 p gp h", g2=2)[g2]), reads=[bk], writes=[bk])
            zb = lambda nm: T[nm][:].unsqueeze(2).to_broadcast([128, 32, 32])
            dve(lambda e: e.tensor_tensor(out=BT1[:], in0=BDr[:], in1=zb('ZR'), op=ALU.mult), ['BDr', 'ZR'], ['BT1'])
            dve(lambda e: e.tensor_tensor(out=BT2[:], in0=BDi[:], in1=zb('ZI'), op=ALU.mult), ['BDi', 'ZI'], ['BT2'])
            dve(lambda e: e.tensor_tensor(out=BBr[:], in0=BT1[:], in1=BT2[:], op=ALU.subtract), ['BT1', 'BT2'], ['BBr'])
            dve(lambda e: e.tensor_tensor(out=BT1[:], in0=BDi[:], in1=zb('ZR'), op=ALU.mult), ['BDi', 'ZR'], ['BT1'])
            dve(lambda e: e.tensor_tensor(out=BT2[:], in0=BDr[:], in1=zb('ZI'), op=ALU.mult), ['BDr', 'ZI'], ['BT2'])
            dve(lambda e: e.tensor_tensor(out=BBi[:], in0=BT1[:], in1=BT2[:], op=ALU.add), ['BT1', 'BT2'], ['BBi'])
            ei = 0
            for ri, (bb, bbk) in enumerate([(BBr, 'BBr'), (BBi, 'BBi')]):
                for g0 in range(0, 32, 4):
                    ps, pk = self.psum_next()
                    for j in range(4):
                        S.op('pe', lambda e, ps=ps, bb=bb, g0=g0, j=j: e.transpose(out=ps[:32, j * 128:(j + 1) * 128], in_=bb[:, g0 + j, :],
                                                                                  identity=self.identf[:]), reads=[bbk, 'identf'], writes=[pk])
                    ei += 1
                    self.evac(ei, WvT[ri][:, g0:g0 + 4, :], ps[:32, 0:512].rearrange("p (j q) -> p j q", j=4), [pk], ['WvT%d' % ri])
            for ri, src in enumerate([I['s5_c_re'], I['s5_c_im']]):
                S.op('pool', lambda e, ri=ri: e.memset(CBD[ri][:], 0.0), writes=['CBD%d' % ri])
                for g2 in range(2):
                    S.dma('sp', lambda e, ri=ri, src=src, g2=g2: e.dma_start(
                        out=CBD[ri][g2 * 16:(g2 + 1) * 16, :, g2 * 64:(g2 + 1) * 64],
                        in_=src.rearrange("(gp g2) h p -> g2 h gp p", g2=2)[g2]), reads=['CBD%d' % ri], writes=['CBD%d' % ri])
                for g0 in range(0, 32, 16):
                    ps, pk = self.psum_next()
                    for j in range(16):
                        S.op('pe', lambda e, ps=ps, ri=ri, g0=g0, j=j: e.transpose(out=ps[:, j * 32:(j + 1) * 32], in_=CBD[ri][:, g0 + j, :],
                                                                                  identity=self.identf[:32, :32]),
                             reads=['CBD%d' % ri, 'identf'], writes=[pk])
                    sc = 1.0 if ri == 0 else -1.0
                    dve(lambda e, ps=ps, ri=ri, g0=g0, sc=sc: e.tensor_scalar(out=CT[ri][:, g0:g0 + 16, :],
                                                                             in0=ps[:, 0:512].rearrange("p (j q) -> p j q", j=16),
                                                                             scalar1=sc, scalar2=None, op0=ALU.mult), [pk], ['CT%d' % ri])
            S.dma('sp', lambda e: e.dma_start(out=nat[0][:, 0:32], in_=I['s5_d'].rearrange("(gp g2) h -> gp (g2 h)", g2=2)),
                  reads=['nat0'], writes=['nat0'])
            ps, pk = self.psum_next()
            S.op('pe', lambda e, ps=ps: e.transpose(out=ps[:32, 0:32], in_=nat[0][:, 0:32], identity=self.identf[:32, :32]),
                 reads=['nat0', 'identf'], writes=[pk])
            dve(lambda e, ps=ps: e.tensor_copy(out=Dcol[:], in_=ps[:32, 0:32]), [pk], ['Dcol'])
            dve(lambda e: e.tensor_tensor(out=Dd[:], in0=self.identf[:32, :32].unsqueeze(1).to_broadcast([32, 32, 32]),
                                          in1=Dcol[:].unsqueeze(2).to_broadcast([32, 32, 32]), op=ALU.mult), ['Dcol', 'identf'], ['Dd'])
            LC = 64
            dve(lambda e: e.tensor_copy(out=Gr[:, :, 0:1], in_=T['COS'][:].unsqueeze(2)), ['COS'], ['G'])
            dve(lambda e: e.tensor_copy(out=Gi[:, :, 0:1], in_=T['SIN'][:].unsqueeze(2)), ['SIN'], ['G'])
            n = 1
            while n < LC:
                gr_b = Gr[:, :, n - 1:n].to_broadcast([128, 32, n])
                gi_b = Gi[:, :, n - 1:n].to_broadcast([128, 32, n])
                a0 = GT[0][:, :, 0:n]
                a1 = GT[1][:, :, 0:n]
                dve(lambda e, n=n, gr_b=gr_b, a0=a0: e.tensor_tensor(out=a0, in0=Gr[:, :, 0:n], in1=gr_b, op=ALU.mult), ['G'], ['GT0'])
                dve(lambda e, n=n, gi_b=gi_b, a1=a1: e.tensor_tensor(out=a1, in0=Gi[:, :, 0:n], in1=gi_b, op=ALU.mult), ['G'], ['GT1'])
                dve(lambda e, n=n, a0=a0, a1=a1: e.tensor_tensor(out=Gr[:, :, n:2 * n], in0=a0, in1=a1, op=ALU.subtract), ['GT0', 'GT1', 'G'], ['G'])
                dve(lambda e, n=n, gi_b=gi_b, a0=a0: e.tensor_tensor(out=a0, in0=Gr[:, :, 0:n], in1=gi_b, op=ALU.mult), ['G'], ['GT0'])
                dve(lambda e, n=n, gr_b=gr_b, a1=a1: e.tensor_tensor(out=a1, in0=Gi[:, :, 0:n], in1=gr_b, op=ALU.mult), ['G'], ['GT1'])
                dve(lambda e, n=n, a0=a0, a1=a1: e.tensor_tensor(out=Gi[:, :, n:2 * n], in0=a0, in1=a1, op=ALU.add), ['GT0', 'GT1', 'G'], ['G'])
                n *= 2
            dve(lambda e: e.tensor_copy(out=RHO[:], in_=T['MAG'][:].unsqueeze(2).to_broadcast([128, 32, LC])), ['MAG'], ['RHO'])
            S.op('pool', lambda e: e.memset(RHO[:, :, 0:1], 0.0), reads=['RHO'], writes=['RHO'])

            gcnt = [0]

            def gelu_to(ps, pk, ncol, dst, dkey):
                i = gcnt[0] % 2
                gcnt[0] += 1
                x = gx[i]
                y = g2t[i]
                xk = ('gx', i)
                yk = ('g2t', i)
                S.op('act', lambda e: e.copy(out=x[:, 0:ncol], in_=ps[:32, 0:ncol]), reads=[pk], writes=[xk])
                S.op('act', lambda e: e.activation(out=y[:, 0:ncol], in_=ps[:32, 0:ncol], func=AF.Square, scale=math.sqrt(0.044715)),
                     reads=[pk], writes=[yk])
                dve(lambda e: e.scalar_tensor_tensor(out=y[:, 0:ncol], in0=y[:, 0:ncol], scalar=1.0, in1=x[:, 0:ncol], op0=ALU.add, op1=ALU.mult),
                    [xk, yk], [yk])
                S.op('act', lambda e: e.activation(out=y[:, 0:ncol], in_=y[:, 0:ncol], func=AF.Sigmoid, scale=2.0 * math.sqrt(2.0 / math.pi)),
                     reads=[yk], writes=[yk])
                dve(lambda e: e.tensor_tensor(out=dst, in0=x[:, 0:ncol].rearrange("p (j q) -> p j q", j=dst.shape[1]),
                                              in1=y[:, 0:ncol].rearrange("p (j q) -> p j q", j=dst.shape[1]), op=ALU.mult), [xk, yk], [dkey])

            S.barrier()
            S.flush()
            es1.close()
            es2 = es.enter_context(ExitStack())
            UB = 128
            CPB = UB // LC
            uS = [self.sb(es2, [32, 32, UB], BF16, "uS") for _ in range(2)]
            ub = self.sb(es2, [32, 32, UB], BF16, "ub")
            fl5 = self.sb(es2, [128, 2], F32, "fl5")
            S.dma('sp', lambda e: e.dma_start(out=fl5[:], in_=I['flag'][:, :]), writes=['fl5'])
            yS = [self.sb(es2, [32, 32, UB], BF16, "yS") for _ in range(2)]
            RI = [self.sb(es2, [128, 32, LC], F32, "RI%d" % i) for i in range(2)]
            RRt = [self.sb(es2, [128, 32, LC], F32, "RR%d" % i) for i in range(2)]
            SB = [self.sb(es2, [128, 32, LC], BF16, "SB%d" % i) for i in range(2)]
            W1 = [self.sb(es2, [128, 512], F32, "W1_%d" % i) for i in range(8)]
            V1 = [self.sb(es2, [128, 32, LC], F32, "V1_%d" % i) for i in range(2)]
            nchunk = NPR // LC
            pl = lambda fn, reads, writes: S.op('pool', fn, reads=reads, writes=writes)
            wcnt = [0]

            NPRE = nchunk // 2

            def cinfo(c):
                blk = c // CPB
                return blk, (c % CPB) * LC, uS[blk % 2], ('uS', blk % 2), yS[blk % 2], ('yS', blk % 2)

            def do_V(c):
                blk, tb, us, uk, ys, yk = cinfo(c)
                if c % CPB == 0:
                    if c < NPRE:
                        S.dma('sp', lambda e: e.dma_start(
                            out=us[:], in_=R['UT'][:, blk * UB:(blk + 1) * UB].rearrange("(gp r) t -> r gp t", r=32)), writes=[uk])
                        dve(lambda e: e.tensor_scalar(out=us[:], in0=us[:], scalar1=fl5[:32, 1:2], scalar2=None, op0=ALU.mult), [uk, 'fl5'], [uk])
                    else:
                        lb = blk - NPRE // CPB
                        S.dma('sp', lambda e: e.dma_start(
                            out=us[:], in_=R['UT'][:, lb * UB:(lb + 1) * UB].rearrange("(gp r) t -> r gp t", r=32)), writes=[uk])
                        S.dma('sp', lambda e: e.dma_start(
                            out=ub[:], in_=R['UT'][:, HPR + lb * UB:HPR + (lb + 1) * UB].rearrange("(gp r) t -> r gp t", r=32)), writes=['ub'])
                        dve(lambda e: e.tensor_scalar(out=us[:], in0=us[:], scalar1=fl5[:32, 0:1], scalar2=None, op0=ALU.mult), [uk, 'fl5'], [uk])
                        dve(lambda e: e.scalar_tensor_tensor(out=us[:], in0=ub[:], scalar=fl5[:32, 1:2], in1=us[:], op0=ALU.mult, op1=ALU.add),
                            [uk, 'ub', 'fl5'], [uk])
                for hf in range(2):
                    pss = [[self.psum_next(), self.psum_next()], [self.psum_next(), self.psum_next()]]
                    for ri in range(2):
                        for j in range(16):
                            gp = hf * 16 + j
                            ps, pk = pss[ri][j // 8]
                            col = (j % 8) * LC
                            S.op('pe', lambda e, ps=ps, ri=ri, gp=gp, col=col: e.matmul(
                                ps[:, col:col + LC], lhsT=WvT[ri][:, gp, :], rhs=us[:, gp, tb:tb + LC], start=True, stop=True),
                                reads=['WvT%d' % ri, uk], writes=[pk])
                    for tl in range(2):
                        g0 = hf * 16 + tl * 8
                        (pr, prk) = pss[0][tl]
                        (pi, pik) = pss[1][tl]
                        grs = Gr[:, g0:g0 + 8, :].rearrange("p g t -> p (g t)")
                        gis = Gi[:, g0:g0 + 8, :].rearrange("p g t -> p (g t)")
                        wi = (wcnt[0] % 2) * 4
                        wcnt[0] += 1
                        w = W1[wi:wi + 4]
                        wk = ['w%d' % (wi + q) for q in range(4)]
                        dve(lambda e, pr=pr, grs=grs, w=w: e.tensor_tensor(out=w[0][:], in0=pr[:, 0:512], in1=grs, op=ALU.mult), [prk, 'G'], [wk[0]])
                        dve(lambda e, pi=pi, gis=gis, w=w: e.tensor_tensor(out=w[1][:], in0=pi[:, 0:512], in1=gis, op=ALU.mult), [pik, 'G'], [wk[1]])
                        dve(lambda e, pi=pi, grs=grs, w=w: e.tensor_tensor(out=w[2][:], in0=pi[:, 0:512], in1=grs, op=ALU.mult), [pik, 'G'], [wk[2]])
                        dve(lambda e, pr=pr, gis=gis, w=w: e.tensor_tensor(out=w[3][:], in0=pr[:, 0:512], in1=gis, op=ALU.mult), [prk, 'G'], [wk[3]])
                        pl(lambda e, g0=g0, w=w: e.tensor_tensor(out=RI[0][:, g0:g0 + 8, :].rearrange("p g t -> p (g t)"), in0=w[0][:], in1=w[1][:],
                                                             op=ALU.add), [wk[0], wk[1]], ['RI0'])
                        pl(lambda e, g0=g0, w=w: e.tensor_tensor(out=RI[1][:, g0:g0 + 8, :].rearrange("p g t -> p (g t)"), in0=w[2][:], in1=w[3][:],
                                                             op=ALU.subtract), [wk[2], wk[3]], ['RI1'])

            def do_scan(c):
                if c > 0:
                    for ri, sp in enumerate(['SPr', 'SPi']):
                        dve(lambda e, sp=sp: e.tensor_tensor(out=T['TC'][:], in0=T[sp][:], in1=T['MAG'][:], op=ALU.mult), [sp, 'MAG'], ['TC'])
                        dve(lambda e, ri=ri: e.tensor_tensor(out=RI[ri][:, :, 0:1], in0=RI[ri][:, :, 0:1], in1=T['TC'][:].unsqueeze(2), op=ALU.add),
                            ['TC', 'RI%d' % ri], ['RI%d' % ri])
                for ri in range(2):
                    dve(lambda e, ri=ri: e.tensor_tensor_scan(out=RRt[ri][:].rearrange("p g t -> p (g t)"), data0=RHO[:].rearrange("p g t -> p (g t)"),
                                                             data1=RI[ri][:].rearrange("p g t -> p (g t)"), initial=0.0, op0=ALU.mult, op1=ALU.add),
                        ['RHO', 'RI%d' % ri], ['RRt%d' % ri])
                L = LC - 1
                col = lambda t_: t_[:, :, L:L + 1]
                dve(lambda e: e.tensor_tensor(out=T['TC'][:].unsqueeze(2), in0=col(RRt[0]), in1=col(Gr), op=ALU.mult), ['RRt0', 'G'], ['TC'])
                dve(lambda e: e.tensor_tensor(out=T['TD'][:].unsqueeze(2), in0=col(RRt[1]), in1=col(Gi), op=ALU.mult), ['RRt1', 'G'], ['TD'])
                dve(lambda e: e.tensor_tensor(out=T['SPr'][:], in0=T['TC'][:], in1=T['TD'][:], op=ALU.subtract), ['TC', 'TD'], ['SPr'])
                dve(lambda e: e.tensor_tensor(out=T['TC'][:].unsqueeze(2), in0=col(RRt[1]), in1=col(Gr), op=ALU.mult), ['RRt1', 'G'], ['TC'])
                dve(lambda e: e.tensor_tensor(out=T['TD'][:].unsqueeze(2), in0=col(RRt[0]), in1=col(Gi), op=ALU.mult), ['RRt0', 'G'], ['TD'])
                dve(lambda e: e.tensor_tensor(out=T['SPi'][:], in0=T['TC'][:], in1=T['TD'][:], op=ALU.add), ['TC', 'TD'], ['SPi'])

            def do_rotout(c):
                pl(lambda e: e.tensor_tensor(out=V1[0][:], in0=RRt[0][:], in1=Gr[:], op=ALU.mult), ['RRt0', 'G'], ['v0'])
                pl(lambda e: e.tensor_tensor(out=V1[1][:], in0=RRt[1][:], in1=Gi[:], op=ALU.mult), ['RRt1', 'G'], ['v1'])
                pl(lambda e: e.tensor_tensor(out=SB[0][:], in0=V1[0][:], in1=V1[1][:], op=ALU.subtract), ['v0', 'v1'], ['SB0'])
                pl(lambda e: e.tensor_tensor(out=V1[0][:], in0=RRt[1][:], in1=Gr[:], op=ALU.mult), ['RRt1', 'G'], ['v0'])
                pl(lambda e: e.tensor_tensor(out=V1[1][:], in0=RRt[0][:], in1=Gi[:], op=ALU.mult), ['RRt0', 'G'], ['v1'])
                pl(lambda e: e.tensor_tensor(out=SB[1][:], in0=V1[0][:], in1=V1[1][:], op=ALU.add), ['v0', 'v1'], ['SB1'])

            def do_y(c):
                blk, tb, us, uk, ys, yk = cinfo(c)
                for g0 in range(0, 32, 8):
                    ps, pk = self.psum_next()
                    for j in range(8):
                        gp = g0 + j
                        cs = j * LC
                        S.op('pe', lambda e, ps=ps, gp=gp, cs=cs: e.matmul(ps[:32, cs:cs + LC], lhsT=CT[0][:, gp, :], rhs=SB[0][:, gp, :],
                                                                         start=True, stop=False), reads=['CT0', 'SB0'], writes=[pk])
                        S.op('pe', lambda e, ps=ps, gp=gp, cs=cs: e.matmul(ps[:32, cs:cs + LC], lhsT=CT[1][:, gp, :], rhs=SB[1][:, gp, :],
                                                                         start=False, stop=False), reads=['CT1', 'SB1'], writes=[pk])
                        S.op('pe', lambda e, ps=ps, gp=gp, cs=cs: e.matmul(ps[:32, cs:cs + LC], lhsT=Dd[:, gp, :], rhs=us[:, gp, tb:tb + LC],
                                                                         start=False, stop=True), reads=['Dd', uk], writes=[pk])
                    gelu_to(ps, pk, 512, ys[:, g0:g0 + 8, tb:tb + LC], yk)
                if c % CPB == CPB - 1:
                    lb = blk - NPRE // CPB
                    S.dma('sp', lambda e: e.dma_start(
                        out=R['S5G'][:, lb * UB:(lb + 1) * UB].rearrange("(gp r) t -> r gp t", r=32), in_=ys[:]),
                        reads=[yk], writes=[('dram', 's5g', blk)])

            do_V(0)
            for c in range(nchunk):
                do_scan(c)
                if c >= NPRE:
                    do_rotout(c)
                if c + 1 < nchunk:
                    do_V(c + 1)
                if c >= NPRE:
                    do_y(c)
            for ri, (sp, dst) in enumerate([('SPr', O['s5r_p']), ('SPi', O['s5i_p'])]):
                ps, pk = self.psum_next()
                S.op('pe', lambda e, ps=ps, sp=sp: e.transpose(out=ps[:32, 0:128], in_=T[sp][:, :], identity=self.identf[:]),
                     reads=[sp, 'identf'], writes=[pk])
                dve(lambda e, ps=ps, ri=ri: e.tensor_copy(out=nat[ri][:], in_=ps[:32, 0:128]), [pk], ['nat%d' % ri])
                S.dma('sp', lambda e, ri=ri, dst=dst: e.dma_start(out=dst.rearrange("(gp g2) p -> gp (g2 p)", g2=2), in_=nat[ri][:]),
                      reads=['nat%d' % ri], writes=[('dram', 's5p', ri)])
            S.barrier()
            S.flush()
            es2.close()
            es3 = es.enter_context(ExitStack())
            sre = [self.sb(es3, [NSM, 4096], F32, "sre%d" % i) for i in range(2)]
            SO = [self.sb(es3, [128, 32, NSM], F32, "SO%d" % i) for i in range(2)]
            SN = [self.sb(es3, [128, 32, NSM], F32, "SN%d" % i) for i in range(2)]
            SNb = [self.sb(es3, [128, 32, NSM], BF16, "SNb%d" % i) for i in range(2)]
            Q1 = [self.sb(es3, [128, 32, NSM], F32, "Q1_%d" % i) for i in range(2)]
            uSs = self.sb(es3, [32, 32, NSM], BF16, "uSs")
            ySs = self.sb(es3, [32, 32, NSM], BF16, "ySs")
            S.dma('sp', lambda e: e.dma_start(out=sre[0][:], in_=I['s5r'][:, :]), writes=['sre0'])
            S.dma('sp', lambda e: e.dma_start(out=sre[1][:], in_=I['s5i'][:, :]), writes=['sre1'])
            S.dma('sp', lambda e: e.dma_start(out=uSs[:], in_=R['UT'][:, NPR:NTOK].rearrange("(gp r) t -> r gp t", r=32)), writes=['uSs'])
            for ri in range(2):
                for g0 in (0, 16):
                    ps, pk = self.psum_next()
                    for j in range(16):
                        S.op('pe', lambda e, ps=ps, ri=ri, g0=g0, j=j: e.transpose(out=ps[:, j * NSM:(j + 1) * NSM],
                                                                                  in_=sre[ri][:NSM, (g0 + j) * 128:(g0 + j + 1) * 128],
                                                                                  identity=self.identf[:NSM, :NSM]),
                             reads=['sre%d' % ri, 'identf'], writes=[pk])
                    dve(lambda e, ps=ps, ri=ri, g0=g0: e.tensor_copy(out=SO[ri][:, g0:g0 + 16, :], in_=ps[:, 0:16 * NSM].rearrange("p (j q) -> p j q", j=16)),
                        [pk], ['SO%d' % ri])
            ab = lambda nm: T[nm][:].unsqueeze(2).to_broadcast([128, 32, NSM])
            for ri in range(2):
                a_, b_ = (0, 1) if ri == 0 else (1, 0)
                dve(lambda e, a_=a_: e.tensor_tensor(out=Q1[0][:], in0=SO[a_][:], in1=ab('AR'), op=ALU.mult), ['SO%d' % a_, 'AR'], ['Q10'])
                dve(lambda e, b_=b_: e.tensor_tensor(out=Q1[1][:], in0=SO[b_][:], in1=ab('AI'), op=ALU.mult), ['SO%d' % b_, 'AI'], ['Q11'])
                dve(lambda e, ri=ri: e.tensor_tensor(out=Q1[0][:], in0=Q1[0][:], in1=Q1[1][:], op=(ALU.subtract if ri == 0 else ALU.add)),
                    ['Q10', 'Q11'], ['Q10'])
                for g0 in (0, 16):
                    ps, pk = self.psum_next()
                    for j in range(16):
                        gp = g0 + j
                        S.op('pe', lambda e, ps=ps, ri=ri, gp=gp, j=j: e.matmul(ps[:, j * NSM:(j + 1) * NSM], lhsT=WvT[ri][:, gp, :], rhs=uSs[:, gp, :],
                                                                               start=True, stop=True), reads=['WvT%d' % ri, 'uSs'], writes=[pk])
                    dve(lambda e, ps=ps, ri=ri, g0=g0: e.tensor_tensor(out=SN[ri][:, g0:g0 + 16, :], in0=Q1[0][:, g0:g0 + 16, :],
                                                                      in1=ps[:, 0:16 * NSM].rearrange("p (j q) -> p j q", j=16), op=ALU.add),
                        [pk, 'Q10'], ['SN%d' % ri])
                dve(lambda e, ri=ri: e.tensor_copy(out=SNb[ri][:], in_=SN[ri][:]), ['SN%d' % ri], ['SNb%d' % ri])
            for g0 in (0, 16):
                ps, pk = self.psum_next()
                for j in range(16):
                    gp = g0 + j
                    cs = j * NSM
                    S.op('pe', lambda e, ps=ps, gp=gp, cs=cs: e.matmul(ps[:32, cs:cs + NSM], lhsT=CT[0][:, gp, :], rhs=SNb[0][:, gp, :],
                                                                     start=True, stop=False), reads=['CT0', 'SNb0'], writes=[pk])
                    S.op('pe', lambda e, ps=ps, gp=gp, cs=cs: e.matmul(ps[:32, cs:cs + NSM], lhsT=CT[1][:, gp, :], rhs=SNb[1][:, gp, :],
                                                                     start=False, stop=False), reads=['CT1', 'SNb1'], writes=[pk])
                    S.op('pe', lambda e, ps=ps, gp=gp, cs=cs: e.matmul(ps[:32, cs:cs + NSM], lhsT=Dd[:, gp, :], rhs=uSs[:, gp, :],
                                                                     start=False, stop=True), reads=['Dd', 'uSs'], writes=[pk])
                gelu_to(ps, pk, 16 * NSM, ySs[:, g0:g0 + 16, :], 'ySs')
            S.dma('sp', lambda e: e.dma_start(out=R['S5G'][:, HPR:HT].rearrange("(gp r) t -> r gp t", r=32), in_=ySs[:]),
                  reads=['ySs'], writes=[('dram', 's5gs')])
            for ri, dst in enumerate([O['s5r_s'], O['s5i_s']]):
                for g0 in range(0, 32, 4):
                    ps, pk = self.psum_next()
                    for j in range(4):
                        S.op('pe', lambda e, ps=ps, ri=ri, g0=g0, j=j: e.transpose(out=ps[:NSM, j * 128:(j + 1) * 128], in_=SN[ri][:, g0 + j, :],
                                                                                  identity=self.identf[:]), reads=['SN%d' % ri, 'identf'], writes=[pk])
                    ei += 1
                    self.evac(ei, sre[ri][:, g0 * 128:(g0 + 4) * 128], ps[:NSM, 0:512], [pk], ['sre%d' % ri])
                S.dma('sp', lambda e, ri=ri, dst=dst: e.dma_start(out=dst[:, :], in_=sre[ri][:]), reads=['sre%d' % ri], writes=[('dram', 's5s', ri)])
            S.barrier()
            S.flush()
            es3.close()
        self.s5_glu()

    def s5_glu(self):
        S = self.S
        I = self.I
        R = self.R
        for blk in [(0, HT)]:
            t0, t1 = blk
            NT = t1 - t0
            with ExitStack() as es:
                gT = self.sb(es, [128, 8, NT], BF16, "gT")
                self.WB = [self.sb(es, [128, 8 * 256], BF16, "wb") for _ in range(2)]
                self.wb_i = 0
                sg = [self.sb(es, [128, 512], F32, "sgl") for _ in range(2)]
                ob = [self.sb(es, [128, 512], BF16, "obl") for _ in range(2)]
                S.dma('sp', lambda e: e.dma_start(out=gT[:], in_=R['S5G'][:, t0:t1].rearrange("(k p) t -> p k t", p=128)), writes=['gT'])
                cnt = 0
                for c0 in range(0, 1024, 256):
                    wv, wkey = self.wload(I['w_s5_glu'][:, c0:c0 + 256], 8, 256)
                    for m in range(2):
                        mi = c0 // 128 + m
                        for (n0, nn) in nchunks(NT):
                            ps, pk = self.psum_next()
                            for k in range(8):
                                S.op('pe', lambda e, ps=ps, wv=wv, k=k, m=m, n0=n0, nn=nn: e.matmul(
                                    ps[:, 0:nn], lhsT=wv[:, k, m * 128:(m + 1) * 128], rhs=gT[:, k, n0:n0 + nn], start=(k == 0), stop=(k == 7)),
                                    reads=[wkey, 'gT'], writes=[pk], inc=(k == 7))
                            sgt = sg[cnt % 2]
                            o = ob[cnt % 2]
                            sk = ('sgl', cnt % 2)
                            ok = ('obl', cnt % 2)
                            cnt += 1
                            S.op('act', lambda e, sgt=sgt, ps=ps, nn=nn: e.activation(out=sgt[:, 0:nn], in_=ps[:, 0:nn], func=AF.Sigmoid),
                                 reads=[pk], writes=[sk])
                            S.op('dve', lambda e, o=o, sgt=sgt, mi=mi, n0=n0, nn=nn: e.tensor_tensor(out=o[:, 0:nn], in0=sgt[:, 0:nn],
                                                                                                    in1=gT[:, mi, n0:n0 + nn], op=ALU.mult),
                                 reads=[sk, 'gT'], writes=[ok])
                            S.dma('sp', lambda e, o=o, mi=mi, n0=n0, nn=nn: e.dma_start(
                                out=R['BR0'][mi * 128:(mi + 1) * 128, t0 + n0:t0 + n0 + nn], in_=o[:, 0:nn]), reads=[ok],
                                writes=[('dram', 'brt0', mi, n0)])
                S.barrier()
                S.flush()

    def mlstm_prompt(self):
        S = self.S
        I = self.I
        R = self.R
        O = self.O
        CL = 64
        NCH = NPR // CL
        dve = lambda fn, reads, writes: S.op('dve', fn, reads=reads, writes=writes)
        act = lambda fn, reads, writes: S.op('act', fn, reads=reads, writes=writes)
        with ExitStack() as es:
            QC = {nm: self.sb(es, [64, NCH, 4], F32, nm) for nm in ['XC', 'EMC', 'WLC', 'WIC']}
            WE = self.sb(es, [128, 4, NCH], F32, "WE")
            Rb = self.sb(es, [64, 4, NPR], F32, "Rb")
            Cf = self.sb(es, [128, 4, 2, 257], F32, "Cf")
            Cb = self.sb(es, [128, 4, 2, 257], BF16, "Cb")
            GH = self.sb(es, [64, 1024], F32, "GH")
            maskT = self.sb(es, [64, 64], F32, "maskT")
            oh4 = self.sb(es, [4, 4, 128], F32, "oh4")
            bi = self.sb(es, [4, 1], F32, "bi")
            bf = self.sb(es, [4, 1], F32, "bf")
            NH = NCH // 2
            flm = self.sb(es, [128, 2], F32, "flm")
            WLCp = self.sb(es, [64, NH, 4], F32, "WLCp")
            WEp = self.sb(es, [128, 4, NH], F32, "WEp")
            S.dma('sp', lambda e: e.dma_start(out=flm[:], in_=I['flag'][:, :]), writes=['flm'])
            es1 = es.enter_context(ExitStack())
            rows = {nm: self.sb(es1, [4, NPR], F32, nm) for nm in ['IG', 'FG', 'L', 'F', 'X', 'Rm', 'RP', 'WI', 'MT', 'EM', 'RE', 'WL', 'ON']}
            r3 = lambda nm: rows[nm][:].rearrange("h (c t) -> h c t", t=CL)
            S.dma('sp', lambda e: e.dma_start(out=rows['IG'][:], in_=R['IGR'][:, 0:NPR]), writes=['IG'])
            S.dma('sp', lambda e: e.dma_start(out=rows['FG'][:], in_=R['FGR'][:, 0:NPR]), writes=['FG'])
            S.dma('sp', lambda e: e.dma_start(out=bi[:], in_=I['b_igate'].rearrange("(h o) -> h o", o=1)), writes=['bi'])
            S.dma('sp', lambda e: e.dma_start(out=bf[:], in_=I['b_fgate'].rearrange("(h o) -> h o", o=1)), writes=['bf'])
            S.dma('sp', lambda e: e.dma_start(out=GH[:], in_=I['g_mlstm_head'].partition_broadcast(64)), writes=['GH'])
            S.op('pool', lambda e: e.memset(maskT[:], 0.0), writes=['maskT'])
            S.op('pool', lambda e: e.affine_select(out=maskT[:], in_=maskT[:], pattern=[[1, 64]], compare_op=ALU.is_ge, fill=-1e30,
                                                   base=0, channel_multiplier=-1), reads=['maskT'], writes=['maskT'])
            S.op('pool', lambda e: e.memset(rows['ON'][:], 1.0), writes=['ON'])
            S.op('pool', lambda e: e.memset(Cf[:], 0.0), writes=['Cf'])
            S.op('pool', lambda e: e.memset(Cb[:], 0.0), writes=['Cb'])
            dve(lambda e: e.tensor_copy(out=oh4[:], in_=self.identf[:4, :4].unsqueeze(2).to_broadcast([4, 4, 128])), ['identf'], ['oh4'])
            dve(lambda e: e.tensor_scalar(out=rows['IG'][:], in0=rows['IG'][:], scalar1=bi[:, 0:1], scalar2=None, op0=ALU.add), ['IG', 'bi'], ['IG'])
            dve(lambda e: e.tensor_scalar(out=rows['FG'][:], in0=rows['FG'][:], scalar1=bf[:, 0:1], scalar2=None, op0=ALU.add), ['FG', 'bf'], ['FG'])
            act(lambda e: e.activation(out=rows['L'][:], in_=rows['FG'][:], func=AF.Exp, scale=-1.0), ['FG'], ['L'])
            act(lambda e: e.activation(out=rows['L'][:], in_=rows['L'][:], func=AF.Ln, bias=1.0), ['L'], ['L'])
            dve(lambda e: e.tensor_tensor_scan(out=rows['F'][:], data0=rows['ON'][:], data1=rows['L'][:], initial=0.0, op0=ALU.mult, op1=ALU.add),
                ['ON', 'L'], ['F'])
            dve(lambda e: e.tensor_tensor(out=rows['X'][:], in0=rows['IG'][:], in1=rows['F'][:], op=ALU.add), ['IG', 'F'], ['X'])
            dve(lambda e: e.tensor_tensor_scan(out=rows['Rm'][:], data0=rows['ON'][:], data1=rows['X'][:], initial=0.0, op0=ALU.mult, op1=ALU.max),
                ['ON', 'X'], ['Rm'])
            dve(lambda e: e.tensor_copy(out=r3('RP')[:, 1:NCH, :], in_=r3('Rm')[:, 0:NCH - 1, CL - 1:CL].to_broadcast([4, NCH - 1, CL])), ['Rm'], ['RP'])
            S.op('pool', lambda e: e.memset(r3('RP')[:, 0:1, :], 0.0), reads=['RP'], writes=['RP'])
            dve(lambda e: e.tensor_tensor(out=rows['WI'][:], in0=rows['RP'][:], in1=rows['Rm'][:], op=ALU.subtract), ['RP', 'Rm'], ['WI'])
            act(lambda e: e.activation(out=rows['WI'][:], in_=rows['WI'][:], func=AF.Exp), ['WI'], ['WI'])
            dve(lambda e: e.tensor_tensor(out=rows['MT'][:], in0=rows['Rm'][:], in1=rows['F'][:], op=ALU.subtract), ['Rm', 'F'], ['MT'])
            act(lambda e: e.activation(out=rows['EM'][:], in_=rows['MT'][:], func=AF.Exp, scale=-1.0), ['MT'], ['EM'])
            dve(lambda e: e.tensor_copy(out=r3('RE'), in_=r3('Rm')[:, :, CL - 1:CL].to_broadcast([4, NCH, CL])), ['Rm'], ['RE'])
            dve(lambda e: e.tensor_tensor(out=rows['WL'][:], in0=rows['X'][:], in1=rows['RE'][:], op=ALU.subtract), ['X', 'RE'], ['WL'])
            act(lambda e: e.activation(out=rows['WL'][:], in_=rows['WL'][:], func=AF.Exp, bias=-math.log(16.0)), ['WL'], ['WL'])
            S.dma('sp', lambda e: e.dma_start(out=O['m_p'][:, :], in_=rows['MT'][:, NPR - 1:NPR]), reads=['MT'], writes=[('dram', 'm_p')])
            for qn, cn in [('X', 'XC'), ('EM', 'EMC'), ('WL', 'WLC'), ('WI', 'WIC')]:
                ps, pk = self.psum_next()
                for c in range(NCH):
                    S.op('pe', lambda e, ps=ps, qn=qn, c=c: e.transpose(out=ps[:64, c * 4:(c + 1) * 4], in_=rows[qn][0:4, c * CL:(c + 1) * CL],
                                                                      identity=self.identf[:4, :4]), reads=[qn, 'identf'], writes=[pk])
                dve(lambda e, ps=ps, cn=cn: e.tensor_copy(out=QC[cn][:].rearrange("p c h -> p (c h)"), in_=ps[:64, 0:NCH * 4]), [pk], [cn])
            ei = 0
            for h in range(4):
                for (n0, nn) in nchunks(NPR):
                    ps, pk = self.psum_next()
                    S.op('pe', lambda e, ps=ps, h=h, n0=n0, nn=nn: e.matmul(ps[:64, 0:nn], lhsT=oh4[:, h, 0:64], rhs=rows['Rm'][:, n0:n0 + nn],
                                                                          start=True, stop=True), reads=['oh4', 'Rm'], writes=[pk])
                    ei += 1
                    self.evac(ei, Rb[:, h, n0:n0 + nn], ps[:64, 0:nn], [pk], ['Rb'])
                ps, pk = self.psum_next()
                S.op('pe', lambda e, ps=ps, h=h: e.matmul(ps[:, 0:NCH], lhsT=oh4[:, h, :], rhs=r3('WI')[:, :, CL - 1], start=True, stop=True),
                     reads=['oh4', 'WI'], writes=[pk])
                dve(lambda e, ps=ps, h=h: e.tensor_copy(out=WE[:, h, :], in_=ps[:, 0:NCH]), [pk], ['WE'])
            f0 = lambda n_: flm[:n_, 0:1]
            f1 = lambda n_: flm[:n_, 1:2]
            dve(lambda e: e.tensor_scalar(out=WLCp[:], in0=QC['WLC'][:, 0:NH, :], scalar1=f1(64), scalar2=None, op0=ALU.mult), ['WLC', 'flm'], ['WLCp'])
            dve(lambda e: e.tensor_copy(out=WEp[:], in_=WE[:, :, 0:NH]), ['WE'], ['WEp'])
            for nm in ['XC', 'EMC', 'WLC', 'WIC']:
                dve(lambda e, nm=nm: e.tensor_scalar(out=QC[nm][:, 0:NH, :], in0=QC[nm][:, 0:NH, :], scalar1=f0(64), scalar2=None, op0=ALU.mult),
                    [nm, 'flm', 'WLCp'], [nm])
                dve(lambda e, nm=nm: e.scalar_tensor_tensor(out=QC[nm][:, 0:NH, :], in0=QC[nm][:, NH:NCH, :], scalar=f1(64), in1=QC[nm][:, 0:NH, :],
                                                           op0=ALU.mult, op1=ALU.add), [nm, 'flm'], [nm])
            dve(lambda e: e.tensor_scalar(out=WE[:, :, 0:NH], in0=WE[:, :, 0:NH], scalar1=f0(128), scalar2=None, op0=ALU.mult), ['WE', 'flm', 'WEp'], ['WE'])
            dve(lambda e: e.scalar_tensor_tensor(out=WE[:, :, 0:NH], in0=WE[:, :, NH:NCH], scalar=f1(128), in1=WE[:, :, 0:NH], op0=ALU.mult, op1=ALU.add),
                ['WE', 'flm'], ['WE'])
            dve(lambda e: e.tensor_scalar(out=Rb[:, :, 0:HPR], in0=Rb[:, :, 0:HPR], scalar1=f0(64), scalar2=None, op0=ALU.mult), ['Rb', 'flm'], ['Rb'])
            dve(lambda e: e.scalar_tensor_tensor(out=Rb[:, :, 0:HPR], in0=Rb[:, :, HPR:NPR], scalar=f1(64), in1=Rb[:, :, 0:HPR], op0=ALU.mult, op1=ALU.add),
                ['Rb', 'flm'], ['Rb'])
            S.barrier()
            S.flush()
            es1.close()
            es2 = es.enter_context(ExitStack())
            BT = 256
            CPB = BT // CL
            qT = [self.sb(es2, [128, 8, BT], BF16, "qTb") for _ in range(2)]
            kT = [self.sb(es2, [128, 8, BT], BF16, "kTb") for _ in range(2)]
            v1 = [self.sb(es2, [64, CPB, 4, 257], BF16, "v1b") for _ in range(2)]
            ktm = [self.sb(es2, [64, CPB, 1024], BF16, "ktmb") for _ in range(2)]
            qTB = self.sb(es2, [128, 8, BT], BF16, "qTB")
            v1B = self.sb(es2, [64, CPB, 4, 256], BF16, "v1B")
            ktmB = self.sb(es2, [64, CPB, 1024], BF16, "ktmB")
            otm = [self.sb(es2, [64, 1024], F32, "otm") for _ in range(2)]
            sgo = [self.sb(es2, [64, 1024], F32, "sgo") for _ in range(2)]
            mlo = [self.sb(es2, [64, 1024], BF16, "mlo") for _ in range(2)]
            mlT = [self.sb(es2, [128, 8, BT], BF16, "mlT") for _ in range(2)]
            tmpw = [self.sb(es2, [64, 64], F32, "tmpw") for _ in range(3)]
            wT = [self.sb(es2, [64, 64], F32, "wT") for _ in range(3)]
            ST = [self.sb(es2, [64, 64], BF16, "ST") for _ in range(3)]
            na = [self.sb(es2, [64, 257], F32, "na") for _ in range(3)]
            num = [self.sb(es2, [64, 257], F32, "num") for _ in range(3)]
            hh = [self.sb(es2, [64, 256], F32, "hh") for _ in range(3)]
            jk = [self.sb(es2, [64, 256], F32, "jk") for _ in range(3)]
            t1 = [self.sb(es2, [64, 256], F32, "t1") for _ in range(3)]
            sm = [self.sb(es2, [64, 8], F32, "sm") for _ in range(3)]
            kw = [self.sb(es2, [64, 256], BF16, "kw") for _ in range(3)]
            for i in range(2):
                S.op('pool', lambda e, i=i: e.memset(v1[i][:], 1.0), writes=[('v1', i)])
            NSLOT = 3

            def stage_u(c, h, i2, bi_, cl, wlc, wend):
                    k_ = lambda nm: (nm, i2)
                    act(lambda e, i2=i2, bi_=bi_, cl=cl, h=h, wlc=wlc: e.activation(out=kw[i2][:], in_=ktm[bi_][:, cl, h * 256:(h + 1) * 256], func=AF.Copy,
                                                                               scale=wlc),
                        [('ktm', bi_), 'WLC', 'WLCp'], [k_('kw')])
                    for dh in range(2):
                        ps_c, pkc = self.psum_next()
                        S.op('pe', lambda e, ps_c=ps_c, i2=i2, dh=dh, bi_=bi_, cl=cl, h=h: e.matmul(
                            ps_c[:, 0:257], lhsT=kw[i2][:, dh * 128:(dh + 1) * 128], rhs=v1[bi_][:, cl, h, :], start=True, stop=True),
                            reads=[k_('kw'), ('v1', bi_)], writes=[pkc])
                        dve(lambda e, ps_c=ps_c, h=h, dh=dh, wend=wend: e.scalar_tensor_tensor(out=Cf[:, h, dh, :], in0=Cf[:, h, dh, :], scalar=wend,
                                                                                       in1=ps_c[:, 0:257], op0=ALU.mult, op1=ALU.add),
                            [pkc, ('Cf', h, dh), 'WE', 'WEp'], [('Cf', h, dh)])
                        S.op('pool', lambda e, h=h, dh=dh: e.tensor_copy(out=Cb[:, h, dh, :], in_=Cf[:, h, dh, :]),
                             reads=[('Cf', h, dh)], writes=[('Cb', h)])


            def stage_a(c, h, i2, bi_, cl, ci, tsl):
                    k_ = lambda nm: (nm, i2)
                    ps_s, pks = self.psum_next()
                    for dh in range(2):
                        S.op('pe', lambda e, ps_s=ps_s, h=h, dh=dh, bi_=bi_, tsl=tsl: e.matmul(
                            ps_s[:64, 0:64], lhsT=kT[bi_][:, h * 2 + dh, tsl], rhs=qT[bi_][:, h * 2 + dh, tsl], start=(dh == 0), stop=(dh == 1)),
                            reads=[('kT', bi_), ('qT', bi_)], writes=[pks])
                    dve(lambda e, i2=i2, h=h, c=c: e.tensor_tensor(out=tmpw[i2][:], in0=maskT[:], in1=Rb[:, h, c * CL:(c + 1) * CL], op=ALU.subtract),
                        ['maskT', 'Rb'], [k_('tmpw')])
                    act(lambda e, i2=i2, h=h, c=c: e.activation(out=wT[i2][:], in_=tmpw[i2][:], func=AF.Exp, bias=QC['XC'][:, c, h:h + 1]),
                        [k_('tmpw'), 'XC'], [k_('wT')])
                    dve(lambda e, i2=i2, ps_s=ps_s: e.scalar_tensor_tensor(out=ST[i2][:], in0=ps_s[:64, 0:64], scalar=1.0 / 16, in1=wT[i2][:],
                                                                         op0=ALU.mult, op1=ALU.mult), [pks, k_('wT')], [k_('ST')])
                    ps_a, pka = self.psum_next()
                    S.op('pe', lambda e, ps_a=ps_a, i2=i2, bi_=bi_, cl=cl, h=h: e.matmul(ps_a[:64, 0:257], lhsT=ST[i2][:], rhs=v1[bi_][:, cl, h, :],
                                                                                        start=True, stop=True),
                         reads=[k_('ST'), ('v1', bi_)], writes=[pka])
                    ps_b, pkb = self.psum_next()
                    for dh in range(2):
                        S.op('pe', lambda e, ps_b=ps_b, h=h, dh=dh, bi_=bi_, tsl=tsl: e.matmul(
                            ps_b[:64, 0:257], lhsT=qT[bi_][:, h * 2 + dh, tsl], rhs=Cb[:, h, dh, :], start=(dh == 0), stop=(dh == 1)),
                            reads=[('qT', bi_), ('Cb', h)], writes=[pkb])
                    act(lambda e, i2=i2, ps_a=ps_a: e.copy(out=na[i2][:], in_=ps_a[:64, 0:257]), [pka], [k_('na')])
                    dve(lambda e, i2=i2, ps_b=ps_b, c=c, h=h: e.scalar_tensor_tensor(out=num[i2][:], in0=ps_b[:64, 0:257], scalar=QC['WIC'][:, c, h:h + 1],
                                                                                   in1=na[i2][:], op0=ALU.mult, op1=ALU.add),
                        [pkb, k_('na'), 'WIC'], [k_('num')])

            def stage_b(c, h, i2, bi_, cl, ci, tsl):
                    k_ = lambda nm: (nm, i2)
                    act(lambda e, i2=i2: e.activation(out=sm[i2][:, 6:7], in_=num[i2][:, 256:257], func=AF.Abs), [k_('num')], [k_('sm')])
                    dve(lambda e, i2=i2, c=c, h=h: e.tensor_scalar(out=sm[i2][:, 0:1], in0=sm[i2][:, 6:7], scalar1=QC['EMC'][:, c, h:h + 1],
                                                                   scalar2=None, op0=ALU.max), [k_('sm'), 'EMC'], [k_('sm')])
                    dve(lambda e, i2=i2: e.reciprocal(out=sm[i2][:, 1:2], in_=sm[i2][:, 0:1]), [k_('sm')], [k_('sm')])
                    act(lambda e, i2=i2: e.activation(out=hh[i2][:], in_=num[i2][:, 0:256], func=AF.Copy, scale=sm[i2][:, 1:2]),
                        [k_('num'), k_('sm')], [k_('hh')])
                    act(lambda e, i2=i2: e.activation(out=jk[i2][:], in_=hh[i2][:], func=AF.Square, accum_out=sm[i2][:, 2:3]),
                        [k_('hh')], [k_('jk'), k_('sm')])
                    dve(lambda e, i2=i2: e.tensor_scalar(out=sm[i2][:, 3:4], in0=sm[i2][:, 2:3], scalar1=1.0 / 256, scalar2=EPS, op0=ALU.mult, op1=ALU.add),
                        [k_('sm')], [k_('sm')])
                    act(lambda e, i2=i2: e.activation(out=sm[i2][:, 4:5], in_=sm[i2][:, 3:4], func=AF.Sqrt), [k_('sm')], [k_('sm')])
                    dve(lambda e, i2=i2: e.reciprocal(out=sm[i2][:, 5:6], in_=sm[i2][:, 4:5]), [k_('sm')], [k_('sm')])
                    dve(lambda e, i2=i2, h=h: e.scalar_tensor_tensor(out=t1[i2][:], in0=hh[i2][:], scalar=sm[i2][:, 5:6], in1=GH[:, h * 256:(h + 1) * 256],
                                                                   op0=ALU.mult, op1=ALU.mult), [k_('hh'), k_('sm'), 'GH'], [k_('t1')])
                    dve(lambda e, i2=i2, h=h, ci=ci: e.tensor_tensor(out=mlo[ci][:, h * 256:(h + 1) * 256], in0=t1[i2][:],
                                                                   in1=sgo[ci][:, h * 256:(h + 1) * 256], op=ALU.mult),
                        [k_('t1'), ('sgo', ci)], [('mlo', ci)])
                    stage_u(c, h, i2, bi_, cl, QC['WLC'][:, c, h:h + 1], WE[:, h, c:c + 1])

            blend = lambda dst, src_b, n_, dk, sk: (
                dve(lambda e: e.tensor_scalar(out=dst, in0=dst, scalar1=flm[:n_, 0:1], scalar2=None, op0=ALU.mult), [dk, 'flm'], [dk]),
                dve(lambda e: e.scalar_tensor_tensor(out=dst, in0=src_b, scalar=flm[:n_, 1:2], in1=dst, op0=ALU.mult, op1=ALU.add), [dk, sk, 'flm'], [dk]))

            def load_kv(bi_, t0):
                for c8 in range(CPB):
                    S.dma('sp', lambda e, c8=c8: e.dma_start(
                        out=v1[bi_][:, c8, :, 0:256], in_=R['VTM'][t0 + c8 * CL:t0 + (c8 + 1) * CL, :].rearrange("s (h v) -> s h v", h=4)),
                        reads=[('v1', bi_)], writes=[('v1', bi_)])
                S.dma('sp', lambda e: e.dma_start(out=ktm[bi_][:], in_=R['KTM'][t0:t0 + BT, :].rearrange("(c s) f -> s c f", s=CL)),
                      writes=[('ktm', bi_)])

            def pre_prefix(c, bi_):
                if c % CPB == 0:
                    load_kv(bi_, (c // CPB) * BT)

            def pre_own(co, bi_):
                if co % CPB == 0:
                    t0 = (co // CPB) * BT
                    load_kv(bi_, t0)
                    for c8 in range(CPB):
                        S.dma('sp', lambda e, c8=c8: e.dma_start(
                            out=v1B[:, c8, :, :], in_=R['VTM'][HPR + t0 + c8 * CL:HPR + t0 + (c8 + 1) * CL, :].rearrange("s (h v) -> s h v", h=4)),
                            writes=['v1B'])
                    S.dma('sp', lambda e: e.dma_start(out=ktmB[:], in_=R['KTM'][HPR + t0:HPR + t0 + BT, :].rearrange("(c s) f -> s c f", s=CL)),
                          writes=['ktmB'])
                    blend(v1[bi_][:, :, :, 0:256], v1B[:], 64, ('v1', bi_), 'v1B')
                    blend(ktm[bi_][:], ktmB[:], 64, ('ktm', bi_), 'ktmB')
                    S.dma('sp', lambda e: e.dma_start(out=qT[bi_][:], in_=R['QTo'][:, t0:t0 + BT].rearrange("(k p) t -> p k t", p=128)),
                          writes=[('qT', bi_)])
                    S.dma('sp', lambda e: e.dma_start(out=kT[bi_][:], in_=R['KT'][:, t0:t0 + BT].rearrange("(k p) t -> p k t", p=128)),
                          writes=[('kT', bi_)])
                    S.dma('sp', lambda e: e.dma_start(out=qTB[:], in_=R['KT'][:, HPR + t0:HPR + t0 + BT].rearrange("(k p) t -> p k t", p=128)),
                          writes=['qTB'])
                    blend(kT[bi_][:], qTB[:], 128, ('kT', bi_), 'qTB')
                ci = co % 2
                S.dma('sp', lambda e: e.dma_start(out=otm[ci][:], in_=R['OTMo'][co * CL:(co + 1) * CL, :]), writes=[('otm', ci)])
                act(lambda e: e.activation(out=sgo[ci][:], in_=otm[ci][:], func=AF.Sigmoid), [('otm', ci)], [('sgo', ci)])

            def chunk_post(co, bi_):
                cl = co % CPB
                ci = co % 2
                tsl = slice(cl * CL, (cl + 1) * CL)
                pb, pbk = self.psumb_next()
                for k in range(8):
                    S.op('pe', lambda e, pb=pb, k=k, ci=ci: e.transpose(out=pb[:, k * 64:(k + 1) * 64], in_=mlo[ci][:64, k * 128:(k + 1) * 128],
                                                                      identity=self.identb[:64, :64]), reads=[('mlo', ci), 'identb'], writes=[pbk])
                act(lambda e, pb=pb: e.copy(out=mlT[bi_][:, :, tsl], in_=pb[:, 0:512].rearrange("p (k t) -> p k t", k=8)),
                    [pbk], [('mlT', bi_)])
                if cl == CPB - 1:
                    t0 = (co // CPB) * BT
                    S.dma('sp', lambda e: e.dma_start(out=R['BR1'][:, t0:t0 + BT].rearrange("(k p) t -> p k t", p=128), in_=mlT[bi_][:]),
                          reads=[('mlT', bi_)], writes=[('dram', 'br1', t0)])

            n_ = 0
            nblk_pre = NH // CPB
            for c in range(NH):
                bi_ = (c // CPB) % 2
                pre_prefix(c, bi_)
                for h in range(4):
                    stage_u(c, h, n_ % NSLOT, bi_, c % CPB, WLCp[:, c, h:h + 1], WEp[:, h, c:c + 1])
                    n_ += 1
            prev = None
            for co in range(NH):
                bi_ = (nblk_pre + co // CPB) % 2
                for h in range(4):
                    if h == 0:
                        pre_own(co, bi_)
                    cl = co % CPB
                    args = (co, h, n_ % NSLOT, bi_, cl, co % 2, slice(cl * CL, (cl + 1) * CL))
                    n_ += 1
                    stage_a(*args)
                    if prev is not None:
                        stage_b(*prev)
                        if prev[1] == 3:
                            chunk_post(prev[0], prev[3])
                    prev = args
            stage_b(*prev)
            chunk_post(prev[0], prev[3])
            for h in range(4):
                for dh in range(2):
                    S.dma('sp', lambda e, h=h, dh=dh: e.dma_start(out=O['C_p'][h, dh * 128:(dh + 1) * 128, :], in_=Cf[:, h, dh, 0:256]),
                          reads=[('Cf', h, dh)], writes=[('dram', 'C_p', h, dh)])
                    S.dma('sp', lambda e, h=h, dh=dh: e.dma_start(out=O['n_p'][h, dh * 128:(dh + 1) * 128].rearrange("(p o) -> p o", o=1),
                                                                 in_=Cf[:, h, dh, 256:257]),
                          reads=[('Cf', h, dh)], writes=[('dram', 'n_p', h, dh)])
            S.barrier()
            S.flush()
            es2.close()

    def mlstm_sample(self):
        S = self.S
        I = self.I
        R = self.R
        O = self.O
        B = NSM
        dve = lambda fn, reads, writes: S.op('dve', fn, reads=reads, writes=writes)
        act = lambda fn, reads, writes: S.op('act', fn, reads=reads, writes=writes)
        with ExitStack() as es:
            sc = {nm: self.sb(es, [B, 4], F32, "ms_" + nm) for nm in
                  ['g', 'bi', 'bf', 'iv', 'l', 'm0', 'gi', 'mt', 'wi', 'wa', 'em', 'qk', 'qn', 's', 'nq', 'rd', 'ss', 'rs', 't']}
            G8 = self.sb(es, [B, 8], F32, "G8")
            qS = self.sb(es, [128, 8, B], BF16, "qSm")
            qtm = self.sb(es, [B, 1024], BF16, "qtmm")
            ktm = self.sb(es, [B, 1024], BF16, "ktmm")
            vtm = self.sb(es, [B, 1024], BF16, "vtmm")
            otm = self.sb(es, [B, 1024], F32, "otmm")
            n0 = self.sb(es, [B, 1024], F32, "n0m")
            GH = self.sb(es, [B, 1024], F32, "GHm")
            p1 = self.sb(es, [B, 1024], F32, "p1m")
            p2 = self.sb(es, [B, 1024], F32, "p2m")
            numt = self.sb(es, [B, 1024], F32, "numt")
            kws = self.sb(es, [B, 1024], BF16, "kws")
            Km = [self.sb(es, [B, 1024], BF16, "Km") for _ in range(2)]
            mlb = self.sb(es, [B, 1024], BF16, "mlb")
            mlT = self.sb(es, [128, 8, B], BF16, "mlTs")
            IDB = self.sb(es, [128, B, B], BF16, "IDB")
            Qm = self.sb(es, [128, 8, B, B], BF16, "Qm")
            WD = self.sb(es, [B, B, 4], F32, "WD")
            ones32 = self.sb(es, [B, 128], F32, "ones32")
            Wb = self.sb(es, [128, B * 4], F32, "Wb")
            C32 = [self.sb(es, [128, 4, 2, 256], F32, "C32") for _ in range(3)]
            C16 = [self.sb(es, [128, 4, 2, 256], BF16, "C16") for _ in range(2)]
            Co = [self.sb(es, [128, 4, 2, 256], F32, "Co") for _ in range(2)]
            S.dma('sp', lambda e: e.dma_start(out=G8[:], in_=R['GTM'][NPR:NTOK, :]), writes=['G8'])
            S.dma('sp', lambda e: e.dma_start(out=sc['bi'][:], in_=I['b_igate'].partition_broadcast(B)), writes=['bi'])
            S.dma('sp', lambda e: e.dma_start(out=sc['bf'][:], in_=I['b_fgate'].partition_broadcast(B)), writes=['bf'])
            S.dma('sp', lambda e: e.dma_start(out=sc['m0'][:], in_=I['mM'][:, :]), writes=['m0'])
            S.dma('sp', lambda e: e.dma_start(out=n0[:], in_=I['mN'][:, :]), writes=['n0'])
            S.dma('sp', lambda e: e.dma_start(out=GH[:], in_=I['g_mlstm_head'].partition_broadcast(B)), writes=['GH'])
            S.dma('sp', lambda e: e.dma_start(out=ktm[:], in_=R['KTM'][NPR:NTOK, :]), writes=['ktm'])
            S.dma('sp', lambda e: e.dma_start(out=vtm[:], in_=R['VTM'][NPR:NTOK, :]), writes=['vtm'])
            S.dma('sp', lambda e: e.dma_start(out=otm[:], in_=R['OTMo'][HPR:HT, :]), writes=['otm'])
            S.dma('sp', lambda e: e.dma_start(out=qS[:], in_=R['QTo'][:, HPR:HT].rearrange("(k p) t -> p k t", p=128)), writes=['qS'])
            for kq in range(0, 8, 4):
                pb, pbk = self.psumb_next()
                for j in range(4):
                    S.op('pe', lambda e, pb=pb, j=j, kq=kq: e.transpose(out=pb[:B, j * 128:(j + 1) * 128], in_=qS[:, kq + j, :],
                                                                       identity=self.identb[:]), reads=['qS', 'identb'], writes=[pbk])
                act(lambda e, pb=pb, kq=kq: e.copy(out=qtm[:, kq * 128:(kq + 4) * 128], in_=pb[:B, 0:512]), [pbk], ['qtm'])
            S.op('pool', lambda e: e.memset(IDB[:], 1.0), writes=['IDB'])
            S.op('pool', lambda e: e.affine_select(out=IDB[:], in_=IDB[:], pattern=[[1, B], [-1, B]], compare_op=ALU.is_equal, fill=0.0,
                                                   base=0, channel_multiplier=0), reads=['IDB'], writes=['IDB'])
            S.op('pool', lambda e: e.memset(ones32[:], 1.0), writes=['ones32'])
            for k in range(8):
                dve(lambda e, k=k: e.tensor_tensor(out=Qm[:, k, :, :], in0=qS[:, k, :].unsqueeze(1).to_broadcast([128, B, B]), in1=IDB[:], op=ALU.mult),
                    ['qS', 'IDB'], ['Qm'])
            tt = lambda o, a, b, op: dve(lambda e: e.tensor_tensor(out=sc[o][:], in0=sc[a][:], in1=sc[b][:], op=op), [a, b], [o])
            dve(lambda e: e.tensor_tensor(out=sc['iv'][:], in0=G8[:, 0:4], in1=sc['bi'][:], op=ALU.add), ['G8', 'bi'], ['iv'])
            dve(lambda e: e.tensor_tensor(out=sc['g'][:], in0=G8[:, 4:8], in1=sc['bf'][:], op=ALU.add), ['G8', 'bf'], ['g'])
            act(lambda e: e.activation(out=sc['l'][:], in_=sc['g'][:], func=AF.Exp, scale=-1.0), ['g'], ['l'])
            act(lambda e: e.activation(out=sc['l'][:], in_=sc['l'][:], func=AF.Ln, bias=1.0), ['l'], ['l'])
            tt('gi', 'm0', 'l', ALU.subtract)
            tt('mt', 'gi', 'iv', ALU.max)
            S.dma('sp', lambda e: e.dma_start(out=O['m_s'][:, :], in_=sc['mt'][:]), reads=['mt'], writes=[('dram', 'm_s')])
            tt('wi', 'gi', 'mt', ALU.subtract)
            act(lambda e: e.activation(out=sc['wi'][:], in_=sc['wi'][:], func=AF.Exp), ['wi'], ['wi'])
            tt('wa', 'iv', 'mt', ALU.subtract)
            act(lambda e: e.activation(out=sc['wa'][:], in_=sc['wa'][:], func=AF.Exp, bias=-math.log(16.0)), ['wa'], ['wa'])
            act(lambda e: e.activation(out=sc['em'][:], in_=sc['mt'][:], func=AF.Exp, scale=-1.0), ['mt'], ['em'])
            v4 = lambda t_: t_[:].rearrange("b (h d) -> b h d", h=4)
            bc = lambda nm: sc[nm][:].unsqueeze(2).to_broadcast([B, 4, 256])
            dve(lambda e: e.tensor_tensor(out=p1[:], in0=qtm[:], in1=ktm[:], op=ALU.mult), ['qtm', 'ktm'], ['p1'])
            dve(lambda e: e.tensor_reduce(out=sc['qk'][:], in_=v4(p1), axis=AX.X, op=ALU.add), ['p1'], ['qk'])
            dve(lambda e: e.tensor_tensor(out=p2[:], in0=qtm[:], in1=n0[:], op=ALU.mult), ['qtm', 'n0'], ['p2'])
            dve(lambda e: e.tensor_reduce(out=sc['qn'][:], in_=v4(p2), axis=AX.X, op=ALU.add), ['p2'], ['qn'])
            tt('s', 'qk', 'wa', ALU.mult)
            tt('t', 'wi', 'qn', ALU.mult)
            tt('nq', 's', 't', ALU.add)
            act(lambda e: e.activation(out=sc['nq'][:], in_=sc['nq'][:], func=AF.Abs), ['nq'], ['nq'])
            tt('rd', 'nq', 'em', ALU.max)
            dve(lambda e: e.reciprocal(out=sc['rd'][:], in_=sc['rd'][:]), ['rd'], ['rd'])
            dve(lambda e: e.tensor_tensor(out=v4(kws), in0=v4(ktm), in1=bc('wa'), op=ALU.mult), ['ktm', 'wa'], ['kws'])
            dve(lambda e: e.tensor_tensor(out=v4(p1), in0=v4(n0), in1=bc('wi'), op=ALU.mult), ['n0', 'wi'], ['p1'])
            dve(lambda e: e.tensor_tensor(out=p1[:], in0=p1[:], in1=kws[:], op=ALU.add), ['p1', 'kws'], ['p1'])
            S.dma('sp', lambda e: e.dma_start(out=O['n_s'][:, :], in_=p1[:]), reads=['p1'], writes=[('dram', 'n_s')])
            dve(lambda e: e.tensor_tensor(out=WD[:], in0=self.identf[:B, :B].unsqueeze(2).to_broadcast([B, B, 4]),
                                          in1=sc['wi'][:].unsqueeze(1).to_broadcast([B, B, 4]), op=ALU.mult), ['identf', 'wi'], ['WD'])
            ps, pk = self.psum_next()
            S.op('pe', lambda e, ps=ps: e.matmul(ps[:, 0:B * 4], lhsT=ones32[:], rhs=WD[:].rearrange("k b h -> k (b h)"), start=True, stop=True),
                 reads=['ones32', 'WD'], writes=[pk])
            dve(lambda e, ps=ps: e.tensor_copy(out=Wb[:], in_=ps[:, 0:B * 4]), [pk], ['Wb'])
            self.pa_i = 0
            pq = [self.psum_next() for _ in range(4)]
            def ms_load(b):
                i2 = b % 3
                for h in range(4):
                    S.dma('sp', lambda e, i2=i2, b=b, h=h: e.dma_start(out=C32[i2][:, h, :, :], in_=I['mC'][b, h].rearrange("(dh p) v -> p dh v", p=128)),
                          writes=[('C32', i2)])

            def ms_compute(b):
                i2 = b % 2
                i3 = b % 3
                S.op('pool', lambda e, i2=i2, i3=i3: e.tensor_copy(out=C16[i2][:], in_=C32[i3][:]), reads=[('C32', i3)], writes=[('C16', i2)])
                act(lambda e, i2=i2, b=b: e.activation(out=Km[i2][:], in_=kws[:], func=AF.Copy, scale=self.identf[:B, b:b + 1]),
                    ['kws', 'identf'], [('Km', i2)])
                for h in range(4):
                    psq, pqk = pq[h]
                    for dh in range(2):
                        S.op('pe', lambda e, psq=psq, b=b, h=h, dh=dh, i2=i2: e.matmul(
                            psq[:B, 0:256], lhsT=Qm[:, h * 2 + dh, b, :], rhs=C16[i2][:, h, dh, :],
                            start=(b == 0 and dh == 0), stop=(b == B - 1 and dh == 1)), reads=['Qm', ('C16', i2)], writes=[pqk])
                    for dh in range(2):
                        psc, pck = self.psumc_next()
                        S.op('pe', lambda e, psc=psc, i2=i2, h=h, dh=dh: e.matmul(
                            psc[:, 0:256], lhsT=Km[i2][:, h * 256 + dh * 128:h * 256 + (dh + 1) * 128], rhs=vtm[:, h * 256:(h + 1) * 256],
                            start=True, stop=True), reads=[('Km', i2), 'vtm'], writes=[pck])
                        dve(lambda e, psc=psc, i2=i2, i3=i3, b=b, h=h, dh=dh: e.scalar_tensor_tensor(
                            out=Co[i2][:, h, dh, :], in0=C32[i3][:, h, dh, :], scalar=Wb[:, b * 4 + h:b * 4 + h + 1], in1=psc[:, 0:256],
                            op0=ALU.mult, op1=ALU.add), [pck, ('C32', i3), 'Wb'], [('Co', i2)])

            def ms_store(b):
                i2 = b % 2
                for h in range(4):
                    S.dma('sp', lambda e, i2=i2, b=b, h=h: e.dma_start(out=O['C_s'][b, h].rearrange("(dh p) v -> p dh v", p=128), in_=Co[i2][:, h, :, :]),
                          reads=[('Co', i2)], writes=[('dram', 'C_s', b, h)])

            ms_load(0)
            ms_load(1)
            for b in range(B):
                if b + 2 < B:
                    ms_load(b + 2)
                ms_compute(b)
                ms_store(b)
            dve(lambda e: e.tensor_tensor(out=v4(numt), in0=v4(vtm), in1=bc('s'), op=ALU.mult), ['vtm', 's'], ['numt'])
            for h in range(4):
                psq, pqk = pq[h]
                dve(lambda e, psq=psq, h=h: e.scalar_tensor_tensor(out=numt[:, h * 256:(h + 1) * 256], in0=psq[:B, 0:256], scalar=sc['wi'][:, h:h + 1],
                                                                 in1=numt[:, h * 256:(h + 1) * 256], op0=ALU.mult, op1=ALU.add),
                    [pqk, 'wi', 'numt'], ['numt'])
            dve(lambda e: e.tensor_tensor(out=v4(numt), in0=v4(numt), in1=bc('rd'), op=ALU.mult), ['numt', 'rd'], ['numt'])
            dve(lambda e: e.tensor_tensor(out=p2[:], in0=numt[:], in1=numt[:], op=ALU.mult), ['numt'], ['p2'])
            dve(lambda e: e.tensor_reduce(out=sc['ss'][:], in_=v4(p2), axis=AX.X, op=ALU.add), ['p2'], ['ss'])
            dve(lambda e: e.tensor_scalar(out=sc['ss'][:], in0=sc['ss'][:], scalar1=1.0 / 256, scalar2=EPS, op0=ALU.mult, op1=ALU.add), ['ss'], ['ss'])
            act(lambda e: e.activation(out=sc['rs'][:], in_=sc['ss'][:], func=AF.Sqrt), ['ss'], ['rs'])
            dve(lambda e: e.reciprocal(out=sc['rs'][:], in_=sc['rs'][:]), ['rs'], ['rs'])
            dve(lambda e: e.tensor_tensor(out=v4(numt), in0=v4(numt), in1=bc('rs'), op=ALU.mult), ['numt', 'rs'], ['numt'])
            dve(lambda e: e.tensor_tensor(out=numt[:], in0=numt[:], in1=GH[:], op=ALU.mult), ['numt', 'GH'], ['numt'])
            act(lambda e: e.activation(out=otm[:], in_=otm[:], func=AF.Sigmoid), ['otm'], ['otm'])
            dve(lambda e: e.tensor_tensor(out=mlb[:], in0=numt[:], in1=otm[:], op=ALU.mult), ['numt', 'otm'], ['mlb'])
            self.transpose_into(mlb, 'mlb', B, 8, mlT, 'mlTs', 0)
            S.dma('sp', lambda e: e.dma_start(out=R['BR1'][:, HPR:HT].rearrange("(k p) t -> p k t", p=128), in_=mlT[:]),
                  reads=['mlTs'], writes=[('dram', 'brt1s')])
            S.barrier()
            S.flush()

    def psumc_next(self):
        i = 4 + (self.pc_i % 2)
        self.pc_i += 1
        return self.PA[i], ('pa', i)

    def final_norm(self, xsrc, ydst, gvec, nrows=NTOK):
        S = self.S
        with ExitStack() as es:
            gb = self.sb(es, [128, D], F32, "gbf")
            xts = [self.sb(es, [128, D], F32, "xtf") for _ in range(2)]
            ys = [self.sb(es, [128, D], F32, "yf") for _ in range(2)]
            st = self.sb(es, [128, 8], F32, "stf")
            self.load_gain(gb, gvec, 'gbf')
            tiles = token_tiles(0, nrows)

            def fn_load(i):
                r0, nr = tiles[i]
                xt = xts[i % 2]
                xk = ('xtf', i % 2)
                S.dma('sp', lambda e: e.dma_start(out=xt[:nr, :], in_=xsrc[r0:r0 + nr, :]), writes=[xk])
            fn_load(0)
            for i, (r0, nr) in enumerate(tiles):
                if i + 1 < len(tiles):
                    fn_load(i + 1)
                xt = xts[i % 2]
                y = ys[i % 2]
                xk = ('xtf', i % 2)
                yk = ('yf', i % 2)
                S.op('act', lambda e, y=y, xt=xt, nr=nr: e.activation(out=y[:nr, :], in_=xt[:nr, :], func=AF.Square,
                                                                     accum_out=st[:nr, 0:1]), reads=[xk], writes=[yk, 'stf'])
                S.op('dve', lambda e, nr=nr: e.tensor_scalar(out=st[:nr, 1:2], in0=st[:nr, 0:1], scalar1=1.0 / D, scalar2=EPS,
                                                             op0=ALU.mult, op1=ALU.add), reads=['stf'], writes=['stf'])
                S.op('act', lambda e, nr=nr: e.activation(out=st[:nr, 2:3], in_=st[:nr, 1:2], func=AF.Sqrt), reads=['stf'], writes=['stf'])
                S.op('dve', lambda e, nr=nr: e.reciprocal(out=st[:nr, 3:4], in_=st[:nr, 2:3]), reads=['stf'], writes=['stf'])
                S.op('dve', lambda e, y=y, xt=xt, nr=nr: e.scalar_tensor_tensor(out=y[:nr, :], in0=xt[:nr, :], scalar=st[:nr, 3:4],
                                                                               in1=gb[:nr, :], op0=ALU.mult, op1=ALU.mult),
                     reads=[xk, 'stf', 'gbf'], writes=[yk])
                S.dma('sp', lambda e, y=y, r0=r0, nr=nr: e.dma_start(out=ydst[r0:r0 + nr, :], in_=y[:nr, :]), reads=[yk],
                      writes=[('dram', 'y', r0)])
            S.barrier()
            S.flush()

    def build(self):
        nc = self.nc
        dbg = self.dbg
        self.I = I = {}
        I['x'] = self.dram_in('x', [NTOK, D])
        I['mem'] = self.dram_in('mem', [256, D])
        I['cache_k'] = self.dram_in('cache_k', [NSM, 256, 1024])
        I['cache_v'] = self.dram_in('cache_v', [NSM, 256, 1024])
        I['s5r'] = self.dram_in('s5r', [NSM, 4096])
        I['s5i'] = self.dram_in('s5i', [NSM, 4096])
        I['mC'] = self.dram_in('mC', [NSM, 4, 256, 256])
        I['mN'] = self.dram_in('mN', [NSM, 1024])
        I['mM'] = self.dram_in('mM', [NSM, 4])
        I['flag'] = self.dram_in('flag', [128, 2])
        for nm, shp in [('g_ffn1', [D]), ('w1_gate', [D, FF]), ('w1_up', [D, FF]), ('w1_down', [FF, D]),
                        ('g_mix', [D]), ('w_in', [D, DIN]),
                        ('s5_lambda_re', [64, 64]), ('s5_lambda_im', [64, 64]), ('s5_log_step', [64]),
                        ('s5_b_re', [64, 64, 16]), ('s5_b_im', [64, 64, 16]), ('s5_c_re', [64, 16, 64]),
                        ('s5_c_im', [64, 16, 64]), ('s5_d', [64, 16]), ('w_s5_glu', [1024, 1024]),
                        ('b_igate', [4]), ('b_fgate', [4]), ('g_mlstm_head', [1024]), ('g_mem', [D]),
                        ('w_mem_k', [D, 1024]), ('w_mem_v', [D, 1024]),
                        ('w_br_s5', [1024, D]), ('w_br_ml', [1024, D]), ('w_br_xa', [1024, D]), ('w_out', [D, D]),
                        ('g_ffn2', [D]), ('w2_gate', [D, FF]), ('w2_up', [D, FF]), ('w2_down', [FF, D]),
                        ('g_final', [D])]:
            I[nm] = self.dram_in(nm, shp)
        self.O = O = {}
        O['y'] = self.dram_out('y', [HT, D])
        O['mk'] = self.dram_out('mk', [256, 1024])
        O['mv'] = self.dram_out('mv', [256, 1024])
        O['s5r_p'] = self.dram_out('s5r_p', [64, 64])
        O['s5i_p'] = self.dram_out('s5i_p', [64, 64])
        O['C_p'] = self.dram_out('C_p', [4, 256, 256])
        O['n_p'] = self.dram_out('n_p', [4, 256])
        O['m_p'] = self.dram_out('m_p', [4, 1])
        O['s5r_s'] = self.dram_out('s5r_s', [NSM, 4096])
        O['s5i_s'] = self.dram_out('s5i_s', [NSM, 4096])
        O['C_s'] = self.dram_out('C_s', [NSM, 4, 256, 256])
        O['n_s'] = self.dram_out('n_s', [NSM, 1024])
        O['m_s'] = self.dram_out('m_s', [NSM, 4])
        scr = self.dram_out if dbg else (lambda n, s, d=F32: self.dram_scr(n, s, d))
        self.R = R = {}
        R['X1'] = scr('X1', [NTOK, D], F32)
        R['X1h'] = scr('X1h', [HT, D], F32)
        R['X2'] = scr('X2', [HT, D], F32)
        R['X3'] = scr('X3', [HT, D], F32)
        pscr = self.dram_in if self.mixtest else scr
        for nm in ['UT', 'KT']:
            R[nm] = pscr(nm, [1024, NTOK], BF16)
        for nm in ['QTo', 'QXTo']:
            R[nm] = pscr(nm, [1024, HT], BF16)
        R['OTMo'] = pscr('OTMo', [HT, 1024], F32)
        for nm in ['KTM', 'VTM']:
            R[nm] = pscr(nm, [NTOK, 1024], BF16)
        R['GTM'] = pscr('GTM', [NTOK, 8], F32)
        R['IGR'] = pscr('IGR', [4, NTOK], F32)
        R['FGR'] = pscr('FGR', [4, NTOK], F32)
        R['MKT'] = pscr('MKT', [1024, 256], BF16)
        R['MVB'] = self.dram_in('MVB', [256, 1024]) if self.mixtest else O['mv']
        R['BRT'] = scr('BRT', [3, 1024, NTOK], BF16)
        R['XAS'] = scr('XAS', [NSM, 1024], F32)
        R['S5G'] = scr('S5G', [1024, HT], BF16)
        R['BR0'] = scr('BR0', [1024, HT], BF16)
        R['BR2'] = scr('BR2', [1024, HT], BF16)
        R['BR1'] = scr('BR1', [1024, HT], BF16)
        with ExitStack() as es:
            self.S = S = Sched(nc, es)
            S.block = es.enter_context(nc.Block())
            self.PA = [es.enter_context(nc.psum_tensor("pa%d" % i, [128, 512], F32)) for i in range(6)]
            self.PB = [es.enter_context(nc.psum_tensor("pb%d" % i, [128, 1024], BF16)) for i in range(2)]
            self.pa_i = 0
            self.pb_i = 0
            self.ecnt = 0
            self.pc_i = 0
            identf = self.sb(es, [128, 128], F32, "identf")
            self.identb = self.sb(es, [128, 128], BF16, "identb")
            self.identf = identf
            S.op('pool', lambda e: e.memset(identf[:], 1.0), writes=['identf'])
            S.op('pool', lambda e: e.affine_select(out=identf[:], in_=identf[:], pattern=[[-1, 128]], compare_op=ALU.is_equal,
                                                   fill=0.0, base=0, channel_multiplier=1), reads=['identf'], writes=['identf'])
            S.op('dve', lambda e: e.tensor_copy(out=self.identb[:], in_=identf[:]), reads=['identf'], writes=['identb'])
            S.barrier()
            st = self.stage
            if not self.mixtest:
                for blk in BLOCKS:
                    self.ffn(es, I['x'], R['X1'], I['g_ffn1'], I['w1_gate'], I['w1_up'], I['w1_down'], blk)
            if st >= 2 and not self.mixtest:
                self.inproj()
                self.memkv()
            if st >= 3:
                self.mixers()
            if st >= 4:
                self.merge()
                self.ffn(es, R['X2'], R['X3'], I['g_ffn2'], I['w2_gate'], I['w2_up'], I['w2_down'], (0, HT))
                self.final_norm(R['X3'], O['y'], I['g_final'], nrows=HT)
            S.barrier()
            S.flush()
        return nc


_NC_CACHE = {}


def _get_nc():
    if 'nc' not in _NC_CACHE:
        _NC_CACHE['nc'] = Builder(stage=99, dbg=False).build()
    return _NC_CACHE['nc']


_WEIGHTS = ['g_ffn1', 'w1_gate', 'w1_up', 'w1_down', 'g_mix', 'w_in', 's5_lambda_re', 's5_lambda_im', 's5_log_step',
            's5_b_re', 's5_b_im', 's5_c_re', 's5_c_im', 's5_d', 'w_s5_glu', 'b_igate', 'b_fgate', 'g_mlstm_head', 'g_mem',
            'w_mem_k', 'w_mem_v', 'w_br_s5', 'w_br_ml', 'w_br_xa', 'w_out', 'g_ffn2', 'w2_gate', 'w2_up', 'w2_down']


def kernel(**inputs):
    f = lambda a: np.ascontiguousarray(np.asarray(a, dtype=np.float32))
    nc = _get_nc()
    shared = {nm: f(inputs[nm])[0] for nm in _WEIGHTS}
    shared['g_final'] = f(inputs['g_final'])
    xp = f(inputs['x_prompt'])
    xs = f(inputs['x_sample'])
    memp = f(inputs['mem_prompt'])
    ck = f(inputs['cache_mem_k'])[0]
    cv = f(inputs['cache_mem_v'])[0]
    s5r = f(inputs['state_s5_re'])[0]
    s5i = f(inputs['state_s5_im'])[0]
    mC = f(inputs['state_mlstm_C'])[0]
    mN = f(inputs['state_mlstm_n'])[0]
    mM = f(inputs['state_mlstm_m'])[0]
    in_maps = []
    for i in range(8):
        j = i % 4
        s0 = j * 2 * NSM + (i // 4) * NSM
        sl = slice(s0, s0 + NSM)
        m = dict(shared)
        m['x'] = np.ascontiguousarray(np.concatenate([xp[j], xs[sl, 0, :]], axis=0))
        m['mem'] = memp[j]
        m['cache_k'] = np.ascontiguousarray(ck[sl].reshape(NSM, 256, 1024))
        m['cache_v'] = np.ascontiguousarray(cv[sl].reshape(NSM, 256, 1024))
        m['s5r'] = np.ascontiguousarray(s5r[sl].reshape(NSM, 4096))
        m['s5i'] = np.ascontiguousarray(s5i[sl].reshape(NSM, 4096))
        m['mC'] = np.ascontiguousarray(mC[sl])
        m['mN'] = np.ascontiguousarray(mN[sl].reshape(NSM, 1024))
        m['mM'] = np.ascontiguousarray(mM[sl])
        fl = np.zeros((128, 2), np.float32)
        fl[:, 0 if i < 4 else 1] = 1.0
        m['flag'] = fl
        in_maps.append(m)
    res = run_bass_kernel_spmd(nc, in_maps, core_ids=list(range(8)))
    r = res.results
    g = lambda nm, j: np.asarray(r[j][nm], dtype=np.float32)
    order = [c for j in range(4) for c in (j, j + 4)]
    y_prompt = np.stack([np.concatenate([g('y', j)[:HPR], g('y', j + 4)[:HPR]], axis=0) for j in range(4)])
    y_sample = np.concatenate([g('y', c)[HPR:HT] for c in order], axis=0).reshape(8 * NSM, 1, D)
    mk = np.stack([g('mk', j).reshape(256, 4, 256) for j in range(4)])[None]
    mv = np.stack([g('mv', j).reshape(256, 4, 256) for j in range(4)])[None]
    s5r_p = np.stack([g('s5r_p', j + 4) for j in range(4)])[None]
    s5i_p = np.stack([g('s5i_p', j + 4) for j in range(4)])[None]
    C_p = np.stack([g('C_p', j + 4) for j in range(4)])[None]
    n_p = np.stack([g('n_p', j + 4) for j in range(4)])[None]
    m_p = np.stack([g('m_p', j)[:, 0] for j in range(4)])[None]
    s5r_s = np.concatenate([g('s5r_s', c).reshape(NSM, 64, 64) for c in order], axis=0)[None]
    s5i_s = np.concatenate([g('s5i_s', c).reshape(NSM, 64, 64) for c in order], axis=0)[None]
    C_s = np.concatenate([g('C_s', c) for c in order], axis=0)[None]
    n_s = np.concatenate([g('n_s', c).reshape(NSM, 4, 256) for c in order], axis=0)[None]
    m_s = np.concatenate([g('m_s', c) for c in order], axis=0)[None]
    return (y_prompt, y_sample, mk, mv, s5r_p, s5i_p, C_p, n_p, m_p, s5r_s, s5i_s, C_s, n_s, m_s)
```

```python
import math
import numpy as np
import concourse.bass as bass
import concourse.mybir as mybir
from concourse.bass_utils import run_bass_kernel_spmd
from contextlib import ExitStack

F32 = mybir.dt.float32
BF16 = mybir.dt.bfloat16
I32 = mybir.dt.int32
AF = mybir.ActivationFunctionType
ALU = mybir.AluOpType
AX = mybir.AxisListType

D = 2048
FF = 5504
NPR = 2048
NSM = 16
NTOK = NPR + NSM
BLOCKS = [(0, 1024), (1024, NTOK)]
HPR = NPR // 2
HT = HPR + NSM
EPS = 1e-6
NDS = 24
POOL_RING_LIMIT = 700
DIN = 12296


class Sched:
    ENG = ['pe', 'dve', 'act', 'pool', 'sp']

    def __init__(self, nc, es):
        self.nc = nc
        self.prog = {e: [] for e in self.ENG}
        self.sem = {e: es.enter_context(nc.semaphore("s_" + e)) for e in self.ENG}
        self.cnt = {e: 0 for e in self.ENG}
        self.seen = {e: {} for e in self.ENG}
        self.dsem = [es.enter_context(nc.semaphore("d%d" % i)) for i in range(NDS)]
        self.dcount = [0] * NDS
        self.dnext = 0
        self.last_w = {}
        self.readers = {}
        self.pool_pending = []

    def _need(self, eng, tok):
        kind, ident, v = tok
        if v <= 0:
            return False
        if kind == 'e' and ident == eng and eng in ('pe', 'sp'):
            return False
        if self.seen[eng].get((kind, ident), 0) >= v:
            return False
        return True

    def _wait(self, eng, kind, ident, v):
        if self._need(eng, (kind, ident, v)):
            self.prog[eng].append(('wait', (kind, ident, v)))
            self.seen[eng][(kind, ident)] = v

    def _deps(self, eng, reads, writes, extra=()):
        deps = {}

        def add(tok):
            key = (tok[0], tok[1])
            if deps.get(key, 0) < tok[2]:
                deps[key] = tok[2]
        for k in reads:
            if k in self.last_w:
                add(self.last_w[k])
        for k in writes:
            if k in self.last_w:
                add(self.last_w[k])
            for t in self.readers.get(k, ()):
                add(t)
        for t in extra:
            add(t)
        for (kind, ident), v in deps.items():
            self._wait(eng, kind, ident, v)

    def _commit(self, tok, reads, writes):
        for k in writes:
            self.last_w[k] = tok
            self.readers[k] = []
        for k in reads:
            self.readers.setdefault(k, []).append(tok)

    def op(self, eng, fn, reads=(), writes=()):
        self._deps(eng, reads, writes)
        self.cnt[eng] += 1
        tok = ('e', eng, self.cnt[eng])
        self.prog[eng].append(('op', fn))
        self._commit(tok, reads, writes)
        return tok

    def dma(self, q, fn, reads=(), writes=(), ndesc=0):
        i = self.dnext
        self.dnext = (self.dnext + 1) % NDS
        prev = ('d', i, self.dcount[i])
        extra = [prev]
        if q == 'pool':
            pend = self.pool_pending
            while pend and sum(n for _, n in pend) + ndesc > POOL_RING_LIMIT:
                extra.append(pend.pop(0)[0])
        self._deps(q, reads, writes, extra=tuple(extra))
        self.dcount[i] += 16
        tok = ('d', i, self.dcount[i])
        self.prog[q].append(('dma', fn, i))
        self._commit(tok, reads, writes)
        if q == 'pool':
            self.pool_pending.append((tok, ndesc))
        return tok

    def barrier(self):
        for e in self.ENG:
            for e2 in self.ENG:
                if e2 != e:
                    self._wait(e, 'e', e2, self.cnt[e2])
            for i in range(NDS):
                self._wait(e, 'd', i, self.dcount[i])
        self.last_w = {}
        self.readers = {}

    def flush(self):
        block = self.block
        handles = {'pe': block.tensor, 'dve': block.vector, 'act': block.scalar,
                   'pool': block.gpsimd, 'sp': block.sync}
        for e in self.ENG:
            prog = self.prog[e]
            self.prog[e] = []
            if not prog:
                continue

            def body(eng, prog=prog, e=e):
                for item in prog:
                    if item[0] == 'wait':
                        kind, ident, v = item[1]
                        s = self.sem[ident] if kind == 'e' else self.dsem[ident]
                        eng.wait_ge(s, v)
                    elif item[0] == 'op':
                        item[1](eng).then_inc(self.sem[e], 1)
                    else:
                        item[1](eng).then_inc(self.dsem[item[2]], 16)
            handles[e](body)


def token_tiles(t0, t1):
    out = []
    r = t0
    while r < t1:
        nr = min(128, t1 - r)
        out.append((r, nr))
        r += nr
    return out


def nchunks(n, step=512):
    out = []
    c = 0
    while c < n:
        out.append((c, min(step, n - c)))
        c += step
    return out


class Builder:
    def __init__(self, stage=99, dbg=False, which=('xp', 'xs', 's5', 'mp', 'ms'), mixtest=False):
        self.which = which
        self.mixtest = mixtest
        self.stage = stage
        self.dbg = dbg
        self.nc = bass.Bass("TRN2", target_bir_lowering=False)
        self.uid = 0

    def dram_in(self, name, shape, dt=F32):
        return self.nc.dram_tensor(name, list(shape), dt, kind="ExternalInput").ap()

    def dram_out(self, name, shape, dt=F32):
        return self.nc.dram_tensor(name, list(shape), dt, kind="ExternalOutput").ap()

    def dram_scr(self, name, shape, dt):
        return self.nc.dram_tensor(name, list(shape), dt, kind="Internal").ap()

    def sb(self, es, shape, dt, name=None):
        self.uid += 1
        return es.enter_context(self.nc.sbuf_tensor("%s_%d" % (name or "t", self.uid), list(shape), dt))

    def psum_next(self):
        i = self.pa_i
        self.pa_i = (self.pa_i + 1) % len(self.PA)
        return self.PA[i], ('pa', i)

    def psumb_next(self):
        i = self.pb_i
        self.pb_i = (self.pb_i + 1) % len(self.PB)
        return self.PB[i], ('pb', i)

    def load_gain(self, gb, gvec, key):
        S = self.S
        S.dma('sp', lambda e: e.dma_start(out=gb[:], in_=gvec.partition_broadcast(128)), writes=[key])

    def norm_T(self, src_rows, nr, gb, gkey, dstT, dkey, c0, bufs):
        S = self.S
        xts, xn, st = bufs
        if not isinstance(xts, (list, tuple)):
            xts = [xts]
        self.nt_i = getattr(self, 'nt_i', 0) + 1
        xt = xts[self.nt_i % len(xts)]
        xtk = ('xt', self.nt_i % len(xts))
        S.dma('sp', lambda e: e.dma_start(out=xt[:nr, :], in_=src_rows), writes=[xtk])
        S.op('act', lambda e: e.activation(out=xn[:nr, :], in_=xt[:nr, :], func=AF.Square, accum_out=st[:nr, 0:1]),
             reads=[xtk], writes=['xn', 'st'])
        S.op('dve', lambda e: e.tensor_scalar(out=st[:nr, 1:2], in0=st[:nr, 0:1], scalar1=1.0 / D, scalar2=EPS,
                                              op0=ALU.mult, op1=ALU.add), reads=['st'], writes=['st'])
        S.op('act', lambda e: e.activation(out=st[:nr, 2:3], in_=st[:nr, 1:2], func=AF.Sqrt), reads=['st'], writes=['st'])
        S.op('dve', lambda e: e.reciprocal(out=st[:nr, 3:4], in_=st[:nr, 2:3]), reads=['st'], writes=['st'])
        S.op('dve', lambda e: e.scalar_tensor_tensor(out=xn[:nr, :], in0=xt[:nr, :], scalar=st[:nr, 3:4], in1=gb[:nr, :],
                                                     op0=ALU.mult, op1=ALU.mult),
             reads=[xtk, 'st', gkey], writes=['xn'])
        self.transpose_into(xn, 'xn', nr, 16, dstT, dkey, c0)

    def transpose_into(self, src, skey, nr, nk, dstT, dkey, c0):
        S = self.S
        for kq in range(0, nk, 4):
            pb, pk = self.psumb_next()
            n4 = min(4, nk - kq)
            for j in range(n4):
                k = kq + j
                S.op('pe', lambda e, j=j, k=k, pb=pb: e.transpose(out=pb[:, j * 128:j * 128 + nr], in_=src[:nr, k * 128:(k + 1) * 128],
                                                                  identity=self.identb[:nr, :nr]),
                     reads=[skey, 'identb'], writes=[pk])
            eng = 'act' if (kq // 4) % 2 == 0 else 'dve'
            pv = pb[:, 0:n4 * 128].rearrange("p (j t) -> p j t", j=n4)[:, :, 0:nr]
            if eng == 'act':
                S.op('act', lambda e, pv=pv, kq=kq, n4=n4: e.copy(out=dstT[:, kq:kq + n4, c0:c0 + nr], in_=pv),
                     reads=[pk], writes=[dkey])
            else:
                S.op('dve', lambda e, pv=pv, kq=kq, n4=n4: e.tensor_copy(out=dstT[:, kq:kq + n4, c0:c0 + nr], in_=pv),
                     reads=[pk], writes=[dkey])

    def wload(self, wslice, kt, cw, fmt=None):
        S = self.S
        i = self.wb_i
        self.wb_i = (self.wb_i + 1) % len(self.WB)
        wb = self.WB[i]
        view = wb[:, 0:kt * cw].rearrange("p (k n) -> p k n", k=kt)
        S.dma('pool', lambda e: e.dma_start(out=view, in_=wslice.rearrange("(k p) n -> p k n", p=128)),
              writes=[('wb', i)], ndesc=8 * kt)
        return view, ('wb', i)

    def ffn(self, es0, xsrc, xdst, gvec, wg, wu, wd, blk):
        S = self.S
        t0, t1 = blk
        NT = t1 - t0
        with ExitStack() as es:
            xT = self.sb(es, [128, 16, NT], BF16, "xT")
            hT = self.sb(es, [128, 43, NT], BF16, "hT")
            self.WB = [self.sb(es, [128, 11008], BF16, "wb") for _ in range(2)]
            self.wb_i = 0
            xt = [self.sb(es, [128, D], F32, "xt") for _ in range(2)]
            xn = self.sb(es, [128, D], BF16, "xn")
            st = self.sb(es, [128, 8], F32, "st")
            gb = self.sb(es, [128, D], F32, "gb")
            sg = [self.sb(es, [128, 512], F32, "sg") for _ in range(2)]
            xres = [self.sb(es, [128, 256], F32, "xres") for _ in range(4)]
            ot = [self.sb(es, [128, 256], F32, "ot") for _ in range(2)]
            self.load_gain(gb, gvec, 'gb')
            for (r0, nr) in token_tiles(t0, t1):
                self.norm_T(xsrc[r0:r0 + nr, :], nr, gb, 'gb', xT, 'xT', r0 - t0, (xt, xn, st))
            ncs = nchunks(NT)
            cnt = 0
            c0 = 0
            while c0 < FF:
                cw = min(256, FF - c0)
                i = self.wb_i
                self.wb_i = (self.wb_i + 1) % 2
                wb = self.WB[i]
                wkey = ('wb', i)
                gv = wb[:, 0:16 * cw].rearrange("p (k n) -> p k n", k=16)
                uv = wb[:, 16 * cw:32 * cw].rearrange("p (k n) -> p k n", k=16)
                S.dma('pool', lambda e, gv=gv, c0=c0, cw=cw: e.dma_start(
                    out=gv, in_=wg[:, c0:c0 + cw].rearrange("(k p) n -> p k n", p=128)), writes=[wkey], ndesc=128)
                S.dma('pool', lambda e, uv=uv, c0=c0, cw=cw: e.dma_start(
                    out=uv, in_=wu[:, c0:c0 + cw].rearrange("(k p) n -> p k n", p=128)), writes=[wkey], ndesc=128)
                for m in range(cw // 128):
                    f = c0 // 128 + m
                    for (n0, nn) in ncs:
                        pg, pgk = self.psum_next()
                        pu, puk = self.psum_next()
                        for k in range(16):
                            S.op('pe', lambda e, pg=pg, gv=gv, k=k, m=m, n0=n0, nn=nn: e.matmul(
                                pg[:, 0:nn], lhsT=gv[:, k, m * 128:(m + 1) * 128], rhs=xT[:, k, n0:n0 + nn],
                                start=(k == 0), stop=(k == 15)), reads=[wkey, 'xT'], writes=[pgk])
                        for k in range(16):
                            S.op('pe', lambda e, pu=pu, uv=uv, k=k, m=m, n0=n0, nn=nn: e.matmul(
                                pu[:, 0:nn], lhsT=uv[:, k, m * 128:(m + 1) * 128], rhs=xT[:, k, n0:n0 + nn],
                                start=(k == 0), stop=(k == 15)), reads=[wkey, 'xT'], writes=[puk])
                        sgt = sg[cnt % 2]
                        sk = ('sg', cnt % 2)
                        cnt += 1
                        S.op('act', lambda e, sgt=sgt, pg=pg, nn=nn: e.activation(out=sgt[:, 0:nn], in_=pg[:, 0:nn], func=AF.Silu),
                             reads=[pgk], writes=[sk])
                        S.op('dve', lambda e, sgt=sgt, pu=pu, f=f, n0=n0, nn=nn: e.tensor_tensor(
                            out=hT[:, f, n0:n0 + nn], in0=sgt[:, 0:nn], in1=pu[:, 0:nn], op=ALU.mult),
                            reads=[sk, puk], writes=['hT'])
                c0 += cw
            self.proj_tm_res(hT, 'hT', 43, wd, xsrc, xdst, 0.5, blk, xres, ot)
            S.barrier()
            S.flush()

    def proj_tm_res(self, actT, akey, nk, w, xsrc, xdst, scale, blk, xres, ot, cw=256):
        S = self.S
        t0, t1 = blk
        items = [(c0, r0, nr) for c0 in range(0, D, cw) for (r0, nr) in token_tiles(t0, t1)]
        NX = len(xres)
        PF = NX - 1

        def ld(i):
            c0, r0, nr = items[i]
            xr = xres[i % NX]
            S.dma('sp', lambda e: e.dma_start(out=xr[:nr, 0:cw], in_=xsrc[r0:r0 + nr, c0:c0 + cw]), writes=[('xres', i % NX)])
        for i in range(min(PF, len(items))):
            ld(i)
        wv = wkey = None
        for i, (c0, r0, nr) in enumerate(items):
            if r0 == t0:
                wv, wkey = self.wload(w[:, c0:c0 + cw], nk, cw)
            if i + PF < len(items):
                ld(i + PF)
            rl = r0 - t0
            po, pok = self.psum_next()
            for f in range(nk):
                S.op('pe', lambda e, po=po, wv=wv, f=f, rl=rl, nr=nr: e.matmul(
                    po[:nr, 0:cw], lhsT=actT[:, f, rl:rl + nr], rhs=wv[:, f, :], start=(f == 0), stop=(f == nk - 1)),
                    reads=[wkey, akey], writes=[pok])
            xr = xres[i % NX]
            xk = ('xres', i % NX)
            o = ot[i % len(ot)]
            ok = ('ot', i % len(ot))
            S.op('dve', lambda e, o=o, po=po, xr=xr, nr=nr: e.scalar_tensor_tensor(
                out=o[:nr, 0:cw], in0=po[:nr, 0:cw], scalar=scale, in1=xr[:nr, 0:cw], op0=ALU.mult, op1=ALU.add),
                reads=[pok, xk], writes=[ok])
            S.dma('sp', lambda e, o=o, r0=r0, nr=nr, c0=c0: e.dma_start(out=xdst[r0:r0 + nr, c0:c0 + cw], in_=o[:nr, 0:cw]),
                  reads=[ok], writes=[('dram', 'xdst', r0, c0)])

    def evac(self, i, out_ap, in_ap, reads, writes):
        S = self.S
        if i % 2 == 0:
            S.op('act', lambda e: e.copy(out=out_ap, in_=in_ap), reads=reads, writes=writes)
        else:
            S.op('dve', lambda e: e.tensor_copy(out=out_ap, in_=in_ap), reads=reads, writes=writes)

    def proj_fm_store(self, actT, akey, nk, w, col0, ncols, dst, dcol0, NT, ebufs, ekey, cw=256):
        S = self.S
        ncs = nchunks(NT)
        for c0 in range(0, ncols, cw):
            wv, wkey = self.wload(w[:, col0 + c0:col0 + c0 + cw], nk, cw)
            for m in range(cw // 128):
                for (n0, nn) in ncs:
                    ps, pk = self.psum_next()
                    for k in range(nk):
                        S.op('pe', lambda e, ps=ps, wv=wv, k=k, m=m, n0=n0, nn=nn: e.matmul(
                            ps[:, 0:nn], lhsT=wv[:, k, m * 128:(m + 1) * 128], rhs=actT[:, k, n0:n0 + nn],
                            start=(k == 0), stop=(k == nk - 1)), reads=[wkey, akey], writes=[pk])
                    i = self.ecnt
                    self.ecnt += 1
                    eb = ebufs[i % len(ebufs)]
                    ek = (ekey, i % len(ebufs))
                    self.evac(i, eb[:, 0:nn], ps[:, 0:nn], [pk], [ek])
                    r = c0 + m * 128
                    S.dma('sp', lambda e, eb=eb, r=r, n0=n0, nn=nn: e.dma_start(
                        out=dst[r:r + 128, dcol0 + n0:dcol0 + n0 + nn], in_=eb[:, 0:nn]), reads=[ek],
                        writes=[('dram', id(dst), r, n0)])

    def proj_tm_store(self, actT, akey, nk, w, col0, ncols, dst, blk, ebufs, ekey, cw=512, dcol0=0):
        S = self.S
        t0, t1 = blk
        for c0 in range(0, ncols, cw):
            cww = min(cw, ncols - c0)
            wv, wkey = self.wload(w[:, col0 + c0:col0 + c0 + cww], nk, cww)
            for (r0, nr) in token_tiles(t0, t1):
                rl = r0 - t0
                ps, pk = self.psum_next()
                for k in range(nk):
                    S.op('pe', lambda e, ps=ps, wv=wv, k=k, rl=rl, nr=nr, cww=cww: e.matmul(
                        ps[:nr, 0:cww], lhsT=actT[:, k, rl:rl + nr], rhs=wv[:, k, :],
                        start=(k == 0), stop=(k == nk - 1)), reads=[wkey, akey], writes=[pk])
                i = self.ecnt
                self.ecnt += 1
                eb = ebufs[i % len(ebufs)]
                ek = (ekey, i % len(ebufs))
                self.evac(i, eb[:nr, 0:cww], ps[:nr, 0:cww], [pk], [ek])
                S.dma('sp', lambda e, eb=eb, r0=r0, nr=nr, c0=c0, cww=cww: e.dma_start(
                    out=dst[r0:r0 + nr, dcol0 + c0:dcol0 + c0 + cww], in_=eb[:nr, 0:cww]), reads=[ek],
                    writes=[('dram', id(dst), r0, c0)])

    def inproj(self):
        S = self.S
        I = self.I
        R = self.R
        w_in = I['w_in']
        with ExitStack() as es:
            hTs = [self.sb(es, [128, 16, b1 - b0], BF16, "hT%d" % i) for i, (b0, b1) in enumerate(BLOCKS)]
            self.WB = [self.sb(es, [128, 8192], BF16, "wb") for _ in range(3)]
            self.wb_i = 0
            xt = [self.sb(es, [128, D], F32, "xt") for _ in range(2)]
            xn = self.sb(es, [128, D], BF16, "xn")
            st = self.sb(es, [128, 8], F32, "st")
            gb = self.sb(es, [128, D], F32, "gb")
            fli = self.sb(es, [128, 2], F32, "fli")
            eb16 = [self.sb(es, [128, 512], BF16, "eb16") for _ in range(3)]
            eb32 = [self.sb(es, [128, 512], F32, "eb32") for _ in range(3)]
            self.load_gain(gb, I['g_mix'], 'gb')
            S.dma('sp', lambda e: e.dma_start(out=fli[:], in_=I['flag'][:, :]), writes=['fli'])
            for bi, (t0, t1) in enumerate(BLOCKS):
                for (r0, nr) in token_tiles(t0, t1):
                    self.norm_T(R['X1'][r0:r0 + nr, :], nr, gb, 'gb', hTs[bi], 'hT%d' % bi, r0 - t0, (xt, xn, st))
            for bi, blk in enumerate(BLOCKS):
                t0, t1 = blk
                NT = t1 - t0
                hT = hTs[bi]
                hk = 'hT%d' % bi
                for (col0, dst) in [(0, R['UT']), (2048, R['KT'])]:
                    self.proj_fm_store(hT, hk, 16, w_in, col0, 1024, dst, t0, NT, eb16, 'eb16')
                for (col0, dst) in [(2048, R['KTM']), (3072, R['VTM'])]:
                    self.proj_tm_store(hT, hk, 16, w_in, col0, 1024, dst, blk, eb16, 'eb16')
                self.proj_tm_store(hT, hk, 16, w_in, 5120, 8, R['GTM'], blk, eb32, 'eb32', cw=8)
                wv, wkey = self.wload(w_in[:, 5120:5128], 16, 8)
                for gi, dst in [(0, R['IGR']), (1, R['FGR'])]:
                    for (n0, nn) in nchunks(NT):
                        ps, pk = self.psum_next()
                        for k in range(16):
                            S.op('pe', lambda e, ps=ps, wv=wv, k=k, gi=gi, n0=n0, nn=nn, hT=hT: e.matmul(
                                ps[0:4, 0:nn], lhsT=wv[:, k, gi * 4:gi * 4 + 4], rhs=hT[:, k, n0:n0 + nn],
                                start=(k == 0), stop=(k == 15)), reads=[wkey, hk], writes=[pk])
                        i = self.ecnt
                        self.ecnt += 1
                        eb = eb32[i % 3]
                        ek = ('eb32', i % 3)
                        self.evac(i, eb[0:4, 0:nn], ps[0:4, 0:nn], [pk], [ek])
                        S.dma('sp', lambda e, eb=eb, dst=dst, n0=n0, nn=nn, t0=t0: e.dma_start(
                            out=dst[0:4, t0 + n0:t0 + n0 + nn], in_=eb[0:4, 0:nn]), reads=[ek], writes=[('dram', id(dst), t0, n0)])
            h0, h1 = hTs
            S.op('dve', lambda e: e.tensor_scalar(out=h0[:], in0=h0[:], scalar1=fli[:, 0:1], scalar2=None, op0=ALU.mult), reads=['hT0', 'fli'], writes=['hT0'])
            S.op('dve', lambda e: e.scalar_tensor_tensor(out=h0[:], in0=h1[:, :, 0:HPR], scalar=fli[:, 1:2], in1=h0[:], op0=ALU.mult, op1=ALU.add),
                 reads=['hT0', 'hT1', 'fli'], writes=['hT0'])
            segs = [(h0, 'hT0', n0, nn, n0) for (n0, nn) in nchunks(HPR)] + [(h1, 'hT1', HPR, NSM, HPR)]
            for (col0, dst) in [(1024, R['QTo']), (5128, R['QXTo'])]:
                for c0 in range(0, 1024, 256):
                    wv, wkey = self.wload(w_in[:, col0 + c0:col0 + c0 + 256], 16, 256)
                    for m in range(2):
                        for (act_, akey, a0, nn, dcol) in segs:
                            ps, pk = self.psum_next()
                            for k in range(16):
                                S.op('pe', lambda e, ps=ps, wv=wv, k=k, m=m, act_=act_, a0=a0, nn=nn: e.matmul(
                                    ps[:, 0:nn], lhsT=wv[:, k, m * 128:(m + 1) * 128], rhs=act_[:, k, a0:a0 + nn],
                                    start=(k == 0), stop=(k == 15)), reads=[wkey, akey], writes=[pk])
                            i = self.ecnt
                            self.ecnt += 1
                            eb = eb16[i % 3]
                            ek = ('eb16', i % 3)
                            self.evac(i, eb[:, 0:nn], ps[:, 0:nn], [pk], [ek])
                            r = c0 + m * 128
                            S.dma('sp', lambda e, eb=eb, dst=dst, r=r, dcol=dcol, nn=nn: e.dma_start(
                                out=dst[r:r + 128, dcol:dcol + nn], in_=eb[:, 0:nn]), reads=[ek], writes=[('dram', id(dst), r, dcol)])
            tls = [(h0, 'hT0', i * 128, 128, i * 128) for i in range(HPR // 128)] + [(h1, 'hT1', HPR, NSM, HPR)]
            for c0 in range(0, 1024, 512):
                wv, wkey = self.wload(w_in[:, 4096 + c0:4096 + c0 + 512], 16, 512)
                for (act_, akey, rl, nr, row0) in tls:
                    ps, pk = self.psum_next()
                    for k in range(16):
                        S.op('pe', lambda e, ps=ps, wv=wv, k=k, act_=act_, rl=rl, nr=nr: e.matmul(
                            ps[:nr, 0:512], lhsT=act_[:, k, rl:rl + nr], rhs=wv[:, k, :], start=(k == 0), stop=(k == 15)),
                            reads=[wkey, akey], writes=[pk])
                    i = self.ecnt
                    self.ecnt += 1
                    eb = eb32[i % 3]
                    ek = ('eb32', i % 3)
                    self.evac(i, eb[:nr, 0:512], ps[:nr, 0:512], [pk], [ek])
                    S.dma('sp', lambda e, eb=eb, row0=row0, nr=nr, c0=c0: e.dma_start(
                        out=R['OTMo'][row0:row0 + nr, c0:c0 + 512], in_=eb[:nr, 0:512]), reads=[ek], writes=[('dram', 'otmo', row0, c0)])
            S.barrier()
            S.flush()

    def memkv(self):
        S = self.S
        I = self.I
        R = self.R
        O = self.O
        with ExitStack() as es:
            mT = self.sb(es, [128, 16, 256], BF16, "mT")
            self.WB = [self.sb(es, [128, 8192], BF16, "wb") for _ in range(3)]
            self.wb_i = 0
            xt = self.sb(es, [128, D], F32, "xt")
            xn = self.sb(es, [128, D], BF16, "xn")
            st = self.sb(es, [128, 8], F32, "st")
            gb = self.sb(es, [128, D], F32, "gb")
            eb16 = [self.sb(es, [128, 512], BF16, "eb16") for _ in range(3)]
            eb32 = [self.sb(es, [128, 512], F32, "eb32") for _ in range(3)]
            self.load_gain(gb, I['g_mem'], 'gb')
            for (r0, nr) in token_tiles(0, 256):
                self.norm_T(I['mem'][r0:r0 + nr, :], nr, gb, 'gb', mT, 'mT', r0, (xt, xn, st))
            self.proj_tm_store(mT, 'mT', 16, I['w_mem_k'], 0, 1024, O['mk'], (0, 256), eb32, 'eb32')
            self.proj_tm_store(mT, 'mT', 16, I['w_mem_v'], 0, 1024, O['mv'], (0, 256), eb32, 'eb32')
            self.proj_fm_store(mT, 'mT', 16, I['w_mem_k'], 0, 1024, R['MKT'], 0, 256, eb16, 'eb16')
            S.barrier()
            S.flush()

    def merge(self):
        S = self.S
        I = self.I
        R = self.R
        blk = (0, HT)
        t0, t1 = blk
        NT = t1 - t0
        w_in = I['w_in']
        with ExitStack() as es:
            hT = self.sb(es, [128, 16, NT], BF16, "hT")
            mT = self.sb(es, [128, 16, NT], BF16, "mT")
            brT = [self.sb(es, [128, 8, NT], BF16, "brT") for _ in range(3)]
            self.WB = [self.sb(es, [128, 2048], BF16, "wb") for _ in range(9)]
            self.wb_i = 0
            xt = self.sb(es, [128, D], F32, "xt")
            xn = self.sb(es, [128, D], BF16, "xn")
            st = self.sb(es, [128, 8], F32, "st")
            gb = self.sb(es, [128, D], F32, "gb")
            gs4 = self.sb(es, [128, D], F32, "gs4")
            gs = [gs4[:, i * 512:(i + 1) * 512] for i in range(2)]
            tmp = [gs4[:, (2 + i) * 512:(3 + i) * 512] for i in range(2)]
            acc = [self.sb(es, [128, 512], F32, "acc") for _ in range(2)]
            xres = [self.sb(es, [128, 256], F32, "xres") for _ in range(4)]
            ot = [self.sb(es, [128, 256], F32, "ot") for _ in range(2)]
            self.load_gain(gb, I['g_mix'], 'gb')
            fl = self.sb(es, [128, 2], F32, "fl")
            S.dma('sp', lambda e: e.dma_start(out=fl[:], in_=I['flag'][:, :]), writes=['fl'])
            for (r0, nr) in token_tiles(0, HPR):
                S.dma('sp', lambda e, r0=r0, nr=nr: e.dma_start(out=xt[:nr, :], in_=R['X1'][r0:r0 + nr, :]), writes=['xt'])
                S.dma('sp', lambda e, r0=r0, nr=nr: e.dma_start(out=gs4[:nr, :], in_=R['X1'][HPR + r0:HPR + r0 + nr, :]), writes=['gs4'])
                S.op('dve', lambda e, nr=nr: e.tensor_scalar(out=xt[:nr, :], in0=xt[:nr, :], scalar1=fl[:nr, 0:1], scalar2=None, op0=ALU.mult),
                     reads=['xt', 'fl'], writes=['xt'])
                S.op('dve', lambda e, nr=nr: e.scalar_tensor_tensor(out=xt[:nr, :], in0=gs4[:nr, :], scalar=fl[:nr, 1:2], in1=xt[:nr, :],
                                                                   op0=ALU.mult, op1=ALU.add), reads=['xt', 'gs4', 'fl'], writes=['xt'])
                S.dma('sp', lambda e, r0=r0, nr=nr: e.dma_start(out=R['X1h'][r0:r0 + nr, :], in_=xt[:nr, :]), reads=['xt'],
                      writes=[('dram', 'x1h', r0)])
            S.dma('sp', lambda e: e.dma_start(out=xt[:NSM, :], in_=R['X1'][NPR:NTOK, :]), writes=['xt'])
            S.dma('sp', lambda e: e.dma_start(out=R['X1h'][HPR:HT, :], in_=xt[:NSM, :]), reads=['xt'], writes=[('dram', 'x1h', HPR)])
            S.barrier()
            for b, nm in enumerate(['BR0', 'BR1', 'BR2']):
                S.dma('sp', lambda e, b=b, nm=nm: e.dma_start(out=brT[b][:], in_=R[nm].rearrange("(k p) t -> p k t", p=128)), writes=[('brT', b)])
            for (r0, nr) in token_tiles(t0, t1):
                self.norm_T(R['X1h'][r0:r0 + nr, :], nr, gb, 'gb', hT, 'hT', r0 - t0, (xt, xn, st))
            wbr = [I['w_br_s5'], I['w_br_ml'], I['w_br_xa']]
            ncs = nchunks(NT)
            cnt = 0
            for j in range(16):
                wg = []
                for b in range(3):
                    gv, gk = self.wload(w_in[:, 6152 + b * 2048 + j * 128:6152 + b * 2048 + (j + 1) * 128], 16, 128)
                    wg.append((gv, gk))
                wb_ = []
                for b in range(3):
                    bv, bk = self.wload(wbr[b][:, j * 128:(j + 1) * 128], 8, 128)
                    wb_.append((bv, bk))
                for (n0, nn) in ncs:
                    a = acc[cnt % 2]
                    ak = ('acc', cnt % 2)
                    cnt += 1
                    for b in range(3):
                        gv, gk = wg[b]
                        bv, bk = wb_[b]
                        pg, pgk = self.psum_next()
                        pb, pbk = self.psum_next()
                        for k in range(16):
                            S.op('pe', lambda e, pg=pg, gv=gv, k=k, n0=n0, nn=nn: e.matmul(
                                pg[:, 0:nn], lhsT=gv[:, k, :], rhs=hT[:, k, n0:n0 + nn], start=(k == 0), stop=(k == 15)),
                                reads=[gk, 'hT'], writes=[pgk])
                        for k in range(8):
                            S.op('pe', lambda e, pb=pb, bv=bv, k=k, b=b, n0=n0, nn=nn: e.matmul(
                                pb[:, 0:nn], lhsT=bv[:, k, :], rhs=brT[b][:, k, n0:n0 + nn], start=(k == 0), stop=(k == 7)),
                                reads=[bk, ('brT', b)], writes=[pbk])
                        g = gs[b % 2]
                        gsk = ('gs', b % 2)
                        S.op('act', lambda e, g=g, pg=pg, nn=nn: e.activation(out=g[:, 0:nn], in_=pg[:, 0:nn], func=AF.Sigmoid),
                             reads=[pgk], writes=[gsk])
                        if b == 0:
                            S.op('dve', lambda e, a=a, g=g, pb=pb, nn=nn: e.tensor_tensor(out=a[:, 0:nn], in0=g[:, 0:nn], in1=pb[:, 0:nn],
                                                                                         op=ALU.mult), reads=[gsk, pbk], writes=[ak])
                        else:
                            t = tmp[b % 2]
                            tk = ('tmp', b % 2)
                            S.op('dve', lambda e, t=t, g=g, pb=pb, nn=nn: e.tensor_tensor(out=t[:, 0:nn], in0=g[:, 0:nn], in1=pb[:, 0:nn],
                                                                                         op=ALU.mult), reads=[gsk, pbk], writes=[tk])
                            if b == 1:
                                S.op('dve', lambda e, a=a, t=t, nn=nn: e.tensor_tensor(out=a[:, 0:nn], in0=a[:, 0:nn], in1=t[:, 0:nn],
                                                                                       op=ALU.add), reads=[ak, tk], writes=[ak])
                            else:
                                S.op('dve', lambda e, a=a, t=t, j=j, n0=n0, nn=nn: e.tensor_tensor(
                                    out=mT[:, j, n0:n0 + nn], in0=a[:, 0:nn], in1=t[:, 0:nn], op=ALU.add),
                                    reads=[ak, tk], writes=['mT'])
            self.proj_tm_res(mT, 'mT', 16, I['w_out'], R['X1h'], R['X2'], 1.0, blk, xres, ot, cw=128)
            S.barrier()
            S.flush()

    def mixers(self):
        which = self.which
        if 'xp' in which:
            self.xatt_prompt()
        if 'xs' in which:
            self.xatt_sample()
        if 's5' in which:
            self.s5()
        if 'mp' in which:
            self.mlstm_prompt()
        if 'ms' in which:
            self.mlstm_sample()

    def xatt_prompt(self):
        S = self.S
        R = self.R
        O = self.O
        with ExitStack() as es:
            qxT = self.sb(es, [128, 8, HPR], BF16, "qxT")
            mkT = self.sb(es, [128, 8, 256], BF16, "mkT")
            Vb = self.sb(es, [128, 2, 1024], BF16, "Vb")
            pT = [self.sb(es, [128, 2, HPR], BF16, "pT") for _ in range(2)]
            pe_ = [self.sb(es, [128, 256], F32, "pe") for _ in range(2)]
            pn = [self.sb(es, [128, 256], BF16, "pn") for _ in range(2)]
            sts = [self.sb(es, [128, 8], F32, "sts") for _ in range(2)]
            ob = [self.sb(es, [128, 512], BF16, "ob") for _ in range(2)]
            S.dma('sp', lambda e: e.dma_start(out=qxT[:], in_=R['QXTo'][:, 0:HPR].rearrange("(k p) t -> p k t", p=128)), writes=['qxT'])
            S.dma('sp', lambda e: e.dma_start(out=mkT[:], in_=R['MKT'].rearrange("(k p) t -> p k t", p=128)), writes=['mkT'])
            S.dma('pool', lambda e: e.dma_start(out=Vb[:], in_=R['MVB'].rearrange("(mt p) c -> p mt c", p=128)), writes=['Vb'], ndesc=16)
            oc = 0
            NTT = HPR // 128
            items = [(h, tt) for h in range(4) for tt in range(NTT)]
            pss = {}

            def xp_scores(n_):
                h, tt = items[n_]
                ps, pk = self.psum_next()
                for dh in range(2):
                    S.op('pe', lambda e, dh=dh: e.matmul(
                        ps[:, 0:256], lhsT=qxT[:, h * 2 + dh, tt * 128:(tt + 1) * 128], rhs=mkT[:, h * 2 + dh, :],
                        start=(dh == 0), stop=(dh == 1)), reads=['qxT', 'mkT'], writes=[pk])
                pss[n_] = (ps, pk)

            def xp_softmax(n_):
                h, tt = items[n_]
                ps, pk = pss.pop(n_)
                pTh = pT[h % 2]
                pTk = ('pT', h % 2)
                st = sts[n_ % 2]
                sk = ('sts', n_ % 2)
                pe = pe_[n_ % 2]
                pek = ('pe', n_ % 2)
                pnn = pn[n_ % 2]
                pnk = ('pn', n_ % 2)
                S.op('dve', lambda e: e.reduce_max(out=st[:, 0:1], in_=ps[:, 0:256], axis=AX.X), reads=[pk], writes=[sk])
                S.op('dve', lambda e: e.tensor_scalar(out=st[:, 1:2], in0=st[:, 0:1], scalar1=-1.0 / 16, scalar2=None, op0=ALU.mult),
                     reads=[sk], writes=[sk])
                S.op('act', lambda e: e.activation(out=pe[:], in_=ps[:, 0:256], func=AF.Exp, bias=st[:, 1:2],
                                                   scale=1.0 / 16, accum_out=st[:, 2:3]), reads=[pk, sk], writes=[pek, sk])
                S.op('dve', lambda e: e.reciprocal(out=st[:, 3:4], in_=st[:, 2:3]), reads=[sk], writes=[sk])
                S.op('dve', lambda e: e.tensor_scalar(out=pnn[:], in0=pe[:], scalar1=st[:, 3:4], scalar2=None, op0=ALU.mult),
                     reads=[sk, pek], writes=[pnk])
                pb, pbk = self.psumb_next()
                for mt in range(2):
                    S.op('pe', lambda e, mt=mt: e.transpose(out=pb[:, mt * 128:(mt + 1) * 128], in_=pnn[:, mt * 128:(mt + 1) * 128],
                                                            identity=self.identb[:]), reads=[pnk, 'identb'], writes=[pbk])
                S.op('act', lambda e: e.copy(out=pTh[:, :, tt * 128:(tt + 1) * 128], in_=pb[:, 0:256].rearrange("p (m t) -> p m t", m=2)),
                     reads=[pbk], writes=[pTk])

            def xp_pv(h):
                nonlocal oc
                pTh = pT[h % 2]
                pTk = ('pT', h % 2)
                for dh in range(2):
                    for (n0, nn) in nchunks(HPR):
                        ps, pk = self.psum_next()
                        for mt in range(2):
                            S.op('pe', lambda e, ps=ps, dh=dh, mt=mt, n0=n0, nn=nn: e.matmul(
                                ps[:, 0:nn], lhsT=Vb[:, mt, h * 256 + dh * 128:h * 256 + (dh + 1) * 128], rhs=pTh[:, mt, n0:n0 + nn],
                                start=(mt == 0), stop=(mt == 1)), reads=['Vb', pTk], writes=[pk])
                        o = ob[oc % 2]
                        ok = ('ob', oc % 2)
                        oc += 1
                        self.evac(oc, o[:, 0:nn], ps[:, 0:nn], [pk], [ok])
                        r = h * 256 + dh * 128
                        S.dma('sp', lambda e, o=o, r=r, n0=n0, nn=nn: e.dma_start(out=R['BR2'][r:r + 128, n0:n0 + nn], in_=o[:, 0:nn]),
                              reads=[ok], writes=[('dram', 'brt2', r, n0)])

            xp_scores(0)
            for n_ in range(len(items)):
                if n_ + 1 < len(items):
                    xp_scores(n_ + 1)
                xp_softmax(n_)
                if items[n_][1] == NTT - 1:
                    xp_pv(items[n_][0])
            S.barrier()
            S.flush()

    def xatt_sample(self):
        S = self.S
        R = self.R
        I = self.I
        with ExitStack() as es:
            qS = self.sb(es, [128, 8, NSM], BF16, "qS")
            qtm = self.sb(es, [NSM, 1024], BF16, "qtm")
            OH = self.sb(es, [NSM, NSM, 128], BF16, "OH")
            Kt = [self.sb(es, [128, 2, 1024], F32, "Kt") for _ in range(2)]
            Vt = [self.sb(es, [128, 2, 1024], BF16, "Vt") for _ in range(2)]
            prod = [self.sb(es, [128, 1024], F32, "prod") for _ in range(2)]
            SC = self.sb(es, [128, 2, 128], F32, "SC")
            pe = self.sb(es, [128, 256], F32, "pes")
            pn = self.sb(es, [128, 256], F32, "pns")
            st = self.sb(es, [128, 8], F32, "stx")
            P = self.sb(es, [128, 2, 128], BF16, "Pm")
            xrow = [self.sb(es, [1, 1024], F32, "xrow") for _ in range(2)]
            xas = self.sb(es, [NSM, 1024], F32, "xas")
            xasb = self.sb(es, [NSM, 1024], BF16, "xasb")
            xaT = self.sb(es, [128, 8, NSM], BF16, "xaT")
            S.dma('sp', lambda e: e.dma_start(out=qS[:], in_=R['QXTo'][:, HPR:HT].rearrange("(k p) t -> p k t", p=128)), writes=['qS'])
            for kq in range(0, 8, 4):
                pb, pbk = self.psumb_next()
                for j in range(4):
                    S.op('pe', lambda e, pb=pb, j=j, kq=kq: e.transpose(out=pb[:NSM, j * 128:(j + 1) * 128], in_=qS[:, kq + j, :],
                                                                       identity=self.identb[:]), reads=['qS', 'identb'], writes=[pbk])
                S.op('act', lambda e, pb=pb, kq=kq: e.copy(out=qtm[:, kq * 128:(kq + 4) * 128], in_=pb[:NSM, 0:512]), reads=[pbk], writes=['qtm'])
            S.op('dve', lambda e: e.tensor_copy(out=OH[:], in_=self.identb[:NSM, :NSM].unsqueeze(2).to_broadcast([NSM, NSM, 128])),
                 reads=['identb'], writes=['OH'])
            S.op('pool', lambda e: e.memset(SC[:], 0.0), writes=['SC'])
            for b in range(NSM):
                kt = Kt[b % 2]
                kk = ('Kt', b % 2)
                S.dma('sp', lambda e, kt=kt, b=b: e.dma_start(out=kt[:], in_=I['cache_k'][b].rearrange("(mt p) c -> p mt c", p=128)), writes=[kk])
                qb = []
                for hf in range(2):
                    ps, pk = self.psum_next()
                    S.op('pe', lambda e, ps=ps, b=b, hf=hf: e.matmul(ps[:, 0:512], lhsT=OH[:, b, :], rhs=qtm[:, hf * 512:(hf + 1) * 512],
                                                                    start=True, stop=True), reads=['OH', 'qtm'], writes=[pk])
                    qb.append((ps, pk))
                for mt in range(2):
                    pr = prod[mt]
                    prk = ('prod', mt)
                    for hf in range(2):
                        ps, pk = qb[hf]
                        S.op('dve', lambda e, pr=pr, kt=kt, ps=ps, mt=mt, hf=hf: e.tensor_tensor(
                            out=pr[:, hf * 512:(hf + 1) * 512], in0=kt[:, mt, hf * 512:(hf + 1) * 512], in1=ps[:, 0:512], op=ALU.mult),
                            reads=[kk, pk], writes=[prk])
                    S.op('dve', lambda e, pr=pr, mt=mt, b=b: e.tensor_reduce(out=SC[:, mt, b * 4:(b + 1) * 4],
                                                                            in_=pr[:].rearrange("p (h d) -> p h d", h=4), axis=AX.X, op=ALU.add),
                         reads=[prk], writes=['SC'])
            BH = 128
            if self.dbg:
                dSC = self.dram_out('dbg_SC', [128, 256])
                dq = self.dram_out('dbg_qtm', [NSM, 1024], BF16)
                S.dma('sp', lambda e: e.dma_start(out=dSC[:, :], in_=SC[:].rearrange("p a b -> p (a b)")), reads=['SC'], writes=[('dram', 'dsc')])
                S.dma('sp', lambda e: e.dma_start(out=dq[:, :], in_=qtm[:]), reads=['qtm'], writes=[('dram', 'dq')])
            ps, pk = self.psum_next()
            for mt in range(2):
                S.op('pe', lambda e, ps=ps, mt=mt: e.transpose(out=ps[:BH, mt * 128:(mt + 1) * 128], in_=SC[:, mt, :], identity=self.identf[:]),
                     reads=['SC', 'identf'], writes=[pk])
            S.op('dve', lambda e, ps=ps: e.reduce_max(out=st[:BH, 0:1], in_=ps[:BH, 0:256], axis=AX.X), reads=[pk], writes=['stx'])
            S.op('dve', lambda e: e.tensor_scalar(out=st[:BH, 1:2], in0=st[:BH, 0:1], scalar1=-1.0 / 16, scalar2=None, op0=ALU.mult),
                 reads=['stx'], writes=['stx'])
            S.op('act', lambda e, ps=ps: e.activation(out=pe[:BH, :], in_=ps[:BH, 0:256], func=AF.Exp, bias=st[:BH, 1:2], scale=1.0 / 16,
                                               accum_out=st[:BH, 2:3]), reads=[pk, 'stx'], writes=['pes', 'stx'])
            S.op('dve', lambda e: e.reciprocal(out=st[:BH, 3:4], in_=st[:BH, 2:3]), reads=['stx'], writes=['stx'])
            S.op('dve', lambda e: e.tensor_scalar(out=pn[:BH, :], in0=pe[:BH, :], scalar1=st[:BH, 3:4], scalar2=None, op0=ALU.mult),
                 reads=['stx', 'pes'], writes=['pns'])
            if self.dbg:
                dpn = self.dram_out('dbg_pn', [128, 256])
                dst_ = self.dram_out('dbg_st', [128, 8])
                S.dma('sp', lambda e: e.dma_start(out=dpn[:, :], in_=pn[:]), reads=['pns'], writes=[('dram', 'dpn')])
                S.dma('sp', lambda e: e.dma_start(out=dst_[:, :], in_=st[:]), reads=['stx'], writes=[('dram', 'dst')])
            for mt in range(2):
                ps2, pk2 = self.psum_next()
                S.op('pe', lambda e, ps2=ps2, mt=mt: e.transpose(out=ps2[:, 0:BH], in_=pn[:BH, mt * 128:(mt + 1) * 128], identity=self.identf[:BH, :BH]),
                     reads=['pns', 'identf'], writes=[pk2])
                S.op('act', lambda e, ps2=ps2, mt=mt: e.copy(out=P[:, mt, 0:BH], in_=ps2[:, 0:BH]), reads=[pk2], writes=['Pm'])
            for b in range(NSM):
                vt = Vt[b % 2]
                vk = ('Vt', b % 2)
                S.dma('pool', lambda e, vt=vt, b=b: e.dma_start(out=vt[:], in_=I['cache_v'][b].rearrange("(mt p) c -> p mt c", p=128)), writes=[vk], ndesc=16)
                xr = xrow[b % 2]
                xk = ('xrow', b % 2)
                for hp in range(2):
                    ps, pk = self.psum_next()
                    for hh in range(2):
                        h = hp * 2 + hh
                        for mt in range(2):
                            S.op('pe', lambda e, ps=ps, vt=vt, b=b, h=h, hh=hh, mt=mt: e.matmul(
                                ps[0:1, hh * 256:(hh + 1) * 256], lhsT=P[:, mt, b * 4 + h:b * 4 + h + 1], rhs=vt[:, mt, h * 256:(h + 1) * 256],
                                start=(mt == 0), stop=(mt == 1)), reads=['Pm', vk], writes=[pk])
                    S.op('act', lambda e, ps=ps, xr=xr, hp=hp: e.copy(out=xr[0:1, hp * 512:(hp + 1) * 512], in_=ps[0:1, 0:512]),
                         reads=[pk], writes=[xk])
                S.dma('sp', lambda e, xr=xr, b=b: e.dma_start(out=R['XAS'][b:b + 1, :], in_=xr[0:1, :]), reads=[xk], writes=[('dram', 'xas', b)])
            S.barrier()
            S.dma('sp', lambda e: e.dma_start(out=xas[:], in_=R['XAS'][:, :]), writes=['xas'])
            S.op('dve', lambda e: e.tensor_copy(out=xasb[:], in_=xas[:]), reads=['xas'], writes=['xasb'])
            self.transpose_into(xasb, 'xasb', NSM, 8, xaT, 'xaT', 0)
            S.dma('sp', lambda e: e.dma_start(out=R['BR2'][:, HPR:HT].rearrange("(k p) t -> p k t", p=128), in_=xaT[:]),
                  reads=['xaT'], writes=[('dram', 'brt2s')])
            S.barrier()
            S.flush()

    def s5(self):
        S = self.S
        I = self.I
        R = self.R
        O = self.O
        TWO_PI = 6.283185307179586
        dve = lambda fn, reads, writes: S.op('dve', fn, reads=reads, writes=writes)
        with ExitStack() as es:
            T = {}

            def t32(name):
                T[name] = self.sb(es, [128, 32], F32, name)
                return T[name]
            for nm in ['LR', 'LI', 'LS', 'DT', 'X', 'P', 'EM1', 'MAG', 'PHI', 'U', 'NF', 'RR', 'MSK', 'SIN', 'COS', 'AR', 'AI',
                       'CM1', 'NR', 'DEN', 'ZR', 'ZI', 'TA', 'TB', 'SPr', 'SPi', 'TC', 'TD']:
                t32(nm)
            NI = self.sb(es, [128, 32], I32, "NI")
            nat = [self.sb(es, [32, 128], F32, "nat") for _ in range(3)]
            LS2 = self.sb(es, [32, 2], F32, "LS2")
            LC = 64
            WvT = [self.sb(es, [32, 32, 128], BF16, "WvT%d" % i) for i in range(2)]
            CT = [self.sb(es, [128, 32, 32], BF16, "CT%d" % i) for i in range(2)]
            Dd = self.sb(es, [32, 32, 32], BF16, "Dd")
            Dcol = self.sb(es, [32, 32], F32, "Dcol")
            Gr = self.sb(es, [128, 32, LC], F32, "Gr")
            Gi = self.sb(es, [128, 32, LC], F32, "Gi")
            RHO = self.sb(es, [128, 32, LC], F32, "RHO")
            gx = [self.sb(es, [32, 512], F32, "gx") for _ in range(2)]
            g2t = [self.sb(es, [32, 512], F32, "g2t") for _ in range(2)]
            S.dma('sp', lambda e: e.dma_start(out=nat[0][:], in_=I['s5_lambda_re'].rearrange("(gp g2) p -> gp (g2 p)", g2=2)), writes=['nat0'])
            S.dma('sp', lambda e: e.dma_start(out=nat[1][:], in_=I['s5_lambda_im'].rearrange("(gp g2) p -> gp (g2 p)", g2=2)), writes=['nat1'])
            S.dma('sp', lambda e: e.dma_start(out=LS2[:], in_=I['s5_log_step'].rearrange("(gp g2) -> gp g2", g2=2)), writes=['LS2'])
            dve(lambda e: e.tensor_copy(out=nat[2][:].rearrange("g (a p) -> g a p", a=2), in_=LS2[:].unsqueeze(2).to_broadcast([32, 2, 64])),
                ['LS2'], ['nat2'])
            ps, pk = self.psum_next()
            for j in range(3):
                S.op('pe', lambda e, j=j, ps=ps: e.transpose(out=ps[:, j * 32:(j + 1) * 32], in_=nat[j][:, :], identity=self.identf[:32, :32]),
                     reads=['nat%d' % j, 'identf'], writes=[pk])
            for j, nm in enumerate(['LR', 'LI', 'LS']):
                dve(lambda e, j=j, nm=nm, ps=ps: e.tensor_copy(out=T[nm][:], in_=ps[:, j * 32:(j + 1) * 32]), [pk], [nm])
            S.op('act', lambda e: e.activation(out=T['DT'][:], in_=T['LS'][:], func=AF.Exp), reads=['LS'], writes=['DT'])
            tt = lambda o, a, b, op: dve(lambda e: e.tensor_tensor(out=T[o][:], in0=T[a][:], in1=T[b][:], op=op), [a, b], [o])
            ts = lambda o, a, s1, s2, op0, op1: dve(lambda e: e.tensor_scalar(out=T[o][:], in0=T[a][:], scalar1=s1, scalar2=s2, op0=op0, op1=op1), [a], [o])
            tt('X', 'LR', 'DT', ALU.mult)
            ts('P', 'X', 1.0 / 6, 1.0, ALU.mult, ALU.add)
            for dv in [5.0, 4.0, 3.0, 2.0]:
                tt('P', 'P', 'X', ALU.mult)
                ts('P', 'P', 1.0 / dv, 1.0, ALU.mult, ALU.add)
            tt('EM1', 'P', 'X', ALU.mult)
            ts('MAG', 'EM1', 1.0, None, ALU.add, ALU.bypass)
            tt('PHI', 'LI', 'DT', ALU.mult)

            def sin_of(dst, src, shift):
                ts('U', src, shift, 1.0 / TWO_PI, ALU.add, ALU.mult)
                dve(lambda e: e.tensor_copy(out=NI[:], in_=T['U'][:]), ['U'], ['NI'])
                dve(lambda e: e.tensor_copy(out=T['NF'][:], in_=NI[:]), ['NI'], ['NF'])
                ts('TA', src, shift, None, ALU.add, ALU.bypass)
                dve(lambda e: e.scalar_tensor_tensor(out=T['RR'][:], in0=T['NF'][:], scalar=-TWO_PI, in1=T['TA'][:], op0=ALU.mult, op1=ALU.add),
                    ['NF', 'TA'], ['RR'])
                ts('MSK', 'RR', math.pi, -TWO_PI, ALU.is_gt, ALU.mult)
                tt('RR', 'RR', 'MSK', ALU.add)
                ts('MSK', 'RR', -math.pi, TWO_PI, ALU.is_lt, ALU.mult)
                tt('RR', 'RR', 'MSK', ALU.add)
                ts('RR', 'RR', -3.1415925, 3.1415925, ALU.max, ALU.min)
                S.op('act', lambda e: e.activation(out=T[dst][:], in_=T['RR'][:], func=AF.Sin), reads=['RR'], writes=[dst])
            sin_of('SIN', 'PHI', 0.0)
            sin_of('COS', 'PHI', math.pi / 2)
            tt('AR', 'MAG', 'COS', ALU.mult)
            tt('AI', 'MAG', 'SIN', ALU.mult)
            ts('CM1', 'COS', -1.0, None, ALU.add, ALU.bypass)
            tt('NR', 'EM1', 'COS', ALU.mult)
            tt('NR', 'NR', 'CM1', ALU.add)
            tt('TA', 'LR', 'LR', ALU.mult)
            tt('TB', 'LI', 'LI', ALU.mult)
            tt('DEN', 'TA', 'TB', ALU.add)
            dve(lambda e: e.reciprocal(out=T['DEN'][:], in_=T['DEN'][:]), ['DEN'], ['DEN'])
            tt('TA', 'NR', 'LR', ALU.mult)
            tt('TB', 'AI', 'LI', ALU.mult)
            tt('TA', 'TA', 'TB', ALU.add)
            tt('ZR', 'TA', 'DEN', ALU.mult)
            tt('TA', 'AI', 'LR', ALU.mult)
            tt('TB', 'NR', 'LI', ALU.mult)
            tt('TA', 'TA', 'TB', ALU.subtract)
            tt('ZI', 'TA', 'DEN', ALU.mult)
            es1 = es.enter_context(ExitStack())
            BDr = self.sb(es1, [128, 32, 32], F32, "BDr")
            BDi = self.sb(es1, [128, 32, 32], F32, "BDi")
            BBr = self.sb(es1, [128, 32, 32], F32, "BBr")
            BBi = self.sb(es1, [128, 32, 32], F32, "BBi")
            BT1 = self.sb(es1, [128, 32, 32], F32, "BT1")
            BT2 = self.sb(es1, [128, 32, 32], F32, "BT2")
            CBD = [self.sb(es1, [32, 32, 128], F32, "CBD%d" % i) for i in range(2)]
            GT = [self.sb(es1, [128, 32, 32], F32, "GT%d" % i) for i in range(2)]
            S.op('pool', lambda e: e.memset(BDr[:], 0.0), writes=['BDr'])
            S.op('pool', lambda e: e.memset(BDi[:], 0.0), writes=['BDi'])
            for g2 in range(2):
                for (bd, bk, src) in [(BDr, 'BDr', I['s5_b_re']), (BDi, 'BDi', I['s5_b_im'])]:
                    S.dma('sp', lambda e, bd=bd, src=src, g2=g2: e.dma_start(
                        out=bd[g2 * 64:(g2 + 1) * 64, :, g2 * 16:(g2 + 1) * 16],
                        in_=src.rearrange("(gp g2) p h -> g2 p gp h", g2=2)[g2]), reads=[bk], writes=[bk])
            zb = lambda nm: T[nm][:].unsqueeze(2).to_broadcast([128, 32, 32])
            dve(lambda e: e.tensor_tensor(out=BT1[:], in0=BDr[:], in1=zb('ZR'), op=ALU.mult), ['BDr', 'ZR'], ['BT1'])
            dve(lambda e: e.tensor_tensor(out=BT2[:], in0=BDi[:], in1=zb('ZI'), op=ALU.mult), ['BDi', 'ZI'], ['BT2'])
            dve(lambda e: e.tensor_tensor(out=BBr[:], in0=BT1[:], in1=BT2[:], op=ALU.subtract), ['BT1', 'BT2'], ['BBr'])
            dve(lambda e: e.tensor_tensor(out=BT1[:], in0=BDi[:], in1=zb('ZR'), op=ALU.mult), ['BDi', 'ZR'], ['BT1'])
            dve(lambda e: e.tensor_tensor(out=BT2[:], in0=BDr[:], in1=zb('ZI'), op=ALU.mult), ['BDr', 'ZI'], ['BT2'])
            dve(lambda e: e.tensor_tensor(out=BBi[:], in0=BT1[:], in1=BT2[:], op=ALU.add), ['BT1', 'BT2'], ['BBi'])
            ei = 0
            for ri, (bb, bbk) in enumerate([(BBr, 'BBr'), (BBi, 'BBi')]):
                for g0 in range(0, 32, 4):
                    ps, pk = self.psum_next()
                    for j in range(4):
                        S.op('pe', lambda e, ps=ps, bb=bb, g0=g0, j=j: e.transpose(out=ps[:32, j * 128:(j + 1) * 128], in_=bb[:, g0 + j, :],
                                                                                  identity=self.identf[:]), reads=[bbk, 'identf'], writes=[pk])
                    ei += 1
                    self.evac(ei, WvT[ri][:, g0:g0 + 4, :], ps[:32, 0:512].rearrange("p (j q) -> p j q", j=4), [pk], ['WvT%d' % ri])
            for ri, src in enumerate([I['s5_c_re'], I['s5_c_im']]):
                S.op('pool', lambda e, ri=ri: e.memset(CBD[ri][:], 0.0), writes=['CBD%d' % ri])
                for g2 in range(2):
                    S.dma('sp', lambda e, ri=ri, src=src, g2=g2: e.dma_start(
                        out=CBD[ri][g2 * 16:(g2 + 1) * 16, :, g2 * 64:(g2 + 1) * 64],
                        in_=src.rearrange("(gp g2) h p -> g2 h gp p", g2=2)[g2]), reads=['CBD%d' % ri], writes=['CBD%d' % ri])
                for g0 in range(0, 32, 16):
                    ps, pk = self.psum_next()
                    for j in range(16):
                        S.op('pe', lambda e, ps=ps, ri=ri, g0=g0, j=j: e.transpose(out=ps[:, j * 32:(j + 1) * 32], in_=CBD[ri][:, g0 + j, :],
                                                                                  identity=self.identf[:32, :32]),
                             reads=['CBD%d' % ri, 'identf'], writes=[pk])
                    sc = 1.0 if ri == 0 else -1.0
                    dve(lambda e, ps=ps, ri=ri, g0=g0, sc=sc: e.tensor_scalar(out=CT[ri][:, g0:g0 + 16, :],
                                                                             in0=ps[:, 0:512].rearrange("p (j q) -> p j q", j=16),
                                                                             scalar1=sc, scalar2=None, op0=ALU.mult), [pk], ['CT%d' % ri])
            S.dma('sp', lambda e: e.dma_start(out=nat[0][:, 0:32], in_=I['s5_d'].rearrange("(gp g2) h -> gp (g2 h)", g2=2)),
                  reads=['nat0'], writes=['nat0'])
            ps, pk = self.psum_next()
            S.op('pe', lambda e, ps=ps: e.transpose(out=ps[:32, 0:32], in_=nat[0][:, 0:32], identity=self.identf[:32, :32]),
                 reads=['nat0', 'identf'], writes=[pk])
            dve(lambda e, ps=ps: e.tensor_copy(out=Dcol[:], in_=ps[:32, 0:32]), [pk], ['Dcol'])
            dve(lambda e: e.tensor_tensor(out=Dd[:], in0=self.identf[:32, :32].unsqueeze(1).to_broadcast([32, 32, 32]),
                                          in1=Dcol[:].unsqueeze(2).to_broadcast([32, 32, 32]), op=ALU.mult), ['Dcol', 'identf'], ['Dd'])
            LC = 64
            dve(lambda e: e.tensor_copy(out=Gr[:, :, 0:1], in_=T['COS'][:].unsqueeze(2)), ['COS'], ['G'])
            dve(lambda e: e.tensor_copy(out=Gi[:, :, 0:1], in_=T['SIN'][:].unsqueeze(2)), ['SIN'], ['G'])
            n = 1
            while n < LC:
                gr_b = Gr[:, :, n - 1:n].to_broadcast([128, 32, n])
                gi_b = Gi[:, :, n - 1:n].to_broadcast([128, 32, n])
                a0 = GT[0][:, :, 0:n]
                a1 = GT[1][:, :, 0:n]
                dve(lambda e, n=n, gr_b=gr_b, a0=a0: e.tensor_tensor(out=a0, in0=Gr[:, :, 0:n], in1=gr_b, op=ALU.mult), ['G'], ['GT0'])
                dve(lambda e, n=n, gi_b=gi_b, a1=a1: e.tensor_tensor(out=a1, in0=Gi[:, :, 0:n], in1=gi_b, op=ALU.mult), ['G'], ['GT1'])
                dve(lambda e, n=n, a0=a0, a1=a1: e.tensor_tensor(out=Gr[:, :, n:2 * n], in0=a0, in1=a1, op=ALU.subtract), ['GT0', 'GT1', 'G'], ['G'])
                dve(lambda e, n=n, gi_b=gi_b, a0=a0: e.tensor_tensor(out=a0, in0=Gr[:, :, 0:n], in1=gi_b, op=ALU.mult), ['G'], ['GT0'])
                dve(lambda e, n=n, gr_b=gr_b, a1=a1: e.tensor_tensor(out=a1, in0=Gi[:, :, 0:n], in1=gr_b, op=ALU.mult), ['G'], ['GT1'])
                dve(lambda e, n=n, a0=a0, a1=a1: e.tensor_tensor(out=Gi[:, :, n:2 * n], in0=a0, in1=a1, op=ALU.add), ['GT0', 'GT1', 'G'], ['G'])
                n *= 2
            dve(lambda e: e.tensor_copy(out=RHO[:], in_=T['MAG'][:].unsqueeze(2).to_broadcast([128, 32, LC])), ['MAG'], ['RHO'])
            S.op('pool', lambda e: e.memset(RHO[:, :, 0:1], 0.0), reads=['RHO'], writes=['RHO'])

            gcnt = [0]

            def gelu_to(ps, pk, ncol, dst, dkey):
                i = gcnt[0] % 2
                gcnt[0] += 1
                x = gx[i]
                y = g2t[i]
                xk = ('gx', i)
                yk = ('g2t', i)
                S.op('act', lambda e: e.copy(out=x[:, 0:ncol], in_=ps[:32, 0:ncol]), reads=[pk], writes=[xk])
                S.op('act', lambda e: e.activation(out=y[:, 0:ncol], in_=ps[:32, 0:ncol], func=AF.Square, scale=math.sqrt(0.044715)),
                     reads=[pk], writes=[yk])
                dve(lambda e: e.scalar_tensor_tensor(out=y[:, 0:ncol], in0=y[:, 0:ncol], scalar=1.0, in1=x[:, 0:ncol], op0=ALU.add, op1=ALU.mult),
                    [xk, yk], [yk])
                S.op('act', lambda e: e.activation(out=y[:, 0:ncol], in_=y[:, 0:ncol], func=AF.Sigmoid, scale=2.0 * math.sqrt(2.0 / math.pi)),
                     reads=[yk], writes=[yk])
                dve(lambda e: e.tensor_tensor(out=dst, in0=x[:, 0:ncol].rearrange("p (j q) -> p j q", j=dst.shape[1]),
                                              in1=y[:, 0:ncol].rearrange("p (j q) -> p j q", j=dst.shape[1]), op=ALU.mult), [xk, yk], [dkey])

            S.barrier()
            S.flush()
            es1.close()
            es2 = es.enter_context(ExitStack())
            UB = 128
            CPB = UB // LC
            uS = [self.sb(es2, [32, 32, UB], BF16, "uS") for _ in range(3)]
            ubs = [self.sb(es2, [32, 32, UB], BF16, "ub") for _ in range(2)]
            fl5 = self.sb(es2, [128, 2], F32, "fl5")
            S.dma('sp', lambda e: e.dma_start(out=fl5[:], in_=I['flag'][:, :]), writes=['fl5'])
            yS = [self.sb(es2, [32, 32, UB], BF16, "yS") for _ in range(2)]
            RI = [self.sb(es2, [128, 32, LC], F32, "RI%d" % i) for i in range(2)]
            RRt = [self.sb(es2, [128, 32, LC], F32, "RR%d" % i) for i in range(2)]
            SB = [self.sb(es2, [128, 32, LC], BF16, "SB%d" % i) for i in range(2)]
            W1 = [self.sb(es2, [128, 512], F32, "W1_%d" % i) for i in range(8)]
            V1 = [self.sb(es2, [128, 32, LC], F32, "V1_%d" % i) for i in range(2)]
            nchunk = NPR // LC
            pl = lambda fn, reads, writes: S.op('pool', fn, reads=reads, writes=writes)
            wcnt = [0]

            NPRE = nchunk // 2

            def cinfo(c):
                blk = c // CPB
                return blk, (c % CPB) * LC, uS[blk % 3], ('uS', blk % 3), yS[blk % 2], ('yS', blk % 2)

            def load_u(c):
                blk, tb, us, uk, ys, yk = cinfo(c)
                if c < NPRE:
                    S.dma('sp', lambda e: e.dma_start(
                        out=us[:], in_=R['UT'][:, blk * UB:(blk + 1) * UB].rearrange("(gp r) t -> r gp t", r=32)), writes=[uk])
                    dve(lambda e: e.tensor_scalar(out=us[:], in0=us[:], scalar1=fl5[:32, 1:2], scalar2=None, op0=ALU.mult), [uk, 'fl5'], [uk])
                else:
                    lb = blk - NPRE // CPB
                    ub = ubs[blk % 2]
                    ubk = ('ub', blk % 2)
                    S.dma('sp', lambda e: e.dma_start(
                        out=us[:], in_=R['UT'][:, lb * UB:(lb + 1) * UB].rearrange("(gp r) t -> r gp t", r=32)), writes=[uk])
                    S.dma('sp', lambda e: e.dma_start(
                        out=ub[:], in_=R['UT'][:, HPR + lb * UB:HPR + (lb + 1) * UB].rearrange("(gp r) t -> r gp t", r=32)), writes=[ubk])
                    dve(lambda e: e.tensor_scalar(out=us[:], in0=us[:], scalar1=fl5[:32, 0:1], scalar2=None, op0=ALU.mult), [uk, 'fl5'], [uk])
                    dve(lambda e: e.scalar_tensor_tensor(out=us[:], in0=ub[:], scalar=fl5[:32, 1:2], in1=us[:], op0=ALU.mult, op1=ALU.add),
                        [uk, ubk, 'fl5'], [uk])

            def do_V(c):
                blk, tb, us, uk, ys, yk = cinfo(c)
                if c % CPB == 0 and c + CPB < nchunk:
                    load_u(c + CPB)
                for hf in range(2):
                    pss = [[self.psum_next(), self.psum_next()], [self.psum_next(), self.psum_next()]]
                    for ri in range(2):
                        for j in range(16):
                            gp = hf * 16 + j
                            ps, pk = pss[ri][j // 8]
                            col = (j % 8) * LC
                            S.op('pe', lambda e, ps=ps, ri=ri, gp=gp, col=col: e.matmul(
                                ps[:, col:col + LC], lhsT=WvT[ri][:, gp, :], rhs=us[:, gp, tb:tb + LC], start=True, stop=True),
                                reads=['WvT%d' % ri, uk], writes=[pk])
                    for tl in range(2):
                        g0 = hf * 16 + tl * 8
                        (pr, prk) = pss[0][tl]
                        (pi, pik) = pss[1][tl]
                        grs = Gr[:, g0:g0 + 8, :].rearrange("p g t -> p (g t)")
                        gis = Gi[:, g0:g0 + 8, :].rearrange("p g t -> p (g t)")
                        wi = (wcnt[0] % 2) * 4
                        wcnt[0] += 1
                        w = W1[wi:wi + 4]
                        wk = ['w%d' % (wi + q) for q in range(4)]
                        dve(lambda e, pr=pr, grs=grs, w=w: e.tensor_tensor(out=w[0][:], in0=pr[:, 0:512], in1=grs, op=ALU.mult), [prk, 'G'], [wk[0]])
                        dve(lambda e, pi=pi, gis=gis, w=w: e.tensor_tensor(out=w[1][:], in0=pi[:, 0:512], in1=gis, op=ALU.mult), [pik, 'G'], [wk[1]])
                        dve(lambda e, pi=pi, grs=grs, w=w: e.tensor_tensor(out=w[2][:], in0=pi[:, 0:512], in1=grs, op=ALU.mult), [pik, 'G'], [wk[2]])
                        dve(lambda e, pr=pr, gis=gis, w=w: e.tensor_tensor(out=w[3][:], in0=pr[:, 0:512], in1=gis, op=ALU.mult), [prk, 'G'], [wk[3]])
                        pl(lambda e, g0=g0, w=w: e.tensor_tensor(out=RI[0][:, g0:g0 + 8, :].rearrange("p g t -> p (g t)"), in0=w[0][:], in1=w[1][:],
                                                             op=ALU.add), [wk[0], wk[1]], ['RI0'])
                        pl(lambda e, g0=g0, w=w: e.tensor_tensor(out=RI[1][:, g0:g0 + 8, :].rearrange("p g t -> p (g t)"), in0=w[2][:], in1=w[3][:],
                                                             op=ALU.subtract), [wk[2], wk[3]], ['RI1'])

            def do_scan(c):
                if c > 0:
                    for ri, sp in enumerate(['SPr', 'SPi']):
                        dve(lambda e, sp=sp: e.tensor_tensor(out=T['TC'][:], in0=T[sp][:], in1=T['MAG'][:], op=ALU.mult), [sp, 'MAG'], ['TC'])
                        dve(lambda e, ri=ri: e.tensor_tensor(out=RI[ri][:, :, 0:1], in0=RI[ri][:, :, 0:1], in1=T['TC'][:].unsqueeze(2), op=ALU.add),
                            ['TC', 'RI%d' % ri], ['RI%d' % ri])
                for ri in range(2):
                    dve(lambda e, ri=ri: e.tensor_tensor_scan(out=RRt[ri][:].rearrange("p g t -> p (g t)"), data0=RHO[:].rearrange("p g t -> p (g t)"),
                                                             data1=RI[ri][:].rearrange("p g t -> p (g t)"), initial=0.0, op0=ALU.mult, op1=ALU.add),
                        ['RHO', 'RI%d' % ri], ['RRt%d' % ri])
                L = LC - 1
                col = lambda t_: t_[:, :, L:L + 1]
                dve(lambda e: e.tensor_tensor(out=T['TC'][:].unsqueeze(2), in0=col(RRt[0]), in1=col(Gr), op=ALU.mult), ['RRt0', 'G'], ['TC'])
                dve(lambda e: e.tensor_tensor(out=T['TD'][:].unsqueeze(2), in0=col(RRt[1]), in1=col(Gi), op=ALU.mult), ['RRt1', 'G'], ['TD'])
                dve(lambda e: e.tensor_tensor(out=T['SPr'][:], in0=T['TC'][:], in1=T['TD'][:], op=ALU.subtract), ['TC', 'TD'], ['SPr'])
                dve(lambda e: e.tensor_tensor(out=T['TC'][:].unsqueeze(2), in0=col(RRt[1]), in1=col(Gr), op=ALU.mult), ['RRt1', 'G'], ['TC'])
                dve(lambda e: e.tensor_tensor(out=T['TD'][:].unsqueeze(2), in0=col(RRt[0]), in1=col(Gi), op=ALU.mult), ['RRt0', 'G'], ['TD'])
                dve(lambda e: e.tensor_tensor(out=T['SPi'][:], in0=T['TC'][:], in1=T['TD'][:], op=ALU.add), ['TC', 'TD'], ['SPi'])

            def do_rotout(c):
                pl(lambda e: e.tensor_tensor(out=V1[0][:], in0=RRt[0][:], in1=Gr[:], op=ALU.mult), ['RRt0', 'G'], ['v0'])
                pl(lambda e: e.tensor_tensor(out=V1[1][:], in0=RRt[1][:], in1=Gi[:], op=ALU.mult), ['RRt1', 'G'], ['v1'])
                pl(lambda e: e.tensor_tensor(out=SB[0][:], in0=V1[0][:], in1=V1[1][:], op=ALU.subtract), ['v0', 'v1'], ['SB0'])
                pl(lambda e: e.tensor_tensor(out=V1[0][:], in0=RRt[1][:], in1=Gr[:], op=ALU.mult), ['RRt1', 'G'], ['v0'])
                pl(lambda e: e.tensor_tensor(out=V1[1][:], in0=RRt[0][:], in1=Gi[:], op=ALU.mult), ['RRt0', 'G'], ['v1'])
                pl(lambda e: e.tensor_tensor(out=SB[1][:], in0=V1[0][:], in1=V1[1][:], op=ALU.add), ['v0', 'v1'], ['SB1'])

            def do_y(c):
                blk, tb, us, uk, ys, yk = cinfo(c)
                for g0 in range(0, 32, 8):
                    ps, pk = self.psum_next()
                    for j in range(8):
                        gp = g0 + j
                        cs = j * LC
                        S.op('pe', lambda e, ps=ps, gp=gp, cs=cs: e.matmul(ps[:32, cs:cs + LC], lhsT=CT[0][:, gp, :], rhs=SB[0][:, gp, :],
                                                                         start=True, stop=False), reads=['CT0', 'SB0'], writes=[pk])
                        S.op('pe', lambda e, ps=ps, gp=gp, cs=cs: e.matmul(ps[:32, cs:cs + LC], lhsT=CT[1][:, gp, :], rhs=SB[1][:, gp, :],
                                                                         start=False, stop=False), reads=['CT1', 'SB1'], writes=[pk])
                        S.op('pe', lambda e, ps=ps, gp=gp, cs=cs: e.matmul(ps[:32, cs:cs + LC], lhsT=Dd[:, gp, :], rhs=us[:, gp, tb:tb + LC],
                                                                         start=False, stop=True), reads=['Dd', uk], writes=[pk])
                    gelu_to(ps, pk, 512, ys[:, g0:g0 + 8, tb:tb + LC], yk)
                if c % CPB == CPB - 1:
                    lb = blk - NPRE // CPB
                    S.dma('sp', lambda e: e.dma_start(
                        out=R['S5G'][:, lb * UB:(lb + 1) * UB].rearrange("(gp r) t -> r gp t", r=32), in_=ys[:]),
                        reads=[yk], writes=[('dram', 's5g', blk)])

            load_u(0)
            do_V(0)
            for c in range(nchunk):
                do_scan(c)
                if c >= NPRE:
                    do_rotout(c)
                if c + 1 < nchunk:
                    do_V(c + 1)
                if c >= NPRE:
                    do_y(c)
            for ri, (sp, dst) in enumerate([('SPr', O['s5r_p']), ('SPi', O['s5i_p'])]):
                ps, pk = self.psum_next()
                S.op('pe', lambda e, ps=ps, sp=sp: e.transpose(out=ps[:32, 0:128], in_=T[sp][:, :], identity=self.identf[:]),
                     reads=[sp, 'identf'], writes=[pk])
                dve(lambda e, ps=ps, ri=ri: e.tensor_copy(out=nat[ri][:], in_=ps[:32, 0:128]), [pk], ['nat%d' % ri])
                S.dma('sp', lambda e, ri=ri, dst=dst: e.dma_start(out=dst.rearrange("(gp g2) p -> gp (g2 p)", g2=2), in_=nat[ri][:]),
                      reads=['nat%d' % ri], writes=[('dram', 's5p', ri)])
            S.barrier()
            S.flush()
            es2.close()
            es3 = es.enter_context(ExitStack())
            sre = [self.sb(es3, [NSM, 4096], F32, "sre%d" % i) for i in range(2)]
            SO = [self.sb(es3, [128, 32, NSM], F32, "SO%d" % i) for i in range(2)]
            SN = [self.sb(es3, [128, 32, NSM], F32, "SN%d" % i) for i in range(2)]
            SNb = [self.sb(es3, [128, 32, NSM], BF16, "SNb%d" % i) for i in range(2)]
            Q1 = [self.sb(es3, [128, 32, NSM], F32, "Q1_%d" % i) for i in range(2)]
            uSs = self.sb(es3, [32, 32, NSM], BF16, "uSs")
            ySs = self.sb(es3, [32, 32, NSM], BF16, "ySs")
            S.dma('sp', lambda e: e.dma_start(out=sre[0][:], in_=I['s5r'][:, :]), writes=['sre0'])
            S.dma('sp', lambda e: e.dma_start(out=sre[1][:], in_=I['s5i'][:, :]), writes=['sre1'])
            S.dma('sp', lambda e: e.dma_start(out=uSs[:], in_=R['UT'][:, NPR:NTOK].rearrange("(gp r) t -> r gp t", r=32)), writes=['uSs'])
            for ri in range(2):
                for g0 in (0, 16):
                    ps, pk = self.psum_next()
                    for j in range(16):
                        S.op('pe', lambda e, ps=ps, ri=ri, g0=g0, j=j: e.transpose(out=ps[:, j * NSM:(j + 1) * NSM],
                                                                                  in_=sre[ri][:NSM, (g0 + j) * 128:(g0 + j + 1) * 128],
                                                                                  identity=self.identf[:NSM, :NSM]),
                             reads=['sre%d' % ri, 'identf'], writes=[pk])
                    dve(lambda e, ps=ps, ri=ri, g0=g0: e.tensor_copy(out=SO[ri][:, g0:g0 + 16, :], in_=ps[:, 0:16 * NSM].rearrange("p (j q) -> p j q", j=16)),
                        [pk], ['SO%d' % ri])
            ab = lambda nm: T[nm][:].unsqueeze(2).to_broadcast([128, 32, NSM])
            for ri in range(2):
                a_, b_ = (0, 1) if ri == 0 else (1, 0)
                dve(lambda e, a_=a_: e.tensor_tensor(out=Q1[0][:], in0=SO[a_][:], in1=ab('AR'), op=ALU.mult), ['SO%d' % a_, 'AR'], ['Q10'])
                dve(lambda e, b_=b_: e.tensor_tensor(out=Q1[1][:], in0=SO[b_][:], in1=ab('AI'), op=ALU.mult), ['SO%d' % b_, 'AI'], ['Q11'])
                dve(lambda e, ri=ri: e.tensor_tensor(out=Q1[0][:], in0=Q1[0][:], in1=Q1[1][:], op=(ALU.subtract if ri == 0 else ALU.add)),
                    ['Q10', 'Q11'], ['Q10'])
                for g0 in (0, 16):
                    ps, pk = self.psum_next()
                    for j in range(16):
                        gp = g0 + j
                        S.op('pe', lambda e, ps=ps, ri=ri, gp=gp, j=j: e.matmul(ps[:, j * NSM:(j + 1) * NSM], lhsT=WvT[ri][:, gp, :], rhs=uSs[:, gp, :],
                                                                               start=True, stop=True), reads=['WvT%d' % ri, 'uSs'], writes=[pk])
                    dve(lambda e, ps=ps, ri=ri, g0=g0: e.tensor_tensor(out=SN[ri][:, g0:g0 + 16, :], in0=Q1[0][:, g0:g0 + 16, :],
                                                                      in1=ps[:, 0:16 * NSM].rearrange("p (j q) -> p j q", j=16), op=ALU.add),
                        [pk, 'Q10'], ['SN%d' % ri])
                dve(lambda e, ri=ri: e.tensor_copy(out=SNb[ri][:], in_=SN[ri][:]), ['SN%d' % ri], ['SNb%d' % ri])
            for g0 in (0, 16):
                ps, pk = self.psum_next()
                for j in range(16):
                    gp = g0 + j
                    cs = j * NSM
                    S.op('pe', lambda e, ps=ps, gp=gp, cs=cs: e.matmul(ps[:32, cs:cs + NSM], lhsT=CT[0][:, gp, :], rhs=SNb[0][:, gp, :],
                                                                     start=True, stop=False), reads=['CT0', 'SNb0'], writes=[pk])
                    S.op('pe', lambda e, ps=ps, gp=gp, cs=cs: e.matmul(ps[:32, cs:cs + NSM], lhsT=CT[1][:, gp, :], rhs=SNb[1][:, gp, :],
                                                                     start=False, stop=False), reads=['CT1', 'SNb1'], writes=[pk])
                    S.op('pe', lambda e, ps=ps, gp=gp, cs=cs: e.matmul(ps[:32, cs:cs + NSM], lhsT=Dd[:, gp, :], rhs=uSs[:, gp, :],
                                                                     start=False, stop=True), reads=['Dd', 'uSs'], writes=[pk])
                gelu_to(ps, pk, 16 * NSM, ySs[:, g0:g0 + 16, :], 'ySs')
            S.dma('sp', lambda e: e.dma_start(out=R['S5G'][:, HPR:HT].rearrange("(gp r) t -> r gp t", r=32), in_=ySs[:]),
                  reads=['ySs'], writes=[('dram', 's5gs')])
            for ri, dst in enumerate([O['s5r_s'], O['s5i_s']]):
                for g0 in range(0, 32, 4):
                    ps, pk = self.psum_next()
                    for j in range(4):
                        S.op('pe', lambda e, ps=ps, ri=ri, g0=g0, j=j: e.transpose(out=ps[:NSM, j * 128:(j + 1) * 128], in_=SN[ri][:, g0 + j, :],
                                                                                  identity=self.identf[:]), reads=['SN%d' % ri, 'identf'], writes=[pk])
                    ei += 1
                    self.evac(ei, sre[ri][:, g0 * 128:(g0 + 4) * 128], ps[:NSM, 0:512], [pk], ['sre%d' % ri])
                S.dma('sp', lambda e, ri=ri, dst=dst: e.dma_start(out=dst[:, :], in_=sre[ri][:]), reads=['sre%d' % ri], writes=[('dram', 's5s', ri)])
            S.barrier()
            S.flush()
            es3.close()
        self.s5_glu()

    def s5_glu(self):
        S = self.S
        I = self.I
        R = self.R
        for blk in [(0, HT)]:
            t0, t1 = blk
            NT = t1 - t0
            with ExitStack() as es:
                gT = self.sb(es, [128, 8, NT], BF16, "gT")
                self.WB = [self.sb(es, [128, 8 * 256], BF16, "wb") for _ in range(2)]
                self.wb_i = 0
                sg = [self.sb(es, [128, 512], F32, "sgl") for _ in range(2)]
                ob = [self.sb(es, [128, 512], BF16, "obl") for _ in range(2)]
                S.dma('sp', lambda e: e.dma_start(out=gT[:], in_=R['S5G'][:, t0:t1].rearrange("(k p) t -> p k t", p=128)), writes=['gT'])
                cnt = 0
                for c0 in range(0, 1024, 256):
                    wv, wkey = self.wload(I['w_s5_glu'][:, c0:c0 + 256], 8, 256)
                    for m in range(2):
                        mi = c0 // 128 + m
                        for (n0, nn) in nchunks(NT):
                            ps, pk = self.psum_next()
                            for k in range(8):
                                S.op('pe', lambda e, ps=ps, wv=wv, k=k, m=m, n0=n0, nn=nn: e.matmul(
                                    ps[:, 0:nn], lhsT=wv[:, k, m * 128:(m + 1) * 128], rhs=gT[:, k, n0:n0 + nn], start=(k == 0), stop=(k == 7)),
                                    reads=[wkey, 'gT'], writes=[pk])
                            sgt = sg[cnt % 2]
                            o = ob[cnt % 2]
                            sk = ('sgl', cnt % 2)
                            ok = ('obl', cnt % 2)
                            cnt += 1
                            S.op('act', lambda e, sgt=sgt, ps=ps, nn=nn: e.activation(out=sgt[:, 0:nn], in_=ps[:, 0:nn], func=AF.Sigmoid),
                                 reads=[pk], writes=[sk])
                            S.op('dve', lambda e, o=o, sgt=sgt, mi=mi, n0=n0, nn=nn: e.tensor_tensor(out=o[:, 0:nn], in0=sgt[:, 0:nn],
                                                                                                    in1=gT[:, mi, n0:n0 + nn], op=ALU.mult),
                                 reads=[sk, 'gT'], writes=[ok])
                            S.dma('sp', lambda e, o=o, mi=mi, n0=n0, nn=nn: e.dma_start(
                                out=R['BR0'][mi * 128:(mi + 1) * 128, t0 + n0:t0 + n0 + nn], in_=o[:, 0:nn]), reads=[ok],
                                writes=[('dram', 'brt0', mi, n0)])
                S.barrier()
                S.flush()

    def mlstm_prompt(self):
        S = self.S
        I = self.I
        R = self.R
        O = self.O
        CL = 64
        NCH = NPR // CL
        dve = lambda fn, reads, writes: S.op('dve', fn, reads=reads, writes=writes)
        act = lambda fn, reads, writes: S.op('act', fn, reads=reads, writes=writes)
        with ExitStack() as es:
            QC = {nm: self.sb(es, [64, NCH, 4], F32, nm) for nm in ['XC', 'EMC', 'WLC', 'WIC']}
            WE = self.sb(es, [128, 4, NCH], F32, "WE")
            Rb = self.sb(es, [64, 4, NPR], F32, "Rb")
            Cf = self.sb(es, [128, 4, 2, 257], F32, "Cf")
            Cb = self.sb(es, [128, 4, 2, 257], BF16, "Cb")
            GH = self.sb(es, [64, 1024], F32, "GH")
            maskT = self.sb(es, [64, 64], F32, "maskT")
            oh4 = self.sb(es, [4, 4, 128], F32, "oh4")
            bi = self.sb(es, [4, 1], F32, "bi")
            bf = self.sb(es, [4, 1], F32, "bf")
            NH = NCH // 2
            flm = self.sb(es, [128, 2], F32, "flm")
            WLCp = self.sb(es, [64, NH, 4], F32, "WLCp")
            WEp = self.sb(es, [128, 4, NH], F32, "WEp")
            S.dma('sp', lambda e: e.dma_start(out=flm[:], in_=I['flag'][:, :]), writes=['flm'])
            es1 = es.enter_context(ExitStack())
            rows = {nm: self.sb(es1, [4, NPR], F32, nm) for nm in ['IG', 'FG', 'L', 'F', 'X', 'Rm', 'RP', 'WI', 'MT', 'EM', 'RE', 'WL', 'ON']}
            r3 = lambda nm: rows[nm][:].rearrange("h (c t) -> h c t", t=CL)
            S.dma('sp', lambda e: e.dma_start(out=rows['IG'][:], in_=R['IGR'][:, 0:NPR]), writes=['IG'])
            S.dma('sp', lambda e: e.dma_start(out=rows['FG'][:], in_=R['FGR'][:, 0:NPR]), writes=['FG'])
            S.dma('sp', lambda e: e.dma_start(out=bi[:], in_=I['b_igate'].rearrange("(h o) -> h o", o=1)), writes=['bi'])
            S.dma('sp', lambda e: e.dma_start(out=bf[:], in_=I['b_fgate'].rearrange("(h o) -> h o", o=1)), writes=['bf'])
            S.dma('sp', lambda e: e.dma_start(out=GH[:], in_=I['g_mlstm_head'].partition_broadcast(64)), writes=['GH'])
            S.op('pool', lambda e: e.memset(maskT[:], 0.0), writes=['maskT'])
            S.op('pool', lambda e: e.affine_select(out=maskT[:], in_=maskT[:], pattern=[[1, 64]], compare_op=ALU.is_ge, fill=-1e30,
                                                   base=0, channel_multiplier=-1), reads=['maskT'], writes=['maskT'])
            S.op('pool', lambda e: e.memset(rows['ON'][:], 1.0), writes=['ON'])
            S.op('pool', lambda e: e.memset(Cf[:], 0.0), writes=['Cf'])
            S.op('pool', lambda e: e.memset(Cb[:], 0.0), writes=['Cb'])
            dve(lambda e: e.tensor_copy(out=oh4[:], in_=self.identf[:4, :4].unsqueeze(2).to_broadcast([4, 4, 128])), ['identf'], ['oh4'])
            dve(lambda e: e.tensor_scalar(out=rows['IG'][:], in0=rows['IG'][:], scalar1=bi[:, 0:1], scalar2=None, op0=ALU.add), ['IG', 'bi'], ['IG'])
            dve(lambda e: e.tensor_scalar(out=rows['FG'][:], in0=rows['FG'][:], scalar1=bf[:, 0:1], scalar2=None, op0=ALU.add), ['FG', 'bf'], ['FG'])
            act(lambda e: e.activation(out=rows['L'][:], in_=rows['FG'][:], func=AF.Exp, scale=-1.0), ['FG'], ['L'])
            act(lambda e: e.activation(out=rows['L'][:], in_=rows['L'][:], func=AF.Ln, bias=1.0), ['L'], ['L'])
            dve(lambda e: e.tensor_tensor_scan(out=rows['F'][:], data0=rows['ON'][:], data1=rows['L'][:], initial=0.0, op0=ALU.mult, op1=ALU.add),
                ['ON', 'L'], ['F'])
            dve(lambda e: e.tensor_tensor(out=rows['X'][:], in0=rows['IG'][:], in1=rows['F'][:], op=ALU.add), ['IG', 'F'], ['X'])
            dve(lambda e: e.tensor_tensor_scan(out=rows['Rm'][:], data0=rows['ON'][:], data1=rows['X'][:], initial=0.0, op0=ALU.mult, op1=ALU.max),
                ['ON', 'X'], ['Rm'])
            dve(lambda e: e.tensor_copy(out=r3('RP')[:, 1:NCH, :], in_=r3('Rm')[:, 0:NCH - 1, CL - 1:CL].to_broadcast([4, NCH - 1, CL])), ['Rm'], ['RP'])
            S.op('pool', lambda e: e.memset(r3('RP')[:, 0:1, :], 0.0), reads=['RP'], writes=['RP'])
            dve(lambda e: e.tensor_tensor(out=rows['WI'][:], in0=rows['RP'][:], in1=rows['Rm'][:], op=ALU.subtract), ['RP', 'Rm'], ['WI'])
            act(lambda e: e.activation(out=rows['WI'][:], in_=rows['WI'][:], func=AF.Exp), ['WI'], ['WI'])
            dve(lambda e: e.tensor_tensor(out=rows['MT'][:], in0=rows['Rm'][:], in1=rows['F'][:], op=ALU.subtract), ['Rm', 'F'], ['MT'])
            act(lambda e: e.activation(out=rows['EM'][:], in_=rows['MT'][:], func=AF.Exp, scale=-1.0), ['MT'], ['EM'])
            dve(lambda e: e.tensor_copy(out=r3('RE'), in_=r3('Rm')[:, :, CL - 1:CL].to_broadcast([4, NCH, CL])), ['Rm'], ['RE'])
            dve(lambda e: e.tensor_tensor(out=rows['WL'][:], in0=rows['X'][:], in1=rows['RE'][:], op=ALU.subtract), ['X', 'RE'], ['WL'])
            act(lambda e: e.activation(out=rows['WL'][:], in_=rows['WL'][:], func=AF.Exp, bias=-math.log(16.0)), ['WL'], ['WL'])
            S.dma('sp', lambda e: e.dma_start(out=O['m_p'][:, :], in_=rows['MT'][:, NPR - 1:NPR]), reads=['MT'], writes=[('dram', 'm_p')])
            for qn, cn in [('X', 'XC'), ('EM', 'EMC'), ('WL', 'WLC'), ('WI', 'WIC')]:
                ps, pk = self.psum_next()
                for c in range(NCH):
                    S.op('pe', lambda e, ps=ps, qn=qn, c=c: e.transpose(out=ps[:64, c * 4:(c + 1) * 4], in_=rows[qn][0:4, c * CL:(c + 1) * CL],
                                                                      identity=self.identf[:4, :4]), reads=[qn, 'identf'], writes=[pk])
                dve(lambda e, ps=ps, cn=cn: e.tensor_copy(out=QC[cn][:].rearrange("p c h -> p (c h)"), in_=ps[:64, 0:NCH * 4]), [pk], [cn])
            ei = 0
            for h in range(4):
                for (n0, nn) in nchunks(NPR):
                    ps, pk = self.psum_next()
                    S.op('pe', lambda e, ps=ps, h=h, n0=n0, nn=nn: e.matmul(ps[:64, 0:nn], lhsT=oh4[:, h, 0:64], rhs=rows['Rm'][:, n0:n0 + nn],
                                                                          start=True, stop=True), reads=['oh4', 'Rm'], writes=[pk])
                    ei += 1
                    self.evac(ei, Rb[:, h, n0:n0 + nn], ps[:64, 0:nn], [pk], ['Rb'])
                ps, pk = self.psum_next()
                S.op('pe', lambda e, ps=ps, h=h: e.matmul(ps[:, 0:NCH], lhsT=oh4[:, h, :], rhs=r3('WI')[:, :, CL - 1], start=True, stop=True),
                     reads=['oh4', 'WI'], writes=[pk])
                dve(lambda e, ps=ps, h=h: e.tensor_copy(out=WE[:, h, :], in_=ps[:, 0:NCH]), [pk], ['WE'])
            f0 = lambda n_: flm[:n_, 0:1]
            f1 = lambda n_: flm[:n_, 1:2]
            dve(lambda e: e.tensor_scalar(out=WLCp[:], in0=QC['WLC'][:, 0:NH, :], scalar1=f1(64), scalar2=None, op0=ALU.mult), ['WLC', 'flm'], ['WLCp'])
            dve(lambda e: e.tensor_copy(out=WEp[:], in_=WE[:, :, 0:NH]), ['WE'], ['WEp'])
            for nm in ['XC', 'EMC', 'WLC', 'WIC']:
                dve(lambda e, nm=nm: e.tensor_scalar(out=QC[nm][:, 0:NH, :], in0=QC[nm][:, 0:NH, :], scalar1=f0(64), scalar2=None, op0=ALU.mult),
                    [nm, 'flm', 'WLCp'], [nm])
                dve(lambda e, nm=nm: e.scalar_tensor_tensor(out=QC[nm][:, 0:NH, :], in0=QC[nm][:, NH:NCH, :], scalar=f1(64), in1=QC[nm][:, 0:NH, :],
                                                           op0=ALU.mult, op1=ALU.add), [nm, 'flm'], [nm])
            dve(lambda e: e.tensor_scalar(out=WE[:, :, 0:NH], in0=WE[:, :, 0:NH], scalar1=f0(128), scalar2=None, op0=ALU.mult), ['WE', 'flm', 'WEp'], ['WE'])
            dve(lambda e: e.scalar_tensor_tensor(out=WE[:, :, 0:NH], in0=WE[:, :, NH:NCH], scalar=f1(128), in1=WE[:, :, 0:NH], op0=ALU.mult, op1=ALU.add),
                ['WE', 'flm'], ['WE'])
            dve(lambda e: e.tensor_scalar(out=Rb[:, :, 0:HPR], in0=Rb[:, :, 0:HPR], scalar1=f0(64), scalar2=None, op0=ALU.mult), ['Rb', 'flm'], ['Rb'])
            dve(lambda e: e.scalar_tensor_tensor(out=Rb[:, :, 0:HPR], in0=Rb[:, :, HPR:NPR], scalar=f1(64), in1=Rb[:, :, 0:HPR], op0=ALU.mult, op1=ALU.add),
                ['Rb', 'flm'], ['Rb'])
            S.barrier()
            S.flush()
            es1.close()
            es2 = es.enter_context(ExitStack())
            BT = 256
            CPB = BT // CL
            qT = [self.sb(es2, [128, 8, BT], BF16, "qTb") for _ in range(2)]
            kT = [self.sb(es2, [128, 8, BT], BF16, "kTb") for _ in range(2)]
            v1 = [self.sb(es2, [64, CPB, 4, 257], BF16, "v1b") for _ in range(2)]
            ktm = [self.sb(es2, [64, CPB, 1024], BF16, "ktmb") for _ in range(2)]
            qTB = self.sb(es2, [128, 8, BT], BF16, "qTB")
            v1B = self.sb(es2, [64, CPB, 4, 256], BF16, "v1B")
            ktmB = self.sb(es2, [64, CPB, 1024], BF16, "ktmB")
            otm = [self.sb(es2, [64, 1024], F32, "otm") for _ in range(2)]
            sgo = [self.sb(es2, [64, 1024], F32, "sgo") for _ in range(2)]
            mlo = [self.sb(es2, [64, 1024], BF16, "mlo") for _ in range(2)]
            mlT = [self.sb(es2, [128, 8, BT], BF16, "mlT") for _ in range(2)]
            tmpw = [self.sb(es2, [64, 64], F32, "tmpw") for _ in range(3)]
            wT = [self.sb(es2, [64, 64], F32, "wT") for _ in range(3)]
            ST = [self.sb(es2, [64, 64], BF16, "ST") for _ in range(3)]
            na = [self.sb(es2, [64, 257], F32, "na") for _ in range(3)]
            num = [self.sb(es2, [64, 257], F32, "num") for _ in range(3)]
            hh = [self.sb(es2, [64, 256], F32, "hh") for _ in range(3)]
            jk = [self.sb(es2, [64, 256], F32, "jk") for _ in range(3)]
            t1 = [self.sb(es2, [64, 256], F32, "t1") for _ in range(3)]
            sm = [self.sb(es2, [64, 8], F32, "sm") for _ in range(3)]
            kw = [self.sb(es2, [64, 256], BF16, "kw") for _ in range(3)]
            for i in range(2):
                S.op('pool', lambda e, i=i: e.memset(v1[i][:], 1.0), writes=[('v1', i)])
            NSLOT = 3

            def stage_u(c, h, i2, bi_, cl, wlc, wend):
                    k_ = lambda nm: (nm, i2)
                    act(lambda e, i2=i2, bi_=bi_, cl=cl, h=h, wlc=wlc: e.activation(out=kw[i2][:], in_=ktm[bi_][:, cl, h * 256:(h + 1) * 256], func=AF.Copy,
                                                                               scale=wlc),
                        [('ktm', bi_), 'WLC', 'WLCp'], [k_('kw')])
                    for dh in range(2):
                        ps_c, pkc = self.psum_next()
                        S.op('pe', lambda e, ps_c=ps_c, i2=i2, dh=dh, bi_=bi_, cl=cl, h=h: e.matmul(
                            ps_c[:, 0:257], lhsT=kw[i2][:, dh * 128:(dh + 1) * 128], rhs=v1[bi_][:, cl, h, :], start=True, stop=True),
                            reads=[k_('kw'), ('v1', bi_)], writes=[pkc])
                        dve(lambda e, ps_c=ps_c, h=h, dh=dh, wend=wend: e.scalar_tensor_tensor(out=Cf[:, h, dh, :], in0=Cf[:, h, dh, :], scalar=wend,
                                                                                       in1=ps_c[:, 0:257], op0=ALU.mult, op1=ALU.add),
                            [pkc, ('Cf', h, dh), 'WE', 'WEp'], [('Cf', h, dh)])
                        S.op('pool', lambda e, h=h, dh=dh: e.tensor_copy(out=Cb[:, h, dh, :], in_=Cf[:, h, dh, :]),
                             reads=[('Cf', h, dh)], writes=[('Cb', h)])


            def stage_a(c, h, i2, bi_, cl, ci, tsl):
                    k_ = lambda nm: (nm, i2)
                    ps_s, pks = self.psum_next()
                    for dh in range(2):
                        S.op('pe', lambda e, ps_s=ps_s, h=h, dh=dh, bi_=bi_, tsl=tsl: e.matmul(
                            ps_s[:64, 0:64], lhsT=kT[bi_][:, h * 2 + dh, tsl], rhs=qT[bi_][:, h * 2 + dh, tsl], start=(dh == 0), stop=(dh == 1)),
                            reads=[('kT', bi_), ('qT', bi_)], writes=[pks])
                    dve(lambda e, i2=i2, h=h, c=c: e.tensor_tensor(out=tmpw[i2][:], in0=maskT[:], in1=Rb[:, h, c * CL:(c + 1) * CL], op=ALU.subtract),
                        ['maskT', 'Rb'], [k_('tmpw')])
                    act(lambda e, i2=i2, h=h, c=c: e.activation(out=wT[i2][:], in_=tmpw[i2][:], func=AF.Exp, bias=QC['XC'][:, c, h:h + 1]),
                        [k_('tmpw'), 'XC'], [k_('wT')])
                    dve(lambda e, i2=i2, ps_s=ps_s: e.scalar_tensor_tensor(out=ST[i2][:], in0=ps_s[:64, 0:64], scalar=1.0 / 16, in1=wT[i2][:],
                                                                         op0=ALU.mult, op1=ALU.mult), [pks, k_('wT')], [k_('ST')])
                    ps_a, pka = self.psum_next()
                    S.op('pe', lambda e, ps_a=ps_a, i2=i2, bi_=bi_, cl=cl, h=h: e.matmul(ps_a[:64, 0:257], lhsT=ST[i2][:], rhs=v1[bi_][:, cl, h, :],
                                                                                        start=True, stop=True),
                         reads=[k_('ST'), ('v1', bi_)], writes=[pka])
                    ps_b, pkb = self.psum_next()
                    for dh in range(2):
                        S.op('pe', lambda e, ps_b=ps_b, h=h, dh=dh, bi_=bi_, tsl=tsl: e.matmul(
                            ps_b[:64, 0:257], lhsT=qT[bi_][:, h * 2 + dh, tsl], rhs=Cb[:, h, dh, :], start=(dh == 0), stop=(dh == 1)),
                            reads=[('qT', bi_), ('Cb', h)], writes=[pkb])
                    act(lambda e, i2=i2, ps_a=ps_a: e.copy(out=na[i2][:], in_=ps_a[:64, 0:257]), [pka], [k_('na')])
                    dve(lambda e, i2=i2, ps_b=ps_b, c=c, h=h: e.scalar_tensor_tensor(out=num[i2][:], in0=ps_b[:64, 0:257], scalar=QC['WIC'][:, c, h:h + 1],
                                                                                   in1=na[i2][:], op0=ALU.mult, op1=ALU.add),
                        [pkb, k_('na'), 'WIC'], [k_('num')])

            def stage_b(c, h, i2, bi_, cl, ci, tsl):
                    k_ = lambda nm: (nm, i2)
                    act(lambda e, i2=i2: e.activation(out=sm[i2][:, 6:7], in_=num[i2][:, 256:257], func=AF.Abs), [k_('num')], [k_('sm')])
                    dve(lambda e, i2=i2, c=c, h=h: e.tensor_scalar(out=sm[i2][:, 0:1], in0=sm[i2][:, 6:7], scalar1=QC['EMC'][:, c, h:h + 1],
                                                                   scalar2=None, op0=ALU.max), [k_('sm'), 'EMC'], [k_('sm')])
                    dve(lambda e, i2=i2: e.reciprocal(out=sm[i2][:, 1:2], in_=sm[i2][:, 0:1]), [k_('sm')], [k_('sm')])
                    act(lambda e, i2=i2: e.activation(out=hh[i2][:], in_=num[i2][:, 0:256], func=AF.Copy, scale=sm[i2][:, 1:2]),
                        [k_('num'), k_('sm')], [k_('hh')])
                    act(lambda e, i2=i2: e.activation(out=jk[i2][:], in_=hh[i2][:], func=AF.Square, accum_out=sm[i2][:, 2:3]),
                        [k_('hh')], [k_('jk'), k_('sm')])
                    dve(lambda e, i2=i2: e.tensor_scalar(out=sm[i2][:, 3:4], in0=sm[i2][:, 2:3], scalar1=1.0 / 256, scalar2=EPS, op0=ALU.mult, op1=ALU.add),
                        [k_('sm')], [k_('sm')])
                    act(lambda e, i2=i2: e.activation(out=sm[i2][:, 4:5], in_=sm[i2][:, 3:4], func=AF.Sqrt), [k_('sm')], [k_('sm')])
                    dve(lambda e, i2=i2: e.reciprocal(out=sm[i2][:, 5:6], in_=sm[i2][:, 4:5]), [k_('sm')], [k_('sm')])
                    dve(lambda e, i2=i2, h=h: e.scalar_tensor_tensor(out=t1[i2][:], in0=hh[i2][:], scalar=sm[i2][:, 5:6], in1=GH[:, h * 256:(h + 1) * 256],
                                                                   op0=ALU.mult, op1=ALU.mult), [k_('hh'), k_('sm'), 'GH'], [k_('t1')])
                    dve(lambda e, i2=i2, h=h, ci=ci: e.tensor_tensor(out=mlo[ci][:, h * 256:(h + 1) * 256], in0=t1[i2][:],
                                                                   in1=sgo[ci][:, h * 256:(h + 1) * 256], op=ALU.mult),
                        [k_('t1'), ('sgo', ci)], [('mlo', ci)])
                    stage_u(c, h, i2, bi_, cl, QC['WLC'][:, c, h:h + 1], WE[:, h, c:c + 1])

            blend = lambda dst, src_b, n_, dk, sk: (
                dve(lambda e: e.tensor_scalar(out=dst, in0=dst, scalar1=flm[:n_, 0:1], scalar2=None, op0=ALU.mult), [dk, 'flm'], [dk]),
                dve(lambda e: e.scalar_tensor_tensor(out=dst, in0=src_b, scalar=flm[:n_, 1:2], in1=dst, op0=ALU.mult, op1=ALU.add), [dk, sk, 'flm'], [dk]))

            def load_kv(bi_, t0):
                for c8 in range(CPB):
                    S.dma('sp', lambda e, c8=c8: e.dma_start(
                        out=v1[bi_][:, c8, :, 0:256], in_=R['VTM'][t0 + c8 * CL:t0 + (c8 + 1) * CL, :].rearrange("s (h v) -> s h v", h=4)),
                        reads=[('v1', bi_)], writes=[('v1', bi_)])
                S.dma('sp', lambda e: e.dma_start(out=ktm[bi_][:], in_=R['KTM'][t0:t0 + BT, :].rearrange("(c s) f -> s c f", s=CL)),
                      writes=[('ktm', bi_)])

            def pre_prefix(c, bi_):
                if c % CPB == 0:
                    load_kv(bi_, (c // CPB) * BT)

            def pre_own(co, bi_):
                if co % CPB == 0:
                    t0 = (co // CPB) * BT
                    load_kv(bi_, t0)
                    for c8 in range(CPB):
                        S.dma('sp', lambda e, c8=c8: e.dma_start(
                            out=v1B[:, c8, :, :], in_=R['VTM'][HPR + t0 + c8 * CL:HPR + t0 + (c8 + 1) * CL, :].rearrange("s (h v) -> s h v", h=4)),
                            writes=['v1B'])
                    S.dma('sp', lambda e: e.dma_start(out=ktmB[:], in_=R['KTM'][HPR + t0:HPR + t0 + BT, :].rearrange("(c s) f -> s c f", s=CL)),
                          writes=['ktmB'])
                    blend(v1[bi_][:, :, :, 0:256], v1B[:], 64, ('v1', bi_), 'v1B')
                    blend(ktm[bi_][:], ktmB[:], 64, ('ktm', bi_), 'ktmB')
                    S.dma('sp', lambda e: e.dma_start(out=qT[bi_][:], in_=R['QTo'][:, t0:t0 + BT].rearrange("(k p) t -> p k t", p=128)),
                          writes=[('qT', bi_)])
                    S.dma('sp', lambda e: e.dma_start(out=kT[bi_][:], in_=R['KT'][:, t0:t0 + BT].rearrange("(k p) t -> p k t", p=128)),
                          writes=[('kT', bi_)])
                    S.dma('sp', lambda e: e.dma_start(out=qTB[:], in_=R['KT'][:, HPR + t0:HPR + t0 + BT].rearrange("(k p) t -> p k t", p=128)),
                          writes=['qTB'])
                    blend(kT[bi_][:], qTB[:], 128, ('kT', bi_), 'qTB')
                ci = co % 2
                S.dma('sp', lambda e: e.dma_start(out=otm[ci][:], in_=R['OTMo'][co * CL:(co + 1) * CL, :]), writes=[('otm', ci)])
                act(lambda e: e.activation(out=sgo[ci][:], in_=otm[ci][:], func=AF.Sigmoid), [('otm', ci)], [('sgo', ci)])

            def chunk_post(co, bi_):
                cl = co % CPB
                ci = co % 2
                tsl = slice(cl * CL, (cl + 1) * CL)
                pb, pbk = self.psumb_next()
                for k in range(8):
                    S.op('pe', lambda e, pb=pb, k=k, ci=ci: e.transpose(out=pb[:, k * 64:(k + 1) * 64], in_=mlo[ci][:64, k * 128:(k + 1) * 128],
                                                                      identity=self.identb[:64, :64]), reads=[('mlo', ci), 'identb'], writes=[pbk])
                act(lambda e, pb=pb: e.copy(out=mlT[bi_][:, :, tsl], in_=pb[:, 0:512].rearrange("p (k t) -> p k t", k=8)),
                    [pbk], [('mlT', bi_)])
                if cl == CPB - 1:
                    t0 = (co // CPB) * BT
                    S.dma('sp', lambda e: e.dma_start(out=R['BR1'][:, t0:t0 + BT].rearrange("(k p) t -> p k t", p=128), in_=mlT[bi_][:]),
                          reads=[('mlT', bi_)], writes=[('dram', 'br1', t0)])

            n_ = 0
            nblk_pre = NH // CPB
            for c in range(NH):
                bi_ = (c // CPB) % 2
                pre_prefix(c, bi_)
                for h in range(4):
                    stage_u(c, h, n_ % NSLOT, bi_, c % CPB, WLCp[:, c, h:h + 1], WEp[:, h, c:c + 1])
                    n_ += 1
            prev = None
            for co in range(NH):
                bi_ = (nblk_pre + co // CPB) % 2
                for h in range(4):
                    if h == 0:
                        pre_own(co, bi_)
                    cl = co % CPB
                    args = (co, h, n_ % NSLOT, bi_, cl, co % 2, slice(cl * CL, (cl + 1) * CL))
                    n_ += 1
                    stage_a(*args)
                    if prev is not None:
                        stage_b(*prev)
                        if prev[1] == 3:
                            chunk_post(prev[0], prev[3])
                    prev = args
            stage_b(*prev)
            chunk_post(prev[0], prev[3])
            for h in range(4):
                for dh in range(2):
                    S.dma('sp', lambda e, h=h, dh=dh: e.dma_start(out=O['C_p'][h, dh * 128:(dh + 1) * 128, :], in_=Cf[:, h, dh, 0:256]),
                          reads=[('Cf', h, dh)], writes=[('dram', 'C_p', h, dh)])
                    S.dma('sp', lambda e, h=h, dh=dh: e.dma_start(out=O['n_p'][h, dh * 128:(dh + 1) * 128].rearrange("(p o) -> p o", o=1),
                                                                 in_=Cf[:, h, dh, 256:257]),
                          reads=[('Cf', h, dh)], writes=[('dram', 'n_p', h, dh)])
            S.barrier()
            S.flush()
            es2.close()

    def mlstm_sample(self):
        S = self.S
        I = self.I
        R = self.R
        O = self.O
        B = NSM
        dve = lambda fn, reads, writes: S.op('dve', fn, reads=reads, writes=writes)
        act = lambda fn, reads, writes: S.op('act', fn, reads=reads, writes=writes)
        with ExitStack() as es:
            sc = {nm: self.sb(es, [B, 4], F32, "ms_" + nm) for nm in
                  ['g', 'bi', 'bf', 'iv', 'l', 'm0', 'gi', 'mt', 'wi', 'wa', 'em', 'qk', 'qn', 's', 'nq', 'rd', 'ss', 'rs', 't']}
            G8 = self.sb(es, [B, 8], F32, "G8")
            qS = self.sb(es, [128, 8, B], BF16, "qSm")
            qtm = self.sb(es, [B, 1024], BF16, "qtmm")
            ktm = self.sb(es, [B, 1024], BF16, "ktmm")
            vtm = self.sb(es, [B, 1024], BF16, "vtmm")
            otm = self.sb(es, [B, 1024], F32, "otmm")
            n0 = self.sb(es, [B, 1024], F32, "n0m")
            GH = self.sb(es, [B, 1024], F32, "GHm")
            p1 = self.sb(es, [B, 1024], F32, "p1m")
            p2 = self.sb(es, [B, 1024], F32, "p2m")
            numt = self.sb(es, [B, 1024], F32, "numt")
            kws = self.sb(es, [B, 1024], BF16, "kws")
            Km = [self.sb(es, [B, 1024], BF16, "Km") for _ in range(2)]
            mlb = self.sb(es, [B, 1024], BF16, "mlb")
            mlT = self.sb(es, [128, 8, B], BF16, "mlTs")
            IDB = self.sb(es, [128, B, B], BF16, "IDB")
            Qm = self.sb(es, [128, 8, B, B], BF16, "Qm")
            WD = self.sb(es, [B, B, 4], F32, "WD")
            ones32 = self.sb(es, [B, 128], F32, "ones32")
            Wb = self.sb(es, [128, B * 4], F32, "Wb")
            C32 = [self.sb(es, [128, 4, 2, 256], F32, "C32") for _ in range(3)]
            C16 = [self.sb(es, [128, 4, 2, 256], BF16, "C16") for _ in range(2)]
            Co = [self.sb(es, [128, 4, 2, 256], F32, "Co") for _ in range(2)]
            S.dma('sp', lambda e: e.dma_start(out=G8[:], in_=R['GTM'][NPR:NTOK, :]), writes=['G8'])
            S.dma('sp', lambda e: e.dma_start(out=sc['bi'][:], in_=I['b_igate'].partition_broadcast(B)), writes=['bi'])
            S.dma('sp', lambda e: e.dma_start(out=sc['bf'][:], in_=I['b_fgate'].partition_broadcast(B)), writes=['bf'])
            S.dma('sp', lambda e: e.dma_start(out=sc['m0'][:], in_=I['mM'][:, :]), writes=['m0'])
            S.dma('sp', lambda e: e.dma_start(out=n0[:], in_=I['mN'][:, :]), writes=['n0'])
            S.dma('sp', lambda e: e.dma_start(out=GH[:], in_=I['g_mlstm_head'].partition_broadcast(B)), writes=['GH'])
            S.dma('sp', lambda e: e.dma_start(out=ktm[:], in_=R['KTM'][NPR:NTOK, :]), writes=['ktm'])
            S.dma('sp', lambda e: e.dma_start(out=vtm[:], in_=R['VTM'][NPR:NTOK, :]), writes=['vtm'])
            S.dma('sp', lambda e: e.dma_start(out=otm[:], in_=R['OTMo'][HPR:HT, :]), writes=['otm'])
            S.dma('sp', lambda e: e.dma_start(out=qS[:], in_=R['QTo'][:, HPR:HT].rearrange("(k p) t -> p k t", p=128)), writes=['qS'])
            for kq in range(0, 8, 4):
                pb, pbk = self.psumb_next()
                for j in range(4):
                    S.op('pe', lambda e, pb=pb, j=j, kq=kq: e.transpose(out=pb[:B, j * 128:(j + 1) * 128], in_=qS[:, kq + j, :],
                                                                       identity=self.identb[:]), reads=['qS', 'identb'], writes=[pbk])
                act(lambda e, pb=pb, kq=kq: e.copy(out=qtm[:, kq * 128:(kq + 4) * 128], in_=pb[:B, 0:512]), [pbk], ['qtm'])
            S.op('pool', lambda e: e.memset(IDB[:], 1.0), writes=['IDB'])
            S.op('pool', lambda e: e.affine_select(out=IDB[:], in_=IDB[:], pattern=[[1, B], [-1, B]], compare_op=ALU.is_equal, fill=0.0,
                                                   base=0, channel_multiplier=0), reads=['IDB'], writes=['IDB'])
            S.op('pool', lambda e: e.memset(ones32[:], 1.0), writes=['ones32'])
            for k in range(8):
                dve(lambda e, k=k: e.tensor_tensor(out=Qm[:, k, :, :], in0=qS[:, k, :].unsqueeze(1).to_broadcast([128, B, B]), in1=IDB[:], op=ALU.mult),
                    ['qS', 'IDB'], ['Qm'])
            tt = lambda o, a, b, op: dve(lambda e: e.tensor_tensor(out=sc[o][:], in0=sc[a][:], in1=sc[b][:], op=op), [a, b], [o])
            dve(lambda e: e.tensor_tensor(out=sc['iv'][:], in0=G8[:, 0:4], in1=sc['bi'][:], op=ALU.add), ['G8', 'bi'], ['iv'])
            dve(lambda e: e.tensor_tensor(out=sc['g'][:], in0=G8[:, 4:8], in1=sc['bf'][:], op=ALU.add), ['G8', 'bf'], ['g'])
            act(lambda e: e.activation(out=sc['l'][:], in_=sc['g'][:], func=AF.Exp, scale=-1.0), ['g'], ['l'])
            act(lambda e: e.activation(out=sc['l'][:], in_=sc['l'][:], func=AF.Ln, bias=1.0), ['l'], ['l'])
            tt('gi', 'm0', 'l', ALU.subtract)
            tt('mt', 'gi', 'iv', ALU.max)
            S.dma('sp', lambda e: e.dma_start(out=O['m_s'][:, :], in_=sc['mt'][:]), reads=['mt'], writes=[('dram', 'm_s')])
            tt('wi', 'gi', 'mt', ALU.subtract)
            act(lambda e: e.activation(out=sc['wi'][:], in_=sc['wi'][:], func=AF.Exp), ['wi'], ['wi'])
            tt('wa', 'iv', 'mt', ALU.subtract)
            act(lambda e: e.activation(out=sc['wa'][:], in_=sc['wa'][:], func=AF.Exp, bias=-math.log(16.0)), ['wa'], ['wa'])
            act(lambda e: e.activation(out=sc['em'][:], in_=sc['mt'][:], func=AF.Exp, scale=-1.0), ['mt'], ['em'])
            v4 = lambda t_: t_[:].rearrange("b (h d) -> b h d", h=4)
            bc = lambda nm: sc[nm][:].unsqueeze(2).to_broadcast([B, 4, 256])
            dve(lambda e: e.tensor_tensor(out=p1[:], in0=qtm[:], in1=ktm[:], op=ALU.mult), ['qtm', 'ktm'], ['p1'])
            dve(lambda e: e.tensor_reduce(out=sc['qk'][:], in_=v4(p1), axis=AX.X, op=ALU.add), ['p1'], ['qk'])
            dve(lambda e: e.tensor_tensor(out=p2[:], in0=qtm[:], in1=n0[:], op=ALU.mult), ['qtm', 'n0'], ['p2'])
            dve(lambda e: e.tensor_reduce(out=sc['qn'][:], in_=v4(p2), axis=AX.X, op=ALU.add), ['p2'], ['qn'])
            tt('s', 'qk', 'wa', ALU.mult)
            tt('t', 'wi', 'qn', ALU.mult)
            tt('nq', 's', 't', ALU.add)
            act(lambda e: e.activation(out=sc['nq'][:], in_=sc['nq'][:], func=AF.Abs), ['nq'], ['nq'])
            tt('rd', 'nq', 'em', ALU.max)
            dve(lambda e: e.reciprocal(out=sc['rd'][:], in_=sc['rd'][:]), ['rd'], ['rd'])
            dve(lambda e: e.tensor_tensor(out=v4(kws), in0=v4(ktm), in1=bc('wa'), op=ALU.mult), ['ktm', 'wa'], ['kws'])
            dve(lambda e: e.tensor_tensor(out=v4(p1), in0=v4(n0), in1=bc('wi'), op=ALU.mult), ['n0', 'wi'], ['p1'])
            dve(lambda e: e.tensor_tensor(out=p1[:], in0=p1[:], in1=kws[:], op=ALU.add), ['p1', 'kws'], ['p1'])
            S.dma('sp', lambda e: e.dma_start(out=O['n_s'][:, :], in_=p1[:]), reads=['p1'], writes=[('dram', 'n_s')])
            dve(lambda e: e.tensor_tensor(out=WD[:], in0=self.identf[:B, :B].unsqueeze(2).to_broadcast([B, B, 4]),
                                          in1=sc['wi'][:].unsqueeze(1).to_broadcast([B, B, 4]), op=ALU.mult), ['identf', 'wi'], ['WD'])
            ps, pk = self.psum_next()
            S.op('pe', lambda e, ps=ps: e.matmul(ps[:, 0:B * 4], lhsT=ones32[:], rhs=WD[:].rearrange("k b h -> k (b h)"), start=True, stop=True),
                 reads=['ones32', 'WD'], writes=[pk])
            dve(lambda e, ps=ps: e.tensor_copy(out=Wb[:], in_=ps[:, 0:B * 4]), [pk], ['Wb'])
            self.pa_i = 0
            pq = [self.psum_next() for _ in range(4)]
            def ms_load(b):
                i2 = b % 3
                for h in range(4):
                    S.dma('sp', lambda e, i2=i2, b=b, h=h: e.dma_start(out=C32[i2][:, h, :, :], in_=I['mC'][b, h].rearrange("(dh p) v -> p dh v", p=128)),
                          writes=[('C32', i2)])

            def ms_compute(b):
                i2 = b % 2
                i3 = b % 3
                S.op('pool', lambda e, i2=i2, i3=i3: e.tensor_copy(out=C16[i2][:], in_=C32[i3][:]), reads=[('C32', i3)], writes=[('C16', i2)])
                act(lambda e, i2=i2, b=b: e.activation(out=Km[i2][:], in_=kws[:], func=AF.Copy, scale=self.identf[:B, b:b + 1]),
                    ['kws', 'identf'], [('Km', i2)])
                for h in range(4):
                    psq, pqk = pq[h]
                    for dh in range(2):
                        S.op('pe', lambda e, psq=psq, b=b, h=h, dh=dh, i2=i2: e.matmul(
                            psq[:B, 0:256], lhsT=Qm[:, h * 2 + dh, b, :], rhs=C16[i2][:, h, dh, :],
                            start=(b == 0 and dh == 0), stop=(b == B - 1 and dh == 1)), reads=['Qm', ('C16', i2)], writes=[pqk])
                    for dh in range(2):
                        psc, pck = self.psumc_next()
                        S.op('pe', lambda e, psc=psc, i2=i2, h=h, dh=dh: e.matmul(
                            psc[:, 0:256], lhsT=Km[i2][:, h * 256 + dh * 128:h * 256 + (dh + 1) * 128], rhs=vtm[:, h * 256:(h + 1) * 256],
                            start=True, stop=True), reads=[('Km', i2), 'vtm'], writes=[pck])
                        dve(lambda e, psc=psc, i2=i2, i3=i3, b=b, h=h, dh=dh: e.scalar_tensor_tensor(
                            out=Co[i2][:, h, dh, :], in0=C32[i3][:, h, dh, :], scalar=Wb[:, b * 4 + h:b * 4 + h + 1], in1=psc[:, 0:256],
                            op0=ALU.mult, op1=ALU.add), [pck, ('C32', i3), 'Wb'], [('Co', i2)])

            def ms_store(b):
                i2 = b % 2
                for h in range(4):
                    S.dma('sp', lambda e, i2=i2, b=b, h=h: e.dma_start(out=O['C_s'][b, h].rearrange("(dh p) v -> p dh v", p=128), in_=Co[i2][:, h, :, :]),
                          reads=[('Co', i2)], writes=[('dram', 'C_s', b, h)])

            ms_load(0)
            ms_load(1)
            for b in range(B):
                if b + 2 < B:
                    ms_load(b + 2)
                ms_compute(b)
                ms_store(b)
            dve(lambda e: e.tensor_tensor(out=v4(numt), in0=v4(vtm), in1=bc('s'), op=ALU.mult), ['vtm', 's'], ['numt'])
            for h in range(4):
                psq, pqk = pq[h]
                dve(lambda e, psq=psq, h=h: e.scalar_tensor_tensor(out=numt[:, h * 256:(h + 1) * 256], in0=psq[:B, 0:256], scalar=sc['wi'][:, h:h + 1],
                                                                 in1=numt[:, h * 256:(h + 1) * 256], op0=ALU.mult, op1=ALU.add),
                    [pqk, 'wi', 'numt'], ['numt'])
            dve(lambda e: e.tensor_tensor(out=v4(numt), in0=v4(numt), in1=bc('rd'), op=ALU.mult), ['numt', 'rd'], ['numt'])
            dve(lambda e: e.tensor_tensor(out=p2[:], in0=numt[:], in1=numt[:], op=ALU.mult), ['numt'], ['p2'])
            dve(lambda e: e.tensor_reduce(out=sc['ss'][:], in_=v4(p2), axis=AX.X, op=ALU.add), ['p2'], ['ss'])
            dve(lambda e: e.tensor_scalar(out=sc['ss'][:], in0=sc['ss'][:], scalar1=1.0 / 256, scalar2=EPS, op0=ALU.mult, op1=ALU.add), ['ss'], ['ss'])
            act(lambda e: e.activation(out=sc['rs'][:], in_=sc['ss'][:], func=AF.Sqrt), ['ss'], ['rs'])
            dve(lambda e: e.reciprocal(out=sc['rs'][:], in_=sc['rs'][:]), ['rs'], ['rs'])
            dve(lambda e: e.tensor_tensor(out=v4(numt), in0=v4(numt), in1=bc('rs'), op=ALU.mult), ['numt', 'rs'], ['numt'])
            dve(lambda e: e.tensor_tensor(out=numt[:], in0=numt[:], in1=GH[:], op=ALU.mult), ['numt', 'GH'], ['numt'])
            act(lambda e: e.activation(out=otm[:], in_=otm[:], func=AF.Sigmoid), ['otm'], ['otm'])
            dve(lambda e: e.tensor_tensor(out=mlb[:], in0=numt[:], in1=otm[:], op=ALU.mult), ['numt', 'otm'], ['mlb'])
            self.transpose_into(mlb, 'mlb', B, 8, mlT, 'mlTs', 0)
            S.dma('sp', lambda e: e.dma_start(out=R['BR1'][:, HPR:HT].rearrange("(k p) t -> p k t", p=128), in_=mlT[:]),
                  reads=['mlTs'], writes=[('dram', 'brt1s')])
            S.barrier()
            S.flush()

    def psumc_next(self):
        i = 4 + (self.pc_i % 2)
        self.pc_i += 1
        return self.PA[i], ('pa', i)

    def final_norm(self, xsrc, ydst, gvec, nrows=NTOK):
        S = self.S
        with ExitStack() as es:
            gb = self.sb(es, [128, D], F32, "gbf")
            xts = [self.sb(es, [128, D], F32, "xtf") for _ in range(2)]
            ys = [self.sb(es, [128, D], F32, "yf") for _ in range(2)]
            st = self.sb(es, [128, 8], F32, "stf")
            self.load_gain(gb, gvec, 'gbf')
            tiles = token_tiles(0, nrows)

            def fn_load(i):
                r0, nr = tiles[i]
                xt = xts[i % 2]
                xk = ('xtf', i % 2)
                S.dma('sp', lambda e: e.dma_start(out=xt[:nr, :], in_=xsrc[r0:r0 + nr, :]), writes=[xk])
            fn_load(0)
            for i, (r0, nr) in enumerate(tiles):
                if i + 1 < len(tiles):
                    fn_load(i + 1)
                xt = xts[i % 2]
                y = ys[i % 2]
                xk = ('xtf', i % 2)
                yk = ('yf', i % 2)
                S.op('act', lambda e, y=y, xt=xt, nr=nr: e.activation(out=y[:nr, :], in_=xt[:nr, :], func=AF.Square,
                                                                     accum_out=st[:nr, 0:1]), reads=[xk], writes=[yk, 'stf'])
                S.op('dve', lambda e, nr=nr: e.tensor_scalar(out=st[:nr, 1:2], in0=st[:nr, 0:1], scalar1=1.0 / D, scalar2=EPS,
                                                             op0=ALU.mult, op1=ALU.add), reads=['stf'], writes=['stf'])
                S.op('act', lambda e, nr=nr: e.activation(out=st[:nr, 2:3], in_=st[:nr, 1:2], func=AF.Sqrt), reads=['stf'], writes=['stf'])
                S.op('dve', lambda e, nr=nr: e.reciprocal(out=st[:nr, 3:4], in_=st[:nr, 2:3]), reads=['stf'], writes=['stf'])
                S.op('dve', lambda e, y=y, xt=xt, nr=nr: e.scalar_tensor_tensor(out=y[:nr, :], in0=xt[:nr, :], scalar=st[:nr, 3:4],
                                                                               in1=gb[:nr, :], op0=ALU.mult, op1=ALU.mult),
                     reads=[xk, 'stf', 'gbf'], writes=[yk])
                S.dma('sp', lambda e, y=y, r0=r0, nr=nr: e.dma_start(out=ydst[r0:r0 + nr, :], in_=y[:nr, :]), reads=[yk],
                      writes=[('dram', 'y', r0)])
            S.barrier()
            S.flush()

    def build(self):
        nc = self.nc
        dbg = self.dbg
        self.I = I = {}
        I['x'] = self.dram_in('x', [NTOK, D])
        I['mem'] = self.dram_in('mem', [256, D])
        I['cache_k'] = self.dram_in('cache_k', [NSM, 256, 1024])
        I['cache_v'] = self.dram_in('cache_v', [NSM, 256, 1024])
        I['s5r'] = self.dram_in('s5r', [NSM, 4096])
        I['s5i'] = self.dram_in('s5i', [NSM, 4096])
        I['mC'] = self.dram_in('mC', [NSM, 4, 256, 256])
        I['mN'] = self.dram_in('mN', [NSM, 1024])
        I['mM'] = self.dram_in('mM', [NSM, 4])
        I['flag'] = self.dram_in('flag', [128, 2])
        for nm, shp in [('g_ffn1', [D]), ('w1_gate', [D, FF]), ('w1_up', [D, FF]), ('w1_down', [FF, D]),
                        ('g_mix', [D]), ('w_in', [D, DIN]),
                        ('s5_lambda_re', [64, 64]), ('s5_lambda_im', [64, 64]), ('s5_log_step', [64]),
                        ('s5_b_re', [64, 64, 16]), ('s5_b_im', [64, 64, 16]), ('s5_c_re', [64, 16, 64]),
                        ('s5_c_im', [64, 16, 64]), ('s5_d', [64, 16]), ('w_s5_glu', [1024, 1024]),
                        ('b_igate', [4]), ('b_fgate', [4]), ('g_mlstm_head', [1024]), ('g_mem', [D]),
                        ('w_mem_k', [D, 1024]), ('w_mem_v', [D, 1024]),
                        ('w_br_s5', [1024, D]), ('w_br_ml', [1024, D]), ('w_br_xa', [1024, D]), ('w_out', [D, D]),
                        ('g_ffn2', [D]), ('w2_gate', [D, FF]), ('w2_up', [D, FF]), ('w2_down', [FF, D]),
                        ('g_final', [D])]:
            I[nm] = self.dram_in(nm, shp)
        self.O = O = {}
        O['y'] = self.dram_out('y', [HT, D])
        O['mk'] = self.dram_out('mk', [256, 1024])
        O['mv'] = self.dram_out('mv', [256, 1024])
        O['s5r_p'] = self.dram_out('s5r_p', [64, 64])
        O['s5i_p'] = self.dram_out('s5i_p', [64, 64])
        O['C_p'] = self.dram_out('C_p', [4, 256, 256])
        O['n_p'] = self.dram_out('n_p', [4, 256])
        O['m_p'] = self.dram_out('m_p', [4, 1])
        O['s5r_s'] = self.dram_out('s5r_s', [NSM, 4096])
        O['s5i_s'] = self.dram_out('s5i_s', [NSM, 4096])
        O['C_s'] = self.dram_out('C_s', [NSM, 4, 256, 256])
        O['n_s'] = self.dram_out('n_s', [NSM, 1024])
        O['m_s'] = self.dram_out('m_s', [NSM, 4])
        scr = self.dram_out if dbg else (lambda n, s, d=F32: self.dram_scr(n, s, d))
        self.R = R = {}
        R['X1'] = scr('X1', [NTOK, D], F32)
        R['X1h'] = scr('X1h', [HT, D], F32)
        R['X2'] = scr('X2', [HT, D], F32)
        R['X3'] = scr('X3', [HT, D], F32)
        pscr = self.dram_in if self.mixtest else scr
        for nm in ['UT', 'KT']:
            R[nm] = pscr(nm, [1024, NTOK], BF16)
        for nm in ['QTo', 'QXTo']:
            R[nm] = pscr(nm, [1024, HT], BF16)
        R['OTMo'] = pscr('OTMo', [HT, 1024], F32)
        for nm in ['KTM', 'VTM']:
            R[nm] = pscr(nm, [NTOK, 1024], BF16)
        R['GTM'] = pscr('GTM', [NTOK, 8], F32)
        R['IGR'] = pscr('IGR', [4, NTOK], F32)
        R['FGR'] = pscr('FGR', [4, NTOK], F32)
        R['MKT'] = pscr('MKT', [1024, 256], BF16)
        R['MVB'] = self.dram_in('MVB', [256, 1024]) if self.mixtest else O['mv']
        R['BRT'] = scr('BRT', [3, 1024, NTOK], BF16)
        R['XAS'] = scr('XAS', [NSM, 1024], F32)
        R['S5G'] = scr('S5G', [1024, HT], BF16)
        R['BR0'] = scr('BR0', [1024, HT], BF16)
        R['BR2'] = scr('BR2', [1024, HT], BF16)
        R['BR1'] = scr('BR1', [1024, HT], BF16)
        with ExitStack() as es:
            self.S = S = Sched(nc, es)
            S.block = es.enter_context(nc.Block())
            self.PA = [es.enter_context(nc.psum_tensor("pa%d" % i, [128, 512], F32)) for i in range(6)]
            self.PB = [es.enter_context(nc.psum_tensor("pb%d" % i, [128, 1024], BF16)) for i in range(2)]
            self.pa_i = 0
            self.pb_i = 0
            self.ecnt = 0
            self.pc_i = 0
            identf = self.sb(es, [128, 128], F32, "identf")
            self.identb = self.sb(es, [128, 128], BF16, "identb")
            self.identf = identf
            S.op('pool', lambda e: e.memset(identf[:], 1.0), writes=['identf'])
            S.op('pool', lambda e: e.affine_select(out=identf[:], in_=identf[:], pattern=[[-1, 128]], compare_op=ALU.is_equal,
                                                   fill=0.0, base=0, channel_multiplier=1), reads=['identf'], writes=['identf'])
            S.op('dve', lambda e: e.tensor_copy(out=self.identb[:], in_=identf[:]), reads=['identf'], writes=['identb'])
            S.barrier()
            st = self.stage
            if not self.mixtest:
                for blk in BLOCKS:
                    self.ffn(es, I['x'], R['X1'], I['g_ffn1'], I['w1_gate'], I['w1_up'], I['w1_down'], blk)
            if st >= 2 and not self.mixtest:
                self.inproj()
                self.memkv()
            if st >= 3:
                self.mixers()
            if st >= 4:
                self.merge()
                self.ffn(es, R['X2'], R['X3'], I['g_ffn2'], I['w2_gate'], I['w2_up'], I['w2_down'], (0, HT))
                self.final_norm(R['X3'], O['y'], I['g_final'], nrows=HT)
            S.barrier()
            S.flush()
        return nc


_NC_CACHE = {}


def _get_nc():
    if 'nc' not in _NC_CACHE:
        _NC_CACHE['nc'] = Builder(stage=99, dbg=False).build()
    return _NC_CACHE['nc']


_WEIGHTS = ['g_ffn1', 'w1_gate', 'w1_up', 'w1_down', 'g_mix', 'w_in', 's5_lambda_re', 's5_lambda_im', 's5_log_step',
            's5_b_re', 's5_b_im', 's5_c_re', 's5_c_im', 's5_d', 'w_s5_glu', 'b_igate', 'b_fgate', 'g_mlstm_head', 'g_mem',
            'w_mem_k', 'w_mem_v', 'w_br_s5', 'w_br_ml', 'w_br_xa', 'w_out', 'g_ffn2', 'w2_gate', 'w2_up', 'w2_down']


def kernel(**inputs):
    f = lambda a: np.ascontiguousarray(np.asarray(a, dtype=np.float32))
    nc = _get_nc()
    shared = {nm: f(inputs[nm])[0] for nm in _WEIGHTS}
    shared['g_final'] = f(inputs['g_final'])
    xp = f(inputs['x_prompt'])
    xs = f(inputs['x_sample'])
    memp = f(inputs['mem_prompt'])
    ck = f(inputs['cache_mem_k'])[0]
    cv = f(inputs['cache_mem_v'])[0]
    s5r = f(inputs['state_s5_re'])[0]
    s5i = f(inputs['state_s5_im'])[0]
    mC = f(inputs['state_mlstm_C'])[0]
    mN = f(inputs['state_mlstm_n'])[0]
    mM = f(inputs['state_mlstm_m'])[0]
    in_maps = []
    for i in range(8):
        j = i % 4
        s0 = j * 2 * NSM + (i // 4) * NSM
        sl = slice(s0, s0 + NSM)
        m = dict(shared)
        m['x'] = np.ascontiguousarray(np.concatenate([xp[j], xs[sl, 0, :]], axis=0))
        m['mem'] = memp[j]
        m['cache_k'] = np.ascontiguousarray(ck[sl].reshape(NSM, 256, 1024))
        m['cache_v'] = np.ascontiguousarray(cv[sl].reshape(NSM, 256, 1024))
        m['s5r'] = np.ascontiguousarray(s5r[sl].reshape(NSM, 4096))
        m['s5i'] = np.ascontiguousarray(s5i[sl].reshape(NSM, 4096))
        m['mC'] = np.ascontiguousarray(mC[sl])
        m['mN'] = np.ascontiguousarray(mN[sl].reshape(NSM, 1024))
        m['mM'] = np.ascontiguousarray(mM[sl])
        fl = np.zeros((128, 2), np.float32)
        fl[:, 0 if i < 4 else 1] = 1.0
        m['flag'] = fl
        in_maps.append(m)
    res = run_bass_kernel_spmd(nc, in_maps, core_ids=list(range(8)))
    r = res.results
    g = lambda nm, j: np.asarray(r[j][nm], dtype=np.float32)
    order = [c for j in range(4) for c in (j, j + 4)]
    y_prompt = np.stack([np.concatenate([g('y', j)[:HPR], g('y', j + 4)[:HPR]], axis=0) for j in range(4)])
    y_sample = np.concatenate([g('y', c)[HPR:HT] for c in order], axis=0).reshape(8 * NSM, 1, D)
    mk = np.stack([g('mk', j).reshape(256, 4, 256) for j in range(4)])[None]
    mv = np.stack([g('mv', j).reshape(256, 4, 256) for j in range(4)])[None]
    s5r_p = np.stack([g('s5r_p', j + 4) for j in range(4)])[None]
    s5i_p = np.stack([g('s5i_p', j + 4) for j in range(4)])[None]
    C_p = np.stack([g('C_p', j + 4) for j in range(4)])[None]
    n_p = np.stack([g('n_p', j + 4) for j in range(4)])[None]
    m_p = np.stack([g('m_p', j)[:, 0] for j in range(4)])[None]
    s5r_s = np.concatenate([g('s5r_s', c).reshape(NSM, 64, 64) for c in order], axis=0)[None]
    s5i_s = np.concatenate([g('s5i_s', c).reshape(NSM, 64, 64) for c in order], axis=0)[None]
    C_s = np.concatenate([g('C_s', c) for c in order], axis=0)[None]
    n_s = np.concatenate([g('n_s', c).reshape(NSM, 4, 256) for c in order], axis=0)[None]
    m_s = np.concatenate([g('m_s', c) for c in order], axis=0)[None]
    return (y_prompt, y_sample, mk, mv, s5r_p, s5i_p, C_p, n_p, m_p, s5r_s, s5i_s, C_s, n_s, m_s)
```

```python
import math
import numpy as np
import concourse.bass as bass
import concourse.mybir as mybir
from concourse.bass_utils import run_bass_kernel_spmd
from contextlib import ExitStack

F32 = mybir.dt.float32
BF16 = mybir.dt.bfloat16
I32 = mybir.dt.int32
AF = mybir.ActivationFunctionType
ALU = mybir.AluOpType
AX = mybir.AxisListType

D = 2048
FF = 5504
NPR = 2048
NSM = 16
NTOK = NPR + NSM
BLOCKS = [(0, 1024), (1024, NTOK)]
HPR = NPR // 2
HT = HPR + NSM
EPS = 1e-6
NDS = 24
POOL_RING_LIMIT = 700
DIN = 12296


class Sched:
    ENG = ['pe', 'dve', 'act', 'pool', 'sp']

    def __init__(self, nc, es):
        self.nc = nc
        self.prog = {e: [] for e in self.ENG}
        self.sem = {e: es.enter_context(nc.semaphore("s_" + e)) for e in self.ENG}
        self.cnt = {e: 0 for e in self.ENG}
        self.seen = {e: {} for e in self.ENG}
        self.dsem = [es.enter_context(nc.semaphore("d%d" % i)) for i in range(NDS)]
        self.dcount = [0] * NDS
        self.dnext = 0
        self.last_w = {}
        self.readers = {}
        self.pool_pending = []

    def _need(self, eng, tok):
        kind, ident, v = tok
        if v <= 0:
            return False
        if kind == 'e' and ident == eng and eng in ('pe', 'sp'):
            return False
        if self.seen[eng].get((kind, ident), 0) >= v:
            return False
        return True

    def _wait(self, eng, kind, ident, v):
        if self._need(eng, (kind, ident, v)):
            self.prog[eng].append(('wait', (kind, ident, v)))
            self.seen[eng][(kind, ident)] = v

    def _deps(self, eng, reads, writes, extra=()):
        deps = {}

        def add(tok):
            key = (tok[0], tok[1])
            if deps.get(key, 0) < tok[2]:
                deps[key] = tok[2]
        for k in reads:
            if k in self.last_w:
                add(self.last_w[k])
        for k in writes:
            if k in self.last_w:
                add(self.last_w[k])
            for t in self.readers.get(k, ()):
                add(t)
        for t in extra:
            add(t)
        for (kind, ident), v in deps.items():
            self._wait(eng, kind, ident, v)

    def _commit(self, tok, reads, writes):
        for k in writes:
            self.last_w[k] = tok
            self.readers[k] = []
        for k in reads:
            self.readers.setdefault(k, []).append(tok)

    def op(self, eng, fn, reads=(), writes=()):
        self._deps(eng, reads, writes)
        self.cnt[eng] += 1
        tok = ('e', eng, self.cnt[eng])
        self.prog[eng].append(('op', fn))
        self._commit(tok, reads, writes)
        return tok

    def dma(self, q, fn, reads=(), writes=(), ndesc=0):
        i = self.dnext
        self.dnext = (self.dnext + 1) % NDS
        prev = ('d', i, self.dcount[i])
        extra = [prev]
        if q == 'pool':
            pend = self.pool_pending
            while pend and sum(n for _, n in pend) + ndesc > POOL_RING_LIMIT:
                extra.append(pend.pop(0)[0])
        self._deps(q, reads, writes, extra=tuple(extra))
        self.dcount[i] += 16
        tok = ('d', i, self.dcount[i])
        self.prog[q].append(('dma', fn, i))
        self._commit(tok, reads, writes)
        if q == 'pool':
            self.pool_pending.append((tok, ndesc))
        return tok

    def barrier(self):
        for e in self.ENG:
            for e2 in self.ENG:
                if e2 != e:
                    self._wait(e, 'e', e2, self.cnt[e2])
            for i in range(NDS):
                self._wait(e, 'd', i, self.dcount[i])
        self.last_w = {}
        self.readers = {}

    def flush(self):
        block = self.block
        handles = {'pe': block.tensor, 'dve': block.vector, 'act': block.scalar,
                   'pool': block.gpsimd, 'sp': block.sync}
        for e in self.ENG:
            prog = self.prog[e]
            self.prog[e] = []
            if not prog:
                continue

            def body(eng, prog=prog, e=e):
                for item in prog:
                    if item[0] == 'wait':
                        kind, ident, v = item[1]
                        s = self.sem[ident] if kind == 'e' else self.dsem[ident]
                        eng.wait_ge(s, v)
                    elif item[0] == 'op':
                        item[1](eng).then_inc(self.sem[e], 1)
                    else:
                        item[1](eng).then_inc(self.dsem[item[2]], 16)
            handles[e](body)


def token_tiles(t0, t1):
    out = []
    r = t0
    while r < t1:
        nr = min(128, t1 - r)
        out.append((r, nr))
        r += nr
    return out


def nchunks(n, step=512):
    out = []
    c = 0
    while c < n:
        out.append((c, min(step, n - c)))
        c += step
    return out


class Builder:
    def __init__(self, stage=99, dbg=False, which=('xp', 'xs', 's5', 'mp', 'ms'), mixtest=False):
        self.which = which
        self.mixtest = mixtest
        self.stage = stage
        self.dbg = dbg
        self.nc = bass.Bass("TRN2", target_bir_lowering=False)
        self.uid = 0

    def dram_in(self, name, shape, dt=F32):
        return self.nc.dram_tensor(name, list(shape), dt, kind="ExternalInput").ap()

    def dram_out(self, name, shape, dt=F32):
        return self.nc.dram_tensor(name, list(shape), dt, kind="ExternalOutput").ap()

    def dram_scr(self, name, shape, dt):
        return self.nc.dram_tensor(name, list(shape), dt, kind="Internal").ap()

    def sb(self, es, shape, dt, name=None):
        self.uid += 1
        return es.enter_context(self.nc.sbuf_tensor("%s_%d" % (name or "t", self.uid), list(shape), dt))

    def psum_next(self):
        i = self.pa_i
        self.pa_i = (self.pa_i + 1) % len(self.PA)
        return self.PA[i], ('pa', i)

    def psumb_next(self):
        i = self.pb_i
        self.pb_i = (self.pb_i + 1) % len(self.PB)
        return self.PB[i], ('pb', i)

    def load_gain(self, gb, gvec, key):
        S = self.S
        S.dma('sp', lambda e: e.dma_start(out=gb[:], in_=gvec.partition_broadcast(128)), writes=[key])

    def norm_T(self, src_rows, nr, gb, gkey, dstT, dkey, c0, bufs):
        S = self.S
        xts, xn, st = bufs
        if not isinstance(xts, (list, tuple)):
            xts = [xts]
        self.nt_i = getattr(self, 'nt_i', 0) + 1
        xt = xts[self.nt_i % len(xts)]
        xtk = ('xt', self.nt_i % len(xts))
        S.dma('sp', lambda e: e.dma_start(out=xt[:nr, :], in_=src_rows), writes=[xtk])
        S.op('act', lambda e: e.activation(out=xn[:nr, :], in_=xt[:nr, :], func=AF.Square, accum_out=st[:nr, 0:1]),
             reads=[xtk], writes=['xn', 'st'])
        S.op('dve', lambda e: e.tensor_scalar(out=st[:nr, 1:2], in0=st[:nr, 0:1], scalar1=1.0 / D, scalar2=EPS,
                                              op0=ALU.mult, op1=ALU.add), reads=['st'], writes=['st'])
        S.op('act', lambda e: e.activation(out=st[:nr, 2:3], in_=st[:nr, 1:2], func=AF.Sqrt), reads=['st'], writes=['st'])
        S.op('dve', lambda e: e.reciprocal(out=st[:nr, 3:4], in_=st[:nr, 2:3]), reads=['st'], writes=['st'])
        S.op('dve', lambda e: e.scalar_tensor_tensor(out=xn[:nr, :], in0=xt[:nr, :], scalar=st[:nr, 3:4], in1=gb[:nr, :],
                                                     op0=ALU.mult, op1=ALU.mult),
             reads=[xtk, 'st', gkey], writes=['xn'])
        self.transpose_into(xn, 'xn', nr, 16, dstT, dkey, c0)

    def transpose_into(self, src, skey, nr, nk, dstT, dkey, c0):
        S = self.S
        for kq in range(0, nk, 4):
            pb, pk = self.psumb_next()
            n4 = min(4, nk - kq)
            for j in range(n4):
                k = kq + j
                S.op('pe', lambda e, j=j, k=k, pb=pb: e.transpose(out=pb[:, j * 128:j * 128 + nr], in_=src[:nr, k * 128:(k + 1) * 128],
                                                                  identity=self.identb[:nr, :nr]),
                     reads=[skey, 'identb'], writes=[pk])
            eng = 'act' if (kq // 4) % 2 == 0 else 'dve'
            pv = pb[:, 0:n4 * 128].rearrange("p (j t) -> p j t", j=n4)[:, :, 0:nr]
            if eng == 'act':
                S.op('act', lambda e, pv=pv, kq=kq, n4=n4: e.copy(out=dstT[:, kq:kq + n4, c0:c0 + nr], in_=pv),
                     reads=[pk], writes=[dkey])
            else:
                S.op('dve', lambda e, pv=pv, kq=kq, n4=n4: e.tensor_copy(out=dstT[:, kq:kq + n4, c0:c0 + nr], in_=pv),
                     reads=[pk], writes=[dkey])

    def wload(self, wslice, kt, cw, fmt=None):
        S = self.S
        i = self.wb_i
        self.wb_i = (self.wb_i + 1) % len(self.WB)
        wb = self.WB[i]
        view = wb[:, 0:kt * cw].rearrange("p (k n) -> p k n", k=kt)
        S.dma('pool', lambda e: e.dma_start(out=view, in_=wslice.rearrange("(k p) n -> p k n", p=128)),
              writes=[('wb', i)], ndesc=8 * kt)
        return view, ('wb', i)

    def ffn(self, es0, xsrc, xdst, gvec, wg, wu, wd, blk):
        S = self.S
        t0, t1 = blk
        NT = t1 - t0
        with ExitStack() as es:
            xT = self.sb(es, [128, 16, NT], BF16, "xT")
            hT = self.sb(es, [128, 43, NT], BF16, "hT")
            self.WB = [self.sb(es, [128, 11008], BF16, "wb") for _ in range(2)]
            self.wb_i = 0
            xt = [self.sb(es, [128, D], F32, "xt") for _ in range(2)]
            xn = self.sb(es, [128, D], BF16, "xn")
            st = self.sb(es, [128, 8], F32, "st")
            gb = self.sb(es, [128, D], F32, "gb")
            sg = [self.sb(es, [128, 512], F32, "sg") for _ in range(2)]
            xres = [self.sb(es, [128, 256], F32, "xres") for _ in range(4)]
            ot = [self.sb(es, [128, 256], F32, "ot") for _ in range(2)]
            self.load_gain(gb, gvec, 'gb')
            for (r0, nr) in token_tiles(t0, t1):
                self.norm_T(xsrc[r0:r0 + nr, :], nr, gb, 'gb', xT, 'xT', r0 - t0, (xt, xn, st))
            ncs = nchunks(NT)
            cnt = 0
            c0 = 0
            while c0 < FF:
                cw = min(256, FF - c0)
                i = self.wb_i
                self.wb_i = (self.wb_i + 1) % 2
                wb = self.WB[i]
                wkey = ('wb', i)
                gv = wb[:, 0:16 * cw].rearrange("p (k n) -> p k n", k=16)
                uv = wb[:, 16 * cw:32 * cw].rearrange("p (k n) -> p k n", k=16)
                S.dma('pool', lambda e, gv=gv, c0=c0, cw=cw: e.dma_start(
                    out=gv, in_=wg[:, c0:c0 + cw].rearrange("(k p) n -> p k n", p=128)), writes=[wkey], ndesc=128)
                S.dma('pool', lambda e, uv=uv, c0=c0, cw=cw: e.dma_start(
                    out=uv, in_=wu[:, c0:c0 + cw].rearrange("(k p) n -> p k n", p=128)), writes=[wkey], ndesc=128)
                for m in range(cw // 128):
                    f = c0 // 128 + m
                    for (n0, nn) in ncs:
                        pg, pgk = self.psum_next()
                        pu, puk = self.psum_next()
                        for k in range(16):
                            S.op('pe', lambda e, pg=pg, gv=gv, k=k, m=m, n0=n0, nn=nn: e.matmul(
                                pg[:, 0:nn], lhsT=gv[:, k, m * 128:(m + 1) * 128], rhs=xT[:, k, n0:n0 + nn],
                                start=(k == 0), stop=(k == 15)), reads=[wkey, 'xT'], writes=[pgk])
                        for k in range(16):
                            S.op('pe', lambda e, pu=pu, uv=uv, k=k, m=m, n0=n0, nn=nn: e.matmul(
                                pu[:, 0:nn], lhsT=uv[:, k, m * 128:(m + 1) * 128], rhs=xT[:, k, n0:n0 + nn],
                                start=(k == 0), stop=(k == 15)), reads=[wkey, 'xT'], writes=[puk])
                        sgt = sg[cnt % 2]
                        sk = ('sg', cnt % 2)
                        cnt += 1
                        S.op('act', lambda e, sgt=sgt, pg=pg, nn=nn: e.activation(out=sgt[:, 0:nn], in_=pg[:, 0:nn], func=AF.Silu),
                             reads=[pgk], writes=[sk])
                        S.op('dve', lambda e, sgt=sgt, pu=pu, f=f, n0=n0, nn=nn: e.tensor_tensor(
                            out=hT[:, f, n0:n0 + nn], in0=sgt[:, 0:nn], in1=pu[:, 0:nn], op=ALU.mult),
                            reads=[sk, puk], writes=['hT'])
                c0 += cw
            self.proj_tm_res(hT, 'hT', 43, wd, xsrc, xdst, 0.5, blk, xres, ot)
            S.barrier()
            S.flush()

    def proj_tm_res(self, actT, akey, nk, w, xsrc, xdst, scale, blk, xres, ot, cw=256):
        S = self.S
        t0, t1 = blk
        items = [(c0, r0, nr) for c0 in range(0, D, cw) for (r0, nr) in token_tiles(t0, t1)]
        NX = len(xres)
        PF = NX - 1

        def ld(i):
            c0, r0, nr = items[i]
            xr = xres[i % NX]
            S.dma('sp', lambda e: e.dma_start(out=xr[:nr, 0:cw], in_=xsrc[r0:r0 + nr, c0:c0 + cw]), writes=[('xres', i % NX)])
        for i in range(min(PF, len(items))):
            ld(i)
        wv = wkey = None
        for i, (c0, r0, nr) in enumerate(items):
            if r0 == t0:
                wv, wkey = self.wload(w[:, c0:c0 + cw], nk, cw)
            if i + PF < len(items):
                ld(i + PF)
            rl = r0 - t0
            po, pok = self.psum_next()
            for f in range(nk):
                S.op('pe', lambda e, po=po, wv=wv, f=f, rl=rl, nr=nr: e.matmul(
                    po[:nr, 0:cw], lhsT=actT[:, f, rl:rl + nr], rhs=wv[:, f, :], start=(f == 0), stop=(f == nk - 1)),
                    reads=[wkey, akey], writes=[pok])
            xr = xres[i % NX]
            xk = ('xres', i % NX)
            o = ot[i % len(ot)]
            ok = ('ot', i % len(ot))
            S.op('dve', lambda e, o=o, po=po, xr=xr, nr=nr: e.scalar_tensor_tensor(
                out=o[:nr, 0:cw], in0=po[:nr, 0:cw], scalar=scale, in1=xr[:nr, 0:cw], op0=ALU.mult, op1=ALU.add),
                reads=[pok, xk], writes=[ok])
            S.dma('sp', lambda e, o=o, r0=r0, nr=nr, c0=c0: e.dma_start(out=xdst[r0:r0 + nr, c0:c0 + cw], in_=o[:nr, 0:cw]),
                  reads=[ok], writes=[('dram', 'xdst', r0, c0)])

    def evac(self, i, out_ap, in_ap, reads, writes):
        S = self.S
        if i % 2 == 0:
            S.op('act', lambda e: e.copy(out=out_ap, in_=in_ap), reads=reads, writes=writes)
        else:
            S.op('dve', lambda e: e.tensor_copy(out=out_ap, in_=in_ap), reads=reads, writes=writes)

    def proj_fm_store(self, actT, akey, nk, w, col0, ncols, dst, dcol0, NT, ebufs, ekey, cw=256):
        S = self.S
        ncs = nchunks(NT)
        for c0 in range(0, ncols, cw):
            wv, wkey = self.wload(w[:, col0 + c0:col0 + c0 + cw], nk, cw)
            for m in range(cw // 128):
                for (n0, nn) in ncs:
                    ps, pk = self.psum_next()
                    for k in range(nk):
                        S.op('pe', lambda e, ps=ps, wv=wv, k=k, m=m, n0=n0, nn=nn: e.matmul(
                            ps[:, 0:nn], lhsT=wv[:, k, m * 128:(m + 1) * 128], rhs=actT[:, k, n0:n0 + nn],
                            start=(k == 0), stop=(k == nk - 1)), reads=[wkey, akey], writes=[pk])
                    i = self.ecnt
                    self.ecnt += 1
                    eb = ebufs[i % len(ebufs)]
                    ek = (ekey, i % len(ebufs))
                    self.evac(i, eb[:, 0:nn], ps[:, 0:nn], [pk], [ek])
                    r = c0 + m * 128
                    S.dma('sp', lambda e, eb=eb, r=r, n0=n0, nn=nn: e.dma_start(
                        out=dst[r:r + 128, dcol0 + n0:dcol0 + n0 + nn], in_=eb[:, 0:nn]), reads=[ek],
                        writes=[('dram', id(dst), r, n0)])

    def proj_tm_store(self, actT, akey, nk, w, col0, ncols, dst, blk, ebufs, ekey, cw=512, dcol0=0):
        S = self.S
        t0, t1 = blk
        for c0 in range(0, ncols, cw):
            cww = min(cw, ncols - c0)
            wv, wkey = self.wload(w[:, col0 + c0:col0 + c0 + cww], nk, cww)
            for (r0, nr) in token_tiles(t0, t1):
                rl = r0 - t0
                ps, pk = self.psum_next()
                for k in range(nk):
                    S.op('pe', lambda e, ps=ps, wv=wv, k=k, rl=rl, nr=nr, cww=cww: e.matmul(
                        ps[:nr, 0:cww], lhsT=actT[:, k, rl:rl + nr], rhs=wv[:, k, :],
                        start=(k == 0), stop=(k == nk - 1)), reads=[wkey, akey], writes=[pk])
                i = self.ecnt
                self.ecnt += 1
                eb = ebufs[i % len(ebufs)]
                ek = (ekey, i % len(ebufs))
                self.evac(i, eb[:nr, 0:cww], ps[:nr, 0:cww], [pk], [ek])
                S.dma('sp', lambda e, eb=eb, r0=r0, nr=nr, c0=c0, cww=cww: e.dma_start(
                    out=dst[r0:r0 + nr, dcol0 + c0:dcol0 + c0 + cww], in_=eb[:nr, 0:cww]), reads=[ek],
                    writes=[('dram', id(dst), r0, c0)])

    def inproj(self):
        S = self.S
        I = self.I
        R = self.R
        w_in = I['w_in']
        with ExitStack() as es:
            hTs = [self.sb(es, [128, 16, b1 - b0], BF16, "hT%d" % i) for i, (b0, b1) in enumerate(BLOCKS)]
            self.WB = [self.sb(es, [128, 8192], BF16, "wb") for _ in range(3)]
            self.wb_i = 0
            xt = [self.sb(es, [128, D], F32, "xt") for _ in range(2)]
            xn = self.sb(es, [128, D], BF16, "xn")
            st = self.sb(es, [128, 8], F32, "st")
            gb = self.sb(es, [128, D], F32, "gb")
            fli = self.sb(es, [128, 2], F32, "fli")
            eb16 = [self.sb(es, [128, 512], BF16, "eb16") for _ in range(3)]
            eb32 = [self.sb(es, [128, 512], F32, "eb32") for _ in range(3)]
            self.load_gain(gb, I['g_mix'], 'gb')
            S.dma('sp', lambda e: e.dma_start(out=fli[:], in_=I['flag'][:, :]), writes=['fli'])
            for bi, (t0, t1) in enumerate(BLOCKS):
                for (r0, nr) in token_tiles(t0, t1):
                    self.norm_T(R['X1'][r0:r0 + nr, :], nr, gb, 'gb', hTs[bi], 'hT%d' % bi, r0 - t0, (xt, xn, st))
            for bi, blk in enumerate(BLOCKS):
                t0, t1 = blk
                NT = t1 - t0
                hT = hTs[bi]
                hk = 'hT%d' % bi
                for (col0, dst) in [(0, R['UT']), (2048, R['KT'])]:
                    self.proj_fm_store(hT, hk, 16, w_in, col0, 1024, dst, t0, NT, eb16, 'eb16')
                for (col0, dst) in [(2048, R['KTM']), (3072, R['VTM'])]:
                    self.proj_tm_store(hT, hk, 16, w_in, col0, 1024, dst, blk, eb16, 'eb16')
                self.proj_tm_store(hT, hk, 16, w_in, 5120, 8, R['GTM'], blk, eb32, 'eb32', cw=8)
                wv, wkey = self.wload(w_in[:, 5120:5128], 16, 8)
                for gi, dst in [(0, R['IGR']), (1, R['FGR'])]:
                    for (n0, nn) in nchunks(NT):
                        ps, pk = self.psum_next()
                        for k in range(16):
                            S.op('pe', lambda e, ps=ps, wv=wv, k=k, gi=gi, n0=n0, nn=nn, hT=hT: e.matmul(
                                ps[0:4, 0:nn], lhsT=wv[:, k, gi * 4:gi * 4 + 4], rhs=hT[:, k, n0:n0 + nn],
                                start=(k == 0), stop=(k == 15)), reads=[wkey, hk], writes=[pk])
                        i = self.ecnt
                        self.ecnt += 1
                        eb = eb32[i % 3]
                        ek = ('eb32', i % 3)
                        self.evac(i, eb[0:4, 0:nn], ps[0:4, 0:nn], [pk], [ek])
                        S.dma('sp', lambda e, eb=eb, dst=dst, n0=n0, nn=nn, t0=t0: e.dma_start(
                            out=dst[0:4, t0 + n0:t0 + n0 + nn], in_=eb[0:4, 0:nn]), reads=[ek], writes=[('dram', id(dst), t0, n0)])
            h0, h1 = hTs
            S.op('dve', lambda e: e.tensor_scalar(out=h0[:], in0=h0[:], scalar1=fli[:, 0:1], scalar2=None, op0=ALU.mult), reads=['hT0', 'fli'], writes=['hT0'])
            S.op('dve', lambda e: e.scalar_tensor_tensor(out=h0[:], in0=h1[:, :, 0:HPR], scalar=fli[:, 1:2], in1=h0[:], op0=ALU.mult, op1=ALU.add),
                 reads=['hT0', 'hT1', 'fli'], writes=['hT0'])
            segs = [(h0, 'hT0', n0, nn, n0) for (n0, nn) in nchunks(HPR)] + [(h1, 'hT1', HPR, NSM, HPR)]
            for (col0, dst) in [(1024, R['QTo']), (5128, R['QXTo'])]:
                for c0 in range(0, 1024, 256):
                    wv, wkey = self.wload(w_in[:, col0 + c0:col0 + c0 + 256], 16, 256)
                    for m in range(2):
                        for (act_, akey, a0, nn, dcol) in segs:
                            ps, pk = self.psum_next()
                            for k in range(16):
                                S.op('pe', lambda e, ps=ps, wv=wv, k=k, m=m, act_=act_, a0=a0, nn=nn: e.matmul(
                                    ps[:, 0:nn], lhsT=wv[:, k, m * 128:(m + 1) * 128], rhs=act_[:, k, a0:a0 + nn],
                                    start=(k == 0), stop=(k == 15)), reads=[wkey, akey], writes=[pk])
                            i = self.ecnt
                            self.ecnt += 1
                            eb = eb16[i % 3]
                            ek = ('eb16', i % 3)
                            self.evac(i, eb[:, 0:nn], ps[:, 0:nn], [pk], [ek])
                            r = c0 + m * 128
                            S.dma('sp', lambda e, eb=eb, dst=dst, r=r, dcol=dcol, nn=nn: e.dma_start(
                                out=dst[r:r + 128, dcol:dcol + nn], in_=eb[:, 0:nn]), reads=[ek], writes=[('dram', id(dst), r, dcol)])
            tls = [(h0, 'hT0', i * 128, 128, i * 128) for i in range(HPR // 128)] + [(h1, 'hT1', HPR, NSM, HPR)]
            for c0 in range(0, 1024, 512):
                wv, wkey = self.wload(w_in[:, 4096 + c0:4096 + c0 + 512], 16, 512)
                for (act_, akey, rl, nr, row0) in tls:
                    ps, pk = self.psum_next()
                    for k in range(16):
                        S.op('pe', lambda e, ps=ps, wv=wv, k=k, act_=act_, rl=rl, nr=nr: e.matmul(
                            ps[:nr, 0:512], lhsT=act_[:, k, rl:rl + nr], rhs=wv[:, k, :], start=(k == 0), stop=(k == 15)),
                            reads=[wkey, akey], writes=[pk])
                    i = self.ecnt
                    self.ecnt += 1
                    eb = eb32[i % 3]
                    ek = ('eb32', i % 3)
                    self.evac(i, eb[:nr, 0:512], ps[:nr, 0:512], [pk], [ek])
                    S.dma('sp', lambda e, eb=eb, row0=row0, nr=nr, c0=c0: e.dma_start(
                        out=R['OTMo'][row0:row0 + nr, c0:c0 + 512], in_=eb[:nr, 0:512]), reads=[ek], writes=[('dram', 'otmo', row0, c0)])
            S.barrier()
            S.flush()

    def memkv(self):
        S = self.S
        I = self.I
        R = self.R
        O = self.O
        with ExitStack() as es:
            mT = self.sb(es, [128, 16, 256], BF16, "mT")
            self.WB = [self.sb(es, [128, 8192], BF16, "wb") for _ in range(3)]
            self.wb_i = 0
            xt = self.sb(es, [128, D], F32, "xt")
            xn = self.sb(es, [128, D], BF16, "xn")
            st = self.sb(es, [128, 8], F32, "st")
            gb = self.sb(es, [128, D], F32, "gb")
            eb16 = [self.sb(es, [128, 512], BF16, "eb16") for _ in range(3)]
            eb32 = [self.sb(es, [128, 512], F32, "eb32") for _ in range(3)]
            self.load_gain(gb, I['g_mem'], 'gb')
            for (r0, nr) in token_tiles(0, 256):
                self.norm_T(I['mem'][r0:r0 + nr, :], nr, gb, 'gb', mT, 'mT', r0, (xt, xn, st))
            self.proj_tm_store(mT, 'mT', 16, I['w_mem_k'], 0, 1024, O['mk'], (0, 256), eb32, 'eb32')
            self.proj_tm_store(mT, 'mT', 16, I['w_mem_v'], 0, 1024, O['mv'], (0, 256), eb32, 'eb32')
            self.proj_fm_store(mT, 'mT', 16, I['w_mem_k'], 0, 1024, R['MKT'], 0, 256, eb16, 'eb16')
            S.barrier()
            S.flush()

    def merge(self):
        S = self.S
        I = self.I
        R = self.R
        blk = (0, HT)
        t0, t1 = blk
        NT = t1 - t0
        w_in = I['w_in']
        with ExitStack() as es:
            hT = self.sb(es, [128, 16, NT], BF16, "hT")
            mT = self.sb(es, [128, 16, NT], BF16, "mT")
            brT = [self.sb(es, [128, 8, NT], BF16, "brT") for _ in range(3)]
            self.WB = [self.sb(es, [128, 2048], BF16, "wb") for _ in range(9)]
            self.wb_i = 0
            xt = self.sb(es, [128, D], F32, "xt")
            xn = self.sb(es, [128, D], BF16, "xn")
            st = self.sb(es, [128, 8], F32, "st")
            gb = self.sb(es, [128, D], F32, "gb")
            gs4 = self.sb(es, [128, D], F32, "gs4")
            gs = [gs4[:, i * 512:(i + 1) * 512] for i in range(2)]
            tmp = [gs4[:, (2 + i) * 512:(3 + i) * 512] for i in range(2)]
            acc = [self.sb(es, [128, 512], F32, "acc") for _ in range(2)]
            xres = [self.sb(es, [128, 256], F32, "xres") for _ in range(4)]
            ot = [self.sb(es, [128, 256], F32, "ot") for _ in range(2)]
            self.load_gain(gb, I['g_mix'], 'gb')
            fl = self.sb(es, [128, 2], F32, "fl")
            S.dma('sp', lambda e: e.dma_start(out=fl[:], in_=I['flag'][:, :]), writes=['fl'])
            for (r0, nr) in token_tiles(0, HPR):
                S.dma('sp', lambda e, r0=r0, nr=nr: e.dma_start(out=xt[:nr, :], in_=R['X1'][r0:r0 + nr, :]), writes=['xt'])
                S.dma('sp', lambda e, r0=r0, nr=nr: e.dma_start(out=gs4[:nr, :], in_=R['X1'][HPR + r0:HPR + r0 + nr, :]), writes=['gs4'])
                S.op('dve', lambda e, nr=nr: e.tensor_scalar(out=xt[:nr, :], in0=xt[:nr, :], scalar1=fl[:nr, 0:1], scalar2=None, op0=ALU.mult),
                     reads=['xt', 'fl'], writes=['xt'])
                S.op('dve', lambda e, nr=nr: e.scalar_tensor_tensor(out=xt[:nr, :], in0=gs4[:nr, :], scalar=fl[:nr, 1:2], in1=xt[:nr, :],
                                                                   op0=ALU.mult, op1=ALU.add), reads=['xt', 'gs4', 'fl'], writes=['xt'])
                S.dma('sp', lambda e, r0=r0, nr=nr: e.dma_start(out=R['X1h'][r0:r0 + nr, :], in_=xt[:nr, :]), reads=['xt'],
                      writes=[('dram', 'x1h', r0)])
            S.dma('sp', lambda e: e.dma_start(out=xt[:NSM, :], in_=R['X1'][NPR:NTOK, :]), writes=['xt'])
            S.dma('sp', lambda e: e.dma_start(out=R['X1h'][HPR:HT, :], in_=xt[:NSM, :]), reads=['xt'], writes=[('dram', 'x1h', HPR)])
            S.barrier()
            for b, nm in enumerate(['BR0', 'BR1', 'BR2']):
                S.dma('sp', lambda e, b=b, nm=nm: e.dma_start(out=brT[b][:], in_=R[nm].rearrange("(k p) t -> p k t", p=128)), writes=[('brT', b)])
            for (r0, nr) in token_tiles(t0, t1):
                self.norm_T(R['X1h'][r0:r0 + nr, :], nr, gb, 'gb', hT, 'hT', r0 - t0, (xt, xn, st))
            wbr = [I['w_br_s5'], I['w_br_ml'], I['w_br_xa']]
            ncs = nchunks(NT)
            cnt = 0
            for j in range(16):
                wg = []
                for b in range(3):
                    gv, gk = self.wload(w_in[:, 6152 + b * 2048 + j * 128:6152 + b * 2048 + (j + 1) * 128], 16, 128)
                    wg.append((gv, gk))
                wb_ = []
                for b in range(3):
                    bv, bk = self.wload(wbr[b][:, j * 128:(j + 1) * 128], 8, 128)
                    wb_.append((bv, bk))
                for (n0, nn) in ncs:
                    a = acc[cnt % 2]
                    ak = ('acc', cnt % 2)
                    cnt += 1
                    for b in range(3):
                        gv, gk = wg[b]
                        bv, bk = wb_[b]
                        pg, pgk = self.psum_next()
                        pb, pbk = self.psum_next()
                        for k in range(16):
                            S.op('pe', lambda e, pg=pg, gv=gv, k=k, n0=n0, nn=nn: e.matmul(
                                pg[:, 0:nn], lhsT=gv[:, k, :], rhs=hT[:, k, n0:n0 + nn], start=(k == 0), stop=(k == 15)),
                                reads=[gk, 'hT'], writes=[pgk])
                        for k in range(8):
                            S.op('pe', lambda e, pb=pb, bv=bv, k=k, b=b, n0=n0, nn=nn: e.matmul(
                                pb[:, 0:nn], lhsT=bv[:, k, :], rhs=brT[b][:, k, n0:n0 + nn], start=(k == 0), stop=(k == 7)),
                                reads=[bk, ('brT', b)], writes=[pbk])
                        g = gs[b % 2]
                        gsk = ('gs', b % 2)
                        S.op('act', lambda e, g=g, pg=pg, nn=nn: e.activation(out=g[:, 0:nn], in_=pg[:, 0:nn], func=AF.Sigmoid),
                             reads=[pgk], writes=[gsk])
                        if b == 0:
                            S.op('dve', lambda e, a=a, g=g, pb=pb, nn=nn: e.tensor_tensor(out=a[:, 0:nn], in0=g[:, 0:nn], in1=pb[:, 0:nn],
                                                                                         op=ALU.mult), reads=[gsk, pbk], writes=[ak])
                        else:
                            t = tmp[b % 2]
                            tk = ('tmp', b % 2)
                            S.op('dve', lambda e, t=t, g=g, pb=pb, nn=nn: e.tensor_tensor(out=t[:, 0:nn], in0=g[:, 0:nn], in1=pb[:, 0:nn],
                                                                                         op=ALU.mult), reads=[gsk, pbk], writes=[tk])
                            if b == 1:
                                S.op('dve', lambda e, a=a, t=t, nn=nn: e.tensor_tensor(out=a[:, 0:nn], in0=a[:, 0:nn], in1=t[:, 0:nn],
                                                                                       op=ALU.add), reads=[ak, tk], writes=[ak])
                            else:
                                S.op('dve', lambda e, a=a, t=t, j=j, n0=n0, nn=nn: e.tensor_tensor(
                                    out=mT[:, j, n0:n0 + nn], in0=a[:, 0:nn], in1=t[:, 0:nn], op=ALU.add),
                                    reads=[ak, tk], writes=['mT'])
            self.proj_tm_res(mT, 'mT', 16, I['w_out'], R['X1h'], R['X2'], 1.0, blk, xres, ot, cw=128)
            S.barrier()
            S.flush()

    def mixers(self):
        which = self.which
        if 'xp' in which:
            self.xatt_prompt()
        if 'xs' in which:
            self.xatt_sample()
        if 's5' in which:
            self.s5()
        if 'mp' in which:
            self.mlstm_prompt()
        if 'ms' in which:
            self.mlstm_sample()

    def xatt_prompt(self):
        S = self.S
        R = self.R
        O = self.O
        with ExitStack() as es:
            qxT = self.sb(es, [128, 8, HPR], BF16, "qxT")
            mkT = self.sb(es, [128, 8, 256], BF16, "mkT")
            Vb = self.sb(es, [128, 2, 1024], BF16, "Vb")
            pT = [self.sb(es, [128, 2, HPR], BF16, "pT") for _ in range(2)]
            pe_ = [self.sb(es, [128, 256], F32, "pe") for _ in range(2)]
            pn = [self.sb(es, [128, 256], BF16, "pn") for _ in range(2)]
            sts = [self.sb(es, [128, 8], F32, "sts") for _ in range(2)]
            ob = [self.sb(es, [128, 512], BF16, "ob") for _ in range(2)]
            S.dma('sp', lambda e: e.dma_start(out=qxT[:], in_=R['QXTo'][:, 0:HPR].rearrange("(k p) t -> p k t", p=128)), writes=['qxT'])
            S.dma('sp', lambda e: e.dma_start(out=mkT[:], in_=R['MKT'].rearrange("(k p) t -> p k t", p=128)), writes=['mkT'])
            S.dma('pool', lambda e: e.dma_start(out=Vb[:], in_=R['MVB'].rearrange("(mt p) c -> p mt c", p=128)), writes=['Vb'], ndesc=16)
            oc = 0
            NTT = HPR // 128
            items = [(h, tt) for h in range(4) for tt in range(NTT)]
            pss = {}

            def xp_scores(n_):
                h, tt = items[n_]
                ps, pk = self.psum_next()
                for dh in range(2):
                    S.op('pe', lambda e, dh=dh: e.matmul(
                        ps[:, 0:256], lhsT=qxT[:, h * 2 + dh, tt * 128:(tt + 1) * 128], rhs=mkT[:, h * 2 + dh, :],
                        start=(dh == 0), stop=(dh == 1)), reads=['qxT', 'mkT'], writes=[pk])
                pss[n_] = (ps, pk)

            def xp_softmax(n_):
                h, tt = items[n_]
                ps, pk = pss.pop(n_)
                pTh = pT[h % 2]
                pTk = ('pT', h % 2)
                st = sts[n_ % 2]
                sk = ('sts', n_ % 2)
                pe = pe_[n_ % 2]
                pek = ('pe', n_ % 2)
                pnn = pn[n_ % 2]
                pnk = ('pn', n_ % 2)
                S.op('dve', lambda e: e.reduce_max(out=st[:, 0:1], in_=ps[:, 0:256], axis=AX.X), reads=[pk], writes=[sk])
                S.op('dve', lambda e: e.tensor_scalar(out=st[:, 1:2], in0=st[:, 0:1], scalar1=-1.0 / 16, scalar2=None, op0=ALU.mult),
                     reads=[sk], writes=[sk])
                S.op('act', lambda e: e.activation(out=pe[:], in_=ps[:, 0:256], func=AF.Exp, bias=st[:, 1:2],
                                                   scale=1.0 / 16, accum_out=st[:, 2:3]), reads=[pk, sk], writes=[pek, sk])
                S.op('dve', lambda e: e.reciprocal(out=st[:, 3:4], in_=st[:, 2:3]), reads=[sk], writes=[sk])
                S.op('dve', lambda e: e.tensor_scalar(out=pnn[:], in0=pe[:], scalar1=st[:, 3:4], scalar2=None, op0=ALU.mult),
                     reads=[sk, pek], writes=[pnk])
                pb, pbk = self.psumb_next()
                for mt in range(2):
                    S.op('pe', lambda e, mt=mt: e.transpose(out=pb[:, mt * 128:(mt + 1) * 128], in_=pnn[:, mt * 128:(mt + 1) * 128],
                                                            identity=self.identb[:]), reads=[pnk, 'identb'], writes=[pbk])
                S.op('act', lambda e: e.copy(out=pTh[:, :, tt * 128:(tt + 1) * 128], in_=pb[:, 0:256].rearrange("p (m t) -> p m t", m=2)),
                     reads=[pbk], writes=[pTk])

            def xp_pv(h):
                nonlocal oc
                pTh = pT[h % 2]
                pTk = ('pT', h % 2)
                for dh in range(2):
                    for (n0, nn) in nchunks(HPR):
                        ps, pk = self.psum_next()
                        for mt in range(2):
                            S.op('pe', lambda e, ps=ps, dh=dh, mt=mt, n0=n0, nn=nn: e.matmul(
                                ps[:, 0:nn], lhsT=Vb[:, mt, h * 256 + dh * 128:h * 256 + (dh + 1) * 128], rhs=pTh[:, mt, n0:n0 + nn],
                                start=(mt == 0), stop=(mt == 1)), reads=['Vb', pTk], writes=[pk])
                        o = ob[oc % 2]
                        ok = ('ob', oc % 2)
                        oc += 1
                        self.evac(oc, o[:, 0:nn], ps[:, 0:nn], [pk], [ok])
                        r = h * 256 + dh * 128
                        S.dma('sp', lambda e, o=o, r=r, n0=n0, nn=nn: e.dma_start(out=R['BR2'][r:r + 128, n0:n0 + nn], in_=o[:, 0:nn]),
                              reads=[ok], writes=[('dram', 'brt2', r, n0)])

            xp_scores(0)
            for n_ in range(len(items)):
                if n_ + 1 < len(items):
                    xp_scores(n_ + 1)
                xp_softmax(n_)
                if items[n_][1] == NTT - 1:
                    xp_pv(items[n_][0])
            S.barrier()
            S.flush()

    def xatt_sample(self):
        S = self.S
        R = self.R
        I = self.I
        with ExitStack() as es:
            qS = self.sb(es, [128, 8, NSM], BF16, "qS")
            qtm = self.sb(es, [NSM, 1024], BF16, "qtm")
            OH = self.sb(es, [NSM, NSM, 128], BF16, "OH")
            Kt = [self.sb(es, [128, 2, 1024], F32, "Kt") for _ in range(2)]
            Vt = [self.sb(es, [128, 2, 1024], BF16, "Vt") for _ in range(2)]
            prod = [self.sb(es, [128, 1024], F32, "prod") for _ in range(2)]
            SC = self.sb(es, [128, 2, 128], F32, "SC")
            pe = self.sb(es, [128, 256], F32, "pes")
            pn = self.sb(es, [128, 256], F32, "pns")
            st = self.sb(es, [128, 8], F32, "stx")
            P = self.sb(es, [128, 2, 128], BF16, "Pm")
            xrow = [self.sb(es, [1, 1024], F32, "xrow") for _ in range(2)]
            xas = self.sb(es, [NSM, 1024], F32, "xas")
            xasb = self.sb(es, [NSM, 1024], BF16, "xasb")
            xaT = self.sb(es, [128, 8, NSM], BF16, "xaT")
            S.dma('sp', lambda e: e.dma_start(out=qS[:], in_=R['QXTo'][:, HPR:HT].rearrange("(k p) t -> p k t", p=128)), writes=['qS'])
            for kq in range(0, 8, 4):
                pb, pbk = self.psumb_next()
                for j in range(4):
                    S.op('pe', lambda e, pb=pb, j=j, kq=kq: e.transpose(out=pb[:NSM, j * 128:(j + 1) * 128], in_=qS[:, kq + j, :],
                                                                       identity=self.identb[:]), reads=['qS', 'identb'], writes=[pbk])
                S.op('act', lambda e, pb=pb, kq=kq: e.copy(out=qtm[:, kq * 128:(kq + 4) * 128], in_=pb[:NSM, 0:512]), reads=[pbk], writes=['qtm'])
            S.op('dve', lambda e: e.tensor_copy(out=OH[:], in_=self.identb[:NSM, :NSM].unsqueeze(2).to_broadcast([NSM, NSM, 128])),
                 reads=['identb'], writes=['OH'])
            S.op('pool', lambda e: e.memset(SC[:], 0.0), writes=['SC'])
            for b in range(NSM):
                kt = Kt[b % 2]
                kk = ('Kt', b % 2)
                S.dma('sp', lambda e, kt=kt, b=b: e.dma_start(out=kt[:], in_=I['cache_k'][b].rearrange("(mt p) c -> p mt c", p=128)), writes=[kk])
                qb = []
                for hf in range(2):
                    ps, pk = self.psum_next()
                    S.op('pe', lambda e, ps=ps, b=b, hf=hf: e.matmul(ps[:, 0:512], lhsT=OH[:, b, :], rhs=qtm[:, hf * 512:(hf + 1) * 512],
                                                                    start=True, stop=True), reads=['OH', 'qtm'], writes=[pk])
                    qb.append((ps, pk))
                for mt in range(2):
                    pr = prod[mt]
                    prk = ('prod', mt)
                    for hf in range(2):
                        ps, pk = qb[hf]
                        S.op('dve', lambda e, pr=pr, kt=kt, ps=ps, mt=mt, hf=hf: e.tensor_tensor(
                            out=pr[:, hf * 512:(hf + 1) * 512], in0=kt[:, mt, hf * 512:(hf + 1) * 512], in1=ps[:, 0:512], op=ALU.mult),
                            reads=[kk, pk], writes=[prk])
                    S.op('dve', lambda e, pr=pr, mt=mt, b=b: e.tensor_reduce(out=SC[:, mt, b * 4:(b + 1) * 4],
                                                                            in_=pr[:].rearrange("p (h d) -> p h d", h=4), axis=AX.X, op=ALU.add),
                         reads=[prk], writes=['SC'])
            BH = 128
            if self.dbg:
                dSC = self.dram_out('dbg_SC', [128, 256])
                dq = self.dram_out('dbg_qtm', [NSM, 1024], BF16)
                S.dma('sp', lambda e: e.dma_start(out=dSC[:, :], in_=SC[:].rearrange("p a b -> p (a b)")), reads=['SC'], writes=[('dram', 'dsc')])
                S.dma('sp', lambda e: e.dma_start(out=dq[:, :], in_=qtm[:]), reads=['qtm'], writes=[('dram', 'dq')])
            ps, pk = self.psum_next()
            for mt in range(2):
                S.op('pe', lambda e, ps=ps, mt=mt: e.transpose(out=ps[:BH, mt * 128:(mt + 1) * 128], in_=SC[:, mt, :], identity=self.identf[:]),
                     reads=['SC', 'identf'], writes=[pk])
            S.op('dve', lambda e, ps=ps: e.reduce_max(out=st[:BH, 0:1], in_=ps[:BH, 0:256], axis=AX.X), reads=[pk], writes=['stx'])
            S.op('dve', lambda e: e.tensor_scalar(out=st[:BH, 1:2], in0=st[:BH, 0:1], scalar1=-1.0 / 16, scalar2=None, op0=ALU.mult),
                 reads=['stx'], writes=['stx'])
            S.op('act', lambda e, ps=ps: e.activation(out=pe[:BH, :], in_=ps[:BH, 0:256], func=AF.Exp, bias=st[:BH, 1:2], scale=1.0 / 16,
                                               accum_out=st[:BH, 2:3]), reads=[pk, 'stx'], writes=['pes', 'stx'])
            S.op('dve', lambda e: e.reciprocal(out=st[:BH, 3:4], in_=st[:BH, 2:3]), reads=['stx'], writes=['stx'])
            S.op('dve', lambda e: e.tensor_scalar(out=pn[:BH, :], in0=pe[:BH, :], scalar1=st[:BH, 3:4], scalar2=None, op0=ALU.mult),
                 reads=['stx', 'pes'], writes=['pns'])
            if self.dbg:
                dpn = self.dram_out('dbg_pn', [128, 256])
                dst_ = self.dram_out('dbg_st', [128, 8])
                S.dma('sp', lambda e: e.dma_start(out=dpn[:, :], in_=pn[:]), reads=['pns'], writes=[('dram', 'dpn')])
                S.dma('sp', lambda e: e.dma_start(out=dst_[:, :], in_=st[:]), reads=['stx'], writes=[('dram', 'dst')])
            for mt in range(2):
                ps2, pk2 = self.psum_next()
                S.op('pe', lambda e, ps2=ps2, mt=mt: e.transpose(out=ps2[:, 0:BH], in_=pn[:BH, mt * 128:(mt + 1) * 128], identity=self.identf[:BH, :BH]),
                     reads=['pns', 'identf'], writes=[pk2])
                S.op('act', lambda e, ps2=ps2, mt=mt: e.copy(out=P[:, mt, 0:BH], in_=ps2[:, 0:BH]), reads=[pk2], writes=['Pm'])
            for b in range(NSM):
                vt = Vt[b % 2]
                vk = ('Vt', b % 2)
                S.dma('pool', lambda e, vt=vt, b=b: e.dma_start(out=vt[:], in_=I['cache_v'][b].rearrange("(mt p) c -> p mt c", p=128)), writes=[vk], ndesc=16)
                xr = xrow[b % 2]
                xk = ('xrow', b % 2)
                for hp in range(2):
                    ps, pk = self.psum_next()
                    for hh in range(2):
                        h = hp * 2 + hh
                        for mt in range(2):
                            S.op('pe', lambda e, ps=ps, vt=vt, b=b, h=h, hh=hh, mt=mt: e.matmul(
                                ps[0:1, hh * 256:(hh + 1) * 256], lhsT=P[:, mt, b * 4 + h:b * 4 + h + 1], rhs=vt[:, mt, h * 256:(h + 1) * 256],
                                start=(mt == 0), stop=(mt == 1)), reads=['Pm', vk], writes=[pk])
                    S.op('act', lambda e, ps=ps, xr=xr, hp=hp: e.copy(out=xr[0:1, hp * 512:(hp + 1) * 512], in_=ps[0:1, 0:512]),
                         reads=[pk], writes=[xk])
                S.dma('sp', lambda e, xr=xr, b=b: e.dma_start(out=R['XAS'][b:b + 1, :], in_=xr[0:1, :]), reads=[xk], writes=[('dram', 'xas', b)])
            S.barrier()
            S.dma('sp', lambda e: e.dma_start(out=xas[:], in_=R['XAS'][:, :]), writes=['xas'])
            S.op('dve', lambda e: e.tensor_copy(out=xasb[:], in_=xas[:]), reads=['xas'], writes=['xasb'])
            self.transpose_into(xasb, 'xasb', NSM, 8, xaT, 'xaT', 0)
            S.dma('sp', lambda e: e.dma_start(out=R['BR2'][:, HPR:HT].rearrange("(k p) t -> p k t", p=128), in_=xaT[:]),
                  reads=['xaT'], writes=[('dram', 'brt2s')])
            S.barrier()
            S.flush()

    def s5(self):
        S = self.S
        I = self.I
        R = self.R
        O = self.O
        TWO_PI = 6.283185307179586
        dve = lambda fn, reads, writes: S.op('dve', fn, reads=reads, writes=writes)
        with ExitStack() as es:
            T = {}

            def t32(name):
                T[name] = self.sb(es, [128, 32], F32, name)
                return T[name]
            for nm in ['LR', 'LI', 'LS', 'DT', 'X', 'P', 'EM1', 'MAG', 'PHI', 'U', 'NF', 'RR', 'MSK', 'SIN', 'COS', 'AR', 'AI',
                       'CM1', 'NR', 'DEN', 'ZR', 'ZI', 'TA', 'TB', 'SPr', 'SPi', 'TC', 'TD']:
                t32(nm)
            NI = self.sb(es, [128, 32], I32, "NI")
            nat = [self.sb(es, [32, 128], F32, "nat") for _ in range(3)]
            LS2 = self.sb(es, [32, 2], F32, "LS2")
            LC = 64
            WvT = [self.sb(es, [32, 32, 128], BF16, "WvT%d" % i) for i in range(2)]
            CT = [self.sb(es, [128, 32, 32], BF16, "CT%d" % i) for i in range(2)]
            Dd = self.sb(es, [32, 32, 32], BF16, "Dd")
            Dcol = self.sb(es, [32, 32], F32, "Dcol")
            Gr = self.sb(es, [128, 32, LC], F32, "Gr")
            Gi = self.sb(es, [128, 32, LC], F32, "Gi")
            RHO = self.sb(es, [128, 32, LC], F32, "RHO")
            gx = [self.sb(es, [32, 512], F32, "gx") for _ in range(2)]
            g2t = [self.sb(es, [32, 512], F32, "g2t") for _ in range(2)]
            S.dma('sp', lambda e: e.dma_start(out=nat[0][:], in_=I['s5_lambda_re'].rearrange("(gp g2) p -> gp (g2 p)", g2=2)), writes=['nat0'])
            S.dma('sp', lambda e: e.dma_start(out=nat[1][:], in_=I['s5_lambda_im'].rearrange("(gp g2) p -> gp (g2 p)", g2=2)), writes=['nat1'])
            S.dma('sp', lambda e: e.dma_start(out=LS2[:], in_=I['s5_log_step'].rearrange("(gp g2) -> gp g2", g2=2)), writes=['LS2'])
            dve(lambda e: e.tensor_copy(out=nat[2][:].rearrange("g (a p) -> g a p", a=2), in_=LS2[:].unsqueeze(2).to_broadcast([32, 2, 64])),
                ['LS2'], ['nat2'])
            ps, pk = self.psum_next()
            for j in range(3):
                S.op('pe', lambda e, j=j, ps=ps: e.transpose(out=ps[:, j * 32:(j + 1) * 32], in_=nat[j][:, :], identity=self.identf[:32, :32]),
                     reads=['nat%d' % j, 'identf'], writes=[pk])
            for j, nm in enumerate(['LR', 'LI', 'LS']):
                dve(lambda e, j=j, nm=nm, ps=ps: e.tensor_copy(out=T[nm][:], in_=ps[:, j * 32:(j + 1) * 32]), [pk], [nm])
            S.op('act', lambda e: e.activation(out=T['DT'][:], in_=T['LS'][:], func=AF.Exp), reads=['LS'], writes=['DT'])
            tt = lambda o, a, b, op: dve(lambda e: e.tensor_tensor(out=T[o][:], in0=T[a][:], in1=T[b][:], op=op), [a, b], [o])
            ts = lambda o, a, s1, s2, op0, op1: dve(lambda e: e.tensor_scalar(out=T[o][:], in0=T[a][:], scalar1=s1, scalar2=s2, op0=op0, op1=op1), [a], [o])
            tt('X', 'LR', 'DT', ALU.mult)
            ts('P', 'X', 1.0 / 6, 1.0, ALU.mult, ALU.add)
            for dv in [5.0, 4.0, 3.0, 2.0]:
                tt('P', 'P', 'X', ALU.mult)
                ts('P', 'P', 1.0 / dv, 1.0, ALU.mult, ALU.add)
            tt('EM1', 'P', 'X', ALU.mult)
            ts('MAG', 'EM1', 1.0, None, ALU.add, ALU.bypass)
            tt('PHI', 'LI', 'DT', ALU.mult)

            def sin_of(dst, src, shift):
                ts('U', src, shift, 1.0 / TWO_PI, ALU.add, ALU.mult)
                dve(lambda e: e.tensor_copy(out=NI[:], in_=T['U'][:]), ['U'], ['NI'])
                dve(lambda e: e.tensor_copy(out=T['NF'][:], in_=NI[:]), ['NI'], ['NF'])
                ts('TA', src, shift, None, ALU.add, ALU.bypass)
                dve(lambda e: e.scalar_tensor_tensor(out=T['RR'][:], in0=T['NF'][:], scalar=-TWO_PI, in1=T['TA'][:], op0=ALU.mult, op1=ALU.add),
                    ['NF', 'TA'], ['RR'])
                ts('MSK', 'RR', math.pi, -TWO_PI, ALU.is_gt, ALU.mult)
                tt('RR', 'RR', 'MSK', ALU.add)
                ts('MSK', 'RR', -math.pi, TWO_PI, ALU.is_lt, ALU.mult)
                tt('RR', 'RR', 'MSK', ALU.add)
                ts('RR', 'RR', -3.1415925, 3.1415925, ALU.max, ALU.min)
                S.op('act', lambda e: e.activation(out=T[dst][:], in_=T['RR'][:], func=AF.Sin), reads=['RR'], writes=[dst])
            sin_of('SIN', 'PHI', 0.0)
            sin_of('COS', 'PHI', math.pi / 2)
            tt('AR', 'MAG', 'COS', ALU.mult)
            tt('AI', 'MAG', 'SIN', ALU.mult)
            ts('CM1', 'COS', -1.0, None, ALU.add, ALU.bypass)
            tt('NR', 'EM1', 'COS', ALU.mult)
            tt('NR', 'NR', 'CM1', ALU.add)
            tt('TA', 'LR', 'LR', ALU.mult)
            tt('TB', 'LI', 'LI', ALU.mult)
            tt('DEN', 'TA', 'TB', ALU.add)
            dve(lambda e: e.reciprocal(out=T['DEN'][:], in_=T['DEN'][:]), ['DEN'], ['DEN'])
            tt('TA', 'NR', 'LR', ALU.mult)
            tt('TB', 'AI', 'LI', ALU.mult)
            tt('TA', 'TA', 'TB', ALU.add)
            tt('ZR', 'TA', 'DEN', ALU.mult)
            tt('TA', 'AI', 'LR', ALU.mult)
            tt('TB', 'NR', 'LI', ALU.mult)
            tt('TA', 'TA', 'TB', ALU.subtract)
            tt('ZI', 'TA', 'DEN', ALU.mult)
            es1 = es.enter_context(ExitStack())
            BDr = self.sb(es1, [128, 32, 32], F32, "BDr")
            BDi = self.sb(es1, [128, 32, 32], F32, "BDi")
            BBr = self.sb(es1, [128, 32, 32], F32, "BBr")
            BBi = self.sb(es1, [128, 32, 32], F32, "BBi")
            BT1 = self.sb(es1, [128, 32, 32], F32, "BT1")
            BT2 = self.sb(es1, [128, 32, 32], F32, "BT2")
            CBD = [self.sb(es1, [32, 32, 128], F32, "CBD%d" % i) for i in range(2)]
            GT = [self.sb(es1, [128, 32, 32], F32, "GT%d" % i) for i in range(2)]
            S.op('pool', lambda e: e.memset(BDr[:], 0.0), writes=['BDr'])
            S.op('pool', lambda e: e.memset(BDi[:], 0.0), writes=['BDi'])
            for g2 in range(2):
                for (bd, bk, src) in [(BDr, 'BDr', I['s5_b_re']), (BDi, 'BDi', I['s5_b_im'])]:
                    S.dma('sp', lambda e, bd=bd, src=src, g2=g2: e.dma_start(
                        out=bd[g2 * 64:(g2 + 1) * 64, :, g2 * 16:(g2 + 1) * 16],
                        in_=src.rearrange("(gp g2) p h -> g2 p gp h", g2=2)[g2]), reads=[bk], writes=[bk])
            zb = lambda nm: T[nm][:].unsqueeze(2).to_broadcast([128, 32, 32])
            dve(lambda e: e.tensor_tensor(out=BT1[:], in0=BDr[:], in1=zb('ZR'), op=ALU.mult), ['BDr', 'ZR'], ['BT1'])
            dve(lambda e: e.tensor_tensor(out=BT2[:], in0=BDi[:], in1=zb('ZI'), op=ALU.mult), ['BDi', 'ZI'], ['BT2'])
            dve(lambda e: e.tensor_tensor(out=BBr[:], in0=BT1[:], in1=BT2[:], op=ALU.subtract), ['BT1', 'BT2'], ['BBr'])
            dve(lambda e: e.tensor_tensor(out=BT1[:], in0=BDi[:], in1=zb('ZR'), op=ALU.mult), ['BDi', 'ZR'], ['BT1'])
            dve(lambda e: e.tensor_tensor(out=BT2[:], in0=BDr[:], in1=zb('ZI'), op=ALU.mult), ['BDr', 'ZI'], ['BT2'])
            dve(lambda e: e.tensor_tensor(out=BBi[:], in0=BT1[:], in1=BT2[:], op=ALU.add), ['BT1', 'BT2'], ['BBi'])
            ei = 0
            for ri, (bb, bbk) in enumerate([(BBr, 'BBr'), (BBi, 'BBi')]):
                for g0 in range(0, 32, 4):
                    ps, pk = self.psum_next()
                    for j in range(4):
                        S.op('pe', lambda e, ps=ps, bb=bb, g0=g0, j=j: e.transpose(out=ps[:32, j * 128:(j + 1) * 128], in_=bb[:, g0 + j, :],
                                                                                  identity=self.identf[:]), reads=[bbk, 'identf'], writes=[pk])
                    ei += 1
                    self.evac(ei, WvT[ri][:, g0:g0 + 4, :], ps[:32, 0:512].rearrange("p (j q) -> p j q", j=4), [pk], ['WvT%d' % ri])
            for ri, src in enumerate([I['s5_c_re'], I['s5_c_im']]):
                S.op('pool', lambda e, ri=ri: e.memset(CBD[ri][:], 0.0), writes=['CBD%d' % ri])
                for g2 in range(2):
                    S.dma('sp', lambda e, ri=ri, src=src, g2=g2: e.dma_start(
                        out=CBD[ri][g2 * 16:(g2 + 1) * 16, :, g2 * 64:(g2 + 1) * 64],
                        in_=src.rearrange("(gp g2) h p -> g2 h gp p", g2=2)[g2]), reads=['CBD%d' % ri], writes=['CBD%d' % ri])
                for g0 in range(0, 32, 16):
                    ps, pk = self.psum_next()
                    for j in range(16):
                        S.op('pe', lambda e, ps=ps, ri=ri, g0=g0, j=j: e.transpose(out=ps[:, j * 32:(j + 1) * 32], in_=CBD[ri][:, g0 + j, :],
                                                                                  identity=self.identf[:32, :32]),
                             reads=['CBD%d' % ri, 'identf'], writes=[pk])
                    sc = 1.0 if ri == 0 else -1.0
                    dve(lambda e, ps=ps, ri=ri, g0=g0, sc=sc: e.tensor_scalar(out=CT[ri][:, g0:g0 + 16, :],
                                                                             in0=ps[:, 0:512].rearrange("p (j q) -> p j q", j=16),
                                                                             scalar1=sc, scalar2=None, op0=ALU.mult), [pk], ['CT%d' % ri])
            S.dma('sp', lambda e: e.dma_start(out=nat[0][:, 0:32], in_=I['s5_d'].rearrange("(gp g2) h -> gp (g2 h)", g2=2)),
                  reads=['nat0'], writes=['nat0'])
            ps, pk = self.psum_next()
            S.op('pe', lambda e, ps=ps: e.transpose(out=ps[:32, 0:32], in_=nat[0][:, 0:32], identity=self.identf[:32, :32]),
                 reads=['nat0', 'identf'], writes=[pk])
            dve(lambda e, ps=ps: e.tensor_copy(out=Dcol[:], in_=ps[:32, 0:32]), [pk], ['Dcol'])
            dve(lambda e: e.tensor_tensor(out=Dd[:], in0=self.identf[:32, :32].unsqueeze(1).to_broadcast([32, 32, 32]),
                                          in1=Dcol[:].unsqueeze(2).to_broadcast([32, 32, 32]), op=ALU.mult), ['Dcol', 'identf'], ['Dd'])
            LC = 64
            dve(lambda e: e.tensor_copy(out=Gr[:, :, 0:1], in_=T['COS'][:].unsqueeze(2)), ['COS'], ['G'])
            dve(lambda e: e.tensor_copy(out=Gi[:, :, 0:1], in_=T['SIN'][:].unsqueeze(2)), ['SIN'], ['G'])
            n = 1
            while n < LC:
                gr_b = Gr[:, :, n - 1:n].to_broadcast([128, 32, n])
                gi_b = Gi[:, :, n - 1:n].to_broadcast([128, 32, n])
                a0 = GT[0][:, :, 0:n]
                a1 = GT[1][:, :, 0:n]
                dve(lambda e, n=n, gr_b=gr_b, a0=a0: e.tensor_tensor(out=a0, in0=Gr[:, :, 0:n], in1=gr_b, op=ALU.mult), ['G'], ['GT0'])
                dve(lambda e, n=n, gi_b=gi_b, a1=a1: e.tensor_tensor(out=a1, in0=Gi[:, :, 0:n], in1=gi_b, op=ALU.mult), ['G'], ['GT1'])
                dve(lambda e, n=n, a0=a0, a1=a1: e.tensor_tensor(out=Gr[:, :, n:2 * n], in0=a0, in1=a1, op=ALU.subtract), ['GT0', 'GT1', 'G'], ['G'])
                dve(lambda e, n=n, gi_b=gi_b, a0=a0: e.tensor_tensor(out=a0, in0=Gr[:, :, 0:n], in1=gi_b, op=ALU.mult), ['G'], ['GT0'])
                dve(lambda e, n=n, gr_b=gr_b, a1=a1: e.tensor_tensor(out=a1, in0=Gi[:, :, 0:n], in1=gr_b, op=ALU.mult), ['G'], ['GT1'])
                dve(lambda e, n=n, a0=a0, a1=a1: e.tensor_tensor(out=Gi[:, :, n:2 * n], in0=a0, in1=a1, op=ALU.add), ['GT0', 'GT1', 'G'], ['G'])
                n *= 2
            dve(lambda e: e.tensor_copy(out=RHO[:], in_=T['MAG'][:].unsqueeze(2).to_broadcast([128, 32, LC])), ['MAG'], ['RHO'])
            S.op('pool', lambda e: e.memset(RHO[:, :, 0:1], 0.0), reads=['RHO'], writes=['RHO'])

            gcnt = [0]

            def gelu_to(ps, pk, ncol, dst, dkey):
                i = gcnt[0] % 2
                gcnt[0] += 1
                x = gx[i]
                y = g2t[i]
                xk = ('gx', i)
                yk = ('g2t', i)
                S.op('act', lambda e: e.copy(out=x[:, 0:ncol], in_=ps[:32, 0:ncol]), reads=[pk], writes=[xk])
                S.op('act', lambda e: e.activation(out=y[:, 0:ncol], in_=ps[:32, 0:ncol], func=AF.Square, scale=math.sqrt(0.044715)),
                     reads=[pk], writes=[yk])
                dve(lambda e: e.scalar_tensor_tensor(out=y[:, 0:ncol], in0=y[:, 0:ncol], scalar=1.0, in1=x[:, 0:ncol], op0=ALU.add, op1=ALU.mult),
                    [xk, yk], [yk])
                S.op('act', lambda e: e.activation(out=y[:, 0:ncol], in_=y[:, 0:ncol], func=AF.Sigmoid, scale=2.0 * math.sqrt(2.0 / math.pi)),
                     reads=[yk], writes=[yk])
                dve(lambda e: e.tensor_tensor(out=dst, in0=x[:, 0:ncol].rearrange("p (j q) -> p j q", j=dst.shape[1]),
                                              in1=y[:, 0:ncol].rearrange("p (j q) -> p j q", j=dst.shape[1]), op=ALU.mult), [xk, yk], [dkey])

            S.barrier()
            S.flush()
            es1.close()
            es2 = es.enter_context(ExitStack())
            UB = 128
            CPB = UB // LC
            uS = [self.sb(es2, [32, 32, UB], BF16, "uS") for _ in range(3)]
            ubs = [self.sb(es2, [32, 32, UB], BF16, "ub") for _ in range(2)]
            fl5 = self.sb(es2, [128, 2], F32, "fl5")
            S.dma('sp', lambda e: e.dma_start(out=fl5[:], in_=I['flag'][:, :]), writes=['fl5'])
            yS = [self.sb(es2, [32, 32, UB], BF16, "yS") for _ in range(2)]
            RI = [self.sb(es2, [128, 32, LC], F32, "RI%d" % i) for i in range(2)]
            RRt = [self.sb(es2, [128, 32, LC], F32, "RR%d" % i) for i in range(2)]
            SB = [self.sb(es2, [128, 32, LC], BF16, "SB%d" % i) for i in range(2)]
            W1 = [self.sb(es2, [128, 512], F32, "W1_%d" % i) for i in range(8)]
            V1 = [self.sb(es2, [128, 32, LC], F32, "V1_%d" % i) for i in range(2)]
            nchunk = NPR // LC
            pl = lambda fn, reads, writes: S.op('pool', fn, reads=reads, writes=writes)
            wcnt = [0]

            NPRE = nchunk // 2

            def cinfo(c):
                blk = c // CPB
                return blk, (c % CPB) * LC, uS[blk % 3], ('uS', blk % 3), yS[blk % 2], ('yS', blk % 2)

            def load_u(c):
                blk, tb, us, uk, ys, yk = cinfo(c)
                if c < NPRE:
                    S.dma('sp', lambda e: e.dma_start(
                        out=us[:], in_=R['UT'][:, blk * UB:(blk + 1) * UB].rearrange("(gp r) t -> r gp t", r=32)), writes=[uk])
                else:
                    lb = blk - NPRE // CPB
                    ub = ubs[blk % 2]
                    S.dma('sp', lambda e: e.dma_start(
                        out=us[:], in_=R['UT'][:, lb * UB:(lb + 1) * UB].rearrange("(gp r) t -> r gp t", r=32)), writes=[uk])
                    S.dma('sp', lambda e: e.dma_start(
                        out=ub[:], in_=R['UT'][:, HPR + lb * UB:HPR + (lb + 1) * UB].rearrange("(gp r) t -> r gp t", r=32)), writes=[('ub', blk % 2)])

            def blend_u(c):
                blk, tb, us, uk, ys, yk = cinfo(c)
                if c < NPRE:
                    dve(lambda e: e.tensor_scalar(out=us[:], in0=us[:], scalar1=fl5[:32, 1:2], scalar2=None, op0=ALU.mult), [uk, 'fl5'], [uk])
                else:
                    ub = ubs[blk % 2]
                    ubk = ('ub', blk % 2)
                    dve(lambda e: e.tensor_scalar(out=us[:], in0=us[:], scalar1=fl5[:32, 0:1], scalar2=None, op0=ALU.mult), [uk, 'fl5'], [uk])
                    dve(lambda e: e.scalar_tensor_tensor(out=us[:], in0=ub[:], scalar=fl5[:32, 1:2], in1=us[:], op0=ALU.mult, op1=ALU.add),
                        [uk, ubk, 'fl5'], [uk])

            def do_V(c):
                blk, tb, us, uk, ys, yk = cinfo(c)
                if c % CPB == 0 and c + CPB < nchunk:
                    load_u(c + CPB)
                if c % CPB == CPB - 1 and c + 1 < nchunk:
                    blend_u(c + 1)
                for hf in range(2):
                    pss = [[self.psum_next(), self.psum_next()], [self.psum_next(), self.psum_next()]]
                    for ri in range(2):
                        for j in range(16):
                            gp = hf * 16 + j
                            ps, pk = pss[ri][j // 8]
                            col = (j % 8) * LC
                            S.op('pe', lambda e, ps=ps, ri=ri, gp=gp, col=col: e.matmul(
                                ps[:, col:col + LC], lhsT=WvT[ri][:, gp, :], rhs=us[:, gp, tb:tb + LC], start=True, stop=True),
                                reads=['WvT%d' % ri, uk], writes=[pk])
                    for tl in range(2):
                        g0 = hf * 16 + tl * 8
                        (pr, prk) = pss[0][tl]
                        (pi, pik) = pss[1][tl]
                        grs = Gr[:, g0:g0 + 8, :].rearrange("p g t -> p (g t)")
                        gis = Gi[:, g0:g0 + 8, :].rearrange("p g t -> p (g t)")
                        wi = (wcnt[0] % 2) * 4
                        wcnt[0] += 1
                        w = W1[wi:wi + 4]
                        wk = ['w%d' % (wi + q) for q in range(4)]
                        dve(lambda e, pr=pr, grs=grs, w=w: e.tensor_tensor(out=w[0][:], in0=pr[:, 0:512], in1=grs, op=ALU.mult), [prk, 'G'], [wk[0]])
                        dve(lambda e, pi=pi, gis=gis, w=w: e.tensor_tensor(out=w[1][:], in0=pi[:, 0:512], in1=gis, op=ALU.mult), [pik, 'G'], [wk[1]])
                        dve(lambda e, pi=pi, grs=grs, w=w: e.tensor_tensor(out=w[2][:], in0=pi[:, 0:512], in1=grs, op=ALU.mult), [pik, 'G'], [wk[2]])
                        dve(lambda e, pr=pr, gis=gis, w=w: e.tensor_tensor(out=w[3][:], in0=pr[:, 0:512], in1=gis, op=ALU.mult), [prk, 'G'], [wk[3]])
                        pl(lambda e, g0=g0, w=w: e.tensor_tensor(out=RI[0][:, g0:g0 + 8, :].rearrange("p g t -> p (g t)"), in0=w[0][:], in1=w[1][:],
                                                             op=ALU.add), [wk[0], wk[1]], ['RI0'])
                        pl(lambda e, g0=g0, w=w: e.tensor_tensor(out=RI[1][:, g0:g0 + 8, :].rearrange("p g t -> p (g t)"), in0=w[2][:], in1=w[3][:],
                                                             op=ALU.subtract), [wk[2], wk[3]], ['RI1'])

            def do_scan(c):
                if c > 0:
                    for ri, sp in enumerate(['SPr', 'SPi']):
                        dve(lambda e, sp=sp: e.tensor_tensor(out=T['TC'][:], in0=T[sp][:], in1=T['MAG'][:], op=ALU.mult), [sp, 'MAG'], ['TC'])
                        dve(lambda e, ri=ri: e.tensor_tensor(out=RI[ri][:, :, 0:1], in0=RI[ri][:, :, 0:1], in1=T['TC'][:].unsqueeze(2), op=ALU.add),
                            ['TC', 'RI%d' % ri], ['RI%d' % ri])
                for ri in range(2):
                    dve(lambda e, ri=ri: e.tensor_tensor_scan(out=RRt[ri][:].rearrange("p g t -> p (g t)"), data0=RHO[:].rearrange("p g t -> p (g t)"),
                                                             data1=RI[ri][:].rearrange("p g t -> p (g t)"), initial=0.0, op0=ALU.mult, op1=ALU.add),
                        ['RHO', 'RI%d' % ri], ['RRt%d' % ri])
                L = LC - 1
                col = lambda t_: t_[:, :, L:L + 1]
                dve(lambda e: e.tensor_tensor(out=T['TC'][:].unsqueeze(2), in0=col(RRt[0]), in1=col(Gr), op=ALU.mult), ['RRt0', 'G'], ['TC'])
                dve(lambda e: e.tensor_tensor(out=T['TD'][:].unsqueeze(2), in0=col(RRt[1]), in1=col(Gi), op=ALU.mult), ['RRt1', 'G'], ['TD'])
                dve(lambda e: e.tensor_tensor(out=T['SPr'][:], in0=T['TC'][:], in1=T['TD'][:], op=ALU.subtract), ['TC', 'TD'], ['SPr'])
                dve(lambda e: e.tensor_tensor(out=T['TC'][:].unsqueeze(2), in0=col(RRt[1]), in1=col(Gr), op=ALU.mult), ['RRt1', 'G'], ['TC'])
                dve(lambda e: e.tensor_tensor(out=T['TD'][:].unsqueeze(2), in0=col(RRt[0]), in1=col(Gi), op=ALU.mult), ['RRt0', 'G'], ['TD'])
                dve(lambda e: e.tensor_tensor(out=T['SPi'][:], in0=T['TC'][:], in1=T['TD'][:], op=ALU.add), ['TC', 'TD'], ['SPi'])

            def do_rotout(c):
                pl(lambda e: e.tensor_tensor(out=V1[0][:], in0=RRt[0][:], in1=Gr[:], op=ALU.mult), ['RRt0', 'G'], ['v0'])
                pl(lambda e: e.tensor_tensor(out=V1[1][:], in0=RRt[1][:], in1=Gi[:], op=ALU.mult), ['RRt1', 'G'], ['v1'])
                pl(lambda e: e.tensor_tensor(out=SB[0][:], in0=V1[0][:], in1=V1[1][:], op=ALU.subtract), ['v0', 'v1'], ['SB0'])
                pl(lambda e: e.tensor_tensor(out=V1[0][:], in0=RRt[1][:], in1=Gr[:], op=ALU.mult), ['RRt1', 'G'], ['v0'])
                pl(lambda e: e.tensor_tensor(out=V1[1][:], in0=RRt[0][:], in1=Gi[:], op=ALU.mult), ['RRt0', 'G'], ['v1'])
                pl(lambda e: e.tensor_tensor(out=SB[1][:], in0=V1[0][:], in1=V1[1][:], op=ALU.add), ['v0', 'v1'], ['SB1'])

            def do_y(c):
                blk, tb, us, uk, ys, yk = cinfo(c)
                for g0 in range(0, 32, 8):
                    ps, pk = self.psum_next()
                    for j in range(8):
                        gp = g0 + j
                        cs = j * LC
                        S.op('pe', lambda e, ps=ps, gp=gp, cs=cs: e.matmul(ps[:32, cs:cs + LC], lhsT=CT[0][:, gp, :], rhs=SB[0][:, gp, :],
                                                                         start=True, stop=False), reads=['CT0', 'SB0'], writes=[pk])
                        S.op('pe', lambda e, ps=ps, gp=gp, cs=cs: e.matmul(ps[:32, cs:cs + LC], lhsT=CT[1][:, gp, :], rhs=SB[1][:, gp, :],
                                                                         start=False, stop=False), reads=['CT1', 'SB1'], writes=[pk])
                        S.op('pe', lambda e, ps=ps, gp=gp, cs=cs: e.matmul(ps[:32, cs:cs + LC], lhsT=Dd[:, gp, :], rhs=us[:, gp, tb:tb + LC],
                                                                         start=False, stop=True), reads=['Dd', uk], writes=[pk])
                    gelu_to(ps, pk, 512, ys[:, g0:g0 + 8, tb:tb + LC], yk)
                if c % CPB == CPB - 1:
                    lb = blk - NPRE // CPB
                    S.dma('sp', lambda e: e.dma_start(
                        out=R['S5G'][:, lb * UB:(lb + 1) * UB].rearrange("(gp r) t -> r gp t", r=32), in_=ys[:]),
                        reads=[yk], writes=[('dram', 's5g', blk)])

            load_u(0)
            blend_u(0)
            do_V(0)
            for c in range(nchunk):
                do_scan(c)
                if c >= NPRE:
                    do_rotout(c)
                if c + 1 < nchunk:
                    do_V(c + 1)
                if c >= NPRE:
                    do_y(c)
            for ri, (sp, dst) in enumerate([('SPr', O['s5r_p']), ('SPi', O['s5i_p'])]):
                ps, pk = self.psum_next()
                S.op('pe', lambda e, ps=ps, sp=sp: e.transpose(out=ps[:32, 0:128], in_=T[sp][:, :], identity=self.identf[:]),
                     reads=[sp, 'identf'], writes=[pk])
                dve(lambda e, ps=ps, ri=ri: e.tensor_copy(out=nat[ri][:], in_=ps[:32, 0:128]), [pk], ['nat%d' % ri])
                S.dma('sp', lambda e, ri=ri, dst=dst: e.dma_start(out=dst.rearrange("(gp g2) p -> gp (g2 p)", g2=2), in_=nat[ri][:]),
                      reads=['nat%d' % ri], writes=[('dram', 's5p', ri)])
            S.barrier()
            S.flush()
            es2.close()
            es3 = es.enter_context(ExitStack())
            sre = [self.sb(es3, [NSM, 4096], F32, "sre%d" % i) for i in range(2)]
            SO = [self.sb(es3, [128, 32, NSM], F32, "SO%d" % i) for i in range(2)]
            SN = [self.sb(es3, [128, 32, NSM], F32, "SN%d" % i) for i in range(2)]
            SNb = [self.sb(es3, [128, 32, NSM], BF16, "SNb%d" % i) for i in range(2)]
            Q1 = [self.sb(es3, [128, 32, NSM], F32, "Q1_%d" % i) for i in range(2)]
            uSs = self.sb(es3, [32, 32, NSM], BF16, "uSs")
            ySs = self.sb(es3, [32, 32, NSM], BF16, "ySs")
            S.dma('sp', lambda e: e.dma_start(out=sre[0][:], in_=I['s5r'][:, :]), writes=['sre0'])
            S.dma('sp', lambda e: e.dma_start(out=sre[1][:], in_=I['s5i'][:, :]), writes=['sre1'])
            S.dma('sp', lambda e: e.dma_start(out=uSs[:], in_=R['UT'][:, NPR:NTOK].rearrange("(gp r) t -> r gp t", r=32)), writes=['uSs'])
            for ri in range(2):
                for g0 in (0, 16):
                    ps, pk = self.psum_next()
                    for j in range(16):
                        S.op('pe', lambda e, ps=ps, ri=ri, g0=g0, j=j: e.transpose(out=ps[:, j * NSM:(j + 1) * NSM],
                                                                                  in_=sre[ri][:NSM, (g0 + j) * 128:(g0 + j + 1) * 128],
                                                                                  identity=self.identf[:NSM, :NSM]),
                             reads=['sre%d' % ri, 'identf'], writes=[pk])
                    dve(lambda e, ps=ps, ri=ri, g0=g0: e.tensor_copy(out=SO[ri][:, g0:g0 + 16, :], in_=ps[:, 0:16 * NSM].rearrange("p (j q) -> p j q", j=16)),
                        [pk], ['SO%d' % ri])
            ab = lambda nm: T[nm][:].unsqueeze(2).to_broadcast([128, 32, NSM])
            for ri in range(2):
                a_, b_ = (0, 1) if ri == 0 else (1, 0)
                dve(lambda e, a_=a_: e.tensor_tensor(out=Q1[0][:], in0=SO[a_][:], in1=ab('AR'), op=ALU.mult), ['SO%d' % a_, 'AR'], ['Q10'])
                dve(lambda e, b_=b_: e.tensor_tensor(out=Q1[1][:], in0=SO[b_][:], in1=ab('AI'), op=ALU.mult), ['SO%d' % b_, 'AI'], ['Q11'])
                dve(lambda e, ri=ri: e.tensor_tensor(out=Q1[0][:], in0=Q1[0][:], in1=Q1[1][:], op=(ALU.subtract if ri == 0 else ALU.add)),
                    ['Q10', 'Q11'], ['Q10'])
                for g0 in (0, 16):
                    ps, pk = self.psum_next()
                    for j in range(16):
                        gp = g0 + j
                        S.op('pe', lambda e, ps=ps, ri=ri, gp=gp, j=j: e.matmul(ps[:, j * NSM:(j + 1) * NSM], lhsT=WvT[ri][:, gp, :], rhs=uSs[:, gp, :],
                                                                               start=True, stop=True), reads=['WvT%d' % ri, 'uSs'], writes=[pk])
                    dve(lambda e, ps=ps, ri=ri, g0=g0: e.tensor_tensor(out=SN[ri][:, g0:g0 + 16, :], in0=Q1[0][:, g0:g0 + 16, :],
                                                                      in1=ps[:, 0:16 * NSM].rearrange("p (j q) -> p j q", j=16), op=ALU.add),
                        [pk, 'Q10'], ['SN%d' % ri])
                dve(lambda e, ri=ri: e.tensor_copy(out=SNb[ri][:], in_=SN[ri][:]), ['SN%d' % ri], ['SNb%d' % ri])
            for g0 in (0, 16):
                ps, pk = self.psum_next()
                for j in range(16):
                    gp = g0 + j
                    cs = j * NSM
                    S.op('pe', lambda e, ps=ps, gp=gp, cs=cs: e.matmul(ps[:32, cs:cs + NSM], lhsT=CT[0][:, gp, :], rhs=SNb[0][:, gp, :],
                                                                     start=True, stop=False), reads=['CT0', 'SNb0'], writes=[pk])
                    S.op('pe', lambda e, ps=ps, gp=gp, cs=cs: e.matmul(ps[:32, cs:cs + NSM], lhsT=CT[1][:, gp, :], rhs=SNb[1][:, gp, :],
                                                                     start=False, stop=False), reads=['CT1', 'SNb1'], writes=[pk])
                    S.op('pe', lambda e, ps=ps, gp=gp, cs=cs: e.matmul(ps[:32, cs:cs + NSM], lhsT=Dd[:, gp, :], rhs=uSs[:, gp, :],
                                                                     start=False, stop=True), reads=['Dd', 'uSs'], writes=[pk])
                gelu_to(ps, pk, 16 * NSM, ySs[:, g0:g0 + 16, :], 'ySs')
            S.dma('sp', lambda e: e.dma_start(out=R['S5G'][:, HPR:HT].rearrange("(gp r) t -> r gp t", r=32), in_=ySs[:]),
                  reads=['ySs'], writes=[('dram', 's5gs')])
            for ri, dst in enumerate([O['s5r_s'], O['s5i_s']]):
                for g0 in range(0, 32, 4):
                    ps, pk = self.psum_next()
                    for j in range(4):
                        S.op('pe', lambda e, ps=ps, ri=ri, g0=g0, j=j: e.transpose(out=ps[:NSM, j * 128:(j + 1) * 128], in_=SN[ri][:, g0 + j, :],
                                                                                  identity=self.identf[:]), reads=['SN%d' % ri, 'identf'], writes=[pk])
                    ei += 1
                    self.evac(ei, sre[ri][:, g0 * 128:(g0 + 4) * 128], ps[:NSM, 0:512], [pk], ['sre%d' % ri])
                S.dma('sp', lambda e, ri=ri, dst=dst: e.dma_start(out=dst[:, :], in_=sre[ri][:]), reads=['sre%d' % ri], writes=[('dram', 's5s', ri)])
            S.barrier()
            S.flush()
            es3.close()
        self.s5_glu()

    def s5_glu(self):
        S = self.S
        I = self.I
        R = self.R
        for blk in [(0, HT)]:
            t0, t1 = blk
            NT = t1 - t0
            with ExitStack() as es:
                gT = self.sb(es, [128, 8, NT], BF16, "gT")
                self.WB = [self.sb(es, [128, 8 * 256], BF16, "wb") for _ in range(2)]
                self.wb_i = 0
                sg = [self.sb(es, [128, 512], F32, "sgl") for _ in range(2)]
                ob = [self.sb(es, [128, 512], BF16, "obl") for _ in range(2)]
                S.dma('sp', lambda e: e.dma_start(out=gT[:], in_=R['S5G'][:, t0:t1].rearrange("(k p) t -> p k t", p=128)), writes=['gT'])
                cnt = 0
                for c0 in range(0, 1024, 256):
                    wv, wkey = self.wload(I['w_s5_glu'][:, c0:c0 + 256], 8, 256)
                    for m in range(2):
                        mi = c0 // 128 + m
                        for (n0, nn) in nchunks(NT):
                            ps, pk = self.psum_next()
                            for k in range(8):
                                S.op('pe', lambda e, ps=ps, wv=wv, k=k, m=m, n0=n0, nn=nn: e.matmul(
                                    ps[:, 0:nn], lhsT=wv[:, k, m * 128:(m + 1) * 128], rhs=gT[:, k, n0:n0 + nn], start=(k == 0), stop=(k == 7)),
                                    reads=[wkey, 'gT'], writes=[pk])
                            sgt = sg[cnt % 2]
                            o = ob[cnt % 2]
                            sk = ('sgl', cnt % 2)
                            ok = ('obl', cnt % 2)
                            cnt += 1
                            S.op('act', lambda e, sgt=sgt, ps=ps, nn=nn: e.activation(out=sgt[:, 0:nn], in_=ps[:, 0:nn], func=AF.Sigmoid),
                                 reads=[pk], writes=[sk])
                            S.op('dve', lambda e, o=o, sgt=sgt, mi=mi, n0=n0, nn=nn: e.tensor_tensor(out=o[:, 0:nn], in0=sgt[:, 0:nn],
                                                                                                    in1=gT[:, mi, n0:n0 + nn], op=ALU.mult),
                                 reads=[sk, 'gT'], writes=[ok])
                            S.dma('sp', lambda e, o=o, mi=mi, n0=n0, nn=nn: e.dma_start(
                                out=R['BR0'][mi * 128:(mi + 1) * 128, t0 + n0:t0 + n0 + nn], in_=o[:, 0:nn]), reads=[ok],
                                writes=[('dram', 'brt0', mi, n0)])
                S.barrier()
                S.flush()

    def mlstm_prompt(self):
        S = self.S
        I = self.I
        R = self.R
        O = self.O
        CL = 64
        NCH = NPR // CL
        dve = lambda fn, reads, writes: S.op('dve', fn, reads=reads, writes=writes)
        act = lambda fn, reads, writes: S.op('act', fn, reads=reads, writes=writes)
        with ExitStack() as es:
            QC = {nm: self.sb(es, [64, NCH, 4], F32, nm) for nm in ['XC', 'EMC', 'WLC', 'WIC']}
            WE = self.sb(es, [128, 4, NCH], F32, "WE")
            Rb = self.sb(es, [64, 4, NPR], F32, "Rb")
            Cf = self.sb(es, [128, 4, 2, 257], F32, "Cf")
            Cb = self.sb(es, [128, 4, 2, 257], BF16, "Cb")
            GH = self.sb(es, [64, 1024], F32, "GH")
            maskT = self.sb(es, [64, 64], F32, "maskT")
            oh4 = self.sb(es, [4, 4, 128], F32, "oh4")
            bi = self.sb(es, [4, 1], F32, "bi")
            bf = self.sb(es, [4, 1], F32, "bf")
            NH = NCH // 2
            flm = self.sb(es, [128, 2], F32, "flm")
            WLCp = self.sb(es, [64, NH, 4], F32, "WLCp")
            WEp = self.sb(es, [128, 4, NH], F32, "WEp")
            S.dma('sp', lambda e: e.dma_start(out=flm[:], in_=I['flag'][:, :]), writes=['flm'])
            es1 = es.enter_context(ExitStack())
            rows = {nm: self.sb(es1, [4, NPR], F32, nm) for nm in ['IG', 'FG', 'L', 'F', 'X', 'Rm', 'RP', 'WI', 'MT', 'EM', 'RE', 'WL', 'ON']}
            r3 = lambda nm: rows[nm][:].rearrange("h (c t) -> h c t", t=CL)
            S.dma('sp', lambda e: e.dma_start(out=rows['IG'][:], in_=R['IGR'][:, 0:NPR]), writes=['IG'])
            S.dma('sp', lambda e: e.dma_start(out=rows['FG'][:], in_=R['FGR'][:, 0:NPR]), writes=['FG'])
            S.dma('sp', lambda e: e.dma_start(out=bi[:], in_=I['b_igate'].rearrange("(h o) -> h o", o=1)), writes=['bi'])
            S.dma('sp', lambda e: e.dma_start(out=bf[:], in_=I['b_fgate'].rearrange("(h o) -> h o", o=1)), writes=['bf'])
            S.dma('sp', lambda e: e.dma_start(out=GH[:], in_=I['g_mlstm_head'].partition_broadcast(64)), writes=['GH'])
            S.op('pool', lambda e: e.memset(maskT[:], 0.0), writes=['maskT'])
            S.op('pool', lambda e: e.affine_select(out=maskT[:], in_=maskT[:], pattern=[[1, 64]], compare_op=ALU.is_ge, fill=-1e30,
                                                   base=0, channel_multiplier=-1), reads=['maskT'], writes=['maskT'])
            S.op('pool', lambda e: e.memset(rows['ON'][:], 1.0), writes=['ON'])
            S.op('pool', lambda e: e.memset(Cf[:], 0.0), writes=['Cf'])
            S.op('pool', lambda e: e.memset(Cb[:], 0.0), writes=['Cb'])
            dve(lambda e: e.tensor_copy(out=oh4[:], in_=self.identf[:4, :4].unsqueeze(2).to_broadcast([4, 4, 128])), ['identf'], ['oh4'])
            dve(lambda e: e.tensor_scalar(out=rows['IG'][:], in0=rows['IG'][:], scalar1=bi[:, 0:1], scalar2=None, op0=ALU.add), ['IG', 'bi'], ['IG'])
            dve(lambda e: e.tensor_scalar(out=rows['FG'][:], in0=rows['FG'][:], scalar1=bf[:, 0:1], scalar2=None, op0=ALU.add), ['FG', 'bf'], ['FG'])
            act(lambda e: e.activation(out=rows['L'][:], in_=rows['FG'][:], func=AF.Exp, scale=-1.0), ['FG'], ['L'])
            act(lambda e: e.activation(out=rows['L'][:], in_=rows['L'][:], func=AF.Ln, bias=1.0), ['L'], ['L'])
            dve(lambda e: e.tensor_tensor_scan(out=rows['F'][:], data0=rows['ON'][:], data1=rows['L'][:], initial=0.0, op0=ALU.mult, op1=ALU.add),
                ['ON', 'L'], ['F'])
            dve(lambda e: e.tensor_tensor(out=rows['X'][:], in0=rows['IG'][:], in1=rows['F'][:], op=ALU.add), ['IG', 'F'], ['X'])
            dve(lambda e: e.tensor_tensor_scan(out=rows['Rm'][:], data0=rows['ON'][:], data1=rows['X'][:], initial=0.0, op0=ALU.mult, op1=ALU.max),
                ['ON', 'X'], ['Rm'])
            dve(lambda e: e.tensor_copy(out=r3('RP')[:, 1:NCH, :], in_=r3('Rm')[:, 0:NCH - 1, CL - 1:CL].to_broadcast([4, NCH - 1, CL])), ['Rm'], ['RP'])
            S.op('pool', lambda e: e.memset(r3('RP')[:, 0:1, :], 0.0), reads=['RP'], writes=['RP'])
            dve(lambda e: e.tensor_tensor(out=rows['WI'][:], in0=rows['RP'][:], in1=rows['Rm'][:], op=ALU.subtract), ['RP', 'Rm'], ['WI'])
            act(lambda e: e.activation(out=rows['WI'][:], in_=rows['WI'][:], func=AF.Exp), ['WI'], ['WI'])
            dve(lambda e: e.tensor_tensor(out=rows['MT'][:], in0=rows['Rm'][:], in1=rows['F'][:], op=ALU.subtract), ['Rm', 'F'], ['MT'])
            act(lambda e: e.activation(out=rows['EM'][:], in_=rows['MT'][:], func=AF.Exp, scale=-1.0), ['MT'], ['EM'])
            dve(lambda e: e.tensor_copy(out=r3('RE'), in_=r3('Rm')[:, :, CL - 1:CL].to_broadcast([4, NCH, CL])), ['Rm'], ['RE'])
            dve(lambda e: e.tensor_tensor(out=rows['WL'][:], in0=rows['X'][:], in1=rows['RE'][:], op=ALU.subtract), ['X', 'RE'], ['WL'])
            act(lambda e: e.activation(out=rows['WL'][:], in_=rows['WL'][:], func=AF.Exp, bias=-math.log(16.0)), ['WL'], ['WL'])
            S.dma('sp', lambda e: e.dma_start(out=O['m_p'][:, :], in_=rows['MT'][:, NPR - 1:NPR]), reads=['MT'], writes=[('dram', 'm_p')])
            for qn, cn in [('X', 'XC'), ('EM', 'EMC'), ('WL', 'WLC'), ('WI', 'WIC')]:
                ps, pk = self.psum_next()
                for c in range(NCH):
                    S.op('pe', lambda e, ps=ps, qn=qn, c=c: e.transpose(out=ps[:64, c * 4:(c + 1) * 4], in_=rows[qn][0:4, c * CL:(c + 1) * CL],
                                                                      identity=self.identf[:4, :4]), reads=[qn, 'identf'], writes=[pk])
                dve(lambda e, ps=ps, cn=cn: e.tensor_copy(out=QC[cn][:].rearrange("p c h -> p (c h)"), in_=ps[:64, 0:NCH * 4]), [pk], [cn])
            ei = 0
            for h in range(4):
                for (n0, nn) in nchunks(NPR):
                    ps, pk = self.psum_next()
                    S.op('pe', lambda e, ps=ps, h=h, n0=n0, nn=nn: e.matmul(ps[:64, 0:nn], lhsT=oh4[:, h, 0:64], rhs=rows['Rm'][:, n0:n0 + nn],
                                                                          start=True, stop=True), reads=['oh4', 'Rm'], writes=[pk])
                    ei += 1
                    self.evac(ei, Rb[:, h, n0:n0 + nn], ps[:64, 0:nn], [pk], ['Rb'])
                ps, pk = self.psum_next()
                S.op('pe', lambda e, ps=ps, h=h: e.matmul(ps[:, 0:NCH], lhsT=oh4[:, h, :], rhs=r3('WI')[:, :, CL - 1], start=True, stop=True),
                     reads=['oh4', 'WI'], writes=[pk])
                dve(lambda e, ps=ps, h=h: e.tensor_copy(out=WE[:, h, :], in_=ps[:, 0:NCH]), [pk], ['WE'])
            f0 = lambda n_: flm[:n_, 0:1]
            f1 = lambda n_: flm[:n_, 1:2]
            dve(lambda e: e.tensor_scalar(out=WLCp[:], in0=QC['WLC'][:, 0:NH, :], scalar1=f1(64), scalar2=None, op0=ALU.mult), ['WLC', 'flm'], ['WLCp'])
            dve(lambda e: e.tensor_copy(out=WEp[:], in_=WE[:, :, 0:NH]), ['WE'], ['WEp'])
            for nm in ['XC', 'EMC', 'WLC', 'WIC']:
                dve(lambda e, nm=nm: e.tensor_scalar(out=QC[nm][:, 0:NH, :], in0=QC[nm][:, 0:NH, :], scalar1=f0(64), scalar2=None, op0=ALU.mult),
                    [nm, 'flm', 'WLCp'], [nm])
                dve(lambda e, nm=nm: e.scalar_tensor_tensor(out=QC[nm][:, 0:NH, :], in0=QC[nm][:, NH:NCH, :], scalar=f1(64), in1=QC[nm][:, 0:NH, :],
                                                           op0=ALU.mult, op1=ALU.add), [nm, 'flm'], [nm])
            dve(lambda e: e.tensor_scalar(out=WE[:, :, 0:NH], in0=WE[:, :, 0:NH], scalar1=f0(128), scalar2=None, op0=ALU.mult), ['WE', 'flm', 'WEp'], ['WE'])
            dve(lambda e: e.scalar_tensor_tensor(out=WE[:, :, 0:NH], in0=WE[:, :, NH:NCH], scalar=f1(128), in1=WE[:, :, 0:NH], op0=ALU.mult, op1=ALU.add),
                ['WE', 'flm'], ['WE'])
            dve(lambda e: e.tensor_scalar(out=Rb[:, :, 0:HPR], in0=Rb[:, :, 0:HPR], scalar1=f0(64), scalar2=None, op0=ALU.mult), ['Rb', 'flm'], ['Rb'])
            dve(lambda e: e.scalar_tensor_tensor(out=Rb[:, :, 0:HPR], in0=Rb[:, :, HPR:NPR], scalar=f1(64), in1=Rb[:, :, 0:HPR], op0=ALU.mult, op1=ALU.add),
                ['Rb', 'flm'], ['Rb'])
            S.barrier()
            S.flush()
            es1.close()
            es2 = es.enter_context(ExitStack())
            BT = 256
            CPB = BT // CL
            qT = [self.sb(es2, [128, 8, BT], BF16, "qTb") for _ in range(2)]
            kT = [self.sb(es2, [128, 8, BT], BF16, "kTb") for _ in range(2)]
            v1 = [self.sb(es2, [64, CPB, 4, 257], BF16, "v1b") for _ in range(2)]
            ktm = [self.sb(es2, [64, CPB, 1024], BF16, "ktmb") for _ in range(2)]
            qTB = self.sb(es2, [128, 8, BT], BF16, "qTB")
            v1B = self.sb(es2, [64, CPB, 4, 256], BF16, "v1B")
            ktmB = self.sb(es2, [64, CPB, 1024], BF16, "ktmB")
            otm = [self.sb(es2, [64, 1024], F32, "otm") for _ in range(2)]
            sgo = [self.sb(es2, [64, 1024], F32, "sgo") for _ in range(2)]
            mlo = [self.sb(es2, [64, 1024], BF16, "mlo") for _ in range(2)]
            mlT = [self.sb(es2, [128, 8, BT], BF16, "mlT") for _ in range(2)]
            tmpw = [self.sb(es2, [64, 64], F32, "tmpw") for _ in range(3)]
            wT = [self.sb(es2, [64, 64], F32, "wT") for _ in range(3)]
            ST = [self.sb(es2, [64, 64], BF16, "ST") for _ in range(3)]
            na = [self.sb(es2, [64, 257], F32, "na") for _ in range(3)]
            num = [self.sb(es2, [64, 257], F32, "num") for _ in range(3)]
            hh = [self.sb(es2, [64, 256], F32, "hh") for _ in range(3)]
            jk = [self.sb(es2, [64, 256], F32, "jk") for _ in range(3)]
            t1 = [self.sb(es2, [64, 256], F32, "t1") for _ in range(3)]
            sm = [self.sb(es2, [64, 8], F32, "sm") for _ in range(3)]
            kw = [self.sb(es2, [64, 256], BF16, "kw") for _ in range(3)]
            for i in range(2):
                S.op('pool', lambda e, i=i: e.memset(v1[i][:], 1.0), writes=[('v1', i)])
            NSLOT = 3

            def stage_u(c, h, i2, bi_, cl, wlc, wend):
                    k_ = lambda nm: (nm, i2)
                    act(lambda e, i2=i2, bi_=bi_, cl=cl, h=h, wlc=wlc: e.activation(out=kw[i2][:], in_=ktm[bi_][:, cl, h * 256:(h + 1) * 256], func=AF.Copy,
                                                                               scale=wlc),
                        [('ktm', bi_), 'WLC', 'WLCp'], [k_('kw')])
                    for dh in range(2):
                        ps_c, pkc = self.psum_next()
                        S.op('pe', lambda e, ps_c=ps_c, i2=i2, dh=dh, bi_=bi_, cl=cl, h=h: e.matmul(
                            ps_c[:, 0:257], lhsT=kw[i2][:, dh * 128:(dh + 1) * 128], rhs=v1[bi_][:, cl, h, :], start=True, stop=True),
                            reads=[k_('kw'), ('v1', bi_)], writes=[pkc])
                        dve(lambda e, ps_c=ps_c, h=h, dh=dh, wend=wend: e.scalar_tensor_tensor(out=Cf[:, h, dh, :], in0=Cf[:, h, dh, :], scalar=wend,
                                                                                       in1=ps_c[:, 0:257], op0=ALU.mult, op1=ALU.add),
                            [pkc, ('Cf', h, dh), 'WE', 'WEp'], [('Cf', h, dh)])
                        S.op('pool', lambda e, h=h, dh=dh: e.tensor_copy(out=Cb[:, h, dh, :], in_=Cf[:, h, dh, :]),
                             reads=[('Cf', h, dh)], writes=[('Cb', h)])


            def stage_a(c, h, i2, bi_, cl, ci, tsl):
                    k_ = lambda nm: (nm, i2)
                    ps_s, pks = self.psum_next()
                    for dh in range(2):
                        S.op('pe', lambda e, ps_s=ps_s, h=h, dh=dh, bi_=bi_, tsl=tsl: e.matmul(
                            ps_s[:64, 0:64], lhsT=kT[bi_][:, h * 2 + dh, tsl], rhs=qT[bi_][:, h * 2 + dh, tsl], start=(dh == 0), stop=(dh == 1)),
                            reads=[('kT', bi_), ('qT', bi_)], writes=[pks])
                    dve(lambda e, i2=i2, h=h, c=c: e.tensor_tensor(out=tmpw[i2][:], in0=maskT[:], in1=Rb[:, h, c * CL:(c + 1) * CL], op=ALU.subtract),
                        ['maskT', 'Rb'], [k_('tmpw')])
                    act(lambda e, i2=i2, h=h, c=c: e.activation(out=wT[i2][:], in_=tmpw[i2][:], func=AF.Exp, bias=QC['XC'][:, c, h:h + 1]),
                        [k_('tmpw'), 'XC'], [k_('wT')])
                    dve(lambda e, i2=i2, ps_s=ps_s: e.scalar_tensor_tensor(out=ST[i2][:], in0=ps_s[:64, 0:64], scalar=1.0 / 16, in1=wT[i2][:],
                                                                         op0=ALU.mult, op1=ALU.mult), [pks, k_('wT')], [k_('ST')])
                    ps_a, pka = self.psum_next()
                    S.op('pe', lambda e, ps_a=ps_a, i2=i2, bi_=bi_, cl=cl, h=h: e.matmul(ps_a[:64, 0:257], lhsT=ST[i2][:], rhs=v1[bi_][:, cl, h, :],
                                                                                        start=True, stop=True),
                         reads=[k_('ST'), ('v1', bi_)], writes=[pka])
                    ps_b, pkb = self.psum_next()
                    for dh in range(2):
                        S.op('pe', lambda e, ps_b=ps_b, h=h, dh=dh, bi_=bi_, tsl=tsl: e.matmul(
                            ps_b[:64, 0:257], lhsT=qT[bi_][:, h * 2 + dh, tsl], rhs=Cb[:, h, dh, :], start=(dh == 0), stop=(dh == 1)),
                            reads=[('qT', bi_), ('Cb', h)], writes=[pkb])
                    act(lambda e, i2=i2, ps_a=ps_a: e.copy(out=na[i2][:], in_=ps_a[:64, 0:257]), [pka], [k_('na')])
                    dve(lambda e, i2=i2, ps_b=ps_b, c=c, h=h: e.scalar_tensor_tensor(out=num[i2][:], in0=ps_b[:64, 0:257], scalar=QC['WIC'][:, c, h:h + 1],
                                                                                   in1=na[i2][:], op0=ALU.mult, op1=ALU.add),
                        [pkb, k_('na'), 'WIC'], [k_('num')])

            def stage_b(c, h, i2, bi_, cl, ci, tsl):
                    k_ = lambda nm: (nm, i2)
                    act(lambda e, i2=i2: e.activation(out=sm[i2][:, 6:7], in_=num[i2][:, 256:257], func=AF.Abs), [k_('num')], [k_('sm')])
                    dve(lambda e, i2=i2, c=c, h=h: e.tensor_scalar(out=sm[i2][:, 0:1], in0=sm[i2][:, 6:7], scalar1=QC['EMC'][:, c, h:h + 1],
                                                                   scalar2=None, op0=ALU.max), [k_('sm'), 'EMC'], [k_('sm')])
                    dve(lambda e, i2=i2: e.reciprocal(out=sm[i2][:, 1:2], in_=sm[i2][:, 0:1]), [k_('sm')], [k_('sm')])
                    act(lambda e, i2=i2: e.activation(out=hh[i2][:], in_=num[i2][:, 0:256], func=AF.Copy, scale=sm[i2][:, 1:2]),
                        [k_('num'), k_('sm')], [k_('hh')])
                    act(lambda e, i2=i2: e.activation(out=jk[i2][:], in_=hh[i2][:], func=AF.Square, accum_out=sm[i2][:, 2:3]),
                        [k_('hh')], [k_('jk'), k_('sm')])
                    dve(lambda e, i2=i2: e.tensor_scalar(out=sm[i2][:, 3:4], in0=sm[i2][:, 2:3], scalar1=1.0 / 256, scalar2=EPS, op0=ALU.mult, op1=ALU.add),
                        [k_('sm')], [k_('sm')])
                    act(lambda e, i2=i2: e.activation(out=sm[i2][:, 4:5], in_=sm[i2][:, 3:4], func=AF.Sqrt), [k_('sm')], [k_('sm')])
                    dve(lambda e, i2=i2: e.reciprocal(out=sm[i2][:, 5:6], in_=sm[i2][:, 4:5]), [k_('sm')], [k_('sm')])
                    dve(lambda e, i2=i2, h=h: e.scalar_tensor_tensor(out=t1[i2][:], in0=hh[i2][:], scalar=sm[i2][:, 5:6], in1=GH[:, h * 256:(h + 1) * 256],
                                                                   op0=ALU.mult, op1=ALU.mult), [k_('hh'), k_('sm'), 'GH'], [k_('t1')])
                    dve(lambda e, i2=i2, h=h, ci=ci: e.tensor_tensor(out=mlo[ci][:, h * 256:(h + 1) * 256], in0=t1[i2][:],
                                                                   in1=sgo[ci][:, h * 256:(h + 1) * 256], op=ALU.mult),
                        [k_('t1'), ('sgo', ci)], [('mlo', ci)])
                    stage_u(c, h, i2, bi_, cl, QC['WLC'][:, c, h:h + 1], WE[:, h, c:c + 1])

            blend = lambda dst, src_b, n_, dk, sk: (
                dve(lambda e: e.tensor_scalar(out=dst, in0=dst, scalar1=flm[:n_, 0:1], scalar2=None, op0=ALU.mult), [dk, 'flm'], [dk]),
                dve(lambda e: e.scalar_tensor_tensor(out=dst, in0=src_b, scalar=flm[:n_, 1:2], in1=dst, op0=ALU.mult, op1=ALU.add), [dk, sk, 'flm'], [dk]))

            def load_kv(bi_, t0):
                for c8 in range(CPB):
                    S.dma('sp', lambda e, c8=c8: e.dma_start(
                        out=v1[bi_][:, c8, :, 0:256], in_=R['VTM'][t0 + c8 * CL:t0 + (c8 + 1) * CL, :].rearrange("s (h v) -> s h v", h=4)),
                        reads=[('v1', bi_)], writes=[('v1', bi_)])
                S.dma('sp', lambda e: e.dma_start(out=ktm[bi_][:], in_=R['KTM'][t0:t0 + BT, :].rearrange("(c s) f -> s c f", s=CL)),
                      writes=[('ktm', bi_)])

            def pre_prefix(c, bi_):
                if c % CPB == 0:
                    load_kv(bi_, (c // CPB) * BT)

            def load_otm(co):
                ci = co % 2
                S.dma('sp', lambda e: e.dma_start(out=otm[ci][:], in_=R['OTMo'][co * CL:(co + 1) * CL, :]), writes=[('otm', ci)])

            def pre_own(co, bi_):
                if co % CPB == 0:
                    t0 = (co // CPB) * BT
                    load_kv(bi_, t0)
                    for c8 in range(CPB):
                        S.dma('sp', lambda e, c8=c8: e.dma_start(
                            out=v1B[:, c8, :, :], in_=R['VTM'][HPR + t0 + c8 * CL:HPR + t0 + (c8 + 1) * CL, :].rearrange("s (h v) -> s h v", h=4)),
                            writes=['v1B'])
                    S.dma('sp', lambda e: e.dma_start(out=ktmB[:], in_=R['KTM'][HPR + t0:HPR + t0 + BT, :].rearrange("(c s) f -> s c f", s=CL)),
                          writes=['ktmB'])
                    blend(v1[bi_][:, :, :, 0:256], v1B[:], 64, ('v1', bi_), 'v1B')
                    blend(ktm[bi_][:], ktmB[:], 64, ('ktm', bi_), 'ktmB')
                    S.dma('sp', lambda e: e.dma_start(out=qT[bi_][:], in_=R['QTo'][:, t0:t0 + BT].rearrange("(k p) t -> p k t", p=128)),
                          writes=[('qT', bi_)])
                    S.dma('sp', lambda e: e.dma_start(out=kT[bi_][:], in_=R['KT'][:, t0:t0 + BT].rearrange("(k p) t -> p k t", p=128)),
                          writes=[('kT', bi_)])
                    S.dma('sp', lambda e: e.dma_start(out=qTB[:], in_=R['KT'][:, HPR + t0:HPR + t0 + BT].rearrange("(k p) t -> p k t", p=128)),
                          writes=['qTB'])
                    blend(kT[bi_][:], qTB[:], 128, ('kT', bi_), 'qTB')
                ci = co % 2
                if co == 0:
                    load_otm(0)
                if co + 1 < NH:
                    load_otm(co + 1)
                act(lambda e: e.activation(out=sgo[ci][:], in_=otm[ci][:], func=AF.Sigmoid), [('otm', ci)], [('sgo', ci)])

            def chunk_post(co, bi_):
                cl = co % CPB
                ci = co % 2
                tsl = slice(cl * CL, (cl + 1) * CL)
                pb, pbk = self.psumb_next()
                for k in range(8):
                    S.op('pe', lambda e, pb=pb, k=k, ci=ci: e.transpose(out=pb[:, k * 64:(k + 1) * 64], in_=mlo[ci][:64, k * 128:(k + 1) * 128],
                                                                      identity=self.identb[:64, :64]), reads=[('mlo', ci), 'identb'], writes=[pbk])
                act(lambda e, pb=pb: e.copy(out=mlT[bi_][:, :, tsl], in_=pb[:, 0:512].rearrange("p (k t) -> p k t", k=8)),
                    [pbk], [('mlT', bi_)])
                if cl == CPB - 1:
                    t0 = (co // CPB) * BT
                    S.dma('sp', lambda e: e.dma_start(out=R['BR1'][:, t0:t0 + BT].rearrange("(k p) t -> p k t", p=128), in_=mlT[bi_][:]),
                          reads=[('mlT', bi_)], writes=[('dram', 'br1', t0)])

            n_ = 0
            nblk_pre = NH // CPB
            for c in range(NH):
                bi_ = (c // CPB) % 2
                pre_prefix(c, bi_)
                for h in range(4):
                    stage_u(c, h, n_ % NSLOT, bi_, c % CPB, WLCp[:, c, h:h + 1], WEp[:, h, c:c + 1])
                    n_ += 1
            prev = None
            for co in range(NH):
                bi_ = (nblk_pre + co // CPB) % 2
                for h in range(4):
                    if h == 0:
                        pre_own(co, bi_)
                    cl = co % CPB
                    args = (co, h, n_ % NSLOT, bi_, cl, co % 2, slice(cl * CL, (cl + 1) * CL))
                    n_ += 1
                    stage_a(*args)
                    if prev is not None:
                        stage_b(*prev)
                        if prev[1] == 3:
                            chunk_post(prev[0], prev[3])
                    prev = args
            stage_b(*prev)
            chunk_post(prev[0], prev[3])
            for h in range(4):
                for dh in range(2):
                    S.dma('sp', lambda e, h=h, dh=dh: e.dma_start(out=O['C_p'][h, dh * 128:(dh + 1) * 128, :], in_=Cf[:, h, dh, 0:256]),
                          reads=[('Cf', h, dh)], writes=[('dram', 'C_p', h, dh)])
                    S.dma('sp', lambda e, h=h, dh=dh: e.dma_start(out=O['n_p'][h, dh * 128:(dh + 1) * 128].rearrange("(p o) -> p o", o=1),
                                                                 in_=Cf[:, h, dh, 256:257]),
                          reads=[('Cf', h, dh)], writes=[('dram', 'n_p', h, dh)])
            S.barrier()
            S.flush()
            es2.close()

    def mlstm_sample(self):
        S = self.S
        I = self.I
        R = self.R
        O = self.O
        B = NSM
        dve = lambda fn, reads, writes: S.op('dve', fn, reads=reads, writes=writes)
        act = lambda fn, reads, writes: S.op('act', fn, reads=reads, writes=writes)
        with ExitStack() as es:
            sc = {nm: self.sb(es, [B, 4], F32, "ms_" + nm) for nm in
                  ['g', 'bi', 'bf', 'iv', 'l', 'm0', 'gi', 'mt', 'wi', 'wa', 'em', 'qk', 'qn', 's', 'nq', 'rd', 'ss', 'rs', 't']}
            G8 = self.sb(es, [B, 8], F32, "G8")
            qS = self.sb(es, [128, 8, B], BF16, "qSm")
            qtm = self.sb(es, [B, 1024], BF16, "qtmm")
            ktm = self.sb(es, [B, 1024], BF16, "ktmm")
            vtm = self.sb(es, [B, 1024], BF16, "vtmm")
            otm = self.sb(es, [B, 1024], F32, "otmm")
            n0 = self.sb(es, [B, 1024], F32, "n0m")
            GH = self.sb(es, [B, 1024], F32, "GHm")
            p1 = self.sb(es, [B, 1024], F32, "p1m")
            p2 = self.sb(es, [B, 1024], F32, "p2m")
            numt = self.sb(es, [B, 1024], F32, "numt")
            kws = self.sb(es, [B, 1024], BF16, "kws")
            Km = [self.sb(es, [B, 1024], BF16, "Km") for _ in range(2)]
            mlb = self.sb(es, [B, 1024], BF16, "mlb")
            mlT = self.sb(es, [128, 8, B], BF16, "mlTs")
            IDB = self.sb(es, [128, B, B], BF16, "IDB")
            Qm = self.sb(es, [128, 8, B, B], BF16, "Qm")
            WD = self.sb(es, [B, B, 4], F32, "WD")
            ones32 = self.sb(es, [B, 128], F32, "ones32")
            Wb = self.sb(es, [128, B * 4], F32, "Wb")
            C32 = [self.sb(es, [128, 4, 2, 256], F32, "C32") for _ in range(3)]
            C16 = [self.sb(es, [128, 4, 2, 256], BF16, "C16") for _ in range(2)]
            Co = [self.sb(es, [128, 4, 2, 256], F32, "Co") for _ in range(2)]
            S.dma('sp', lambda e: e.dma_start(out=G8[:], in_=R['GTM'][NPR:NTOK, :]), writes=['G8'])
            S.dma('sp', lambda e: e.dma_start(out=sc['bi'][:], in_=I['b_igate'].partition_broadcast(B)), writes=['bi'])
            S.dma('sp', lambda e: e.dma_start(out=sc['bf'][:], in_=I['b_fgate'].partition_broadcast(B)), writes=['bf'])
            S.dma('sp', lambda e: e.dma_start(out=sc['m0'][:], in_=I['mM'][:, :]), writes=['m0'])
            S.dma('sp', lambda e: e.dma_start(out=n0[:], in_=I['mN'][:, :]), writes=['n0'])
            S.dma('sp', lambda e: e.dma_start(out=GH[:], in_=I['g_mlstm_head'].partition_broadcast(B)), writes=['GH'])
            S.dma('sp', lambda e: e.dma_start(out=ktm[:], in_=R['KTM'][NPR:NTOK, :]), writes=['ktm'])
            S.dma('sp', lambda e: e.dma_start(out=vtm[:], in_=R['VTM'][NPR:NTOK, :]), writes=['vtm'])
            S.dma('sp', lambda e: e.dma_start(out=otm[:], in_=R['OTMo'][HPR:HT, :]), writes=['otm'])
            S.dma('sp', lambda e: e.dma_start(out=qS[:], in_=R['QTo'][:, HPR:HT].rearrange("(k p) t -> p k t", p=128)), writes=['qS'])
            for kq in range(0, 8, 4):
                pb, pbk = self.psumb_next()
                for j in range(4):
                    S.op('pe', lambda e, pb=pb, j=j, kq=kq: e.transpose(out=pb[:B, j * 128:(j + 1) * 128], in_=qS[:, kq + j, :],
                                                                       identity=self.identb[:]), reads=['qS', 'identb'], writes=[pbk])
                act(lambda e, pb=pb, kq=kq: e.copy(out=qtm[:, kq * 128:(kq + 4) * 128], in_=pb[:B, 0:512]), [pbk], ['qtm'])
            S.op('pool', lambda e: e.memset(IDB[:], 1.0), writes=['IDB'])
            S.op('pool', lambda e: e.affine_select(out=IDB[:], in_=IDB[:], pattern=[[1, B], [-1, B]], compare_op=ALU.is_equal, fill=0.0,
                                                   base=0, channel_multiplier=0), reads=['IDB'], writes=['IDB'])
            S.op('pool', lambda e: e.memset(ones32[:], 1.0), writes=['ones32'])
            for k in range(8):
                dve(lambda e, k=k: e.tensor_tensor(out=Qm[:, k, :, :], in0=qS[:, k, :].unsqueeze(1).to_broadcast([128, B, B]), in1=IDB[:], op=ALU.mult),
                    ['qS', 'IDB'], ['Qm'])
            tt = lambda o, a, b, op: dve(lambda e: e.tensor_tensor(out=sc[o][:], in0=sc[a][:], in1=sc[b][:], op=op), [a, b], [o])
            dve(lambda e: e.tensor_tensor(out=sc['iv'][:], in0=G8[:, 0:4], in1=sc['bi'][:], op=ALU.add), ['G8', 'bi'], ['iv'])
            dve(lambda e: e.tensor_tensor(out=sc['g'][:], in0=G8[:, 4:8], in1=sc['bf'][:], op=ALU.add), ['G8', 'bf'], ['g'])
            act(lambda e: e.activation(out=sc['l'][:], in_=sc['g'][:], func=AF.Exp, scale=-1.0), ['g'], ['l'])
            act(lambda e: e.activation(out=sc['l'][:], in_=sc['l'][:], func=AF.Ln, bias=1.0), ['l'], ['l'])
            tt('gi', 'm0', 'l', ALU.subtract)
            tt('mt', 'gi', 'iv', ALU.max)
            S.dma('sp', lambda e: e.dma_start(out=O['m_s'][:, :], in_=sc['mt'][:]), reads=['mt'], writes=[('dram', 'm_s')])
            tt('wi', 'gi', 'mt', ALU.subtract)
            act(lambda e: e.activation(out=sc['wi'][:], in_=sc['wi'][:], func=AF.Exp), ['wi'], ['wi'])
            tt('wa', 'iv', 'mt', ALU.subtract)
            act(lambda e: e.activation(out=sc['wa'][:], in_=sc['wa'][:], func=AF.Exp, bias=-math.log(16.0)), ['wa'], ['wa'])
            act(lambda e: e.activation(out=sc['em'][:], in_=sc['mt'][:], func=AF.Exp, scale=-1.0), ['mt'], ['em'])
            v4 = lambda t_: t_[:].rearrange("b (h d) -> b h d", h=4)
            bc = lambda nm: sc[nm][:].unsqueeze(2).to_broadcast([B, 4, 256])
            dve(lambda e: e.tensor_tensor(out=p1[:], in0=qtm[:], in1=ktm[:], op=ALU.mult), ['qtm', 'ktm'], ['p1'])
            dve(lambda e: e.tensor_reduce(out=sc['qk'][:], in_=v4(p1), axis=AX.X, op=ALU.add), ['p1'], ['qk'])
            dve(lambda e: e.tensor_tensor(out=p2[:], in0=qtm[:], in1=n0[:], op=ALU.mult), ['qtm', 'n0'], ['p2'])
            dve(lambda e: e.tensor_reduce(out=sc['qn'][:], in_=v4(p2), axis=AX.X, op=ALU.add), ['p2'], ['qn'])
            tt('s', 'qk', 'wa', ALU.mult)
            tt('t', 'wi', 'qn', ALU.mult)
            tt('nq', 's', 't', ALU.add)
            act(lambda e: e.activation(out=sc['nq'][:], in_=sc['nq'][:], func=AF.Abs), ['nq'], ['nq'])
            tt('rd', 'nq', 'em', ALU.max)
            dve(lambda e: e.reciprocal(out=sc['rd'][:], in_=sc['rd'][:]), ['rd'], ['rd'])
            dve(lambda e: e.tensor_tensor(out=v4(kws), in0=v4(ktm), in1=bc('wa'), op=ALU.mult), ['ktm', 'wa'], ['kws'])
            dve(lambda e: e.tensor_tensor(out=v4(p1), in0=v4(n0), in1=bc('wi'), op=ALU.mult), ['n0', 'wi'], ['p1'])
            dve(lambda e: e.tensor_tensor(out=p1[:], in0=p1[:], in1=kws[:], op=ALU.add), ['p1', 'kws'], ['p1'])
            S.dma('sp', lambda e: e.dma_start(out=O['n_s'][:, :], in_=p1[:]), reads=['p1'], writes=[('dram', 'n_s')])
            dve(lambda e: e.tensor_tensor(out=WD[:], in0=self.identf[:B, :B].unsqueeze(2).to_broadcast([B, B, 4]),
                                          in1=sc['wi'][:].unsqueeze(1).to_broadcast([B, B, 4]), op=ALU.mult), ['identf', 'wi'], ['WD'])
            ps, pk = self.psum_next()
            S.op('pe', lambda e, ps=ps: e.matmul(ps[:, 0:B * 4], lhsT=ones32[:], rhs=WD[:].rearrange("k b h -> k (b h)"), start=True, stop=True),
                 reads=['ones32', 'WD'], writes=[pk])
            dve(lambda e, ps=ps: e.tensor_copy(out=Wb[:], in_=ps[:, 0:B * 4]), [pk], ['Wb'])
            self.pa_i = 0
            pq = [self.psum_next() for _ in range(4)]
            def ms_load(b):
                i2 = b % 3
                for h in range(4):
                    S.dma('sp', lambda e, i2=i2, b=b, h=h: e.dma_start(out=C32[i2][:, h, :, :], in_=I['mC'][b, h].rearrange("(dh p) v -> p dh v", p=128)),
                          writes=[('C32', i2)])

            def ms_compute(b):
                i2 = b % 2
                i3 = b % 3
                S.op('pool', lambda e, i2=i2, i3=i3: e.tensor_copy(out=C16[i2][:], in_=C32[i3][:]), reads=[('C32', i3)], writes=[('C16', i2)])
                act(lambda e, i2=i2, b=b: e.activation(out=Km[i2][:], in_=kws[:], func=AF.Copy, scale=self.identf[:B, b:b + 1]),
                    ['kws', 'identf'], [('Km', i2)])
                for h in range(4):
                    psq, pqk = pq[h]
                    for dh in range(2):
                        S.op('pe', lambda e, psq=psq, b=b, h=h, dh=dh, i2=i2: e.matmul(
                            psq[:B, 0:256], lhsT=Qm[:, h * 2 + dh, b, :], rhs=C16[i2][:, h, dh, :],
                            start=(b == 0 and dh == 0), stop=(b == B - 1 and dh == 1)), reads=['Qm', ('C16', i2)], writes=[pqk])
                    for dh in range(2):
                        psc, pck = self.psumc_next()
                        S.op('pe', lambda e, psc=psc, i2=i2, h=h, dh=dh: e.matmul(
                            psc[:, 0:256], lhsT=Km[i2][:, h * 256 + dh * 128:h * 256 + (dh + 1) * 128], rhs=vtm[:, h * 256:(h + 1) * 256],
                            start=True, stop=True), reads=[('Km', i2), 'vtm'], writes=[pck])
                        dve(lambda e, psc=psc, i2=i2, i3=i3, b=b, h=h, dh=dh: e.scalar_tensor_tensor(
                            out=Co[i2][:, h, dh, :], in0=C32[i3][:, h, dh, :], scalar=Wb[:, b * 4 + h:b * 4 + h + 1], in1=psc[:, 0:256],
                            op0=ALU.mult, op1=ALU.add), [pck, ('C32', i3), 'Wb'], [('Co', i2)])

            def ms_store(b):
                i2 = b % 2
                for h in range(4):
                    S.dma('sp', lambda e, i2=i2, b=b, h=h: e.dma_start(out=O['C_s'][b, h].rearrange("(dh p) v -> p dh v", p=128), in_=Co[i2][:, h, :, :]),
                          reads=[('Co', i2)], writes=[('dram', 'C_s', b, h)])

            ms_load(0)
            ms_load(1)
            for b in range(B):
                if b + 2 < B:
                    ms_load(b + 2)
                ms_compute(b)
                ms_store(b)
            dve(lambda e: e.tensor_tensor(out=v4(numt), in0=v4(vtm), in1=bc('s'), op=ALU.mult), ['vtm', 's'], ['numt'])
            for h in range(4):
                psq, pqk = pq[h]
                dve(lambda e, psq=psq, h=h: e.scalar_tensor_tensor(out=numt[:, h * 256:(h + 1) * 256], in0=psq[:B, 0:256], scalar=sc['wi'][:, h:h + 1],
                                                                 in1=numt[:, h * 256:(h + 1) * 256], op0=ALU.mult, op1=ALU.add),
                    [pqk, 'wi', 'numt'], ['numt'])
            dve(lambda e: e.tensor_tensor(out=v4(numt), in0=v4(numt), in1=bc('rd'), op=ALU.mult), ['numt', 'rd'], ['numt'])
            dve(lambda e: e.tensor_tensor(out=p2[:], in0=numt[:], in1=numt[:], op=ALU.mult), ['numt'], ['p2'])
            dve(lambda e: e.tensor_reduce(out=sc['ss'][:], in_=v4(p2), axis=AX.X, op=ALU.add), ['p2'], ['ss'])
            dve(lambda e: e.tensor_scalar(out=sc['ss'][:], in0=sc['ss'][:], scalar1=1.0 / 256, scalar2=EPS, op0=ALU.mult, op1=ALU.add), ['ss'], ['ss'])
            act(lambda e: e.activation(out=sc['rs'][:], in_=sc['ss'][:], func=AF.Sqrt), ['ss'], ['rs'])
            dve(lambda e: e.reciprocal(out=sc['rs'][:], in_=sc['rs'][:]), ['rs'], ['rs'])
            dve(lambda e: e.tensor_tensor(out=v4(numt), in0=v4(numt), in1=bc('rs'), op=ALU.mult), ['numt', 'rs'], ['numt'])
            dve(lambda e: e.tensor_tensor(out=numt[:], in0=numt[:], in1=GH[:], op=ALU.mult), ['numt', 'GH'], ['numt'])
            act(lambda e: e.activation(out=otm[:], in_=otm[:], func=AF.Sigmoid), ['otm'], ['otm'])
            dve(lambda e: e.tensor_tensor(out=mlb[:], in0=numt[:], in1=otm[:], op=ALU.mult), ['numt', 'otm'], ['mlb'])
            self.transpose_into(mlb, 'mlb', B, 8, mlT, 'mlTs', 0)
            S.dma('sp', lambda e: e.dma_start(out=R['BR1'][:, HPR:HT].rearrange("(k p) t -> p k t", p=128), in_=mlT[:]),
                  reads=['mlTs'], writes=[('dram', 'brt1s')])
            S.barrier()
            S.flush()

    def psumc_next(self):
        i = 4 + (self.pc_i % 2)
        self.pc_i += 1
        return self.PA[i], ('pa', i)

    def final_norm(self, xsrc, ydst, gvec, nrows=NTOK):
        S = self.S
        with ExitStack() as es:
            gb = self.sb(es, [128, D], F32, "gbf")
            xts = [self.sb(es, [128, D], F32, "xtf") for _ in range(2)]
            ys = [self.sb(es, [128, D], F32, "yf") for _ in range(2)]
            st = self.sb(es, [128, 8], F32, "stf")
            self.load_gain(gb, gvec, 'gbf')
            tiles = token_tiles(0, nrows)

            def fn_load(i):
                r0, nr = tiles[i]
                xt = xts[i % 2]
                xk = ('xtf', i % 2)
                S.dma('sp', lambda e: e.dma_start(out=xt[:nr, :], in_=xsrc[r0:r0 + nr, :]), writes=[xk])
            fn_load(0)
            for i, (r0, nr) in enumerate(tiles):
                if i + 1 < len(tiles):
                    fn_load(i + 1)
                xt = xts[i % 2]
                y = ys[i % 2]
                xk = ('xtf', i % 2)
                yk = ('yf', i % 2)
                S.op('act', lambda e, y=y, xt=xt, nr=nr: e.activation(out=y[:nr, :], in_=xt[:nr, :], func=AF.Square,
                                                                     accum_out=st[:nr, 0:1]), reads=[xk], writes=[yk, 'stf'])
                S.op('dve', lambda e, nr=nr: e.tensor_scalar(out=st[:nr, 1:2], in0=st[:nr, 0:1], scalar1=1.0 / D, scalar2=EPS,
                                                             op0=ALU.mult, op1=ALU.add), reads=['stf'], writes=['stf'])
                S.op('act', lambda e, nr=nr: e.activation(out=st[:nr, 2:3], in_=st[:nr, 1:2], func=AF.Sqrt), reads=['stf'], writes=['stf'])
                S.op('dve', lambda e, nr=nr: e.reciprocal(out=st[:nr, 3:4], in_=st[:nr, 2:3]), reads=['stf'], writes=['stf'])
                S.op('dve', lambda e, y=y, xt=xt, nr=nr: e.scalar_tensor_tensor(out=y[:nr, :], in0=xt[:nr, :], scalar=st[:nr, 3:4],
                                                                               in1=gb[:nr, :], op0=ALU.mult, op1=ALU.mult),
                     reads=[xk, 'stf', 'gbf'], writes=[yk])
                S.dma('sp', lambda e, y=y, r0=r0, nr=nr: e.dma_start(out=ydst[r0:r0 + nr, :], in_=y[:nr, :]), reads=[yk],
                      writes=[('dram', 'y', r0)])
            S.barrier()
            S.flush()

    def build(self):
        nc = self.nc
        dbg = self.dbg
        self.I = I = {}
        I['x'] = self.dram_in('x', [NTOK, D])
        I['mem'] = self.dram_in('mem', [256, D])
        I['cache_k'] = self.dram_in('cache_k', [NSM, 256, 1024])
        I['cache_v'] = self.dram_in('cache_v', [NSM, 256, 1024])
        I['s5r'] = self.dram_in('s5r', [NSM, 4096])
        I['s5i'] = self.dram_in('s5i', [NSM, 4096])
        I['mC'] = self.dram_in('mC', [NSM, 4, 256, 256])
        I['mN'] = self.dram_in('mN', [NSM, 1024])
        I['mM'] = self.dram_in('mM', [NSM, 4])
        I['flag'] = self.dram_in('flag', [128, 2])
        for nm, shp in [('g_ffn1', [D]), ('w1_gate', [D, FF]), ('w1_up', [D, FF]), ('w1_down', [FF, D]),
                        ('g_mix', [D]), ('w_in', [D, DIN]),
                        ('s5_lambda_re', [64, 64]), ('s5_lambda_im', [64, 64]), ('s5_log_step', [64]),
                        ('s5_b_re', [64, 64, 16]), ('s5_b_im', [64, 64, 16]), ('s5_c_re', [64, 16, 64]),
                        ('s5_c_im', [64, 16, 64]), ('s5_d', [64, 16]), ('w_s5_glu', [1024, 1024]),
                        ('b_igate', [4]), ('b_fgate', [4]), ('g_mlstm_head', [1024]), ('g_mem', [D]),
                        ('w_mem_k', [D, 1024]), ('w_mem_v', [D, 1024]),
                        ('w_br_s5', [1024, D]), ('w_br_ml', [1024, D]), ('w_br_xa', [1024, D]), ('w_out', [D, D]),
                        ('g_ffn2', [D]), ('w2_gate', [D, FF]), ('w2_up', [D, FF]), ('w2_down', [FF, D]),
                        ('g_final', [D])]:
            I[nm] = self.dram_in(nm, shp)
        self.O = O = {}
        O['y'] = self.dram_out('y', [HT, D])
        O['mk'] = self.dram_out('mk', [256, 1024])
        O['mv'] = self.dram_out('mv', [256, 1024])
        O['s5r_p'] = self.dram_out('s5r_p', [64, 64])
        O['s5i_p'] = self.dram_out('s5i_p', [64, 64])
        O['C_p'] = self.dram_out('C_p', [4, 256, 256])
        O['n_p'] = self.dram_out('n_p', [4, 256])
        O['m_p'] = self.dram_out('m_p', [4, 1])
        O['s5r_s'] = self.dram_out('s5r_s', [NSM, 4096])
        O['s5i_s'] = self.dram_out('s5i_s', [NSM, 4096])
        O['C_s'] = self.dram_out('C_s', [NSM, 4, 256, 256])
        O['n_s'] = self.dram_out('n_s', [NSM, 1024])
        O['m_s'] = self.dram_out('m_s', [NSM, 4])
        scr = self.dram_out if dbg else (lambda n, s, d=F32: self.dram_scr(n, s, d))
        self.R = R = {}
        R['X1'] = scr('X1', [NTOK, D], F32)
        R['X1h'] = scr('X1h', [HT, D], F32)
        R['X2'] = scr('X2', [HT, D], F32)
        R['X3'] = scr('X3', [HT, D], F32)
        pscr = self.dram_in if self.mixtest else scr
        for nm in ['UT', 'KT']:
            R[nm] = pscr(nm, [1024, NTOK], BF16)
        for nm in ['QTo', 'QXTo']:
            R[nm] = pscr(nm, [1024, HT], BF16)
        R['OTMo'] = pscr('OTMo', [HT, 1024], F32)
        for nm in ['KTM', 'VTM']:
            R[nm] = pscr(nm, [NTOK, 1024], BF16)
        R['GTM'] = pscr('GTM', [NTOK, 8], F32)
        R['IGR'] = pscr('IGR', [4, NTOK], F32)
        R['FGR'] = pscr('FGR', [4, NTOK], F32)
        R['MKT'] = pscr('MKT', [1024, 256], BF16)
        R['MVB'] = self.dram_in('MVB', [256, 1024]) if self.mixtest else O['mv']
        R['BRT'] = scr('BRT', [3, 1024, NTOK], BF16)
        R['XAS'] = scr('XAS', [NSM, 1024], F32)
        R['S5G'] = scr('S5G', [1024, HT], BF16)
        R['BR0'] = scr('BR0', [1024, HT], BF16)
        R['BR2'] = scr('BR2', [1024, HT], BF16)
        R['BR1'] = scr('BR1', [1024, HT], BF16)
        with ExitStack() as es:
            self.S = S = Sched(nc, es)
            S.block = es.enter_context(nc.Block())
            self.PA = [es.enter_context(nc.psum_tensor("pa%d" % i, [128, 512], F32)) for i in range(6)]
            self.PB = [es.enter_context(nc.psum_tensor("pb%d" % i, [128, 1024], BF16)) for i in range(2)]
            self.pa_i = 0
            self.pb_i = 0
            self.ecnt = 0
            self.pc_i = 0
            identf = self.sb(es, [128, 128], F32, "identf")
            self.identb = self.sb(es, [128, 128], BF16, "identb")
            self.identf = identf
            S.op('pool', lambda e: e.memset(identf[:], 1.0), writes=['identf'])
            S.op('pool', lambda e: e.affine_select(out=identf[:], in_=identf[:], pattern=[[-1, 128]], compare_op=ALU.is_equal,
                                                   fill=0.0, base=0, channel_multiplier=1), reads=['identf'], writes=['identf'])
            S.op('dve', lambda e: e.tensor_copy(out=self.identb[:], in_=identf[:]), reads=['identf'], writes=['identb'])
            S.barrier()
            st = self.stage
            if not self.mixtest:
                for blk in BLOCKS:
                    self.ffn(es, I['x'], R['X1'], I['g_ffn1'], I['w1_gate'], I['w1_up'], I['w1_down'], blk)
            if st >= 2 and not self.mixtest:
                self.inproj()
                self.memkv()
            if st >= 3:
                self.mixers()
            if st >= 4:
                self.merge()
                self.ffn(es, R['X2'], R['X3'], I['g_ffn2'], I['w2_gate'], I['w2_up'], I['w2_down'], (0, HT))
                self.final_norm(R['X3'], O['y'], I['g_final'], nrows=HT)
            S.barrier()
            S.flush()
        return nc


_NC_CACHE = {}


def _get_nc():
    if 'nc' not in _NC_CACHE:
        _NC_CACHE['nc'] = Builder(stage=99, dbg=False).build()
    return _NC_CACHE['nc']


_WEIGHTS = ['g_ffn1', 'w1_gate', 'w1_up', 'w1_down', 'g_mix', 'w_in', 's5_lambda_re', 's5_lambda_im', 's5_log_step',
            's5_b_re', 's5_b_im', 's5_c_re', 's5_c_im', 's5_d', 'w_s5_glu', 'b_igate', 'b_fgate', 'g_mlstm_head', 'g_mem',
            'w_mem_k', 'w_mem_v', 'w_br_s5', 'w_br_ml', 'w_br_xa', 'w_out', 'g_ffn2', 'w2_gate', 'w2_up', 'w2_down']


def kernel(**inputs):
    f = lambda a: np.ascontiguousarray(np.asarray(a, dtype=np.float32))
    nc = _get_nc()
    shared = {nm: f(inputs[nm])[0] for nm in _WEIGHTS}
    shared['g_final'] = f(inputs['g_final'])
    xp = f(inputs['x_prompt'])
    xs = f(inputs['x_sample'])
    memp = f(inputs['mem_prompt'])
    ck = f(inputs['cache_mem_k'])[0]
    cv = f(inputs['cache_mem_v'])[0]
    s5r = f(inputs['state_s5_re'])[0]
    s5i = f(inputs['state_s5_im'])[0]
    mC = f(inputs['state_mlstm_C'])[0]
    mN = f(inputs['state_mlstm_n'])[0]
    mM = f(inputs['state_mlstm_m'])[0]
    in_maps = []
    for i in range(8):
        j = i % 4
        s0 = j * 2 * NSM + (i // 4) * NSM
        sl = slice(s0, s0 + NSM)
        m = dict(shared)
        m['x'] = np.ascontiguousarray(np.concatenate([xp[j], xs[sl, 0, :]], axis=0))
        m['mem'] = memp[j]
        m['cache_k'] = np.ascontiguousarray(ck[sl].reshape(NSM, 256, 1024))
        m['cache_v'] = np.ascontiguousarray(cv[sl].reshape(NSM, 256, 1024))
        m['s5r'] = np.ascontiguousarray(s5r[sl].reshape(NSM, 4096))
        m['s5i'] = np.ascontiguousarray(s5i[sl].reshape(NSM, 4096))
        m['mC'] = np.ascontiguousarray(mC[sl])
        m['mN'] = np.ascontiguousarray(mN[sl].reshape(NSM, 1024))
        m['mM'] = np.ascontiguousarray(mM[sl])
        fl = np.zeros((128, 2), np.float32)
        fl[:, 0 if i < 4 else 1] = 1.0
        m['flag'] = fl
        in_maps.append(m)
    res = run_bass_kernel_spmd(nc, in_maps, core_ids=list(range(8)))
    r = res.results
    g = lambda nm, j: np.asarray(r[j][nm], dtype=np.float32)
    order = [c for j in range(4) for c in (j, j + 4)]
    y_prompt = np.stack([np.concatenate([g('y', j)[:HPR], g('y', j + 4)[:HPR]], axis=0) for j in range(4)])
    y_sample = np.concatenate([g('y', c)[HPR:HT] for c in order], axis=0).reshape(8 * NSM, 1, D)
    mk = np.stack([g('mk', j).reshape(256, 4, 256) for j in range(4)])[None]
    mv = np.stack([g('mv', j).reshape(256, 4, 256) for j in range(4)])[None]
    s5r_p = np.stack([g('s5r_p', j + 4) for j in range(4)])[None]
    s5i_p = np.stack([g('s5i_p', j + 4) for j in range(4)])[None]
    C_p = np.stack([g('C_p', j + 4) for j in range(4)])[None]
    n_p = np.stack([g('n_p', j + 4) for j in range(4)])[None]
    m_p = np.stack([g('m_p', j)[:, 0] for j in range(4)])[None]
    s5r_s = np.concatenate([g('s5r_s', c).reshape(NSM, 64, 64) for c in order], axis=0)[None]
    s5i_s = np.concatenate([g('s5i_s', c).reshape(NSM, 64, 64) for c in order], axis=0)[None]
    C_s = np.concatenate([g('C_s', c) for c in order], axis=0)[None]
    n_s = np.concatenate([g('n_s', c).reshape(NSM, 4, 256) for c in order], axis=0)[None]
    m_s = np.concatenate([g('m_s', c) for c in order], axis=0)[None]
    return (y_prompt, y_sample, mk, mv, s5r_p, s5i_p, C_p, n_p, m_p, s5r_s, s5i_s, C_s, n_s, m_s)
```

```python
import math
import numpy as np
import concourse.bass as bass
import concourse.mybir as mybir
from concourse.bass_utils import run_bass_kernel_spmd
from contextlib import ExitStack

F32 = mybir.dt.float32
BF16 = mybir.dt.bfloat16
I32 = mybir.dt.int32
AF = mybir.ActivationFunctionType
ALU = mybir.AluOpType
AX = mybir.AxisListType

D = 2048
FF = 5504
NPR = 2048
NSM = 16
NTOK = NPR + NSM
BLOCKS = [(0, 1024), (1024, NTOK)]
HPR = NPR // 2
HT = HPR + NSM
EPS = 1e-6
NDS = 24
POOL_RING_LIMIT = 700
DIN = 12296


class Sched:
    ENG = ['pe', 'dve', 'act', 'pool', 'sp']

    def __init__(self, nc, es):
        self.nc = nc
        self.prog = {e: [] for e in self.ENG}
        self.sem = {e: es.enter_context(nc.semaphore("s_" + e)) for e in self.ENG}
        self.cnt = {e: 0 for e in self.ENG}
        self.seen = {e: {} for e in self.ENG}
        self.dsem = [es.enter_context(nc.semaphore("d%d" % i)) for i in range(NDS)]
        self.dcount = [0] * NDS
        self.dnext = 0
        self.dn = {}
        self.last_w = {}
        self.readers = {}
        self.pool_pending = []

    def _need(self, eng, tok):
        kind, ident, v = tok
        if v <= 0:
            return False
        if kind == 'e' and ident == eng and eng in ('pe', 'sp'):
            return False
        if self.seen[eng].get((kind, ident), 0) >= v:
            return False
        return True

    def _wait(self, eng, kind, ident, v):
        if self._need(eng, (kind, ident, v)):
            self.prog[eng].append(('wait', (kind, ident, v)))
            self.seen[eng][(kind, ident)] = v

    def _deps(self, eng, reads, writes, extra=()):
        deps = {}

        def add(tok):
            key = (tok[0], tok[1])
            if deps.get(key, 0) < tok[2]:
                deps[key] = tok[2]
        for k in reads:
            if k in self.last_w:
                add(self.last_w[k])
        for k in writes:
            if k in self.last_w:
                add(self.last_w[k])
            for t in self.readers.get(k, ()):
                add(t)
        for t in extra:
            add(t)
        for (kind, ident), v in deps.items():
            self._wait(eng, kind, ident, v)

    def _commit(self, tok, reads, writes):
        for k in writes:
            self.last_w[k] = tok
            self.readers[k] = []
        for k in reads:
            self.readers.setdefault(k, []).append(tok)

    def op(self, eng, fn, reads=(), writes=()):
        self._deps(eng, reads, writes)
        self.cnt[eng] += 1
        tok = ('e', eng, self.cnt[eng])
        self.prog[eng].append(('op', fn))
        self._commit(tok, reads, writes)
        return tok

    def dma(self, q, fn, reads=(), writes=(), ndesc=0):
        lo, hi = (16, NDS) if q == 'pool' else (0, 16)
        self.dn[q] = self.dn.get(q, 0) + 1
        i = lo + (self.dn[q] - 1) % (hi - lo)
        prev = ('d', i, self.dcount[i])
        extra = [prev]
        if q == 'pool':
            pend = self.pool_pending
            while pend and sum(n for _, n in pend) + ndesc > POOL_RING_LIMIT:
                extra.append(pend.pop(0)[0])
        self._deps(q, reads, writes, extra=tuple(extra))
        self.dcount[i] += 16
        tok = ('d', i, self.dcount[i])
        self.prog[q].append(('dma', fn, i))
        self._commit(tok, reads, writes)
        if q == 'pool':
            self.pool_pending.append((tok, ndesc))
        return tok

    def barrier(self):
        for e in self.ENG:
            for e2 in self.ENG:
                if e2 != e:
                    self._wait(e, 'e', e2, self.cnt[e2])
            for i in range(NDS):
                self._wait(e, 'd', i, self.dcount[i])
        self.last_w = {}
        self.readers = {}

    def flush(self):
        block = self.block
        handles = {'pe': block.tensor, 'dve': block.vector, 'act': block.scalar,
                   'pool': block.gpsimd, 'sp': block.sync}
        for e in self.ENG:
            prog = self.prog[e]
            self.prog[e] = []
            if not prog:
                continue

            def body(eng, prog=prog, e=e):
                for item in prog:
                    if item[0] == 'wait':
                        kind, ident, v = item[1]
                        s = self.sem[ident] if kind == 'e' else self.dsem[ident]
                        eng.wait_ge(s, v)
                    elif item[0] == 'op':
                        item[1](eng).then_inc(self.sem[e], 1)
                    else:
                        item[1](eng).then_inc(self.dsem[item[2]], 16)
            handles[e](body)


def token_tiles(t0, t1):
    out = []
    r = t0
    while r < t1:
        nr = min(128, t1 - r)
        out.append((r, nr))
        r += nr
    return out


def nchunks(n, step=512):
    out = []
    c = 0
    while c < n:
        out.append((c, min(step, n - c)))
        c += step
    return out


class Builder:
    def __init__(self, stage=99, dbg=False, which=('xp', 'xs', 's5', 'mp', 'ms'), mixtest=False):
        self.which = which
        self.mixtest = mixtest
        self.stage = stage
        self.dbg = dbg
        self.nc = bass.Bass("TRN2", target_bir_lowering=False)
        self.uid = 0

    def dram_in(self, name, shape, dt=F32):
        return self.nc.dram_tensor(name, list(shape), dt, kind="ExternalInput").ap()

    def dram_out(self, name, shape, dt=F32):
        return self.nc.dram_tensor(name, list(shape), dt, kind="ExternalOutput").ap()

    def dram_scr(self, name, shape, dt):
        return self.nc.dram_tensor(name, list(shape), dt, kind="Internal").ap()

    def sb(self, es, shape, dt, name=None):
        self.uid += 1
        return es.enter_context(self.nc.sbuf_tensor("%s_%d" % (name or "t", self.uid), list(shape), dt))

    def psum_next(self):
        i = self.pa_i
        self.pa_i = (self.pa_i + 1) % len(self.PA)
        return self.PA[i], ('pa', i)

    def psumb_next(self):
        i = self.pb_i
        self.pb_i = (self.pb_i + 1) % len(self.PB)
        return self.PB[i], ('pb', i)

    def load_gain(self, gb, gvec, key):
        S = self.S
        S.dma('sp', lambda e: e.dma_start(out=gb[:], in_=gvec.partition_broadcast(128)), writes=[key])

    def norm_T(self, src_rows, nr, gb, gkey, dstT, dkey, c0, bufs):
        S = self.S
        xts, xn, st = bufs
        if not isinstance(xts, (list, tuple)):
            xts = [xts]
        self.nt_i = getattr(self, 'nt_i', 0) + 1
        xt = xts[self.nt_i % len(xts)]
        xtk = ('xt', self.nt_i % len(xts))
        S.dma('sp', lambda e: e.dma_start(out=xt[:nr, :], in_=src_rows), writes=[xtk])
        S.op('act', lambda e: e.activation(out=xn[:nr, :], in_=xt[:nr, :], func=AF.Square, accum_out=st[:nr, 0:1]),
             reads=[xtk], writes=['xn', 'st'])
        S.op('dve', lambda e: e.tensor_scalar(out=st[:nr, 1:2], in0=st[:nr, 0:1], scalar1=1.0 / D, scalar2=EPS,
                                              op0=ALU.mult, op1=ALU.add), reads=['st'], writes=['st'])
        S.op('act', lambda e: e.activation(out=st[:nr, 2:3], in_=st[:nr, 1:2], func=AF.Sqrt), reads=['st'], writes=['st'])
        S.op('dve', lambda e: e.reciprocal(out=st[:nr, 3:4], in_=st[:nr, 2:3]), reads=['st'], writes=['st'])
        S.op('dve', lambda e: e.scalar_tensor_tensor(out=xn[:nr, :], in0=xt[:nr, :], scalar=st[:nr, 3:4], in1=gb[:nr, :],
                                                     op0=ALU.mult, op1=ALU.mult),
             reads=[xtk, 'st', gkey], writes=['xn'])
        self.transpose_into(xn, 'xn', nr, 16, dstT, dkey, c0)

    def transpose_into(self, src, skey, nr, nk, dstT, dkey, c0):
        S = self.S
        for kq in range(0, nk, 4):
            pb, pk = self.psumb_next()
            n4 = min(4, nk - kq)
            for j in range(n4):
                k = kq + j
                S.op('pe', lambda e, j=j, k=k, pb=pb: e.transpose(out=pb[:, j * 128:j * 128 + nr], in_=src[:nr, k * 128:(k + 1) * 128],
                                                                  identity=self.identb[:nr, :nr]),
                     reads=[skey, 'identb'], writes=[pk])
            eng = 'act' if (kq // 4) % 2 == 0 else 'dve'
            pv = pb[:, 0:n4 * 128].rearrange("p (j t) -> p j t", j=n4)[:, :, 0:nr]
            if eng == 'act':
                S.op('act', lambda e, pv=pv, kq=kq, n4=n4: e.copy(out=dstT[:, kq:kq + n4, c0:c0 + nr], in_=pv),
                     reads=[pk], writes=[dkey])
            else:
                S.op('dve', lambda e, pv=pv, kq=kq, n4=n4: e.tensor_copy(out=dstT[:, kq:kq + n4, c0:c0 + nr], in_=pv),
                     reads=[pk], writes=[dkey])

    def wload(self, wslice, kt, cw, fmt=None):
        S = self.S
        i = self.wb_i
        self.wb_i = (self.wb_i + 1) % len(self.WB)
        wb = self.WB[i]
        view = wb[:, 0:kt * cw].rearrange("p (k n) -> p k n", k=kt)
        S.dma('pool', lambda e: e.dma_start(out=view, in_=wslice.rearrange("(k p) n -> p k n", p=128)),
              writes=[('wb', i)], ndesc=8 * kt)
        return view, ('wb', i)

    def ffn(self, es0, xsrc, xdst, gvec, wg, wu, wd, blk):
        S = self.S
        t0, t1 = blk
        NT = t1 - t0
        with ExitStack() as es:
            xT = self.sb(es, [128, 16, NT], BF16, "xT")
            hT = self.sb(es, [128, 43, NT], BF16, "hT")
            self.WB = [self.sb(es, [128, 11008], BF16, "wb") for _ in range(2)]
            self.wb_i = 0
            xt = [self.sb(es, [128, D], F32, "xt") for _ in range(2)]
            xn = self.sb(es, [128, D], BF16, "xn")
            st = self.sb(es, [128, 8], F32, "st")
            gb = self.sb(es, [128, D], F32, "gb")
            sg = [self.sb(es, [128, 512], F32, "sg") for _ in range(2)]
            xres = [self.sb(es, [128, 256], F32, "xres") for _ in range(4)]
            ot = [self.sb(es, [128, 256], F32, "ot") for _ in range(2)]
            self.load_gain(gb, gvec, 'gb')
            for (r0, nr) in token_tiles(t0, t1):
                self.norm_T(xsrc[r0:r0 + nr, :], nr, gb, 'gb', xT, 'xT', r0 - t0, (xt, xn, st))
            ncs = nchunks(NT)
            cnt = 0
            c0 = 0
            while c0 < FF:
                cw = min(256, FF - c0)
                i = self.wb_i
                self.wb_i = (self.wb_i + 1) % 2
                wb = self.WB[i]
                wkey = ('wb', i)
                gv = wb[:, 0:16 * cw].rearrange("p (k n) -> p k n", k=16)
                uv = wb[:, 16 * cw:32 * cw].rearrange("p (k n) -> p k n", k=16)
                S.dma('pool', lambda e, gv=gv, c0=c0, cw=cw: e.dma_start(
                    out=gv, in_=wg[:, c0:c0 + cw].rearrange("(k p) n -> p k n", p=128)), writes=[wkey], ndesc=128)
                S.dma('pool', lambda e, uv=uv, c0=c0, cw=cw: e.dma_start(
                    out=uv, in_=wu[:, c0:c0 + cw].rearrange("(k p) n -> p k n", p=128)), writes=[wkey], ndesc=128)
                for m in range(cw // 128):
                    f = c0 // 128 + m
                    for (n0, nn) in ncs:
                        pg, pgk = self.psum_next()
                        pu, puk = self.psum_next()
                        for k in range(16):
                            S.op('pe', lambda e, pg=pg, gv=gv, k=k, m=m, n0=n0, nn=nn: e.matmul(
                                pg[:, 0:nn], lhsT=gv[:, k, m * 128:(m + 1) * 128], rhs=xT[:, k, n0:n0 + nn],
                                start=(k == 0), stop=(k == 15)), reads=[wkey, 'xT'], writes=[pgk])
                        for k in range(16):
                            S.op('pe', lambda e, pu=pu, uv=uv, k=k, m=m, n0=n0, nn=nn: e.matmul(
                                pu[:, 0:nn], lhsT=uv[:, k, m * 128:(m + 1) * 128], rhs=xT[:, k, n0:n0 + nn],
                                start=(k == 0), stop=(k == 15)), reads=[wkey, 'xT'], writes=[puk])
                        sgt = sg[cnt % 2]
                        sk = ('sg', cnt % 2)
                        cnt += 1
                        S.op('act', lambda e, sgt=sgt, pg=pg, nn=nn: e.activation(out=sgt[:, 0:nn], in_=pg[:, 0:nn], func=AF.Silu),
                             reads=[pgk], writes=[sk])
                        S.op('dve', lambda e, sgt=sgt, pu=pu, f=f, n0=n0, nn=nn: e.tensor_tensor(
                            out=hT[:, f, n0:n0 + nn], in0=sgt[:, 0:nn], in1=pu[:, 0:nn], op=ALU.mult),
                            reads=[sk, puk], writes=['hT'])
                c0 += cw
            self.proj_tm_res(hT, 'hT', 43, wd, xsrc, xdst, 0.5, blk, xres, ot)
            S.barrier()
            S.flush()

    def proj_tm_res(self, actT, akey, nk, w, xsrc, xdst, scale, blk, xres, ot, cw=256):
        S = self.S
        t0, t1 = blk
        items = [(c0, r0, nr) for c0 in range(0, D, cw) for (r0, nr) in token_tiles(t0, t1)]
        NX = len(xres)
        PF = NX - 1

        def ld(i):
            c0, r0, nr = items[i]
            xr = xres[i % NX]
            S.dma('sp', lambda e: e.dma_start(out=xr[:nr, 0:cw], in_=xsrc[r0:r0 + nr, c0:c0 + cw]), writes=[('xres', i % NX)])
        for i in range(min(PF, len(items))):
            ld(i)
        wv = wkey = None
        for i, (c0, r0, nr) in enumerate(items):
            if r0 == t0:
                wv, wkey = self.wload(w[:, c0:c0 + cw], nk, cw)
            if i + PF < len(items):
                ld(i + PF)
            rl = r0 - t0
            po, pok = self.psum_next()
            for f in range(nk):
                S.op('pe', lambda e, po=po, wv=wv, f=f, rl=rl, nr=nr: e.matmul(
                    po[:nr, 0:cw], lhsT=actT[:, f, rl:rl + nr], rhs=wv[:, f, :], start=(f == 0), stop=(f == nk - 1)),
                    reads=[wkey, akey], writes=[pok])
            xr = xres[i % NX]
            xk = ('xres', i % NX)
            o = ot[i % len(ot)]
            ok = ('ot', i % len(ot))
            S.op('dve', lambda e, o=o, po=po, xr=xr, nr=nr: e.scalar_tensor_tensor(
                out=o[:nr, 0:cw], in0=po[:nr, 0:cw], scalar=scale, in1=xr[:nr, 0:cw], op0=ALU.mult, op1=ALU.add),
                reads=[pok, xk], writes=[ok])
            S.dma('sp', lambda e, o=o, r0=r0, nr=nr, c0=c0: e.dma_start(out=xdst[r0:r0 + nr, c0:c0 + cw], in_=o[:nr, 0:cw]),
                  reads=[ok], writes=[('dram', 'xdst', r0, c0)])

    def evac(self, i, out_ap, in_ap, reads, writes):
        S = self.S
        if i % 2 == 0:
            S.op('act', lambda e: e.copy(out=out_ap, in_=in_ap), reads=reads, writes=writes)
        else:
            S.op('dve', lambda e: e.tensor_copy(out=out_ap, in_=in_ap), reads=reads, writes=writes)

    def proj_fm_store(self, actT, akey, nk, w, col0, ncols, dst, dcol0, NT, ebufs, ekey, cw=256):
        S = self.S
        ncs = nchunks(NT)
        for c0 in range(0, ncols, cw):
            wv, wkey = self.wload(w[:, col0 + c0:col0 + c0 + cw], nk, cw)
            for m in range(cw // 128):
                for (n0, nn) in ncs:
                    ps, pk = self.psum_next()
                    for k in range(nk):
                        S.op('pe', lambda e, ps=ps, wv=wv, k=k, m=m, n0=n0, nn=nn: e.matmul(
                            ps[:, 0:nn], lhsT=wv[:, k, m * 128:(m + 1) * 128], rhs=actT[:, k, n0:n0 + nn],
                            start=(k == 0), stop=(k == nk - 1)), reads=[wkey, akey], writes=[pk])
                    i = self.ecnt
                    self.ecnt += 1
                    eb = ebufs[i % len(ebufs)]
                    ek = (ekey, i % len(ebufs))
                    self.evac(i, eb[:, 0:nn], ps[:, 0:nn], [pk], [ek])
                    r = c0 + m * 128
                    S.dma('sp', lambda e, eb=eb, r=r, n0=n0, nn=nn: e.dma_start(
                        out=dst[r:r + 128, dcol0 + n0:dcol0 + n0 + nn], in_=eb[:, 0:nn]), reads=[ek],
                        writes=[('dram', id(dst), r, n0)])

    def proj_tm_store(self, actT, akey, nk, w, col0, ncols, dst, blk, ebufs, ekey, cw=512, dcol0=0):
        S = self.S
        t0, t1 = blk
        for c0 in range(0, ncols, cw):
            cww = min(cw, ncols - c0)
            wv, wkey = self.wload(w[:, col0 + c0:col0 + c0 + cww], nk, cww)
            for (r0, nr) in token_tiles(t0, t1):
                rl = r0 - t0
                ps, pk = self.psum_next()
                for k in range(nk):
                    S.op('pe', lambda e, ps=ps, wv=wv, k=k, rl=rl, nr=nr, cww=cww: e.matmul(
                        ps[:nr, 0:cww], lhsT=actT[:, k, rl:rl + nr], rhs=wv[:, k, :],
                        start=(k == 0), stop=(k == nk - 1)), reads=[wkey, akey], writes=[pk])
                i = self.ecnt
                self.ecnt += 1
                eb = ebufs[i % len(ebufs)]
                ek = (ekey, i % len(ebufs))
                self.evac(i, eb[:nr, 0:cww], ps[:nr, 0:cww], [pk], [ek])
                S.dma('sp', lambda e, eb=eb, r0=r0, nr=nr, c0=c0, cww=cww: e.dma_start(
                    out=dst[r0:r0 + nr, dcol0 + c0:dcol0 + c0 + cww], in_=eb[:nr, 0:cww]), reads=[ek],
                    writes=[('dram', id(dst), r0, c0)])

    def inproj(self):
        S = self.S
        I = self.I
        R = self.R
        w_in = I['w_in']
        with ExitStack() as es:
            hTs = [self.sb(es, [128, 16, b1 - b0], BF16, "hT%d" % i) for i, (b0, b1) in enumerate(BLOCKS)]
            self.WB = [self.sb(es, [128, 8192], BF16, "wb") for _ in range(3)]
            self.wb_i = 0
            xt = [self.sb(es, [128, D], F32, "xt") for _ in range(2)]
            xn = self.sb(es, [128, D], BF16, "xn")
            st = self.sb(es, [128, 8], F32, "st")
            gb = self.sb(es, [128, D], F32, "gb")
            fli = self.sb(es, [128, 2], F32, "fli")
            eb16 = [self.sb(es, [128, 512], BF16, "eb16") for _ in range(3)]
            eb32 = [self.sb(es, [128, 512], F32, "eb32") for _ in range(3)]
            self.load_gain(gb, I['g_mix'], 'gb')
            S.dma('sp', lambda e: e.dma_start(out=fli[:], in_=I['flag'][:, :]), writes=['fli'])
            for bi, (t0, t1) in enumerate(BLOCKS):
                for (r0, nr) in token_tiles(t0, t1):
                    self.norm_T(R['X1'][r0:r0 + nr, :], nr, gb, 'gb', hTs[bi], 'hT%d' % bi, r0 - t0, (xt, xn, st))
            for bi, blk in enumerate(BLOCKS):
                t0, t1 = blk
                NT = t1 - t0
                hT = hTs[bi]
                hk = 'hT%d' % bi
                for (col0, dst) in [(0, R['UT']), (2048, R['KT'])]:
                    self.proj_fm_store(hT, hk, 16, w_in, col0, 1024, dst, t0, NT, eb16, 'eb16')
                for (col0, dst) in [(2048, R['KTM']), (3072, R['VTM'])]:
                    self.proj_tm_store(hT, hk, 16, w_in, col0, 1024, dst, blk, eb16, 'eb16')
                self.proj_tm_store(hT, hk, 16, w_in, 5120, 8, R['GTM'], blk, eb32, 'eb32', cw=8)
                wv, wkey = self.wload(w_in[:, 5120:5128], 16, 8)
                for gi, dst in [(0, R['IGR']), (1, R['FGR'])]:
                    for (n0, nn) in nchunks(NT):
                        ps, pk = self.psum_next()
                        for k in range(16):
                            S.op('pe', lambda e, ps=ps, wv=wv, k=k, gi=gi, n0=n0, nn=nn, hT=hT: e.matmul(
                                ps[0:4, 0:nn], lhsT=wv[:, k, gi * 4:gi * 4 + 4], rhs=hT[:, k, n0:n0 + nn],
                                start=(k == 0), stop=(k == 15)), reads=[wkey, hk], writes=[pk])
                        i = self.ecnt
                        self.ecnt += 1
                        eb = eb32[i % 3]
                        ek = ('eb32', i % 3)
                        self.evac(i, eb[0:4, 0:nn], ps[0:4, 0:nn], [pk], [ek])
                        S.dma('sp', lambda e, eb=eb, dst=dst, n0=n0, nn=nn, t0=t0: e.dma_start(
                            out=dst[0:4, t0 + n0:t0 + n0 + nn], in_=eb[0:4, 0:nn]), reads=[ek], writes=[('dram', id(dst), t0, n0)])
            h0, h1 = hTs
            S.op('dve', lambda e: e.tensor_scalar(out=h0[:], in0=h0[:], scalar1=fli[:, 0:1], scalar2=None, op0=ALU.mult), reads=['hT0', 'fli'], writes=['hT0'])
            S.op('dve', lambda e: e.scalar_tensor_tensor(out=h0[:], in0=h1[:, :, 0:HPR], scalar=fli[:, 1:2], in1=h0[:], op0=ALU.mult, op1=ALU.add),
                 reads=['hT0', 'hT1', 'fli'], writes=['hT0'])
            segs = [(h0, 'hT0', n0, nn, n0) for (n0, nn) in nchunks(HPR)] + [(h1, 'hT1', HPR, NSM, HPR)]
            for (col0, dst) in [(1024, R['QTo']), (5128, R['QXTo'])]:
                for c0 in range(0, 1024, 256):
                    wv, wkey = self.wload(w_in[:, col0 + c0:col0 + c0 + 256], 16, 256)
                    for m in range(2):
                        for (act_, akey, a0, nn, dcol) in segs:
                            ps, pk = self.psum_next()
                            for k in range(16):
                                S.op('pe', lambda e, ps=ps, wv=wv, k=k, m=m, act_=act_, a0=a0, nn=nn: e.matmul(
                                    ps[:, 0:nn], lhsT=wv[:, k, m * 128:(m + 1) * 128], rhs=act_[:, k, a0:a0 + nn],
                                    start=(k == 0), stop=(k == 15)), reads=[wkey, akey], writes=[pk])
                            i = self.ecnt
                            self.ecnt += 1
                            eb = eb16[i % 3]
                            ek = ('eb16', i % 3)
                            self.evac(i, eb[:, 0:nn], ps[:, 0:nn], [pk], [ek])
                            r = c0 + m * 128
                            S.dma('sp', lambda e, eb=eb, dst=dst, r=r, dcol=dcol, nn=nn: e.dma_start(
                                out=dst[r:r + 128, dcol:dcol + nn], in_=eb[:, 0:nn]), reads=[ek], writes=[('dram', id(dst), r, dcol)])
            tls = [(h0, 'hT0', i * 128, 128, i * 128) for i in range(HPR // 128)] + [(h1, 'hT1', HPR, NSM, HPR)]
            for c0 in range(0, 1024, 512):
                wv, wkey = self.wload(w_in[:, 4096 + c0:4096 + c0 + 512], 16, 512)
                for (act_, akey, rl, nr, row0) in tls:
                    ps, pk = self.psum_next()
                    for k in range(16):
                        S.op('pe', lambda e, ps=ps, wv=wv, k=k, act_=act_, rl=rl, nr=nr: e.matmul(
                            ps[:nr, 0:512], lhsT=act_[:, k, rl:rl + nr], rhs=wv[:, k, :], start=(k == 0), stop=(k == 15)),
                            reads=[wkey, akey], writes=[pk])
                    i = self.ecnt
                    self.ecnt += 1
                    eb = eb32[i % 3]
                    ek = ('eb32', i % 3)
                    self.evac(i, eb[:nr, 0:512], ps[:nr, 0:512], [pk], [ek])
                    S.dma('sp', lambda e, eb=eb, row0=row0, nr=nr, c0=c0: e.dma_start(
                        out=R['OTMo'][row0:row0 + nr, c0:c0 + 512], in_=eb[:nr, 0:512]), reads=[ek], writes=[('dram', 'otmo', row0, c0)])
            S.barrier()
            S.flush()

    def memkv(self):
        S = self.S
        I = self.I
        R = self.R
        O = self.O
        with ExitStack() as es:
            mT = self.sb(es, [128, 16, 256], BF16, "mT")
            self.WB = [self.sb(es, [128, 8192], BF16, "wb") for _ in range(3)]
            self.wb_i = 0
            xt = self.sb(es, [128, D], F32, "xt")
            xn = self.sb(es, [128, D], BF16, "xn")
            st = self.sb(es, [128, 8], F32, "st")
            gb = self.sb(es, [128, D], F32, "gb")
            eb16 = [self.sb(es, [128, 512], BF16, "eb16") for _ in range(3)]
            eb32 = [self.sb(es, [128, 512], F32, "eb32") for _ in range(3)]
            self.load_gain(gb, I['g_mem'], 'gb')
            for (r0, nr) in token_tiles(0, 256):
                self.norm_T(I['mem'][r0:r0 + nr, :], nr, gb, 'gb', mT, 'mT', r0, (xt, xn, st))
            self.proj_tm_store(mT, 'mT', 16, I['w_mem_k'], 0, 1024, O['mk'], (0, 256), eb32, 'eb32')
            self.proj_tm_store(mT, 'mT', 16, I['w_mem_v'], 0, 1024, O['mv'], (0, 256), eb32, 'eb32')
            self.proj_fm_store(mT, 'mT', 16, I['w_mem_k'], 0, 1024, R['MKT'], 0, 256, eb16, 'eb16')
            S.barrier()
            S.flush()

    def merge(self):
        S = self.S
        I = self.I
        R = self.R
        blk = (0, HT)
        t0, t1 = blk
        NT = t1 - t0
        w_in = I['w_in']
        with ExitStack() as es:
            hT = self.sb(es, [128, 16, NT], BF16, "hT")
            mT = self.sb(es, [128, 16, NT], BF16, "mT")
            brT = [self.sb(es, [128, 8, NT], BF16, "brT") for _ in range(3)]
            self.WB = [self.sb(es, [128, 2048], BF16, "wb") for _ in range(9)]
            self.wb_i = 0
            xt = self.sb(es, [128, D], F32, "xt")
            xn = self.sb(es, [128, D], BF16, "xn")
            st = self.sb(es, [128, 8], F32, "st")
            gb = self.sb(es, [128, D], F32, "gb")
            gs4 = self.sb(es, [128, D], F32, "gs4")
            gs = [gs4[:, i * 512:(i + 1) * 512] for i in range(2)]
            tmp = [gs4[:, (2 + i) * 512:(3 + i) * 512] for i in range(2)]
            acc = [self.sb(es, [128, 512], F32, "acc") for _ in range(2)]
            xres = [self.sb(es, [128, 256], F32, "xres") for _ in range(4)]
            ot = [self.sb(es, [128, 256], F32, "ot") for _ in range(2)]
            self.load_gain(gb, I['g_mix'], 'gb')
            fl = self.sb(es, [128, 2], F32, "fl")
            S.dma('sp', lambda e: e.dma_start(out=fl[:], in_=I['flag'][:, :]), writes=['fl'])
            for (r0, nr) in token_tiles(0, HPR):
                S.dma('sp', lambda e, r0=r0, nr=nr: e.dma_start(out=xt[:nr, :], in_=R['X1'][r0:r0 + nr, :]), writes=['xt'])
                S.dma('sp', lambda e, r0=r0, nr=nr: e.dma_start(out=gs4[:nr, :], in_=R['X1'][HPR + r0:HPR + r0 + nr, :]), writes=['gs4'])
                S.op('dve', lambda e, nr=nr: e.tensor_scalar(out=xt[:nr, :], in0=xt[:nr, :], scalar1=fl[:nr, 0:1], scalar2=None, op0=ALU.mult),
                     reads=['xt', 'fl'], writes=['xt'])
                S.op('dve', lambda e, nr=nr: e.scalar_tensor_tensor(out=xt[:nr, :], in0=gs4[:nr, :], scalar=fl[:nr, 1:2], in1=xt[:nr, :],
                                                                   op0=ALU.mult, op1=ALU.add), reads=['xt', 'gs4', 'fl'], writes=['xt'])
                S.dma('sp', lambda e, r0=r0, nr=nr: e.dma_start(out=R['X1h'][r0:r0 + nr, :], in_=xt[:nr, :]), reads=['xt'],
                      writes=[('dram', 'x1h', r0)])
            S.dma('sp', lambda e: e.dma_start(out=xt[:NSM, :], in_=R['X1'][NPR:NTOK, :]), writes=['xt'])
            S.dma('sp', lambda e: e.dma_start(out=R['X1h'][HPR:HT, :], in_=xt[:NSM, :]), reads=['xt'], writes=[('dram', 'x1h', HPR)])
            S.barrier()
            for b, nm in enumerate(['BR0', 'BR1', 'BR2']):
                S.dma('sp', lambda e, b=b, nm=nm: e.dma_start(out=brT[b][:], in_=R[nm].rearrange("(k p) t -> p k t", p=128)), writes=[('brT', b)])
            for (r0, nr) in token_tiles(t0, t1):
                self.norm_T(R['X1h'][r0:r0 + nr, :], nr, gb, 'gb', hT, 'hT', r0 - t0, (xt, xn, st))
            wbr = [I['w_br_s5'], I['w_br_ml'], I['w_br_xa']]
            ncs = nchunks(NT)
            cnt = 0
            for j in range(16):
                wg = []
                for b in range(3):
                    gv, gk = self.wload(w_in[:, 6152 + b * 2048 + j * 128:6152 + b * 2048 + (j + 1) * 128], 16, 128)
                    wg.append((gv, gk))
                wb_ = []
                for b in range(3):
                    bv, bk = self.wload(wbr[b][:, j * 128:(j + 1) * 128], 8, 128)
                    wb_.append((bv, bk))
                for (n0, nn) in ncs:
                    a = acc[cnt % 2]
                    ak = ('acc', cnt % 2)
                    cnt += 1
                    for b in range(3):
                        gv, gk = wg[b]
                        bv, bk = wb_[b]
                        pg, pgk = self.psum_next()
                        pb, pbk = self.psum_next()
                        for k in range(16):
                            S.op('pe', lambda e, pg=pg, gv=gv, k=k, n0=n0, nn=nn: e.matmul(
                                pg[:, 0:nn], lhsT=gv[:, k, :], rhs=hT[:, k, n0:n0 + nn], start=(k == 0), stop=(k == 15)),
                                reads=[gk, 'hT'], writes=[pgk])
                        for k in range(8):
                            S.op('pe', lambda e, pb=pb, bv=bv, k=k, b=b, n0=n0, nn=nn: e.matmul(
                                pb[:, 0:nn], lhsT=bv[:, k, :], rhs=brT[b][:, k, n0:n0 + nn], start=(k == 0), stop=(k == 7)),
                                reads=[bk, ('brT', b)], writes=[pbk])
                        g = gs[b % 2]
                        gsk = ('gs', b % 2)
                        S.op('act', lambda e, g=g, pg=pg, nn=nn: e.activation(out=g[:, 0:nn], in_=pg[:, 0:nn], func=AF.Sigmoid),
                             reads=[pgk], writes=[gsk])
                        if b == 0:
                            S.op('dve', lambda e, a=a, g=g, pb=pb, nn=nn: e.tensor_tensor(out=a[:, 0:nn], in0=g[:, 0:nn], in1=pb[:, 0:nn],
                                                                                         op=ALU.mult), reads=[gsk, pbk], writes=[ak])
                        else:
                            t = tmp[b % 2]
                            tk = ('tmp', b % 2)
                            S.op('dve', lambda e, t=t, g=g, pb=pb, nn=nn: e.tensor_tensor(out=t[:, 0:nn], in0=g[:, 0:nn], in1=pb[:, 0:nn],
                                                                                         op=ALU.mult), reads=[gsk, pbk], writes=[tk])
                            if b == 1:
                                S.op('dve', lambda e, a=a, t=t, nn=nn: e.tensor_tensor(out=a[:, 0:nn], in0=a[:, 0:nn], in1=t[:, 0:nn],
                                                                                       op=ALU.add), reads=[ak, tk], writes=[ak])
                            else:
                                S.op('dve', lambda e, a=a, t=t, j=j, n0=n0, nn=nn: e.tensor_tensor(
                                    out=mT[:, j, n0:n0 + nn], in0=a[:, 0:nn], in1=t[:, 0:nn], op=ALU.add),
                                    reads=[ak, tk], writes=['mT'])
            self.proj_tm_res(mT, 'mT', 16, I['w_out'], R['X1h'], R['X2'], 1.0, blk, xres, ot, cw=128)
            S.barrier()
            S.flush()

    def mixers(self):
        which = self.which
        if 'xp' in which:
            self.xatt_prompt()
        if 'xs' in which:
            self.xatt_sample()
        if 's5' in which:
            self.s5()
        if 'mp' in which:
            self.mlstm_prompt()
        if 'ms' in which:
            self.mlstm_sample()

    def xatt_prompt(self):
        S = self.S
        R = self.R
        O = self.O
        with ExitStack() as es:
            qxT = self.sb(es, [128, 8, HPR], BF16, "qxT")
            mkT = self.sb(es, [128, 8, 256], BF16, "mkT")
            Vb = self.sb(es, [128, 2, 1024], BF16, "Vb")
            pT = [self.sb(es, [128, 2, HPR], BF16, "pT") for _ in range(2)]
            pe_ = [self.sb(es, [128, 256], F32, "pe") for _ in range(2)]
            pn = [self.sb(es, [128, 256], BF16, "pn") for _ in range(2)]
            sts = [self.sb(es, [128, 8], F32, "sts") for _ in range(2)]
            ob = [self.sb(es, [128, 512], BF16, "ob") for _ in range(2)]
            S.dma('sp', lambda e: e.dma_start(out=qxT[:], in_=R['QXTo'][:, 0:HPR].rearrange("(k p) t -> p k t", p=128)), writes=['qxT'])
            S.dma('sp', lambda e: e.dma_start(out=mkT[:], in_=R['MKT'].rearrange("(k p) t -> p k t", p=128)), writes=['mkT'])
            S.dma('pool', lambda e: e.dma_start(out=Vb[:], in_=R['MVB'].rearrange("(mt p) c -> p mt c", p=128)), writes=['Vb'], ndesc=16)
            oc = 0
            NTT = HPR // 128
            items = [(h, tt) for h in range(4) for tt in range(NTT)]
            pss = {}

            def xp_scores(n_):
                h, tt = items[n_]
                ps, pk = self.psum_next()
                for dh in range(2):
                    S.op('pe', lambda e, dh=dh: e.matmul(
                        ps[:, 0:256], lhsT=qxT[:, h * 2 + dh, tt * 128:(tt + 1) * 128], rhs=mkT[:, h * 2 + dh, :],
                        start=(dh == 0), stop=(dh == 1)), reads=['qxT', 'mkT'], writes=[pk])
                pss[n_] = (ps, pk)

            def xp_softmax(n_):
                h, tt = items[n_]
                ps, pk = pss.pop(n_)
                pTh = pT[h % 2]
                pTk = ('pT', h % 2)
                st = sts[n_ % 2]
                sk = ('sts', n_ % 2)
                pe = pe_[n_ % 2]
                pek = ('pe', n_ % 2)
                pnn = pn[n_ % 2]
                pnk = ('pn', n_ % 2)
                S.op('dve', lambda e: e.reduce_max(out=st[:, 0:1], in_=ps[:, 0:256], axis=AX.X), reads=[pk], writes=[sk])
                S.op('dve', lambda e: e.tensor_scalar(out=st[:, 1:2], in0=st[:, 0:1], scalar1=-1.0 / 16, scalar2=None, op0=ALU.mult),
                     reads=[sk], writes=[sk])
                S.op('act', lambda e: e.activation(out=pe[:], in_=ps[:, 0:256], func=AF.Exp, bias=st[:, 1:2],
                                                   scale=1.0 / 16, accum_out=st[:, 2:3]), reads=[pk, sk], writes=[pek, sk])
                S.op('dve', lambda e: e.reciprocal(out=st[:, 3:4], in_=st[:, 2:3]), reads=[sk], writes=[sk])
                S.op('dve', lambda e: e.tensor_scalar(out=pnn[:], in0=pe[:], scalar1=st[:, 3:4], scalar2=None, op0=ALU.mult),
                     reads=[sk, pek], writes=[pnk])
                pb, pbk = self.psumb_next()
                for mt in range(2):
                    S.op('pe', lambda e, mt=mt: e.transpose(out=pb[:, mt * 128:(mt + 1) * 128], in_=pnn[:, mt * 128:(mt + 1) * 128],
                                                            identity=self.identb[:]), reads=[pnk, 'identb'], writes=[pbk])
                S.op('act', lambda e: e.copy(out=pTh[:, :, tt * 128:(tt + 1) * 128], in_=pb[:, 0:256].rearrange("p (m t) -> p m t", m=2)),
                     reads=[pbk], writes=[pTk])

            def xp_pv(h):
                nonlocal oc
                pTh = pT[h % 2]
                pTk = ('pT', h % 2)
                for dh in range(2):
                    for (n0, nn) in nchunks(HPR):
                        ps, pk = self.psum_next()
                        for mt in range(2):
                            S.op('pe', lambda e, ps=ps, dh=dh, mt=mt, n0=n0, nn=nn: e.matmul(
                                ps[:, 0:nn], lhsT=Vb[:, mt, h * 256 + dh * 128:h * 256 + (dh + 1) * 128], rhs=pTh[:, mt, n0:n0 + nn],
                                start=(mt == 0), stop=(mt == 1)), reads=['Vb', pTk], writes=[pk])
                        o = ob[oc % 2]
                        ok = ('ob', oc % 2)
                        oc += 1
                        self.evac(oc, o[:, 0:nn], ps[:, 0:nn], [pk], [ok])
                        r = h * 256 + dh * 128
                        S.dma('sp', lambda e, o=o, r=r, n0=n0, nn=nn: e.dma_start(out=R['BR2'][r:r + 128, n0:n0 + nn], in_=o[:, 0:nn]),
                              reads=[ok], writes=[('dram', 'brt2', r, n0)])

            xp_scores(0)
            for n_ in range(len(items)):
                if n_ + 1 < len(items):
                    xp_scores(n_ + 1)
                xp_softmax(n_)
                if items[n_][1] == NTT - 1:
                    xp_pv(items[n_][0])
            S.barrier()
            S.flush()

    def xatt_sample(self):
        S = self.S
        R = self.R
        I = self.I
        with ExitStack() as es:
            qS = self.sb(es, [128, 8, NSM], BF16, "qS")
            qtm = self.sb(es, [NSM, 1024], BF16, "qtm")
            OH = self.sb(es, [NSM, NSM, 128], BF16, "OH")
            Kt = [self.sb(es, [128, 2, 1024], F32, "Kt") for _ in range(2)]
            Vt = [self.sb(es, [128, 2, 1024], BF16, "Vt") for _ in range(2)]
            prod = [self.sb(es, [128, 1024], F32, "prod") for _ in range(2)]
            SC = self.sb(es, [128, 2, 128], F32, "SC")
            pe = self.sb(es, [128, 256], F32, "pes")
            pn = self.sb(es, [128, 256], F32, "pns")
            st = self.sb(es, [128, 8], F32, "stx")
            P = self.sb(es, [128, 2, 128], BF16, "Pm")
            xrow = [self.sb(es, [1, 1024], F32, "xrow") for _ in range(2)]
            xas = self.sb(es, [NSM, 1024], F32, "xas")
            xasb = self.sb(es, [NSM, 1024], BF16, "xasb")
            xaT = self.sb(es, [128, 8, NSM], BF16, "xaT")
            S.dma('sp', lambda e: e.dma_start(out=qS[:], in_=R['QXTo'][:, HPR:HT].rearrange("(k p) t -> p k t", p=128)), writes=['qS'])
            for kq in range(0, 8, 4):
                pb, pbk = self.psumb_next()
                for j in range(4):
                    S.op('pe', lambda e, pb=pb, j=j, kq=kq: e.transpose(out=pb[:NSM, j * 128:(j + 1) * 128], in_=qS[:, kq + j, :],
                                                                       identity=self.identb[:]), reads=['qS', 'identb'], writes=[pbk])
                S.op('act', lambda e, pb=pb, kq=kq: e.copy(out=qtm[:, kq * 128:(kq + 4) * 128], in_=pb[:NSM, 0:512]), reads=[pbk], writes=['qtm'])
            S.op('dve', lambda e: e.tensor_copy(out=OH[:], in_=self.identb[:NSM, :NSM].unsqueeze(2).to_broadcast([NSM, NSM, 128])),
                 reads=['identb'], writes=['OH'])
            S.op('pool', lambda e: e.memset(SC[:], 0.0), writes=['SC'])
            for b in range(NSM):
                kt = Kt[b % 2]
                kk = ('Kt', b % 2)
                S.dma('sp', lambda e, kt=kt, b=b: e.dma_start(out=kt[:], in_=I['cache_k'][b].rearrange("(mt p) c -> p mt c", p=128)), writes=[kk])
                qb = []
                for hf in range(2):
                    ps, pk = self.psum_next()
                    S.op('pe', lambda e, ps=ps, b=b, hf=hf: e.matmul(ps[:, 0:512], lhsT=OH[:, b, :], rhs=qtm[:, hf * 512:(hf + 1) * 512],
                                                                    start=True, stop=True), reads=['OH', 'qtm'], writes=[pk])
                    qb.append((ps, pk))
                for mt in range(2):
                    pr = prod[mt]
                    prk = ('prod', mt)
                    for hf in range(2):
                        ps, pk = qb[hf]
                        S.op('dve', lambda e, pr=pr, kt=kt, ps=ps, mt=mt, hf=hf: e.tensor_tensor(
                            out=pr[:, hf * 512:(hf + 1) * 512], in0=kt[:, mt, hf * 512:(hf + 1) * 512], in1=ps[:, 0:512], op=ALU.mult),
                            reads=[kk, pk], writes=[prk])
                    S.op('dve', lambda e, pr=pr, mt=mt, b=b: e.tensor_reduce(out=SC[:, mt, b * 4:(b + 1) * 4],
                                                                            in_=pr[:].rearrange("p (h d) -> p h d", h=4), axis=AX.X, op=ALU.add),
                         reads=[prk], writes=['SC'])
            BH = 128
            if self.dbg:
                dSC = self.dram_out('dbg_SC', [128, 256])
                dq = self.dram_out('dbg_qtm', [NSM, 1024], BF16)
                S.dma('sp', lambda e: e.dma_start(out=dSC[:, :], in_=SC[:].rearrange("p a b -> p (a b)")), reads=['SC'], writes=[('dram', 'dsc')])
                S.dma('sp', lambda e: e.dma_start(out=dq[:, :], in_=qtm[:]), reads=['qtm'], writes=[('dram', 'dq')])
            ps, pk = self.psum_next()
            for mt in range(2):
                S.op('pe', lambda e, ps=ps, mt=mt: e.transpose(out=ps[:BH, mt * 128:(mt + 1) * 128], in_=SC[:, mt, :], identity=self.identf[:]),
                     reads=['SC', 'identf'], writes=[pk])
            S.op('dve', lambda e, ps=ps: e.reduce_max(out=st[:BH, 0:1], in_=ps[:BH, 0:256], axis=AX.X), reads=[pk], writes=['stx'])
            S.op('dve', lambda e: e.tensor_scalar(out=st[:BH, 1:2], in0=st[:BH, 0:1], scalar1=-1.0 / 16, scalar2=None, op0=ALU.mult),
                 reads=['stx'], writes=['stx'])
            S.op('act', lambda e, ps=ps: e.activation(out=pe[:BH, :], in_=ps[:BH, 0:256], func=AF.Exp, bias=st[:BH, 1:2], scale=1.0 / 16,
                                               accum_out=st[:BH, 2:3]), reads=[pk, 'stx'], writes=['pes', 'stx'])
            S.op('dve', lambda e: e.reciprocal(out=st[:BH, 3:4], in_=st[:BH, 2:3]), reads=['stx'], writes=['stx'])
            S.op('dve', lambda e: e.tensor_scalar(out=pn[:BH, :], in0=pe[:BH, :], scalar1=st[:BH, 3:4], scalar2=None, op0=ALU.mult),
                 reads=['stx', 'pes'], writes=['pns'])
            if self.dbg:
                dpn = self.dram_out('dbg_pn', [128, 256])
                dst_ = self.dram_out('dbg_st', [128, 8])
                S.dma('sp', lambda e: e.dma_start(out=dpn[:, :], in_=pn[:]), reads=['pns'], writes=[('dram', 'dpn')])
                S.dma('sp', lambda e: e.dma_start(out=dst_[:, :], in_=st[:]), reads=['stx'], writes=[('dram', 'dst')])
            for mt in range(2):
                ps2, pk2 = self.psum_next()
                S.op('pe', lambda e, ps2=ps2, mt=mt: e.transpose(out=ps2[:, 0:BH], in_=pn[:BH, mt * 128:(mt + 1) * 128], identity=self.identf[:BH, :BH]),
                     reads=['pns', 'identf'], writes=[pk2])
                S.op('act', lambda e, ps2=ps2, mt=mt: e.copy(out=P[:, mt, 0:BH], in_=ps2[:, 0:BH]), reads=[pk2], writes=['Pm'])
            for b in range(NSM):
                vt = Vt[b % 2]
                vk = ('Vt', b % 2)
                S.dma('pool', lambda e, vt=vt, b=b: e.dma_start(out=vt[:], in_=I['cache_v'][b].rearrange("(mt p) c -> p mt c", p=128)), writes=[vk], ndesc=16)
                xr = xrow[b % 2]
                xk = ('xrow', b % 2)
                for hp in range(2):
                    ps, pk = self.psum_next()
                    for hh in range(2):
                        h = hp * 2 + hh
                        for mt in range(2):
                            S.op('pe', lambda e, ps=ps, vt=vt, b=b, h=h, hh=hh, mt=mt: e.matmul(
                                ps[0:1, hh * 256:(hh + 1) * 256], lhsT=P[:, mt, b * 4 + h:b * 4 + h + 1], rhs=vt[:, mt, h * 256:(h + 1) * 256],
                                start=(mt == 0), stop=(mt == 1)), reads=['Pm', vk], writes=[pk])
                    S.op('act', lambda e, ps=ps, xr=xr, hp=hp: e.copy(out=xr[0:1, hp * 512:(hp + 1) * 512], in_=ps[0:1, 0:512]),
                         reads=[pk], writes=[xk])
                S.dma('sp', lambda e, xr=xr, b=b: e.dma_start(out=R['XAS'][b:b + 1, :], in_=xr[0:1, :]), reads=[xk], writes=[('dram', 'xas', b)])
            S.barrier()
            S.dma('sp', lambda e: e.dma_start(out=xas[:], in_=R['XAS'][:, :]), writes=['xas'])
            S.op('dve', lambda e: e.tensor_copy(out=xasb[:], in_=xas[:]), reads=['xas'], writes=['xasb'])
            self.transpose_into(xasb, 'xasb', NSM, 8, xaT, 'xaT', 0)
            S.dma('sp', lambda e: e.dma_start(out=R['BR2'][:, HPR:HT].rearrange("(k p) t -> p k t", p=128), in_=xaT[:]),
                  reads=['xaT'], writes=[('dram', 'brt2s')])
            S.barrier()
            S.flush()

    def s5(self):
        S = self.S
        I = self.I
        R = self.R
        O = self.O
        TWO_PI = 6.283185307179586
        dve = lambda fn, reads, writes: S.op('dve', fn, reads=reads, writes=writes)
        with ExitStack() as es:
            T = {}

            def t32(name):
                T[name] = self.sb(es, [128, 32], F32, name)
                return T[name]
            for nm in ['LR', 'LI', 'LS', 'DT', 'X', 'P', 'EM1', 'MAG', 'PHI', 'U', 'NF', 'RR', 'MSK', 'SIN', 'COS', 'AR', 'AI',
                       'CM1', 'NR', 'DEN', 'ZR', 'ZI', 'TA', 'TB', 'SPr', 'SPi', 'TC', 'TD']:
                t32(nm)
            NI = self.sb(es, [128, 32], I32, "NI")
            nat = [self.sb(es, [32, 128], F32, "nat") for _ in range(3)]
            LS2 = self.sb(es, [32, 2], F32, "LS2")
            LC = 64
            WvT = [self.sb(es, [32, 32, 128], BF16, "WvT%d" % i) for i in range(2)]
            CT = [self.sb(es, [128, 32, 32], BF16, "CT%d" % i) for i in range(2)]
            Dd = self.sb(es, [32, 32, 32], BF16, "Dd")
            Dcol = self.sb(es, [32, 32], F32, "Dcol")
            Gr = self.sb(es, [128, 32, LC], F32, "Gr")
            Gi = self.sb(es, [128, 32, LC], F32, "Gi")
            RHO = self.sb(es, [128, 32, LC], F32, "RHO")
            gx = [self.sb(es, [32, 512], F32, "gx") for _ in range(2)]
            g2t = [self.sb(es, [32, 512], F32, "g2t") for _ in range(2)]
            S.dma('sp', lambda e: e.dma_start(out=nat[0][:], in_=I['s5_lambda_re'].rearrange("(gp g2) p -> gp (g2 p)", g2=2)), writes=['nat0'])
            S.dma('sp', lambda e: e.dma_start(out=nat[1][:], in_=I['s5_lambda_im'].rearrange("(gp g2) p -> gp (g2 p)", g2=2)), writes=['nat1'])
            S.dma('sp', lambda e: e.dma_start(out=LS2[:], in_=I['s5_log_step'].rearrange("(gp g2) -> gp g2", g2=2)), writes=['LS2'])
            dve(lambda e: e.tensor_copy(out=nat[2][:].rearrange("g (a p) -> g a p", a=2), in_=LS2[:].unsqueeze(2).to_broadcast([32, 2, 64])),
                ['LS2'], ['nat2'])
            ps, pk = self.psum_next()
            for j in range(3):
                S.op('pe', lambda e, j=j, ps=ps: e.transpose(out=ps[:, j * 32:(j + 1) * 32], in_=nat[j][:, :], identity=self.identf[:32, :32]),
                     reads=['nat%d' % j, 'identf'], writes=[pk])
            for j, nm in enumerate(['LR', 'LI', 'LS']):
                dve(lambda e, j=j, nm=nm, ps=ps: e.tensor_copy(out=T[nm][:], in_=ps[:, j * 32:(j + 1) * 32]), [pk], [nm])
            S.op('act', lambda e: e.activation(out=T['DT'][:], in_=T['LS'][:], func=AF.Exp), reads=['LS'], writes=['DT'])
            tt = lambda o, a, b, op: dve(lambda e: e.tensor_tensor(out=T[o][:], in0=T[a][:], in1=T[b][:], op=op), [a, b], [o])
            ts = lambda o, a, s1, s2, op0, op1: dve(lambda e: e.tensor_scalar(out=T[o][:], in0=T[a][:], scalar1=s1, scalar2=s2, op0=op0, op1=op1), [a], [o])
            tt('X', 'LR', 'DT', ALU.mult)
            ts('P', 'X', 1.0 / 6, 1.0, ALU.mult, ALU.add)
            for dv in [5.0, 4.0, 3.0, 2.0]:
                tt('P', 'P', 'X', ALU.mult)
                ts('P', 'P', 1.0 / dv, 1.0, ALU.mult, ALU.add)
            tt('EM1', 'P', 'X', ALU.mult)
            ts('MAG', 'EM1', 1.0, None, ALU.add, ALU.bypass)
            tt('PHI', 'LI', 'DT', ALU.mult)

            def sin_of(dst, src, shift):
                ts('U', src, shift, 1.0 / TWO_PI, ALU.add, ALU.mult)
                dve(lambda e: e.tensor_copy(out=NI[:], in_=T['U'][:]), ['U'], ['NI'])
                dve(lambda e: e.tensor_copy(out=T['NF'][:], in_=NI[:]), ['NI'], ['NF'])
                ts('TA', src, shift, None, ALU.add, ALU.bypass)
                dve(lambda e: e.scalar_tensor_tensor(out=T['RR'][:], in0=T['NF'][:], scalar=-TWO_PI, in1=T['TA'][:], op0=ALU.mult, op1=ALU.add),
                    ['NF', 'TA'], ['RR'])
                ts('MSK', 'RR', math.pi, -TWO_PI, ALU.is_gt, ALU.mult)
                tt('RR', 'RR', 'MSK', ALU.add)
                ts('MSK', 'RR', -math.pi, TWO_PI, ALU.is_lt, ALU.mult)
                tt('RR', 'RR', 'MSK', ALU.add)
                ts('RR', 'RR', -3.1415925, 3.1415925, ALU.max, ALU.min)
                S.op('act', lambda e: e.activation(out=T[dst][:], in_=T['RR'][:], func=AF.Sin), reads=['RR'], writes=[dst])
            sin_of('SIN', 'PHI', 0.0)
            sin_of('COS', 'PHI', math.pi / 2)
            tt('AR', 'MAG', 'COS', ALU.mult)
            tt('AI', 'MAG', 'SIN', ALU.mult)
            ts('CM1', 'COS', -1.0, None, ALU.add, ALU.bypass)
            tt('NR', 'EM1', 'COS', ALU.mult)
            tt('NR', 'NR', 'CM1', ALU.add)
            tt('TA', 'LR', 'LR', ALU.mult)
            tt('TB', 'LI', 'LI', ALU.mult)
            tt('DEN', 'TA', 'TB', ALU.add)
            dve(lambda e: e.reciprocal(out=T['DEN'][:], in_=T['DEN'][:]), ['DEN'], ['DEN'])
            tt('TA', 'NR', 'LR', ALU.mult)
            tt('TB', 'AI', 'LI', ALU.mult)
            tt('TA', 'TA', 'TB', ALU.add)
            tt('ZR', 'TA', 'DEN', ALU.mult)
            tt('TA', 'AI', 'LR', ALU.mult)
            tt('TB', 'NR', 'LI', ALU.mult)
            tt('TA', 'TA', 'TB', ALU.subtract)
            tt('ZI', 'TA', 'DEN', ALU.mult)
            es1 = es.enter_context(ExitStack())
            BDr = self.sb(es1, [128, 32, 32], F32, "BDr")
            BDi = self.sb(es1, [128, 32, 32], F32, "BDi")
            BBr = self.sb(es1, [128, 32, 32], F32, "BBr")
            BBi = self.sb(es1, [128, 32, 32], F32, "BBi")
            BT1 = self.sb(es1, [128, 32, 32], F32, "BT1")
            BT2 = self.sb(es1, [128, 32, 32], F32, "BT2")
            CBD = [self.sb(es1, [32, 32, 128], F32, "CBD%d" % i) for i in range(2)]
            GT = [self.sb(es1, [128, 32, 32], F32, "GT%d" % i) for i in range(2)]
            S.op('pool', lambda e: e.memset(BDr[:], 0.0), writes=['BDr'])
            S.op('pool', lambda e: e.memset(BDi[:], 0.0), writes=['BDi'])
            for g2 in range(2):
                for (bd, bk, src) in [(BDr, 'BDr', I['s5_b_re']), (BDi, 'BDi', I['s5_b_im'])]:
                    S.dma('sp', lambda e, bd=bd, src=src, g2=g2: e.dma_start(
                        out=bd[g2 * 64:(g2 + 1) * 64, :, g2 * 16:(g2 + 1) * 16],
                        in_=src.rearrange("(gp g2) p h -> g2 p gp h", g2=2)[g2]), reads=[bk], writes=[bk])
            zb = lambda nm: T[nm][:].unsqueeze(2).to_broadcast([128, 32, 32])
            dve(lambda e: e.tensor_tensor(out=BT1[:], in0=BDr[:], in1=zb('ZR'), op=ALU.mult), ['BDr', 'ZR'], ['BT1'])
            dve(lambda e: e.tensor_tensor(out=BT2[:], in0=BDi[:], in1=zb('ZI'), op=ALU.mult), ['BDi', 'ZI'], ['BT2'])
            dve(lambda e: e.tensor_tensor(out=BBr[:], in0=BT1[:], in1=BT2[:], op=ALU.subtract), ['BT1', 'BT2'], ['BBr'])
            dve(lambda e: e.tensor_tensor(out=BT1[:], in0=BDi[:], in1=zb('ZR'), op=ALU.mult), ['BDi', 'ZR'], ['BT1'])
            dve(lambda e: e.tensor_tensor(out=BT2[:], in0=BDr[:], in1=zb('ZI'), op=ALU.mult), ['BDr', 'ZI'], ['BT2'])
            dve(lambda e: e.tensor_tensor(out=BBi[:], in0=BT1[:], in1=BT2[:], op=ALU.add), ['BT1', 'BT2'], ['BBi'])
            ei = 0
            for ri, (bb, bbk) in enumerate([(BBr, 'BBr'), (BBi, 'BBi')]):
                for g0 in range(0, 32, 4):
                    ps, pk = self.psum_next()
                    for j in range(4):
                        S.op('pe', lambda e, ps=ps, bb=bb, g0=g0, j=j: e.transpose(out=ps[:32, j * 128:(j + 1) * 128], in_=bb[:, g0 + j, :],
                                                                                  identity=self.identf[:]), reads=[bbk, 'identf'], writes=[pk])
                    ei += 1
                    self.evac(ei, WvT[ri][:, g0:g0 + 4, :], ps[:32, 0:512].rearrange("p (j q) -> p j q", j=4), [pk], ['WvT%d' % ri])
            for ri, src in enumerate([I['s5_c_re'], I['s5_c_im']]):
                S.op('pool', lambda e, ri=ri: e.memset(CBD[ri][:], 0.0), writes=['CBD%d' % ri])
                for g2 in range(2):
                    S.dma('sp', lambda e, ri=ri, src=src, g2=g2: e.dma_start(
                        out=CBD[ri][g2 * 16:(g2 + 1) * 16, :, g2 * 64:(g2 + 1) * 64],
                        in_=src.rearrange("(gp g2) h p -> g2 h gp p", g2=2)[g2]), reads=['CBD%d' % ri], writes=['CBD%d' % ri])
                for g0 in range(0, 32, 16):
                    ps, pk = self.psum_next()
                    for j in range(16):
                        S.op('pe', lambda e, ps=ps, ri=ri, g0=g0, j=j: e.transpose(out=ps[:, j * 32:(j + 1) * 32], in_=CBD[ri][:, g0 + j, :],
                                                                                  identity=self.identf[:32, :32]),
                             reads=['CBD%d' % ri, 'identf'], writes=[pk])
                    sc = 1.0 if ri == 0 else -1.0
                    dve(lambda e, ps=ps, ri=ri, g0=g0, sc=sc: e.tensor_scalar(out=CT[ri][:, g0:g0 + 16, :],
                                                                             in0=ps[:, 0:512].rearrange("p (j q) -> p j q", j=16),
                                                                             scalar1=sc, scalar2=None, op0=ALU.mult), [pk], ['CT%d' % ri])
            S.dma('sp', lambda e: e.dma_start(out=nat[0][:, 0:32], in_=I['s5_d'].rearrange("(gp g2) h -> gp (g2 h)", g2=2)),
                  reads=['nat0'], writes=['nat0'])
            ps, pk = self.psum_next()
            S.op('pe', lambda e, ps=ps: e.transpose(out=ps[:32, 0:32], in_=nat[0][:, 0:32], identity=self.identf[:32, :32]),
                 reads=['nat0', 'identf'], writes=[pk])
            dve(lambda e, ps=ps: e.tensor_copy(out=Dcol[:], in_=ps[:32, 0:32]), [pk], ['Dcol'])
            dve(lambda e: e.tensor_tensor(out=Dd[:], in0=self.identf[:32, :32].unsqueeze(1).to_broadcast([32, 32, 32]),
                                          in1=Dcol[:].unsqueeze(2).to_broadcast([32, 32, 32]), op=ALU.mult), ['Dcol', 'identf'], ['Dd'])
            LC = 64
            dve(lambda e: e.tensor_copy(out=Gr[:, :, 0:1], in_=T['COS'][:].unsqueeze(2)), ['COS'], ['G'])
            dve(lambda e: e.tensor_copy(out=Gi[:, :, 0:1], in_=T['SIN'][:].unsqueeze(2)), ['SIN'], ['G'])
            n = 1
            while n < LC:
                gr_b = Gr[:, :, n - 1:n].to_broadcast([128, 32, n])
                gi_b = Gi[:, :, n - 1:n].to_broadcast([128, 32, n])
                a0 = GT[0][:, :, 0:n]
                a1 = GT[1][:, :, 0:n]
                dve(lambda e, n=n, gr_b=gr_b, a0=a0: e.tensor_tensor(out=a0, in0=Gr[:, :, 0:n], in1=gr_b, op=ALU.mult), ['G'], ['GT0'])
                dve(lambda e, n=n, gi_b=gi_b, a1=a1: e.tensor_tensor(out=a1, in0=Gi[:, :, 0:n], in1=gi_b, op=ALU.mult), ['G'], ['GT1'])
                dve(lambda e, n=n, a0=a0, a1=a1: e.tensor_tensor(out=Gr[:, :, n:2 * n], in0=a0, in1=a1, op=ALU.subtract), ['GT0', 'GT1', 'G'], ['G'])
                dve(lambda e, n=n, gi_b=gi_b, a0=a0: e.tensor_tensor(out=a0, in0=Gr[:, :, 0:n], in1=gi_b, op=ALU.mult), ['G'], ['GT0'])
                dve(lambda e, n=n, gr_b=gr_b, a1=a1: e.tensor_tensor(out=a1, in0=Gi[:, :, 0:n], in1=gr_b, op=ALU.mult), ['G'], ['GT1'])
                dve(lambda e, n=n, a0=a0, a1=a1: e.tensor_tensor(out=Gi[:, :, n:2 * n], in0=a0, in1=a1, op=ALU.add), ['GT0', 'GT1', 'G'], ['G'])
                n *= 2
            dve(lambda e: e.tensor_copy(out=RHO[:], in_=T['MAG'][:].unsqueeze(2).to_broadcast([128, 32, LC])), ['MAG'], ['RHO'])
            S.op('pool', lambda e: e.memset(RHO[:, :, 0:1], 0.0), reads=['RHO'], writes=['RHO'])

            gcnt = [0]

            def gelu_to(ps, pk, ncol, dst, dkey):
                i = gcnt[0] % 2
                gcnt[0] += 1
                x = gx[i]
                y = g2t[i]
                xk = ('gx', i)
                yk = ('g2t', i)
                S.op('act', lambda e: e.copy(out=x[:, 0:ncol], in_=ps[:32, 0:ncol]), reads=[pk], writes=[xk])
                S.op('act', lambda e: e.activation(out=y[:, 0:ncol], in_=ps[:32, 0:ncol], func=AF.Square, scale=math.sqrt(0.044715)),
                     reads=[pk], writes=[yk])
                dve(lambda e: e.scalar_tensor_tensor(out=y[:, 0:ncol], in0=y[:, 0:ncol], scalar=1.0, in1=x[:, 0:ncol], op0=ALU.add, op1=ALU.mult),
                    [xk, yk], [yk])
                S.op('act', lambda e: e.activation(out=y[:, 0:ncol], in_=y[:, 0:ncol], func=AF.Sigmoid, scale=2.0 * math.sqrt(2.0 / math.pi)),
                     reads=[yk], writes=[yk])
                dve(lambda e: e.tensor_tensor(out=dst, in0=x[:, 0:ncol].rearrange("p (j q) -> p j q", j=dst.shape[1]),
                                              in1=y[:, 0:ncol].rearrange("p (j q) -> p j q", j=dst.shape[1]), op=ALU.mult), [xk, yk], [dkey])

            S.barrier()
            S.flush()
            es1.close()
            es2 = es.enter_context(ExitStack())
            UB = 128
            CPB = UB // LC
            uS = [self.sb(es2, [32, 32, UB], BF16, "uS") for _ in range(3)]
            ubs = [self.sb(es2, [32, 32, UB], BF16, "ub") for _ in range(2)]
            fl5 = self.sb(es2, [128, 2], F32, "fl5")
            S.dma('sp', lambda e: e.dma_start(out=fl5[:], in_=I['flag'][:, :]), writes=['fl5'])
            yS = [self.sb(es2, [32, 32, UB], BF16, "yS") for _ in range(2)]
            RI = [self.sb(es2, [128, 32, LC], F32, "RI%d" % i) for i in range(2)]
            RRt = [self.sb(es2, [128, 32, LC], F32, "RR%d" % i) for i in range(2)]
            SB = [self.sb(es2, [128, 32, LC], BF16, "SB%d" % i) for i in range(2)]
            W1 = [self.sb(es2, [128, 512], F32, "W1_%d" % i) for i in range(8)]
            V1 = [self.sb(es2, [128, 32, LC], F32, "V1_%d" % i) for i in range(2)]
            nchunk = NPR // LC
            pl = lambda fn, reads, writes: S.op('pool', fn, reads=reads, writes=writes)
            wcnt = [0]

            NPRE = nchunk // 2

            def cinfo(c):
                blk = c // CPB
                return blk, (c % CPB) * LC, uS[blk % 3], ('uS', blk % 3), yS[blk % 2], ('yS', blk % 2)

            def load_u(c):
                blk, tb, us, uk, ys, yk = cinfo(c)
                if c < NPRE:
                    S.dma('sp', lambda e: e.dma_start(
                        out=us[:], in_=R['UT'][:, blk * UB:(blk + 1) * UB].rearrange("(gp r) t -> r gp t", r=32)), writes=[uk])
                    dve(lambda e: e.tensor_scalar(out=us[:], in0=us[:], scalar1=fl5[:32, 1:2], scalar2=None, op0=ALU.mult), [uk, 'fl5'], [uk])
                else:
                    lb = blk - NPRE // CPB
                    ub = ubs[blk % 2]
                    ubk = ('ub', blk % 2)
                    S.dma('sp', lambda e: e.dma_start(
                        out=us[:], in_=R['UT'][:, lb * UB:(lb + 1) * UB].rearrange("(gp r) t -> r gp t", r=32)), writes=[uk])
                    S.dma('sp', lambda e: e.dma_start(
                        out=ub[:], in_=R['UT'][:, HPR + lb * UB:HPR + (lb + 1) * UB].rearrange("(gp r) t -> r gp t", r=32)), writes=[ubk])
                    dve(lambda e: e.tensor_scalar(out=us[:], in0=us[:], scalar1=fl5[:32, 0:1], scalar2=None, op0=ALU.mult), [uk, 'fl5'], [uk])
                    dve(lambda e: e.scalar_tensor_tensor(out=us[:], in0=ub[:], scalar=fl5[:32, 1:2], in1=us[:], op0=ALU.mult, op1=ALU.add),
                        [uk, ubk, 'fl5'], [uk])

            def do_V(c):
                blk, tb, us, uk, ys, yk = cinfo(c)
                if c % CPB == 0 and c + CPB < nchunk:
                    load_u(c + CPB)
                for hf in range(2):
                    pss = [[self.psum_next(), self.psum_next()], [self.psum_next(), self.psum_next()]]
                    for ri in range(2):
                        for j in range(16):
                            gp = hf * 16 + j
                            ps, pk = pss[ri][j // 8]
                            col = (j % 8) * LC
                            S.op('pe', lambda e, ps=ps, ri=ri, gp=gp, col=col: e.matmul(
                                ps[:, col:col + LC], lhsT=WvT[ri][:, gp, :], rhs=us[:, gp, tb:tb + LC], start=True, stop=True),
                                reads=['WvT%d' % ri, uk], writes=[pk])
                    for tl in range(2):
                        g0 = hf * 16 + tl * 8
                        (pr, prk) = pss[0][tl]
                        (pi, pik) = pss[1][tl]
                        grs = Gr[:, g0:g0 + 8, :].rearrange("p g t -> p (g t)")
                        gis = Gi[:, g0:g0 + 8, :].rearrange("p g t -> p (g t)")
                        wi = (wcnt[0] % 2) * 4
                        wcnt[0] += 1
                        w = W1[wi:wi + 4]
                        wk = ['w%d' % (wi + q) for q in range(4)]
                        dve(lambda e, pr=pr, grs=grs, w=w: e.tensor_tensor(out=w[0][:], in0=pr[:, 0:512], in1=grs, op=ALU.mult), [prk, 'G'], [wk[0]])
                        dve(lambda e, pi=pi, gis=gis, w=w: e.tensor_tensor(out=w[1][:], in0=pi[:, 0:512], in1=gis, op=ALU.mult), [pik, 'G'], [wk[1]])
                        dve(lambda e, pi=pi, grs=grs, w=w: e.tensor_tensor(out=w[2][:], in0=pi[:, 0:512], in1=grs, op=ALU.mult), [pik, 'G'], [wk[2]])
                        dve(lambda e, pr=pr, gis=gis, w=w: e.tensor_tensor(out=w[3][:], in0=pr[:, 0:512], in1=gis, op=ALU.mult), [prk, 'G'], [wk[3]])
                        pl(lambda e, g0=g0, w=w: e.tensor_tensor(out=RI[0][:, g0:g0 + 8, :].rearrange("p g t -> p (g t)"), in0=w[0][:], in1=w[1][:],
                                                             op=ALU.add), [wk[0], wk[1]], ['RI0'])
                        pl(lambda e, g0=g0, w=w: e.tensor_tensor(out=RI[1][:, g0:g0 + 8, :].rearrange("p g t -> p (g t)"), in0=w[2][:], in1=w[3][:],
                                                             op=ALU.subtract), [wk[2], wk[3]], ['RI1'])

            def do_scan(c):
                if c > 0:
                    for ri, sp in enumerate(['SPr', 'SPi']):
                        dve(lambda e, sp=sp: e.tensor_tensor(out=T['TC'][:], in0=T[sp][:], in1=T['MAG'][:], op=ALU.mult), [sp, 'MAG'], ['TC'])
                        dve(lambda e, ri=ri: e.tensor_tensor(out=RI[ri][:, :, 0:1], in0=RI[ri][:, :, 0:1], in1=T['TC'][:].unsqueeze(2), op=ALU.add),
                            ['TC', 'RI%d' % ri], ['RI%d' % ri])
                for ri in range(2):
                    dve(lambda e, ri=ri: e.tensor_tensor_scan(out=RRt[ri][:].rearrange("p g t -> p (g t)"), data0=RHO[:].rearrange("p g t -> p (g t)"),
                                                             data1=RI[ri][:].rearrange("p g t -> p (g t)"), initial=0.0, op0=ALU.mult, op1=ALU.add),
                        ['RHO', 'RI%d' % ri], ['RRt%d' % ri])
                L = LC - 1
                col = lambda t_: t_[:, :, L:L + 1]
                dve(lambda e: e.tensor_tensor(out=T['TC'][:].unsqueeze(2), in0=col(RRt[0]), in1=col(Gr), op=ALU.mult), ['RRt0', 'G'], ['TC'])
                dve(lambda e: e.tensor_tensor(out=T['TD'][:].unsqueeze(2), in0=col(RRt[1]), in1=col(Gi), op=ALU.mult), ['RRt1', 'G'], ['TD'])
                dve(lambda e: e.tensor_tensor(out=T['SPr'][:], in0=T['TC'][:], in1=T['TD'][:], op=ALU.subtract), ['TC', 'TD'], ['SPr'])
                dve(lambda e: e.tensor_tensor(out=T['TC'][:].unsqueeze(2), in0=col(RRt[1]), in1=col(Gr), op=ALU.mult), ['RRt1', 'G'], ['TC'])
                dve(lambda e: e.tensor_tensor(out=T['TD'][:].unsqueeze(2), in0=col(RRt[0]), in1=col(Gi), op=ALU.mult), ['RRt0', 'G'], ['TD'])
                dve(lambda e: e.tensor_tensor(out=T['SPi'][:], in0=T['TC'][:], in1=T['TD'][:], op=ALU.add), ['TC', 'TD'], ['SPi'])

            def do_rotout(c):
                pl(lambda e: e.tensor_tensor(out=V1[0][:], in0=RRt[0][:], in1=Gr[:], op=ALU.mult), ['RRt0', 'G'], ['v0'])
                pl(lambda e: e.tensor_tensor(out=V1[1][:], in0=RRt[1][:], in1=Gi[:], op=ALU.mult), ['RRt1', 'G'], ['v1'])
                pl(lambda e: e.tensor_tensor(out=SB[0][:], in0=V1[0][:], in1=V1[1][:], op=ALU.subtract), ['v0', 'v1'], ['SB0'])
                pl(lambda e: e.tensor_tensor(out=V1[0][:], in0=RRt[1][:], in1=Gr[:], op=ALU.mult), ['RRt1', 'G'], ['v0'])
                pl(lambda e: e.tensor_tensor(out=V1[1][:], in0=RRt[0][:], in1=Gi[:], op=ALU.mult), ['RRt0', 'G'], ['v1'])
                pl(lambda e: e.tensor_tensor(out=SB[1][:], in0=V1[0][:], in1=V1[1][:], op=ALU.add), ['v0', 'v1'], ['SB1'])

            def do_y(c):
                blk, tb, us, uk, ys, yk = cinfo(c)
                for g0 in range(0, 32, 8):
                    ps, pk = self.psum_next()
                    for j in range(8):
                        gp = g0 + j
                        cs = j * LC
                        S.op('pe', lambda e, ps=ps, gp=gp, cs=cs: e.matmul(ps[:32, cs:cs + LC], lhsT=CT[0][:, gp, :], rhs=SB[0][:, gp, :],
                                                                         start=True, stop=False), reads=['CT0', 'SB0'], writes=[pk])
                        S.op('pe', lambda e, ps=ps, gp=gp, cs=cs: e.matmul(ps[:32, cs:cs + LC], lhsT=CT[1][:, gp, :], rhs=SB[1][:, gp, :],
                                                                         start=False, stop=False), reads=['CT1', 'SB1'], writes=[pk])
                        S.op('pe', lambda e, ps=ps, gp=gp, cs=cs: e.matmul(ps[:32, cs:cs + LC], lhsT=Dd[:, gp, :], rhs=us[:, gp, tb:tb + LC],
                                                                         start=False, stop=True), reads=['Dd', uk], writes=[pk])
                    gelu_to(ps, pk, 512, ys[:, g0:g0 + 8, tb:tb + LC], yk)
                if c % CPB == CPB - 1:
                    lb = blk - NPRE // CPB
                    S.dma('sp', lambda e: e.dma_start(
                        out=R['S5G'][:, lb * UB:(lb + 1) * UB].rearrange("(gp r) t -> r gp t", r=32), in_=ys[:]),
                        reads=[yk], writes=[('dram', 's5g', blk)])

            load_u(0)
            do_V(0)
            for c in range(nchunk):
                do_scan(c)
                if c >= NPRE:
                    do_rotout(c)
                if c + 1 < nchunk:
                    do_V(c + 1)
                if c >= NPRE:
                    do_y(c)
            for ri, (sp, dst) in enumerate([('SPr', O['s5r_p']), ('SPi', O['s5i_p'])]):
                ps, pk = self.psum_next()
                S.op('pe', lambda e, ps=ps, sp=sp: e.transpose(out=ps[:32, 0:128], in_=T[sp][:, :], identity=self.identf[:]),
                     reads=[sp, 'identf'], writes=[pk])
                dve(lambda e, ps=ps, ri=ri: e.tensor_copy(out=nat[ri][:], in_=ps[:32, 0:128]), [pk], ['nat%d' % ri])
                S.dma('sp', lambda e, ri=ri, dst=dst: e.dma_start(out=dst.rearrange("(gp g2) p -> gp (g2 p)", g2=2), in_=nat[ri][:]),
                      reads=['nat%d' % ri], writes=[('dram', 's5p', ri)])
            S.barrier()
            S.flush()
            es2.close()
            es3 = es.enter_context(ExitStack())
            sre = [self.sb(es3, [NSM, 4096], F32, "sre%d" % i) for i in range(2)]
            SO = [self.sb(es3, [128, 32, NSM], F32, "SO%d" % i) for i in range(2)]
            SN = [self.sb(es3, [128, 32, NSM], F32, "SN%d" % i) for i in range(2)]
            SNb = [self.sb(es3, [128, 32, NSM], BF16, "SNb%d" % i) for i in range(2)]
            Q1 = [self.sb(es3, [128, 32, NSM], F32, "Q1_%d" % i) for i in range(2)]
            uSs = self.sb(es3, [32, 32, NSM], BF16, "uSs")
            ySs = self.sb(es3, [32, 32, NSM], BF16, "ySs")
            S.dma('sp', lambda e: e.dma_start(out=sre[0][:], in_=I['s5r'][:, :]), writes=['sre0'])
            S.dma('sp', lambda e: e.dma_start(out=sre[1][:], in_=I['s5i'][:, :]), writes=['sre1'])
            S.dma('sp', lambda e: e.dma_start(out=uSs[:], in_=R['UT'][:, NPR:NTOK].rearrange("(gp r) t -> r gp t", r=32)), writes=['uSs'])
            for ri in range(2):
                for g0 in (0, 16):
                    ps, pk = self.psum_next()
                    for j in range(16):
                        S.op('pe', lambda e, ps=ps, ri=ri, g0=g0, j=j: e.transpose(out=ps[:, j * NSM:(j + 1) * NSM],
                                                                                  in_=sre[ri][:NSM, (g0 + j) * 128:(g0 + j + 1) * 128],
                                                                                  identity=self.identf[:NSM, :NSM]),
                             reads=['sre%d' % ri, 'identf'], writes=[pk])
                    dve(lambda e, ps=ps, ri=ri, g0=g0: e.tensor_copy(out=SO[ri][:, g0:g0 + 16, :], in_=ps[:, 0:16 * NSM].rearrange("p (j q) -> p j q", j=16)),
                        [pk], ['SO%d' % ri])
            ab = lambda nm: T[nm][:].unsqueeze(2).to_broadcast([128, 32, NSM])
            for ri in range(2):
                a_, b_ = (0, 1) if ri == 0 else (1, 0)
                dve(lambda e, a_=a_: e.tensor_tensor(out=Q1[0][:], in0=SO[a_][:], in1=ab('AR'), op=ALU.mult), ['SO%d' % a_, 'AR'], ['Q10'])
                dve(lambda e, b_=b_: e.tensor_tensor(out=Q1[1][:], in0=SO[b_][:], in1=ab('AI'), op=ALU.mult), ['SO%d' % b_, 'AI'], ['Q11'])
                dve(lambda e, ri=ri: e.tensor_tensor(out=Q1[0][:], in0=Q1[0][:], in1=Q1[1][:], op=(ALU.subtract if ri == 0 else ALU.add)),
                    ['Q10', 'Q11'], ['Q10'])
                for g0 in (0, 16):
                    ps, pk = self.psum_next()
                    for j in range(16):
                        gp = g0 + j
                        S.op('pe', lambda e, ps=ps, ri=ri, gp=gp, j=j: e.matmul(ps[:, j * NSM:(j + 1) * NSM], lhsT=WvT[ri][:, gp, :], rhs=uSs[:, gp, :],
                                                                               start=True, stop=True), reads=['WvT%d' % ri, 'uSs'], writes=[pk])
                    dve(lambda e, ps=ps, ri=ri, g0=g0: e.tensor_tensor(out=SN[ri][:, g0:g0 + 16, :], in0=Q1[0][:, g0:g0 + 16, :],
                                                                      in1=ps[:, 0:16 * NSM].rearrange("p (j q) -> p j q", j=16), op=ALU.add),
                        [pk, 'Q10'], ['SN%d' % ri])
                dve(lambda e, ri=ri: e.tensor_copy(out=SNb[ri][:], in_=SN[ri][:]), ['SN%d' % ri], ['SNb%d' % ri])
            for g0 in (0, 16):
                ps, pk = self.psum_next()
                for j in range(16):
                    gp = g0 + j
                    cs = j * NSM
                    S.op('pe', lambda e, ps=ps, gp=gp, cs=cs: e.matmul(ps[:32, cs:cs + NSM], lhsT=CT[0][:, gp, :], rhs=SNb[0][:, gp, :],
                                                                     start=True, stop=False), reads=['CT0', 'SNb0'], writes=[pk])
                    S.op('pe', lambda e, ps=ps, gp=gp, cs=cs: e.matmul(ps[:32, cs:cs + NSM], lhsT=CT[1][:, gp, :], rhs=SNb[1][:, gp, :],
                                                                     start=False, stop=False), reads=['CT1', 'SNb1'], writes=[pk])
                    S.op('pe', lambda e, ps=ps, gp=gp, cs=cs: e.matmul(ps[:32, cs:cs + NSM], lhsT=Dd[:, gp, :], rhs=uSs[:, gp, :],
                                                                     start=False, stop=True), reads=['Dd', 'uSs'], writes=[pk])
                gelu_to(ps, pk, 16 * NSM, ySs[:, g0:g0 + 16, :], 'ySs')
            S.dma('sp', lambda e: e.dma_start(out=R['S5G'][:, HPR:HT].rearrange("(gp r) t -> r gp t", r=32), in_=ySs[:]),
                  reads=['ySs'], writes=[('dram', 's5gs')])
            for ri, dst in enumerate([O['s5r_s'], O['s5i_s']]):
                for g0 in range(0, 32, 4):
                    ps, pk = self.psum_next()
                    for j in range(4):
                        S.op('pe', lambda e, ps=ps, ri=ri, g0=g0, j=j: e.transpose(out=ps[:NSM, j * 128:(j + 1) * 128], in_=SN[ri][:, g0 + j, :],
                                                                                  identity=self.identf[:]), reads=['SN%d' % ri, 'identf'], writes=[pk])
                    ei += 1
                    self.evac(ei, sre[ri][:, g0 * 128:(g0 + 4) * 128], ps[:NSM, 0:512], [pk], ['sre%d' % ri])
                S.dma('sp', lambda e, ri=ri, dst=dst: e.dma_start(out=dst[:, :], in_=sre[ri][:]), reads=['sre%d' % ri], writes=[('dram', 's5s', ri)])
            S.barrier()
            S.flush()
            es3.close()
        self.s5_glu()

    def s5_glu(self):
        S = self.S
        I = self.I
        R = self.R
        for blk in [(0, HT)]:
            t0, t1 = blk
            NT = t1 - t0
            with ExitStack() as es:
                gT = self.sb(es, [128, 8, NT], BF16, "gT")
                self.WB = [self.sb(es, [128, 8 * 256], BF16, "wb") for _ in range(2)]
                self.wb_i = 0
                sg = [self.sb(es, [128, 512], F32, "sgl") for _ in range(2)]
                ob = [self.sb(es, [128, 512], BF16, "obl") for _ in range(2)]
                S.dma('sp', lambda e: e.dma_start(out=gT[:], in_=R['S5G'][:, t0:t1].rearrange("(k p) t -> p k t", p=128)), writes=['gT'])
                cnt = 0
                for c0 in range(0, 1024, 256):
                    wv, wkey = self.wload(I['w_s5_glu'][:, c0:c0 + 256], 8, 256)
                    for m in range(2):
                        mi = c0 // 128 + m
                        for (n0, nn) in nchunks(NT):
                            ps, pk = self.psum_next()
                            for k in range(8):
                                S.op('pe', lambda e, ps=ps, wv=wv, k=k, m=m, n0=n0, nn=nn: e.matmul(
                                    ps[:, 0:nn], lhsT=wv[:, k, m * 128:(m + 1) * 128], rhs=gT[:, k, n0:n0 + nn], start=(k == 0), stop=(k == 7)),
                                    reads=[wkey, 'gT'], writes=[pk])
                            sgt = sg[cnt % 2]
                            o = ob[cnt % 2]
                            sk = ('sgl', cnt % 2)
                            ok = ('obl', cnt % 2)
                            cnt += 1
                            S.op('act', lambda e, sgt=sgt, ps=ps, nn=nn: e.activation(out=sgt[:, 0:nn], in_=ps[:, 0:nn], func=AF.Sigmoid),
                                 reads=[pk], writes=[sk])
                            S.op('dve', lambda e, o=o, sgt=sgt, mi=mi, n0=n0, nn=nn: e.tensor_tensor(out=o[:, 0:nn], in0=sgt[:, 0:nn],
                                                                                                    in1=gT[:, mi, n0:n0 + nn], op=ALU.mult),
                                 reads=[sk, 'gT'], writes=[ok])
                            S.dma('sp', lambda e, o=o, mi=mi, n0=n0, nn=nn: e.dma_start(
                                out=R['BR0'][mi * 128:(mi + 1) * 128, t0 + n0:t0 + n0 + nn], in_=o[:, 0:nn]), reads=[ok],
                                writes=[('dram', 'brt0', mi, n0)])
                S.barrier()
                S.flush()

    def mlstm_prompt(self):
        S = self.S
        I = self.I
        R = self.R
        O = self.O
        CL = 64
        NCH = NPR // CL
        dve = lambda fn, reads, writes: S.op('dve', fn, reads=reads, writes=writes)
        act = lambda fn, reads, writes: S.op('act', fn, reads=reads, writes=writes)
        with ExitStack() as es:
            QC = {nm: self.sb(es, [64, NCH, 4], F32, nm) for nm in ['XC', 'EMC', 'WLC', 'WIC']}
            WE = self.sb(es, [128, 4, NCH], F32, "WE")
            Rb = self.sb(es, [64, 4, NPR], F32, "Rb")
            Cf = self.sb(es, [128, 4, 2, 257], F32, "Cf")
            Cb = self.sb(es, [128, 4, 2, 257], BF16, "Cb")
            GH = self.sb(es, [64, 1024], F32, "GH")
            maskT = self.sb(es, [64, 64], F32, "maskT")
            oh4 = self.sb(es, [4, 4, 128], F32, "oh4")
            bi = self.sb(es, [4, 1], F32, "bi")
            bf = self.sb(es, [4, 1], F32, "bf")
            NH = NCH // 2
            flm = self.sb(es, [128, 2], F32, "flm")
            WLCp = self.sb(es, [64, NH, 4], F32, "WLCp")
            WEp = self.sb(es, [128, 4, NH], F32, "WEp")
            S.dma('sp', lambda e: e.dma_start(out=flm[:], in_=I['flag'][:, :]), writes=['flm'])
            es1 = es.enter_context(ExitStack())
            rows = {nm: self.sb(es1, [4, NPR], F32, nm) for nm in ['IG', 'FG', 'L', 'F', 'X', 'Rm', 'RP', 'WI', 'MT', 'EM', 'RE', 'WL', 'ON']}
            r3 = lambda nm: rows[nm][:].rearrange("h (c t) -> h c t", t=CL)
            S.dma('sp', lambda e: e.dma_start(out=rows['IG'][:], in_=R['IGR'][:, 0:NPR]), writes=['IG'])
            S.dma('sp', lambda e: e.dma_start(out=rows['FG'][:], in_=R['FGR'][:, 0:NPR]), writes=['FG'])
            S.dma('sp', lambda e: e.dma_start(out=bi[:], in_=I['b_igate'].rearrange("(h o) -> h o", o=1)), writes=['bi'])
            S.dma('sp', lambda e: e.dma_start(out=bf[:], in_=I['b_fgate'].rearrange("(h o) -> h o", o=1)), writes=['bf'])
            S.dma('sp', lambda e: e.dma_start(out=GH[:], in_=I['g_mlstm_head'].partition_broadcast(64)), writes=['GH'])
            S.op('pool', lambda e: e.memset(maskT[:], 0.0), writes=['maskT'])
            S.op('pool', lambda e: e.affine_select(out=maskT[:], in_=maskT[:], pattern=[[1, 64]], compare_op=ALU.is_ge, fill=-1e30,
                                                   base=0, channel_multiplier=-1), reads=['maskT'], writes=['maskT'])
            S.op('pool', lambda e: e.memset(rows['ON'][:], 1.0), writes=['ON'])
            S.op('pool', lambda e: e.memset(Cf[:], 0.0), writes=['Cf'])
            S.op('pool', lambda e: e.memset(Cb[:], 0.0), writes=['Cb'])
            dve(lambda e: e.tensor_copy(out=oh4[:], in_=self.identf[:4, :4].unsqueeze(2).to_broadcast([4, 4, 128])), ['identf'], ['oh4'])
            dve(lambda e: e.tensor_scalar(out=rows['IG'][:], in0=rows['IG'][:], scalar1=bi[:, 0:1], scalar2=None, op0=ALU.add), ['IG', 'bi'], ['IG'])
            dve(lambda e: e.tensor_scalar(out=rows['FG'][:], in0=rows['FG'][:], scalar1=bf[:, 0:1], scalar2=None, op0=ALU.add), ['FG', 'bf'], ['FG'])
            act(lambda e: e.activation(out=rows['L'][:], in_=rows['FG'][:], func=AF.Exp, scale=-1.0), ['FG'], ['L'])
            act(lambda e: e.activation(out=rows['L'][:], in_=rows['L'][:], func=AF.Ln, bias=1.0), ['L'], ['L'])
            dve(lambda e: e.tensor_tensor_scan(out=rows['F'][:], data0=rows['ON'][:], data1=rows['L'][:], initial=0.0, op0=ALU.mult, op1=ALU.add),
                ['ON', 'L'], ['F'])
            dve(lambda e: e.tensor_tensor(out=rows['X'][:], in0=rows['IG'][:], in1=rows['F'][:], op=ALU.add), ['IG', 'F'], ['X'])
            dve(lambda e: e.tensor_tensor_scan(out=rows['Rm'][:], data0=rows['ON'][:], data1=rows['X'][:], initial=0.0, op0=ALU.mult, op1=ALU.max),
                ['ON', 'X'], ['Rm'])
            dve(lambda e: e.tensor_copy(out=r3('RP')[:, 1:NCH, :], in_=r3('Rm')[:, 0:NCH - 1, CL - 1:CL].to_broadcast([4, NCH - 1, CL])), ['Rm'], ['RP'])
            S.op('pool', lambda e: e.memset(r3('RP')[:, 0:1, :], 0.0), reads=['RP'], writes=['RP'])
            dve(lambda e: e.tensor_tensor(out=rows['WI'][:], in0=rows['RP'][:], in1=rows['Rm'][:], op=ALU.subtract), ['RP', 'Rm'], ['WI'])
            act(lambda e: e.activation(out=rows['WI'][:], in_=rows['WI'][:], func=AF.Exp), ['WI'], ['WI'])
            dve(lambda e: e.tensor_tensor(out=rows['MT'][:], in0=rows['Rm'][:], in1=rows['F'][:], op=ALU.subtract), ['Rm', 'F'], ['MT'])
            act(lambda e: e.activation(out=rows['EM'][:], in_=rows['MT'][:], func=AF.Exp, scale=-1.0), ['MT'], ['EM'])
            dve(lambda e: e.tensor_copy(out=r3('RE'), in_=r3('Rm')[:, :, CL - 1:CL].to_broadcast([4, NCH, CL])), ['Rm'], ['RE'])
            dve(lambda e: e.tensor_tensor(out=rows['WL'][:], in0=rows['X'][:], in1=rows['RE'][:], op=ALU.subtract), ['X', 'RE'], ['WL'])
            act(lambda e: e.activation(out=rows['WL'][:], in_=rows['WL'][:], func=AF.Exp, bias=-math.log(16.0)), ['WL'], ['WL'])
            S.dma('sp', lambda e: e.dma_start(out=O['m_p'][:, :], in_=rows['MT'][:, NPR - 1:NPR]), reads=['MT'], writes=[('dram', 'm_p')])
            for qn, cn in [('X', 'XC'), ('EM', 'EMC'), ('WL', 'WLC'), ('WI', 'WIC')]:
                ps, pk = self.psum_next()
                for c in range(NCH):
                    S.op('pe', lambda e, ps=ps, qn=qn, c=c: e.transpose(out=ps[:64, c * 4:(c + 1) * 4], in_=rows[qn][0:4, c * CL:(c + 1) * CL],
                                                                      identity=self.identf[:4, :4]), reads=[qn, 'identf'], writes=[pk])
                dve(lambda e, ps=ps, cn=cn: e.tensor_copy(out=QC[cn][:].rearrange("p c h -> p (c h)"), in_=ps[:64, 0:NCH * 4]), [pk], [cn])
            ei = 0
            for h in range(4):
                for (n0, nn) in nchunks(NPR):
                    ps, pk = self.psum_next()
                    S.op('pe', lambda e, ps=ps, h=h, n0=n0, nn=nn: e.matmul(ps[:64, 0:nn], lhsT=oh4[:, h, 0:64], rhs=rows['Rm'][:, n0:n0 + nn],
                                                                          start=True, stop=True), reads=['oh4', 'Rm'], writes=[pk])
                    ei += 1
                    self.evac(ei, Rb[:, h, n0:n0 + nn], ps[:64, 0:nn], [pk], ['Rb'])
                ps, pk = self.psum_next()
                S.op('pe', lambda e, ps=ps, h=h: e.matmul(ps[:, 0:NCH], lhsT=oh4[:, h, :], rhs=r3('WI')[:, :, CL - 1], start=True, stop=True),
                     reads=['oh4', 'WI'], writes=[pk])
                dve(lambda e, ps=ps, h=h: e.tensor_copy(out=WE[:, h, :], in_=ps[:, 0:NCH]), [pk], ['WE'])
            f0 = lambda n_: flm[:n_, 0:1]
            f1 = lambda n_: flm[:n_, 1:2]
            dve(lambda e: e.tensor_scalar(out=WLCp[:], in0=QC['WLC'][:, 0:NH, :], scalar1=f1(64), scalar2=None, op0=ALU.mult), ['WLC', 'flm'], ['WLCp'])
            dve(lambda e: e.tensor_copy(out=WEp[:], in_=WE[:, :, 0:NH]), ['WE'], ['WEp'])
            for nm in ['XC', 'EMC', 'WLC', 'WIC']:
                dve(lambda e, nm=nm: e.tensor_scalar(out=QC[nm][:, 0:NH, :], in0=QC[nm][:, 0:NH, :], scalar1=f0(64), scalar2=None, op0=ALU.mult),
                    [nm, 'flm', 'WLCp'], [nm])
                dve(lambda e, nm=nm: e.scalar_tensor_tensor(out=QC[nm][:, 0:NH, :], in0=QC[nm][:, NH:NCH, :], scalar=f1(64), in1=QC[nm][:, 0:NH, :],
                                                           op0=ALU.mult, op1=ALU.add), [nm, 'flm'], [nm])
            dve(lambda e: e.tensor_scalar(out=WE[:, :, 0:NH], in0=WE[:, :, 0:NH], scalar1=f0(128), scalar2=None, op0=ALU.mult), ['WE', 'flm', 'WEp'], ['WE'])
            dve(lambda e: e.scalar_tensor_tensor(out=WE[:, :, 0:NH], in0=WE[:, :, NH:NCH], scalar=f1(128), in1=WE[:, :, 0:NH], op0=ALU.mult, op1=ALU.add),
                ['WE', 'flm'], ['WE'])
            dve(lambda e: e.tensor_scalar(out=Rb[:, :, 0:HPR], in0=Rb[:, :, 0:HPR], scalar1=f0(64), scalar2=None, op0=ALU.mult), ['Rb', 'flm'], ['Rb'])
            dve(lambda e: e.scalar_tensor_tensor(out=Rb[:, :, 0:HPR], in0=Rb[:, :, HPR:NPR], scalar=f1(64), in1=Rb[:, :, 0:HPR], op0=ALU.mult, op1=ALU.add),
                ['Rb', 'flm'], ['Rb'])
            S.barrier()
            S.flush()
            es1.close()
            es2 = es.enter_context(ExitStack())
            BT = 256
            CPB = BT // CL
            qT = [self.sb(es2, [128, 8, BT], BF16, "qTb") for _ in range(2)]
            kT = [self.sb(es2, [128, 8, BT], BF16, "kTb") for _ in range(2)]
            v1 = [self.sb(es2, [64, CPB, 4, 257], BF16, "v1b") for _ in range(2)]
            ktm = [self.sb(es2, [64, CPB, 1024], BF16, "ktmb") for _ in range(2)]
            qTB = self.sb(es2, [128, 8, BT], BF16, "qTB")
            v1B = self.sb(es2, [64, CPB, 4, 256], BF16, "v1B")
            ktmB = self.sb(es2, [64, CPB, 1024], BF16, "ktmB")
            otm = [self.sb(es2, [64, 1024], F32, "otm") for _ in range(2)]
            sgo = [self.sb(es2, [64, 1024], F32, "sgo") for _ in range(2)]
            mlo = [self.sb(es2, [64, 1024], BF16, "mlo") for _ in range(2)]
            mlT = [self.sb(es2, [128, 8, BT], BF16, "mlT") for _ in range(2)]
            tmpw = [self.sb(es2, [64, 64], F32, "tmpw") for _ in range(3)]
            wT = [self.sb(es2, [64, 64], F32, "wT") for _ in range(3)]
            ST = [self.sb(es2, [64, 64], BF16, "ST") for _ in range(3)]
            na = [self.sb(es2, [64, 257], F32, "na") for _ in range(3)]
            num = [self.sb(es2, [64, 257], F32, "num") for _ in range(3)]
            hh = [self.sb(es2, [64, 256], F32, "hh") for _ in range(3)]
            jk = [self.sb(es2, [64, 256], F32, "jk") for _ in range(3)]
            t1 = [self.sb(es2, [64, 256], F32, "t1") for _ in range(3)]
            sm = [self.sb(es2, [64, 8], F32, "sm") for _ in range(3)]
            kw = [self.sb(es2, [64, 256], BF16, "kw") for _ in range(3)]
            for i in range(2):
                S.op('pool', lambda e, i=i: e.memset(v1[i][:], 1.0), writes=[('v1', i)])
            NSLOT = 3

            def stage_u(c, h, i2, bi_, cl, wlc, wend):
                    k_ = lambda nm: (nm, i2)
                    act(lambda e, i2=i2, bi_=bi_, cl=cl, h=h, wlc=wlc: e.activation(out=kw[i2][:], in_=ktm[bi_][:, cl, h * 256:(h + 1) * 256], func=AF.Copy,
                                                                               scale=wlc),
                        [('ktm', bi_), 'WLC', 'WLCp'], [k_('kw')])
                    for dh in range(2):
                        ps_c, pkc = self.psum_next()
                        S.op('pe', lambda e, ps_c=ps_c, i2=i2, dh=dh, bi_=bi_, cl=cl, h=h: e.matmul(
                            ps_c[:, 0:257], lhsT=kw[i2][:, dh * 128:(dh + 1) * 128], rhs=v1[bi_][:, cl, h, :], start=True, stop=True),
                            reads=[k_('kw'), ('v1', bi_)], writes=[pkc])
                        dve(lambda e, ps_c=ps_c, h=h, dh=dh, wend=wend: e.scalar_tensor_tensor(out=Cf[:, h, dh, :], in0=Cf[:, h, dh, :], scalar=wend,
                                                                                       in1=ps_c[:, 0:257], op0=ALU.mult, op1=ALU.add),
                            [pkc, ('Cf', h, dh), 'WE', 'WEp'], [('Cf', h, dh)])
                        S.op('pool', lambda e, h=h, dh=dh: e.tensor_copy(out=Cb[:, h, dh, :], in_=Cf[:, h, dh, :]),
                             reads=[('Cf', h, dh)], writes=[('Cb', h)])


            def stage_a(c, h, i2, bi_, cl, ci, tsl):
                    k_ = lambda nm: (nm, i2)
                    ps_s, pks = self.psum_next()
                    for dh in range(2):
                        S.op('pe', lambda e, ps_s=ps_s, h=h, dh=dh, bi_=bi_, tsl=tsl: e.matmul(
                            ps_s[:64, 0:64], lhsT=kT[bi_][:, h * 2 + dh, tsl], rhs=qT[bi_][:, h * 2 + dh, tsl], start=(dh == 0), stop=(dh == 1)),
                            reads=[('kT', bi_), ('qT', bi_)], writes=[pks])
                    dve(lambda e, i2=i2, h=h, c=c: e.tensor_tensor(out=tmpw[i2][:], in0=maskT[:], in1=Rb[:, h, c * CL:(c + 1) * CL], op=ALU.subtract),
                        ['maskT', 'Rb'], [k_('tmpw')])
                    act(lambda e, i2=i2, h=h, c=c: e.activation(out=wT[i2][:], in_=tmpw[i2][:], func=AF.Exp, bias=QC['XC'][:, c, h:h + 1]),
                        [k_('tmpw'), 'XC'], [k_('wT')])
                    dve(lambda e, i2=i2, ps_s=ps_s: e.scalar_tensor_tensor(out=ST[i2][:], in0=ps_s[:64, 0:64], scalar=1.0 / 16, in1=wT[i2][:],
                                                                         op0=ALU.mult, op1=ALU.mult), [pks, k_('wT')], [k_('ST')])
                    ps_a, pka = self.psum_next()
                    S.op('pe', lambda e, ps_a=ps_a, i2=i2, bi_=bi_, cl=cl, h=h: e.matmul(ps_a[:64, 0:257], lhsT=ST[i2][:], rhs=v1[bi_][:, cl, h, :],
                                                                                        start=True, stop=True),
                         reads=[k_('ST'), ('v1', bi_)], writes=[pka])
                    ps_b, pkb = self.psum_next()
                    for dh in range(2):
                        S.op('pe', lambda e, ps_b=ps_b, h=h, dh=dh, bi_=bi_, tsl=tsl: e.matmul(
                            ps_b[:64, 0:257], lhsT=qT[bi_][:, h * 2 + dh, tsl], rhs=Cb[:, h, dh, :], start=(dh == 0), stop=(dh == 1)),
                            reads=[('qT', bi_), ('Cb', h)], writes=[pkb])
                    act(lambda e, i2=i2, ps_a=ps_a: e.copy(out=na[i2][:], in_=ps_a[:64, 0:257]), [pka], [k_('na')])
                    dve(lambda e, i2=i2, ps_b=ps_b, c=c, h=h: e.scalar_tensor_tensor(out=num[i2][:], in0=ps_b[:64, 0:257], scalar=QC['WIC'][:, c, h:h + 1],
                                                                                   in1=na[i2][:], op0=ALU.mult, op1=ALU.add),
                        [pkb, k_('na'), 'WIC'], [k_('num')])

            def stage_b(c, h, i2, bi_, cl, ci, tsl):
                    k_ = lambda nm: (nm, i2)
                    act(lambda e, i2=i2: e.activation(out=sm[i2][:, 6:7], in_=num[i2][:, 256:257], func=AF.Abs), [k_('num')], [k_('sm')])
                    dve(lambda e, i2=i2, c=c, h=h: e.tensor_scalar(out=sm[i2][:, 0:1], in0=sm[i2][:, 6:7], scalar1=QC['EMC'][:, c, h:h + 1],
                                                                   scalar2=None, op0=ALU.max), [k_('sm'), 'EMC'], [k_('sm')])
                    dve(lambda e, i2=i2: e.reciprocal(out=sm[i2][:, 1:2], in_=sm[i2][:, 0:1]), [k_('sm')], [k_('sm')])
                    act(lambda e, i2=i2: e.activation(out=hh[i2][:], in_=num[i2][:, 0:256], func=AF.Copy, scale=sm[i2][:, 1:2]),
                        [k_('num'), k_('sm')], [k_('hh')])
                    act(lambda e, i2=i2: e.activation(out=jk[i2][:], in_=hh[i2][:], func=AF.Square, accum_out=sm[i2][:, 2:3]),
                        [k_('hh')], [k_('jk'), k_('sm')])
                    dve(lambda e, i2=i2: e.tensor_scalar(out=sm[i2][:, 3:4], in0=sm[i2][:, 2:3], scalar1=1.0 / 256, scalar2=EPS, op0=ALU.mult, op1=ALU.add),
                        [k_('sm')], [k_('sm')])
                    act(lambda e, i2=i2: e.activation(out=sm[i2][:, 4:5], in_=sm[i2][:, 3:4], func=AF.Sqrt), [k_('sm')], [k_('sm')])
                    dve(lambda e, i2=i2: e.reciprocal(out=sm[i2][:, 5:6], in_=sm[i2][:, 4:5]), [k_('sm')], [k_('sm')])
                    dve(lambda e, i2=i2, h=h: e.scalar_tensor_tensor(out=t1[i2][:], in0=hh[i2][:], scalar=sm[i2][:, 5:6], in1=GH[:, h * 256:(h + 1) * 256],
                                                                   op0=ALU.mult, op1=ALU.mult), [k_('hh'), k_('sm'), 'GH'], [k_('t1')])
                    dve(lambda e, i2=i2, h=h, ci=ci: e.tensor_tensor(out=mlo[ci][:, h * 256:(h + 1) * 256], in0=t1[i2][:],
                                                                   in1=sgo[ci][:, h * 256:(h + 1) * 256], op=ALU.mult),
                        [k_('t1'), ('sgo', ci)], [('mlo', ci)])
                    stage_u(c, h, i2, bi_, cl, QC['WLC'][:, c, h:h + 1], WE[:, h, c:c + 1])

            blend = lambda dst, src_b, n_, dk, sk: (
                dve(lambda e: e.tensor_scalar(out=dst, in0=dst, scalar1=flm[:n_, 0:1], scalar2=None, op0=ALU.mult), [dk, 'flm'], [dk]),
                dve(lambda e: e.scalar_tensor_tensor(out=dst, in0=src_b, scalar=flm[:n_, 1:2], in1=dst, op0=ALU.mult, op1=ALU.add), [dk, sk, 'flm'], [dk]))

            def load_kv(bi_, t0):
                for c8 in range(CPB):
                    S.dma('sp', lambda e, c8=c8: e.dma_start(
                        out=v1[bi_][:, c8, :, 0:256], in_=R['VTM'][t0 + c8 * CL:t0 + (c8 + 1) * CL, :].rearrange("s (h v) -> s h v", h=4)),
                        reads=[('v1', bi_)], writes=[('v1', bi_)])
                S.dma('sp', lambda e: e.dma_start(out=ktm[bi_][:], in_=R['KTM'][t0:t0 + BT, :].rearrange("(c s) f -> s c f", s=CL)),
                      writes=[('ktm', bi_)])

            def pre_prefix(c, bi_):
                if c % CPB == 0:
                    load_kv(bi_, (c // CPB) * BT)

            def pre_own(co, bi_):
                if co % CPB == 0:
                    t0 = (co // CPB) * BT
                    load_kv(bi_, t0)
                    for c8 in range(CPB):
                        S.dma('sp', lambda e, c8=c8: e.dma_start(
                            out=v1B[:, c8, :, :], in_=R['VTM'][HPR + t0 + c8 * CL:HPR + t0 + (c8 + 1) * CL, :].rearrange("s (h v) -> s h v", h=4)),
                            writes=['v1B'])
                    S.dma('sp', lambda e: e.dma_start(out=ktmB[:], in_=R['KTM'][HPR + t0:HPR + t0 + BT, :].rearrange("(c s) f -> s c f", s=CL)),
                          writes=['ktmB'])
                    blend(v1[bi_][:, :, :, 0:256], v1B[:], 64, ('v1', bi_), 'v1B')
                    blend(ktm[bi_][:], ktmB[:], 64, ('ktm', bi_), 'ktmB')
                    S.dma('sp', lambda e: e.dma_start(out=qT[bi_][:], in_=R['QTo'][:, t0:t0 + BT].rearrange("(k p) t -> p k t", p=128)),
                          writes=[('qT', bi_)])
                    S.dma('sp', lambda e: e.dma_start(out=kT[bi_][:], in_=R['KT'][:, t0:t0 + BT].rearrange("(k p) t -> p k t", p=128)),
                          writes=[('kT', bi_)])
                    S.dma('sp', lambda e: e.dma_start(out=qTB[:], in_=R['KT'][:, HPR + t0:HPR + t0 + BT].rearrange("(k p) t -> p k t", p=128)),
                          writes=['qTB'])
                    blend(kT[bi_][:], qTB[:], 128, ('kT', bi_), 'qTB')
                ci = co % 2
                S.dma('sp', lambda e: e.dma_start(out=otm[ci][:], in_=R['OTMo'][co * CL:(co + 1) * CL, :]), writes=[('otm', ci)])
                act(lambda e: e.activation(out=sgo[ci][:], in_=otm[ci][:], func=AF.Sigmoid), [('otm', ci)], [('sgo', ci)])

            def chunk_post(co, bi_):
                cl = co % CPB
                ci = co % 2
                tsl = slice(cl * CL, (cl + 1) * CL)
                pb, pbk = self.psumb_next()
                for k in range(8):
                    S.op('pe', lambda e, pb=pb, k=k, ci=ci: e.transpose(out=pb[:, k * 64:(k + 1) * 64], in_=mlo[ci][:64, k * 128:(k + 1) * 128],
                                                                      identity=self.identb[:64, :64]), reads=[('mlo', ci), 'identb'], writes=[pbk])
                act(lambda e, pb=pb: e.copy(out=mlT[bi_][:, :, tsl], in_=pb[:, 0:512].rearrange("p (k t) -> p k t", k=8)),
                    [pbk], [('mlT', bi_)])
                if cl == CPB - 1:
                    t0 = (co // CPB) * BT
                    S.dma('sp', lambda e: e.dma_start(out=R['BR1'][:, t0:t0 + BT].rearrange("(k p) t -> p k t", p=128), in_=mlT[bi_][:]),
                          reads=[('mlT', bi_)], writes=[('dram', 'br1', t0)])

            n_ = 0
            nblk_pre = NH // CPB
            for c in range(NH):
                bi_ = (c // CPB) % 2
                pre_prefix(c, bi_)
                for h in range(4):
                    stage_u(c, h, n_ % NSLOT, bi_, c % CPB, WLCp[:, c, h:h + 1], WEp[:, h, c:c + 1])
                    n_ += 1
            prev = None
            for co in range(NH):
                bi_ = (nblk_pre + co // CPB) % 2
                for h in range(4):
                    if h == 0:
                        pre_own(co, bi_)
                    cl = co % CPB
                    args = (co, h, n_ % NSLOT, bi_, cl, co % 2, slice(cl * CL, (cl + 1) * CL))
                    n_ += 1
                    stage_a(*args)
                    if prev is not None:
                        stage_b(*prev)
                        if prev[1] == 3:
                            chunk_post(prev[0], prev[3])
                    prev = args
            stage_b(*prev)
            chunk_post(prev[0], prev[3])
            for h in range(4):
                for dh in range(2):
                    S.dma('sp', lambda e, h=h, dh=dh: e.dma_start(out=O['C_p'][h, dh * 128:(dh + 1) * 128, :], in_=Cf[:, h, dh, 0:256]),
                          reads=[('Cf', h, dh)], writes=[('dram', 'C_p', h, dh)])
                    S.dma('sp', lambda e, h=h, dh=dh: e.dma_start(out=O['n_p'][h, dh * 128:(dh + 1) * 128].rearrange("(p o) -> p o", o=1),
                                                                 in_=Cf[:, h, dh, 256:257]),
                          reads=[('Cf', h, dh)], writes=[('dram', 'n_p', h, dh)])
            S.barrier()
            S.flush()
            es2.close()

    def mlstm_sample(self):
        S = self.S
        I = self.I
        R = self.R
        O = self.O
        B = NSM
        dve = lambda fn, reads, writes: S.op('dve', fn, reads=reads, writes=writes)
        act = lambda fn, reads, writes: S.op('act', fn, reads=reads, writes=writes)
        with ExitStack() as es:
            sc = {nm: self.sb(es, [B, 4], F32, "ms_" + nm) for nm in
                  ['g', 'bi', 'bf', 'iv', 'l', 'm0', 'gi', 'mt', 'wi', 'wa', 'em', 'qk', 'qn', 's', 'nq', 'rd', 'ss', 'rs', 't']}
            G8 = self.sb(es, [B, 8], F32, "G8")
            qS = self.sb(es, [128, 8, B], BF16, "qSm")
            qtm = self.sb(es, [B, 1024], BF16, "qtmm")
            ktm = self.sb(es, [B, 1024], BF16, "ktmm")
            vtm = self.sb(es, [B, 1024], BF16, "vtmm")
            otm = self.sb(es, [B, 1024], F32, "otmm")
            n0 = self.sb(es, [B, 1024], F32, "n0m")
            GH = self.sb(es, [B, 1024], F32, "GHm")
            p1 = self.sb(es, [B, 1024], F32, "p1m")
            p2 = self.sb(es, [B, 1024], F32, "p2m")
            numt = self.sb(es, [B, 1024], F32, "numt")
            kws = self.sb(es, [B, 1024], BF16, "kws")
            Km = [self.sb(es, [B, 1024], BF16, "Km") for _ in range(2)]
            mlb = self.sb(es, [B, 1024], BF16, "mlb")
            mlT = self.sb(es, [128, 8, B], BF16, "mlTs")
            IDB = self.sb(es, [128, B, B], BF16, "IDB")
            Qm = self.sb(es, [128, 8, B, B], BF16, "Qm")
            WD = self.sb(es, [B, B, 4], F32, "WD")
            ones32 = self.sb(es, [B, 128], F32, "ones32")
            Wb = self.sb(es, [128, B * 4], F32, "Wb")
            C32 = [self.sb(es, [128, 4, 2, 256], F32, "C32") for _ in range(3)]
            C16 = [self.sb(es, [128, 4, 2, 256], BF16, "C16") for _ in range(2)]
            Co = [self.sb(es, [128, 4, 2, 256], F32, "Co") for _ in range(2)]
            S.dma('sp', lambda e: e.dma_start(out=G8[:], in_=R['GTM'][NPR:NTOK, :]), writes=['G8'])
            S.dma('sp', lambda e: e.dma_start(out=sc['bi'][:], in_=I['b_igate'].partition_broadcast(B)), writes=['bi'])
            S.dma('sp', lambda e: e.dma_start(out=sc['bf'][:], in_=I['b_fgate'].partition_broadcast(B)), writes=['bf'])
            S.dma('sp', lambda e: e.dma_start(out=sc['m0'][:], in_=I['mM'][:, :]), writes=['m0'])
            S.dma('sp', lambda e: e.dma_start(out=n0[:], in_=I['mN'][:, :]), writes=['n0'])
            S.dma('sp', lambda e: e.dma_start(out=GH[:], in_=I['g_mlstm_head'].partition_broadcast(B)), writes=['GH'])
            S.dma('sp', lambda e: e.dma_start(out=ktm[:], in_=R['KTM'][NPR:NTOK, :]), writes=['ktm'])
            S.dma('sp', lambda e: e.dma_start(out=vtm[:], in_=R['VTM'][NPR:NTOK, :]), writes=['vtm'])
            S.dma('sp', lambda e: e.dma_start(out=otm[:], in_=R['OTMo'][HPR:HT, :]), writes=['otm'])
            S.dma('sp', lambda e: e.dma_start(out=qS[:], in_=R['QTo'][:, HPR:HT].rearrange("(k p) t -> p k t", p=128)), writes=['qS'])
            for kq in range(0, 8, 4):
                pb, pbk = self.psumb_next()
                for j in range(4):
                    S.op('pe', lambda e, pb=pb, j=j, kq=kq: e.transpose(out=pb[:B, j * 128:(j + 1) * 128], in_=qS[:, kq + j, :],
                                                                       identity=self.identb[:]), reads=['qS', 'identb'], writes=[pbk])
                act(lambda e, pb=pb, kq=kq: e.copy(out=qtm[:, kq * 128:(kq + 4) * 128], in_=pb[:B, 0:512]), [pbk], ['qtm'])
            S.op('pool', lambda e: e.memset(IDB[:], 1.0), writes=['IDB'])
            S.op('pool', lambda e: e.affine_select(out=IDB[:], in_=IDB[:], pattern=[[1, B], [-1, B]], compare_op=ALU.is_equal, fill=0.0,
                                                   base=0, channel_multiplier=0), reads=['IDB'], writes=['IDB'])
            S.op('pool', lambda e: e.memset(ones32[:], 1.0), writes=['ones32'])
            for k in range(8):
                dve(lambda e, k=k: e.tensor_tensor(out=Qm[:, k, :, :], in0=qS[:, k, :].unsqueeze(1).to_broadcast([128, B, B]), in1=IDB[:], op=ALU.mult),
                    ['qS', 'IDB'], ['Qm'])
            tt = lambda o, a, b, op: dve(lambda e: e.tensor_tensor(out=sc[o][:], in0=sc[a][:], in1=sc[b][:], op=op), [a, b], [o])
            dve(lambda e: e.tensor_tensor(out=sc['iv'][:], in0=G8[:, 0:4], in1=sc['bi'][:], op=ALU.add), ['G8', 'bi'], ['iv'])
            dve(lambda e: e.tensor_tensor(out=sc['g'][:], in0=G8[:, 4:8], in1=sc['bf'][:], op=ALU.add), ['G8', 'bf'], ['g'])
            act(lambda e: e.activation(out=sc['l'][:], in_=sc['g'][:], func=AF.Exp, scale=-1.0), ['g'], ['l'])
            act(lambda e: e.activation(out=sc['l'][:], in_=sc['l'][:], func=AF.Ln, bias=1.0), ['l'], ['l'])
            tt('gi', 'm0', 'l', ALU.subtract)
            tt('mt', 'gi', 'iv', ALU.max)
            S.dma('sp', lambda e: e.dma_start(out=O['m_s'][:, :], in_=sc['mt'][:]), reads=['mt'], writes=[('dram', 'm_s')])
            tt('wi', 'gi', 'mt', ALU.subtract)
            act(lambda e: e.activation(out=sc['wi'][:], in_=sc['wi'][:], func=AF.Exp), ['wi'], ['wi'])
            tt('wa', 'iv', 'mt', ALU.subtract)
            act(lambda e: e.activation(out=sc['wa'][:], in_=sc['wa'][:], func=AF.Exp, bias=-math.log(16.0)), ['wa'], ['wa'])
            act(lambda e: e.activation(out=sc['em'][:], in_=sc['mt'][:], func=AF.Exp, scale=-1.0), ['mt'], ['em'])
            v4 = lambda t_: t_[:].rearrange("b (h d) -> b h d", h=4)
            bc = lambda nm: sc[nm][:].unsqueeze(2).to_broadcast([B, 4, 256])
            dve(lambda e: e.tensor_tensor(out=p1[:], in0=qtm[:], in1=ktm[:], op=ALU.mult), ['qtm', 'ktm'], ['p1'])
            dve(lambda e: e.tensor_reduce(out=sc['qk'][:], in_=v4(p1), axis=AX.X, op=ALU.add), ['p1'], ['qk'])
            dve(lambda e: e.tensor_tensor(out=p2[:], in0=qtm[:], in1=n0[:], op=ALU.mult), ['qtm', 'n0'], ['p2'])
            dve(lambda e: e.tensor_reduce(out=sc['qn'][:], in_=v4(p2), axis=AX.X, op=ALU.add), ['p2'], ['qn'])
            tt('s', 'qk', 'wa', ALU.mult)
            tt('t', 'wi', 'qn', ALU.mult)
            tt('nq', 's', 't', ALU.add)
            act(lambda e: e.activation(out=sc['nq'][:], in_=sc['nq'][:], func=AF.Abs), ['nq'], ['nq'])
            tt('rd', 'nq', 'em', ALU.max)
            dve(lambda e: e.reciprocal(out=sc['rd'][:], in_=sc['rd'][:]), ['rd'], ['rd'])
            dve(lambda e: e.tensor_tensor(out=v4(kws), in0=v4(ktm), in1=bc('wa'), op=ALU.mult), ['ktm', 'wa'], ['kws'])
            dve(lambda e: e.tensor_tensor(out=v4(p1), in0=v4(n0), in1=bc('wi'), op=ALU.mult), ['n0', 'wi'], ['p1'])
            dve(lambda e: e.tensor_tensor(out=p1[:], in0=p1[:], in1=kws[:], op=ALU.add), ['p1', 'kws'], ['p1'])
            S.dma('sp', lambda e: e.dma_start(out=O['n_s'][:, :], in_=p1[:]), reads=['p1'], writes=[('dram', 'n_s')])
            dve(lambda e: e.tensor_tensor(out=WD[:], in0=self.identf[:B, :B].unsqueeze(2).to_broadcast([B, B, 4]),
                                          in1=sc['wi'][:].unsqueeze(1).to_broadcast([B, B, 4]), op=ALU.mult), ['identf', 'wi'], ['WD'])
            ps, pk = self.psum_next()
            S.op('pe', lambda e, ps=ps: e.matmul(ps[:, 0:B * 4], lhsT=ones32[:], rhs=WD[:].rearrange("k b h -> k (b h)"), start=True, stop=True),
                 reads=['ones32', 'WD'], writes=[pk])
            dve(lambda e, ps=ps: e.tensor_copy(out=Wb[:], in_=ps[:, 0:B * 4]), [pk], ['Wb'])
            self.pa_i = 0
            pq = [self.psum_next() for _ in range(4)]
            def ms_load(b):
                i2 = b % 3
                for h in range(4):
                    S.dma('sp', lambda e, i2=i2, b=b, h=h: e.dma_start(out=C32[i2][:, h, :, :], in_=I['mC'][b, h].rearrange("(dh p) v -> p dh v", p=128)),
                          writes=[('C32', i2)])

            def ms_compute(b):
                i2 = b % 2
                i3 = b % 3
                S.op('pool', lambda e, i2=i2, i3=i3: e.tensor_copy(out=C16[i2][:], in_=C32[i3][:]), reads=[('C32', i3)], writes=[('C16', i2)])
                act(lambda e, i2=i2, b=b: e.activation(out=Km[i2][:], in_=kws[:], func=AF.Copy, scale=self.identf[:B, b:b + 1]),
                    ['kws', 'identf'], [('Km', i2)])
                for h in range(4):
                    psq, pqk = pq[h]
                    for dh in range(2):
                        S.op('pe', lambda e, psq=psq, b=b, h=h, dh=dh, i2=i2: e.matmul(
                            psq[:B, 0:256], lhsT=Qm[:, h * 2 + dh, b, :], rhs=C16[i2][:, h, dh, :],
                            start=(b == 0 and dh == 0), stop=(b == B - 1 and dh == 1)), reads=['Qm', ('C16', i2)], writes=[pqk])
                    for dh in range(2):
                        psc, pck = self.psumc_next()
                        S.op('pe', lambda e, psc=psc, i2=i2, h=h, dh=dh: e.matmul(
                            psc[:, 0:256], lhsT=Km[i2][:, h * 256 + dh * 128:h * 256 + (dh + 1) * 128], rhs=vtm[:, h * 256:(h + 1) * 256],
                            start=True, stop=True), reads=[('Km', i2), 'vtm'], writes=[pck])
                        dve(lambda e, psc=psc, i2=i2, i3=i3, b=b, h=h, dh=dh: e.scalar_tensor_tensor(
                            out=Co[i2][:, h, dh, :], in0=C32[i3][:, h, dh, :], scalar=Wb[:, b * 4 + h:b * 4 + h + 1], in1=psc[:, 0:256],
                            op0=ALU.mult, op1=ALU.add), [pck, ('C32', i3), 'Wb'], [('Co', i2)])

            def ms_store(b):
                i2 = b % 2
                for h in range(4):
                    S.dma('sp', lambda e, i2=i2, b=b, h=h: e.dma_start(out=O['C_s'][b, h].rearrange("(dh p) v -> p dh v", p=128), in_=Co[i2][:, h, :, :]),
                          reads=[('Co', i2)], writes=[('dram', 'C_s', b, h)])

            ms_load(0)
            ms_load(1)
            for b in range(B):
                if b + 2 < B:
                    ms_load(b + 2)
                ms_compute(b)
                ms_store(b)
            dve(lambda e: e.tensor_tensor(out=v4(numt), in0=v4(vtm), in1=bc('s'), op=ALU.mult), ['vtm', 's'], ['numt'])
            for h in range(4):
                psq, pqk = pq[h]
                dve(lambda e, psq=psq, h=h: e.scalar_tensor_tensor(out=numt[:, h * 256:(h + 1) * 256], in0=psq[:B, 0:256], scalar=sc['wi'][:, h:h + 1],
                                                                 in1=numt[:, h * 256:(h + 1) * 256], op0=ALU.mult, op1=ALU.add),
                    [pqk, 'wi', 'numt'], ['numt'])
            dve(lambda e: e.tensor_tensor(out=v4(numt), in0=v4(numt), in1=bc('rd'), op=ALU.mult), ['numt', 'rd'], ['numt'])
            dve(lambda e: e.tensor_tensor(out=p2[:], in0=numt[:], in1=numt[:], op=ALU.mult), ['numt'], ['p2'])
            dve(lambda e: e.tensor_reduce(out=sc['ss'][:], in_=v4(p2), axis=AX.X, op=ALU.add), ['p2'], ['ss'])
            dve(lambda e: e.tensor_scalar(out=sc['ss'][:], in0=sc['ss'][:], scalar1=1.0 / 256, scalar2=EPS, op0=ALU.mult, op1=ALU.add), ['ss'], ['ss'])
            act(lambda e: e.activation(out=sc['rs'][:], in_=sc['ss'][:], func=AF.Sqrt), ['ss'], ['rs'])
            dve(lambda e: e.reciprocal(out=sc['rs'][:], in_=sc['rs'][:]), ['rs'], ['rs'])
            dve(lambda e: e.tensor_tensor(out=v4(numt), in0=v4(numt), in1=bc('rs'), op=ALU.mult), ['numt', 'rs'], ['numt'])
            dve(lambda e: e.tensor_tensor(out=numt[:], in0=numt[:], in1=GH[:], op=ALU.mult), ['numt', 'GH'], ['numt'])
            act(lambda e: e.activation(out=otm[:], in_=otm[:], func=AF.Sigmoid), ['otm'], ['otm'])
            dve(lambda e: e.tensor_tensor(out=mlb[:], in0=numt[:], in1=otm[:], op=ALU.mult), ['numt', 'otm'], ['mlb'])
            self.transpose_into(mlb, 'mlb', B, 8, mlT, 'mlTs', 0)
            S.dma('sp', lambda e: e.dma_start(out=R['BR1'][:, HPR:HT].rearrange("(k p) t -> p k t", p=128), in_=mlT[:]),
                  reads=['mlTs'], writes=[('dram', 'brt1s')])
            S.barrier()
            S.flush()

    def psumc_next(self):
        i = 4 + (self.pc_i % 2)
        self.pc_i += 1
        return self.PA[i], ('pa', i)

    def final_norm(self, xsrc, ydst, gvec, nrows=NTOK):
        S = self.S
        with ExitStack() as es:
            gb = self.sb(es, [128, D], F32, "gbf")
            xts = [self.sb(es, [128, D], F32, "xtf") for _ in range(2)]
            ys = [self.sb(es, [128, D], F32, "yf") for _ in range(2)]
            st = self.sb(es, [128, 8], F32, "stf")
            self.load_gain(gb, gvec, 'gbf')
            tiles = token_tiles(0, nrows)

            def fn_load(i):
                r0, nr = tiles[i]
                xt = xts[i % 2]
                xk = ('xtf', i % 2)
                S.dma('sp', lambda e: e.dma_start(out=xt[:nr, :], in_=xsrc[r0:r0 + nr, :]), writes=[xk])
            fn_load(0)
            for i, (r0, nr) in enumerate(tiles):
                if i + 1 < len(tiles):
                    fn_load(i + 1)
                xt = xts[i % 2]
                y = ys[i % 2]
                xk = ('xtf', i % 2)
                yk = ('yf', i % 2)
                S.op('act', lambda e, y=y, xt=xt, nr=nr: e.activation(out=y[:nr, :], in_=xt[:nr, :], func=AF.Square,
                                                                     accum_out=st[:nr, 0:1]), reads=[xk], writes=[yk, 'stf'])
                S.op('dve', lambda e, nr=nr: e.tensor_scalar(out=st[:nr, 1:2], in0=st[:nr, 0:1], scalar1=1.0 / D, scalar2=EPS,
                                                             op0=ALU.mult, op1=ALU.add), reads=['stf'], writes=['stf'])
                S.op('act', lambda e, nr=nr: e.activation(out=st[:nr, 2:3], in_=st[:nr, 1:2], func=AF.Sqrt), reads=['stf'], writes=['stf'])
                S.op('dve', lambda e, nr=nr: e.reciprocal(out=st[:nr, 3:4], in_=st[:nr, 2:3]), reads=['stf'], writes=['stf'])
                S.op('dve', lambda e, y=y, xt=xt, nr=nr: e.scalar_tensor_tensor(out=y[:nr, :], in0=xt[:nr, :], scalar=st[:nr, 3:4],
                                                                               in1=gb[:nr, :], op0=ALU.mult, op1=ALU.mult),
                     reads=[xk, 'stf', 'gbf'], writes=[yk])
                S.dma('sp', lambda e, y=y, r0=r0, nr=nr: e.dma_start(out=ydst[r0:r0 + nr, :], in_=y[:nr, :]), reads=[yk],
                      writes=[('dram', 'y', r0)])
            S.barrier()
            S.flush()

    def build(self):
        nc = self.nc
        dbg = self.dbg
        self.I = I = {}
        I['x'] = self.dram_in('x', [NTOK, D])
        I['mem'] = self.dram_in('mem', [256, D])
        I['cache_k'] = self.dram_in('cache_k', [NSM, 256, 1024])
        I['cache_v'] = self.dram_in('cache_v', [NSM, 256, 1024])
        I['s5r'] = self.dram_in('s5r', [NSM, 4096])
        I['s5i'] = self.dram_in('s5i', [NSM, 4096])
        I['mC'] = self.dram_in('mC', [NSM, 4, 256, 256])
        I['mN'] = self.dram_in('mN', [NSM, 1024])
        I['mM'] = self.dram_in('mM', [NSM, 4])
        I['flag'] = self.dram_in('flag', [128, 2])
        for nm, shp in [('g_ffn1', [D]), ('w1_gate', [D, FF]), ('w1_up', [D, FF]), ('w1_down', [FF, D]),
                        ('g_mix', [D]), ('w_in', [D, DIN]),
                        ('s5_lambda_re', [64, 64]), ('s5_lambda_im', [64, 64]), ('s5_log_step', [64]),
                        ('s5_b_re', [64, 64, 16]), ('s5_b_im', [64, 64, 16]), ('s5_c_re', [64, 16, 64]),
                        ('s5_c_im', [64, 16, 64]), ('s5_d', [64, 16]), ('w_s5_glu', [1024, 1024]),
                        ('b_igate', [4]), ('b_fgate', [4]), ('g_mlstm_head', [1024]), ('g_mem', [D]),
                        ('w_mem_k', [D, 1024]), ('w_mem_v', [D, 1024]),
                        ('w_br_s5', [1024, D]), ('w_br_ml', [1024, D]), ('w_br_xa', [1024, D]), ('w_out', [D, D]),
                        ('g_ffn2', [D]), ('w2_gate', [D, FF]), ('w2_up', [D, FF]), ('w2_down', [FF, D]),
                        ('g_final', [D])]:
            I[nm] = self.dram_in(nm, shp)
        self.O = O = {}
        O['y'] = self.dram_out('y', [HT, D])
        O['mk'] = self.dram_out('mk', [256, 1024])
        O['mv'] = self.dram_out('mv', [256, 1024])
        O['s5r_p'] = self.dram_out('s5r_p', [64, 64])
        O['s5i_p'] = self.dram_out('s5i_p', [64, 64])
        O['C_p'] = self.dram_out('C_p', [4, 256, 256])
        O['n_p'] = self.dram_out('n_p', [4, 256])
        O['m_p'] = self.dram_out('m_p', [4, 1])
        O['s5r_s'] = self.dram_out('s5r_s', [NSM, 4096])
        O['s5i_s'] = self.dram_out('s5i_s', [NSM, 4096])
        O['C_s'] = self.dram_out('C_s', [NSM, 4, 256, 256])
        O['n_s'] = self.dram_out('n_s', [NSM, 1024])
        O['m_s'] = self.dram_out('m_s', [NSM, 4])
        scr = self.dram_out if dbg else (lambda n, s, d=F32: self.dram_scr(n, s, d))
        self.R = R = {}
        R['X1'] = scr('X1', [NTOK, D], F32)
        R['X1h'] = scr('X1h', [HT, D], F32)
        R['X2'] = scr('X2', [HT, D], F32)
        R['X3'] = scr('X3', [HT, D], F32)
        pscr = self.dram_in if self.mixtest else scr
        for nm in ['UT', 'KT']:
            R[nm] = pscr(nm, [1024, NTOK], BF16)
        for nm in ['QTo', 'QXTo']:
            R[nm] = pscr(nm, [1024, HT], BF16)
        R['OTMo'] = pscr('OTMo', [HT, 1024], F32)
        for nm in ['KTM', 'VTM']:
            R[nm] = pscr(nm, [NTOK, 1024], BF16)
        R['GTM'] = pscr('GTM', [NTOK, 8], F32)
        R['IGR'] = pscr('IGR', [4, NTOK], F32)
        R['FGR'] = pscr('FGR', [4, NTOK], F32)
        R['MKT'] = pscr('MKT', [1024, 256], BF16)
        R['MVB'] = self.dram_in('MVB', [256, 1024]) if self.mixtest else O['mv']
        R['BRT'] = scr('BRT', [3, 1024, NTOK], BF16)
        R['XAS'] = scr('XAS', [NSM, 1024], F32)
        R['S5G'] = scr('S5G', [1024, HT], BF16)
        R['BR0'] = scr('BR0', [1024, HT], BF16)
        R['BR2'] = scr('BR2', [1024, HT], BF16)
        R['BR1'] = scr('BR1', [1024, HT], BF16)
        with ExitStack() as es:
            self.S = S = Sched(nc, es)
            S.block = es.enter_context(nc.Block())
            self.PA = [es.enter_context(nc.psum_tensor("pa%d" % i, [128, 512], F32)) for i in range(6)]
            self.PB = [es.enter_context(nc.psum_tensor("pb%d" % i, [128, 1024], BF16)) for i in range(2)]
            self.pa_i = 0
            self.pb_i = 0
            self.ecnt = 0
            self.pc_i = 0
            identf = self.sb(es, [128, 128], F32, "identf")
            self.identb = self.sb(es, [128, 128], BF16, "identb")
            self.identf = identf
            S.op('pool', lambda e: e.memset(identf[:], 1.0), writes=['identf'])
            S.op('pool', lambda e: e.affine_select(out=identf[:], in_=identf[:], pattern=[[-1, 128]], compare_op=ALU.is_equal,
                                                   fill=0.0, base=0, channel_multiplier=1), reads=['identf'], writes=['identf'])
            S.op('dve', lambda e: e.tensor_copy(out=self.identb[:], in_=identf[:]), reads=['identf'], writes=['identb'])
            S.barrier()
            st = self.stage
            if not self.mixtest:
                for blk in BLOCKS:
                    self.ffn(es, I['x'], R['X1'], I['g_ffn1'], I['w1_gate'], I['w1_up'], I['w1_down'], blk)
            if st >= 2 and not self.mixtest:
                self.inproj()
                self.memkv()
            if st >= 3:
                self.mixers()
            if st >= 4:
                self.merge()
                self.ffn(es, R['X2'], R['X3'], I['g_ffn2'], I['w2_gate'], I['w2_up'], I['w2_down'], (0, HT))
                self.final_norm(R['X3'], O['y'], I['g_final'], nrows=HT)
            S.barrier()
            S.flush()
        return nc


_NC_CACHE = {}


def _get_nc():
    if 'nc' not in _NC_CACHE:
        _NC_CACHE['nc'] = Builder(stage=99, dbg=False).build()
    return _NC_CACHE['nc']


_WEIGHTS = ['g_ffn1', 'w1_gate', 'w1_up', 'w1_down', 'g_mix', 'w_in', 's5_lambda_re', 's5_lambda_im', 's5_log_step',
            's5_b_re', 's5_b_im', 's5_c_re', 's5_c_im', 's5_d', 'w_s5_glu', 'b_igate', 'b_fgate', 'g_mlstm_head', 'g_mem',
            'w_mem_k', 'w_mem_v', 'w_br_s5', 'w_br_ml', 'w_br_xa', 'w_out', 'g_ffn2', 'w2_gate', 'w2_up', 'w2_down']


def kernel(**inputs):
    f = lambda a: np.ascontiguousarray(np.asarray(a, dtype=np.float32))
    nc = _get_nc()
    shared = {nm: f(inputs[nm])[0] for nm in _WEIGHTS}
    shared['g_final'] = f(inputs['g_final'])
    xp = f(inputs['x_prompt'])
    xs = f(inputs['x_sample'])
    memp = f(inputs['mem_prompt'])
    ck = f(inputs['cache_mem_k'])[0]
    cv = f(inputs['cache_mem_v'])[0]
    s5r = f(inputs['state_s5_re'])[0]
    s5i = f(inputs['state_s5_im'])[0]
    mC = f(inputs['state_mlstm_C'])[0]
    mN = f(inputs['state_mlstm_n'])[0]
    mM = f(inputs['state_mlstm_m'])[0]
    in_maps = []
    for i in range(8):
        j = i % 4
        s0 = j * 2 * NSM + (i // 4) * NSM
        sl = slice(s0, s0 + NSM)
        m = dict(shared)
        m['x'] = np.ascontiguousarray(np.concatenate([xp[j], xs[sl, 0, :]], axis=0))
        m['mem'] = memp[j]
        m['cache_k'] = np.ascontiguousarray(ck[sl].reshape(NSM, 256, 1024))
        m['cache_v'] = np.ascontiguousarray(cv[sl].reshape(NSM, 256, 1024))
        m['s5r'] = np.ascontiguousarray(s5r[sl].reshape(NSM, 4096))
        m['s5i'] = np.ascontiguousarray(s5i[sl].reshape(NSM, 4096))
        m['mC'] = np.ascontiguousarray(mC[sl])
        m['mN'] = np.ascontiguousarray(mN[sl].reshape(NSM, 1024))
        m['mM'] = np.ascontiguousarray(mM[sl])
        fl = np.zeros((128, 2), np.float32)
        fl[:, 0 if i < 4 else 1] = 1.0
        m['flag'] = fl
        in_maps.append(m)
    res = run_bass_kernel_spmd(nc, in_maps, core_ids=list(range(8)))
    r = res.results
    g = lambda nm, j: np.asarray(r[j][nm], dtype=np.float32)
    order = [c for j in range(4) for c in (j, j + 4)]
    y_prompt = np.stack([np.concatenate([g('y', j)[:HPR], g('y', j + 4)[:HPR]], axis=0) for j in range(4)])
    y_sample = np.concatenate([g('y', c)[HPR:HT] for c in order], axis=0).reshape(8 * NSM, 1, D)
    mk = np.stack([g('mk', j).reshape(256, 4, 256) for j in range(4)])[None]
    mv = np.stack([g('mv', j).reshape(256, 4, 256) for j in range(4)])[None]
    s5r_p = np.stack([g('s5r_p', j + 4) for j in range(4)])[None]
    s5i_p = np.stack([g('s5i_p', j + 4) for j in range(4)])[None]
    C_p = np.stack([g('C_p', j + 4) for j in range(4)])[None]
    n_p = np.stack([g('n_p', j + 4) for j in range(4)])[None]
    m_p = np.stack([g('m_p', j)[:, 0] for j in range(4)])[None]
    s5r_s = np.concatenate([g('s5r_s', c).reshape(NSM, 64, 64) for c in order], axis=0)[None]
    s5i_s = np.concatenate([g('s5i_s', c).reshape(NSM, 64, 64) for c in order], axis=0)[None]
    C_s = np.concatenate([g('C_s', c) for c in order], axis=0)[None]
    n_s = np.concatenate([g('n_s', c).reshape(NSM, 4, 256) for c in order], axis=0)[None]
    m_s = np.concatenate([g('m_s', c) for c in order], axis=0)[None]
    return (y_prompt, y_sample, mk, mv, s5r_p, s5i_p, C_p, n_p, m_p, s5r_s, s5i_s, C_s, n_s, m_s)
```
